# Optimizing a Trainium2 kernel written in Bass

```python
import jax, jax.numpy as jnp
from jax import lax
import numpy as np

D_MODEL = 1024
BATCH = 8
SEQ = 4096
DEPTH = 1

CHUNK = 64
LEFT_CHUNKS = 8
BAND = (LEFT_CHUNKS + 1) * CHUNK
HEAD_DIM = 64
RWKV_HEADS = 8
ATTN_HEADS = 8
RWKV_WIDTH = RWKV_HEADS * HEAD_DIM
ATTN_WIDTH = ATTN_HEADS * HEAD_DIM
DECAY_RANK = 64
ICLR_RANK = 64
GATE_RANK = 128
RWKV_COLS = 3 * RWKV_WIDTH + DECAY_RANK + ICLR_RANK + GATE_RANK
ATTN_COLS = 3 * ATTN_WIDTH
GATE_COLS = 2 * D_MODEL
IN_COLS = RWKV_COLS + ATTN_COLS + GATE_COLS
MAX_REL = 128
REL_TABLE = (CHUNK - 1) + MAX_REL + 1
D_FF = 2816
CONV_W = 3
PLE_DIM = 256
NORM_EPS = 1e-6
GN_EPS = 64e-5
NEG_INF = -1e30

kernel_name = "hybrid_rwkv7_chunkattn_convglu_block"


def rms_norm(x, g):
    xf = x.astype(jnp.float32)
    y = xf * lax.rsqrt(jnp.mean(xf * xf, axis=-1, keepdims=True) + NORM_EPS)
    return (y * g.astype(jnp.float32)).astype(x.dtype)


def token_shift(z):
    return jnp.pad(z, ((0, 0), (1, 0), (0, 0)))[:, :-1]


def rwkv7_time_mix(z, mu, w0, w2, a0, a2, g2, k_k, k_a, r_k, lnx_w, lnx_b):
    B, T, _ = z.shape
    H, N = RWKV_HEADS, HEAD_DIM
    zs = z + (token_shift(z) - z) * mu
    W = RWKV_WIDTH
    r, k, v, wd, ad, gd = jnp.split(
        zs, [W, 2 * W, 3 * W, 3 * W + DECAY_RANK, 3 * W + DECAY_RANK + ICLR_RANK], axis=-1)
    f32 = jnp.float32
    w_log = -jax.nn.softplus(-(w0 + jnp.tanh(wd) @ w2).astype(f32)) - 0.5
    decay = jnp.exp(-jnp.exp(w_log))
    a = jax.nn.sigmoid((a0 + ad @ a2).astype(f32))
    g = jax.nn.sigmoid(gd) @ g2
    r, k, v = r.astype(f32), k.astype(f32), v.astype(f32)
    kk = (k * k_k.astype(f32)).reshape(B, T, H, N)
    kk = kk / jnp.maximum(jnp.sqrt(jnp.sum(kk * kk, axis=-1, keepdims=True)), 1e-12)
    k = k * (1.0 + (a - 1.0) * k_a.astype(f32))
    heads = lambda t: t.reshape(B, T, H, N)
    r, k, v, a, decay = heads(r), heads(k), heads(v), heads(a), heads(decay)
    tm = lambda t: jnp.moveaxis(t, 1, 0)

    def step(S, inp):
        r_t, d_t, k_t, v_t, kk_t, a_t = inp
        sa = jnp.einsum('bhij,bhj->bhi', S, -kk_t)
        S = (S * d_t[:, :, None, :]
             + sa[..., :, None] * (kk_t * a_t)[..., None, :]
             + v_t[..., :, None] * k_t[..., None, :])
        y_t = jnp.einsum('bhij,bhj->bhi', S, r_t)
        return S, y_t

    S0 = jnp.zeros((B, H, N, N), f32)
    _, y = lax.scan(step, S0, (tm(r), tm(decay), tm(k), tm(v), tm(kk), tm(a)))
    y = jnp.moveaxis(y, 0, 1)
    mean = jnp.mean(y, axis=-1, keepdims=True)
    var = jnp.mean(jnp.square(y - mean), axis=-1, keepdims=True)
    y = (y - mean) * lax.rsqrt(var + GN_EPS)
    y = y * lnx_w.astype(f32).reshape(H, N) + lnx_b.astype(f32).reshape(H, N)
    bonus = jnp.sum(r * k * r_k.astype(f32), axis=-1, keepdims=True) * v
    y = (y + bonus).reshape(B, T, W)
    return (y * g.astype(f32)).astype(z.dtype)


def chunk_attention(q, k, v, rel_bias):
    B, T, H, Dh = q.shape
    NC = T // CHUNK
    pad = LEFT_CHUNKS * CHUNK
    kp = jnp.pad(k, ((0, 0), (pad, 0), (0, 0), (0, 0)))
    vp = jnp.pad(v, ((0, 0), (pad, 0), (0, 0), (0, 0)))
    qc = jnp.moveaxis(q.reshape(B, NC, CHUNK, H, Dh), 1, 0)
    q_off = jnp.arange(CHUNK)
    k_off = jnp.arange(BAND) - pad
    rel = q_off[:, None] - k_off[None, :]
    idx = jnp.clip(rel, -(CHUNK - 1), MAX_REL) + (CHUNK - 1)
    bias = rel_bias.astype(jnp.float32)[:, idx]
    scale = HEAD_DIM ** -0.5

    def one_chunk(args):
        c, qb = args
        start = c * CHUNK
        kb = lax.dynamic_slice_in_dim(kp, start, BAND, axis=1)
        vb = lax.dynamic_slice_in_dim(vp, start, BAND, axis=1)
        s = jnp.einsum('bqhd,bkhd->bhqk', qb, kb).astype(jnp.float32) * scale + bias
        valid = (k_off + start) >= 0
        s = jnp.where(valid[None, None, None, :], s, NEG_INF)
        pr = jax.nn.softmax(s, axis=-1).astype(vb.dtype)
        return jnp.einsum('bhqk,bkhd->bqhd', pr, vb)

    o = lax.map(one_chunk, (jnp.arange(NC), qc))
    return jnp.moveaxis(o, 0, 1).reshape(B, T, H * Dh)


def causal_dwconv(x, w, b):
    y = lax.conv_general_dilated(
        x, w[:, None, :].astype(x.dtype), window_strides=(1,), padding=[(CONV_W - 1, 0)],
        dimension_numbers=('NWC', 'WIO', 'NWC'), feature_group_count=x.shape[-1])
    return y + b


def conv_glu_ffn(h, w_up, cw, cb, w_down):
    u = h @ w_up
    a, gv = jnp.split(u, 2, axis=-1)
    a = causal_dwconv(a, cw, cb)
    return (jax.nn.gelu(a, approximate=False) * gv) @ w_down


def setup_inputs(seed: int = 0) -> dict:
    key = jax.random.key(seed)
    ks = iter(jax.random.split(key, 40))
    f32 = jnp.float32
    nrm = lambda shape, s: jax.random.normal(next(ks), shape, f32) * s
    ones = lambda shape, s: 1.0 + jax.random.normal(next(ks), shape, f32) * s
    L = DEPTH
    return {
        "x": nrm((BATCH, SEQ, D_MODEL), 1.0),
        "p": nrm((L, BATCH, SEQ, PLE_DIM), 1.0),
        "ln1_g": ones((L, D_MODEL), 0.01),
        "w_in": nrm((L, D_MODEL, IN_COLS), D_MODEL ** -0.5),
        "mix_mu": jax.random.uniform(next(ks), (L, RWKV_COLS), f32, 0.0, 1.0),
        "w0": nrm((L, RWKV_WIDTH), 0.5),
        "w2": nrm((L, DECAY_RANK, RWKV_WIDTH), DECAY_RANK ** -0.5),
        "a0": nrm((L, RWKV_WIDTH), 0.1),
        "a2": nrm((L, ICLR_RANK, RWKV_WIDTH), ICLR_RANK ** -0.5),
        "g2": nrm((L, GATE_RANK, RWKV_WIDTH), GATE_RANK ** -0.5),
        "k_k": ones((L, RWKV_WIDTH), 0.1),
        "k_a": ones((L, RWKV_WIDTH), 0.1),
        "r_k": nrm((L, RWKV_HEADS, HEAD_DIM), 0.1),
        "lnx_w": ones((L, RWKV_WIDTH), 0.01),
        "lnx_b": nrm((L, RWKV_WIDTH), 0.01),
        "rel_bias": nrm((L, ATTN_HEADS, REL_TABLE), 0.1),
        "gate_b": nrm((L, GATE_COLS), 0.01),
        "w_br_rwkv": nrm((L, RWKV_WIDTH, D_MODEL), RWKV_WIDTH ** -0.5),
        "w_br_attn": nrm((L, ATTN_WIDTH, D_MODEL), ATTN_WIDTH ** -0.5),
        "w_o": nrm((L, D_MODEL, D_MODEL), D_MODEL ** -0.5),
        "ln2_g": ones((L, D_MODEL), 0.01),
        "w_ffn_up": nrm((L, D_MODEL, 2 * D_FF), D_MODEL ** -0.5),
        "conv_w": nrm((L, CONV_W, D_FF), CONV_W ** -0.5),
        "conv_b": nrm((L, D_FF), 0.01),
        "w_ffn_down": nrm((L, D_FF, D_MODEL), D_FF ** -0.5),
        "ln3_g": ones((L, D_MODEL), 0.01),
        "w_ple": nrm((L, PLE_DIM, D_MODEL), PLE_DIM ** -0.5),
        "w_pg": nrm((L, D_MODEL, D_MODEL), D_MODEL ** -0.5),
        "b_pg": nrm((L, D_MODEL), 0.01),
        "lnf_g": ones((D_MODEL,), 0.01),
    }


def reference(x, p, ln1_g, w_in, mix_mu, w0, w2, a0, a2, g2, k_k, k_a, r_k, lnx_w, lnx_b,
              rel_bias, gate_b, w_br_rwkv, w_br_attn, w_o, ln2_g, w_ffn_up, conv_w, conv_b,
              w_ffn_down, ln3_g, w_ple, w_pg, b_pg, lnf_g):
    B, T, _ = x.shape
    for i in range(DEPTH):
        h = rms_norm(x, ln1_g[i])
        z = h @ w_in[i]
        z_rwkv, z_q, z_k, z_v, z_gate = jnp.split(
            z, [RWKV_COLS, RWKV_COLS + ATTN_WIDTH, RWKV_COLS + 2 * ATTN_WIDTH,
                RWKV_COLS + ATTN_COLS], axis=-1)
        o_a = rwkv7_time_mix(z_rwkv, mix_mu[i], w0[i], w2[i], a0[i], a2[i], g2[i],
                             k_k[i], k_a[i], r_k[i], lnx_w[i], lnx_b[i])
        heads = lambda t: t.reshape(B, T, ATTN_HEADS, HEAD_DIM)
        o_b = chunk_attention(heads(z_q), heads(z_k), heads(z_v), rel_bias[i])
        gates = jax.nn.sigmoid(z_gate + gate_b[i])
        g_a, g_b = jnp.split(gates, 2, axis=-1)
        merged = g_a * (o_a @ w_br_rwkv[i]) + g_b * (o_b @ w_br_attn[i])
        x = x + merged @ w_o[i]
        h = rms_norm(x, ln2_g[i])
        x = x + conv_glu_ffn(h, w_ffn_up[i], conv_w[i], conv_b[i], w_ffn_down[i])
        h = rms_norm(x, ln3_g[i])
        x = x + (p[i] @ w_ple[i]) * jax.nn.sigmoid(h @ w_pg[i] + b_pg[i])
    return rms_norm(x, lnf_g)
```

```python
import contextlib
import os
CUT = int(os.environ.get("P2CUT", "99"))
import numpy as np
import concourse.bass as bass
import concourse.mybir as mybir
from concourse.bass_utils import run_bass_kernel_spmd

F32 = mybir.dt.float32
BF16 = mybir.dt.bfloat16
AF = mybir.ActivationFunctionType
ALU = mybir.AluOpType
AX = mybir.AxisListType

COMPUTE = ("pe", "act", "dve", "pool")
ENGS = ("pe", "act", "dve", "pool", "sp")


class Buf:
    def __init__(self, name, t=None, is_dram=False):
        self.name = name
        self.t = t
        self.is_dram = is_dram
        self.w = {}
        self.r = {}
        self.dma_key = None
        self.dma_cnt = 0

    def __getitem__(self, k):
        return self.t[k]


class Prog:
    def __init__(self, nc):
        self.nc = nc
        self.stack = contextlib.ExitStack()
        self.semstack = contextlib.ExitStack()
        self.dma_keys = {}
        self.ops = {e: [] for e in ENGS}
        self.cnt = {e: 0 for e in COMPUTE}
        self.pending = {e: False for e in COMPUTE}
        self.waited = {e: {} for e in ENGS}
        self.sems = {}
        self.nbuf = 0

    def sem(self, key):
        if key not in self.sems:
            self.sems[key] = self.semstack.enter_context(self.nc.semaphore("s_" + key))
        return self.sems[key]

    def sbuf(self, name, shape, dtype):
        self.nbuf += 1
        name = "%s_%d" % (name, self.nbuf)
        t = self.stack.enter_context(self.nc.sbuf_tensor(name, list(shape), dtype))
        return Buf(name, t)

    def psum(self, name, shape, dtype):
        self.nbuf += 1
        name = "%s_%d" % (name, self.nbuf)
        t = self.stack.enter_context(self.nc.psum_tensor(name, list(shape), dtype))
        return Buf(name, t)

    def dram(self, name, shape, dtype, kind="Internal"):
        t = self.nc.dram_tensor(name, list(shape), dtype, kind=kind)
        return Buf(name, t, is_dram=True)

    def _collect(self, eng, reads, writes):
        need = {}

        def add(k, v, same_ok):
            if k == eng and not same_ok:
                return
            if need.get(k, 0) < v:
                need[k] = v

        for b in reads:
            for k, v in b.w.items():
                add(k, v, True)
        for b in writes:
            for k, v in b.w.items():
                add(k, v, False)
            for k, v in b.r.items():
                add(k, v, False)
        out = []
        wd = self.waited[eng]
        for k, v in need.items():
            if wd.get(k, 0) < v:
                wd[k] = v
                out.append((k, v))
        return out

    def _record(self, key, val, reads, writes):
        for b in reads:
            if b.r.get(key, 0) < val:
                b.r[key] = val
        for b in writes:
            b.w = {key: val}
            b.r = {}

    def op(self, eng, fn, reads=(), writes=(), signal=True):
        waits = self._collect(eng, reads, writes)
        if signal:
            self.cnt[eng] += 1
            val = self.cnt[eng]
            self.pending[eng] = False
        else:
            val = self.cnt[eng] + 1
            self.pending[eng] = True
        self.ops[eng].append((waits, fn, (eng, 1) if signal else None))
        self._record(eng, val, reads, writes)

    def dma(self, q, out_ap, in_ap, src, dst, sem_on=None, **kw):
        waits = self._collect(q, [src], [dst])
        sb = sem_on if sem_on is not None else (src if dst.is_dram else dst)
        if sb.dma_key is None:
            pool = self.__dict__.setdefault("sem_pool", [])
            sb.dma_sw = (q == "pool")
            if pool and not sb.dma_sw:
                sb.dma_key, sb.dma_cnt = pool.pop()
            else:
                sb.dma_key = "d%d" % len(self.__dict__.setdefault("all_dma_keys", []))
                self.all_dma_keys.append(sb.dma_key)
            if not sb.dma_sw:
                self.__dict__.setdefault("phase_owners", []).append(sb)
        assert sb.dma_sw == (q == "pool"), "buffer %s mixes software and hardware DGE" % sb.name
        key = sb.dma_key
        sb.dma_cnt += 16
        val = sb.dma_cnt
        self.dma_keys[key] = val
        self.ops[q].append((waits, lambda e: e.dma_start(out=out_ap, in_=in_ap, **kw), (key, 16)))
        self._record(key, val, [src], [dst])

    def wait_all(self, eng, bufs):
        waits = self._collect(eng, [], list(bufs))
        if waits:
            self.ops[eng].append((waits, None, None))

    def barrier(self):
        ev = {e: self.cnt[e] for e in COMPUTE if self.cnt[e] > 0}
        ev.update(self.dma_keys)
        for e in ENGS:
            waits = []
            for k, v in ev.items():
                if k == e:
                    continue
                if self.waited[e].get(k, 0) < v:
                    self.waited[e][k] = v
                    waits.append((k, v))
            if waits:
                self.ops[e].append((waits, None, None))

    def emit_phase(self):
        self.phase_idx = getattr(self, "phase_idx", 0) + 1
        with self.nc.named_scope("phase%d" % self.phase_idx):
            self.emit()
        self.ops = {e: [] for e in ENGS}
        for sb in self.__dict__.get("phase_owners", []):
            self.__dict__.setdefault("sem_pool", []).append((sb.dma_key, sb.dma_cnt))
        self.phase_owners = []

    def emit(self):
        nc = self.nc
        for e in COMPUTE:
            assert not self.pending[e], "engine %s ends with an unsignaled op" % e
        handles = {"pe": "tensor", "act": "scalar", "dve": "vector", "pool": "gpsimd", "sp": "sync"}
        for k in list(self.waited["pe"].keys()) + list(COMPUTE):
            self.sem(k)
        for e in ENGS:
            for (waits, fn, inc) in self.ops[e]:
                for k, v in waits:
                    self.sem(k)
                if inc is not None:
                    self.sem(inc[0])
        prog = self

        def replay(name):
            def run(eng):
                for (waits, fn, inc) in prog.ops[name]:
                    for k, v in waits:
                        eng.wait_ge(prog.sems[k], v)
                    if fn is None:
                        continue
                    ins = fn(eng)
                    if inc is not None:
                        ins.then_inc(prog.sems[inc[0]], inc[1])
            return run

        with nc.Block() as block:
            block.tensor(replay("pe"))
            block.scalar(replay("act"))
            block.vector(replay("dve"))
            block.gpsimd(replay("pool"))
            block.sync(replay("sp"))

    def close(self):
        self.stack.close()


D = 1024
NH = 8
HD = 64
RW = 512
RWKV_COLS = 1792
IN_COLS = 5376
DFF = 2816
PLE = 256
CDEC = -0.6065306597126334
NEG = -30000.0


def bc(vec_buf, n):
    return vec_buf.t[0:n].partition_broadcast(128)


class Ctx:
    pass


def declare_io(P, T):
    c = Ctx()
    f = lambda n, s: P.dram(n, s, F32, kind="ExternalInput")
    c.x = f("x", [T, D]); c.p = f("p", [T, PLE])
    c.ln1_g = f("ln1_g", [D]); c.w_in = f("w_in", [D, IN_COLS]); c.mix_mu = f("mix_mu", [RWKV_COLS])
    c.w0 = f("w0", [RW]); c.w2 = f("w2", [64, RW]); c.a0 = f("a0", [RW]); c.a2 = f("a2", [64, RW])
    c.g2 = f("g2", [128, RW]); c.k_k = f("k_k", [RW]); c.k_a = f("k_a", [RW]); c.r_k = f("r_k", [RW])
    c.lnx_w = f("lnx_w", [RW]); c.lnx_b = f("lnx_b", [RW]); c.biasT = f("biasT", [NH, 128, 640])
    c.gate_b = f("gate_b", [2048]); c.w_br_rwkv = f("w_br_rwkv", [RW, D]); c.w_br_attn = f("w_br_attn", [RW, D])
    c.w_o = f("w_o", [D, D]); c.ln2_g = f("ln2_g", [D]); c.w_ffn_up = f("w_ffn_up", [D, 2 * DFF])
    c.conv_w = f("conv_w", [3, DFF]); c.conv_b = f("conv_b", [DFF]); c.w_ffn_down = f("w_ffn_down", [DFF, D])
    c.ln3_g = f("ln3_g", [D]); c.w_ple = f("w_ple", [PLE, D]); c.w_pg = f("w_pg", [D, D]); c.b_pg = f("b_pg", [D])
    c.lnf_g = f("lnf_g", [D])
    c.cst = f("cst", [128, 8, 128])
    c.out = P.dram("out", [T, D], F32, kind="ExternalOutput")
    return c


def rmsnorm_T(P, xt, xn, junk, ss, rs, pst, hT_dst_fn, ident, tag=""):
    P.op("act", lambda e: e.activation(out=junk.t[:, :], in_=xt.t[:, :], func=AF.Square, accum_out=ss.t[:, :]),
         reads=[xt], writes=[junk, ss])
    P.op("act", lambda e: e.activation(out=rs.t[:, :], in_=ss.t[:, :], func=AF.Sqrt, scale=1.0 / D, bias=1e-6),
         reads=[ss], writes=[rs])
    P.op("dve", lambda e: e.reciprocal(out=rs.t[:, :], in_=rs.t[:, :]), reads=[rs], writes=[rs])
    P.op("act", lambda e: e.activation(out=xn.t[:, :], in_=xt.t[:, :], func=AF.Copy, scale=rs.t[:, :]),
         reads=[xt, rs], writes=[xn])
    for c in range(8):
        P.op("pe", lambda e, c=c: e.transpose(out=pst.t[:, c, :], in_=xn.t[:, c * 128:(c + 1) * 128], identity=ident.t[:, :]),
             reads=[xn, ident], writes=[pst], signal=(c == 7))
    hT_dst_fn(pst)


class Ring:
    def __init__(self, items):
        self.items = list(items)
        self.i = 0

    def next(self):
        b = self.items[self.i % len(self.items)]
        self.i += 1
        return b


def load_cols(P, q, dst, vec, n, ncol):
    P.dma(q, dst.t[:, 0:ncol], vec.t.rearrange("(c p) -> p c", p=128), vec, dst, allow_slow_non_contiguous=True)


def phase1(P, c, T, scr):
    NB = T // 512
    P.stack = contextlib.ExitStack()
    ident = P.sbuf("ident", [128, 128], BF16)
    P.dma("pool", ident.t[:, :], c.cst.t[:, 0, :], c.cst, ident)
    g1 = P.sbuf("g1", [128, 8], F32)
    load_cols(P, "sp", g1, c.ln1_g, D, 8)
    gb = P.sbuf("gb", [128, 16], F32)
    load_cols(P, "sp", gb, c.gate_b, 2048, 16)
    mub = P.sbuf("mub", [128, RWKV_COLS], F32)
    omm = P.sbuf("omm", [128, RWKV_COLS], F32)
    P.dma("sp", mub.t[:, :], bc(c.mix_mu, RWKV_COLS), c.mix_mu, mub)
    P.op("dve", lambda e: e.tensor_scalar(out=omm.t[:, :], in0=mub.t[:, :], scalar1=-1.0, scalar2=1.0, op0=ALU.mult, op1=ALU.add),
         reads=[mub], writes=[omm])
    Wr1 = P.sbuf("Wr1", [128, 8, RWKV_COLS], BF16)
    Wr2 = P.sbuf("Wr2", [128, 8, RWKV_COLS], BF16)
    Wo = P.sbuf("Wo", [128, 8, 3584], BF16)
    stg = Ring([P.sbuf("stg%d" % i, [128, RWKV_COLS], F32) for i in range(3)])
    xt = Ring([P.sbuf("xt%d" % i, [128, D], F32) for i in range(2)] + stg.items)
    xn = Ring([P.sbuf("xn%d" % i, [128, D], BF16) for i in range(2)])
    junk = P.sbuf("junk", [128, D], BF16)
    ss = Ring([P.sbuf("ss%d" % i, [128, 1], F32) for i in range(2)])
    rs = Ring([P.sbuf("rs%d" % i, [128, 1], F32) for i in range(2)])
    hT = [P.sbuf("hT%d" % i, [128, 8, 513], BF16) for i in range(2)]
    pst = Ring([P.psum("pst%d" % i, [128, 8, 128], BF16) for i in range(2)])
    pm = Ring([P.psum("pm%d" % i, [128, 512], F32) for i in range(5)])
    of = Ring([P.sbuf("of%d" % i, [128, 512], BF16) for i in range(4)])
    ot = Ring([P.sbuf("ot%d" % i, [128, 1536], BF16) for i in range(2)])
    ov = Ring([P.sbuf("ov%d" % i, [128, 512], BF16) for i in range(2)])

    xq = {}

    def lx(b):
        for j in range(4):
            i = 4 * b + j
            x_ = xt.next()
            P.dma("sp", x_.t[:, 0:D], c.x.t[i * 128:(i + 1) * 128, :], c.x, x_)
            xq[i] = x_

    def nrm(b):
        h = hT[b % 2]
        if b == 0:
            P.op("pool", lambda e, h=h: e.memset(h.t[:, :, 0:1], 0.0), writes=[h])
        else:
            hp = hT[(b - 1) % 2]
            P.op("pool", lambda e, h=h, hp=hp: e.tensor_copy(out=h.t[:, :, 0:1], in_=hp.t[:, :, 512:513]), reads=[hp], writes=[h])
        for j in range(4):
            i = 4 * b + j
            x_ = xq.pop(i)
            xn_ = xn.next(); ss_ = ss.next(); rs_ = rs.next(); ps_ = pst.next()
            P.op("act", lambda e, x_=x_, ss_=ss_: e.activation(out=junk.t[:, :], in_=x_.t[:, 0:D], func=AF.Square, accum_out=ss_.t[:, :]),
                 reads=[x_], writes=[junk, ss_])
            P.op("act", lambda e, ss_=ss_, rs_=rs_: e.activation(out=rs_.t[:, :], in_=ss_.t[:, :], func=AF.Sqrt, scale=1.0 / D, bias=1e-6),
                 reads=[ss_], writes=[rs_])
            P.op("dve", lambda e, rs_=rs_: e.reciprocal(out=rs_.t[:, :], in_=rs_.t[:, :]), reads=[rs_], writes=[rs_])
            P.op("act", lambda e, x_=x_, xn_=xn_, rs_=rs_: e.activation(out=xn_.t[:, :], in_=x_.t[:, 0:D], func=AF.Copy, scale=rs_.t[:, :]),
                 reads=[x_, rs_], writes=[xn_])
            for k in range(8):
                TR_(P, ps_.t[:, k, :], xn_.t[:, k * 128:(k + 1) * 128], ident.t[:, :], [xn_, ident], [ps_], signal=(k == 7))
            CP_(P, "dve", h.t[:, :, 1 + j * 128:1 + (j + 1) * 128], ps_.t[:, :, :], [ps_], [h])

    lx(0)
    nrm(0)
    for pc in range(3):
        for dc in range(8):
            s = stg.next()
            P.dma("sp", s.t[:, :], c.w_in.t[dc * 128:(dc + 1) * 128, pc * 1792:(pc + 1) * 1792], c.w_in, s)
            if pc == 0:
                P.op("dve", lambda e, s=s, dc=dc: e.scalar_tensor_tensor(out=Wr1.t[:, dc, :], in0=s.t[:, :], scalar=g1.t[:, dc:dc + 1],
                                                                   in1=omm.t[:, :], op0=ALU.mult, op1=ALU.mult),
                     reads=[s, g1, omm], writes=[Wr1])
                P.op("dve", lambda e, s=s, dc=dc: e.scalar_tensor_tensor(out=Wr2.t[:, dc, :], in0=s.t[:, :], scalar=g1.t[:, dc:dc + 1],
                                                                    in1=mub.t[:, :], op0=ALU.mult, op1=ALU.mult),
                     reads=[s, g1, mub], writes=[Wr2])
            else:
                P.op("act", lambda e, s=s, dc=dc, pc=pc: e.activation(out=Wo.t[:, dc, (pc - 1) * 1792:pc * 1792], in_=s.t[:, :], func=AF.Copy,
                                                                 scale=g1.t[:, dc:dc + 1]),
                     reads=[s, g1], writes=[Wo])

    for b in range(NB):
        h = hT[b % 2]
        if b + 1 < NB:
            lx(b + 1)
        for j in range(4):
            i = 4 * b + j
            o_ = ot.next()
            for cg in range(3):
                ps = pm.next()
                for dc in range(8):
                    P.op("pe", lambda e, ps=ps, dc=dc, cg=cg, j=j, h=h: e.matmul(out=ps.t[:, :], lhsT=h.t[:, dc, 1 + j * 128:1 + (j + 1) * 128],
                                                                              rhs=Wr1.t[:, dc, cg * 512:(cg + 1) * 512], start=(dc == 0), stop=False),
                         reads=[h, Wr1], writes=[ps], signal=False)
                for dc in range(8):
                    P.op("pe", lambda e, ps=ps, dc=dc, cg=cg, j=j, h=h: e.matmul(out=ps.t[:, :], lhsT=h.t[:, dc, j * 128:(j + 1) * 128],
                                                                              rhs=Wr2.t[:, dc, cg * 512:(cg + 1) * 512], start=False, stop=(dc == 7)),
                         reads=[h, Wr2], writes=[ps], signal=(dc == 7))
                P.op("dve", lambda e, ps=ps, o_=o_, cg=cg: e.tensor_copy(out=o_.t[:, cg * 512:(cg + 1) * 512], in_=ps.t[:, :]), reads=[ps], writes=[o_])
            P.dma("sp", scr.rkv.t[i * 128:(i + 1) * 128, :], o_.t[:, :], o_, scr.rkv)
            ps = pm.next()
            v_ = ov.next()
            for dc in range(8):
                P.op("pe", lambda e, ps=ps, dc=dc, j=j, h=h: e.matmul(out=ps.t[:, :], lhsT=h.t[:, dc, 1 + j * 128:1 + (j + 1) * 128],
                                                                   rhs=Wo.t[:, dc, 1024:1536], start=(dc == 0), stop=(dc == 7)),
                     reads=[h, Wo], writes=[ps], signal=(dc == 7))
            P.op("dve", lambda e, ps=ps, v_=v_: e.tensor_copy(out=v_.t[:, :], in_=ps.t[:, :]), reads=[ps], writes=[v_])
            P.dma("sp", scr.vA.t[i * 128:(i + 1) * 128, :], v_.t[:, :], v_, scr.vA)
        if b + 1 < NB:
            nrm(b + 1)
        tok = slice(b * 512, (b + 1) * 512)
        for g in range(2):
            ps = pm.next()
            c0 = 1536 + g * 128
            for dc in range(8):
                P.op("pe", lambda e, ps=ps, dc=dc, c0=c0, h=h: e.matmul(out=ps.t[:, :], lhsT=Wr1.t[:, dc, c0:c0 + 128], rhs=h.t[:, dc, 1:513],
                                                                     start=(dc == 0), stop=False), reads=[h, Wr1], writes=[ps], signal=False)
            for dc in range(8):
                P.op("pe", lambda e, ps=ps, dc=dc, c0=c0, h=h: e.matmul(out=ps.t[:, :], lhsT=Wr2.t[:, dc, c0:c0 + 128], rhs=h.t[:, dc, 0:512],
                                                                     start=False, stop=(dc == 7)), reads=[h, Wr2], writes=[ps], signal=(dc == 7))
            o_ = of.next()
            if g == 0:
                P.op("act", lambda e, ps=ps, o_=o_: e.activation(out=o_.t[0:64, :], in_=ps.t[0:64, :], func=AF.Tanh), reads=[ps], writes=[o_])
                P.op("act", lambda e, ps=ps, o_=o_: e.copy(out=o_.t[64:128, :], in_=ps.t[64:128, :]), reads=[ps], writes=[o_])
            else:
                P.op("act", lambda e, ps=ps, o_=o_: e.activation(out=o_.t[:, :], in_=ps.t[:, :], func=AF.Sigmoid), reads=[ps], writes=[o_])
            P.dma("sp", scr.lora.t[g * 128:(g + 1) * 128, tok], o_.t[:, :], o_, scr.lora)
        for g in range(8 + 16):
            ps = pm.next()
            c0 = g * 128 if g < 8 else 1536 + (g - 8) * 128
            for dc in range(8):
                P.op("pe", lambda e, ps=ps, dc=dc, c0=c0, h=h: e.matmul(out=ps.t[:, :], lhsT=Wo.t[:, dc, c0:c0 + 128], rhs=h.t[:, dc, 1:513],
                                                                     start=(dc == 0), stop=(dc == 7)), reads=[h, Wo], writes=[ps], signal=(dc == 7))
            o_ = of.next()
            if g < 8:
                P.op("dve", lambda e, ps=ps, o_=o_: e.tensor_copy(out=o_.t[:, :], in_=ps.t[:, :]), reads=[ps], writes=[o_])
                dstb = scr.qT if g < 4 else scr.kT
                P.dma("sp", dstb.t[(g % 4) * 128:(g % 4 + 1) * 128, tok], o_.t[:, :], o_, dstb)
            else:
                gg = g - 8
                P.op("act", lambda e, ps=ps, o_=o_, gg=gg: e.activation(out=o_.t[:, :], in_=ps.t[:, :], func=AF.Sigmoid, bias=gb.t[:, gg:gg + 1]),
                     reads=[ps, gb], writes=[o_])
                P.dma("sp", scr.gates.t[gg * 128:(gg + 1) * 128, tok], o_.t[:, :], o_, scr.gates)
    P.barrier()
    P.emit_phase()
    P.stack.close()


def make_scratch(P, T, dbg=False):
    s = Ctx()
    kind = "ExternalOutput" if dbg else "Internal"
    mk = lambda n, sh, dt=BF16: P.dram(n, sh, dt, kind=kind)
    s.rkv = mk("s_rkv", [T, 1536]); s.vA = mk("s_vA", [T, 512]); s.lora = mk("s_lora", [256, T])
    s.qT = mk("s_qT", [512, T]); s.kT = mk("s_kT", [512, T]); s.gates = mk("s_gates", [2048, T])
    s.oaT = mk("s_oaT", [512, T]); s.obT = mk("s_obT", [512, T]); s.x1 = mk("s_x1", [T, D], F32); s.x2 = mk("s_x2", [T, D], F32)
    return s


def TT_(P, eng, out, in0, in1, op, reads, writes):
    P.op(eng, lambda e: e.tensor_tensor(out=out, in0=in0, in1=in1, op=op), reads=reads, writes=writes)


def STT_(P, out, in0, scalar, in1, op0, op1, reads, writes):
    P.op("dve", lambda e: e.scalar_tensor_tensor(out=out, in0=in0, scalar=scalar, in1=in1, op0=op0, op1=op1), reads=reads, writes=writes)


def TS_(P, eng, out, in0, s1, s2, op0, op1, reads, writes):
    if s2 is None:
        P.op(eng, lambda e: e.tensor_scalar(out=out, in0=in0, scalar1=s1, scalar2=None, op0=op0), reads=reads, writes=writes)
    else:
        P.op(eng, lambda e: e.tensor_scalar(out=out, in0=in0, scalar1=s1, scalar2=s2, op0=op0, op1=op1), reads=reads, writes=writes)


def ACT_(P, out, in_, func, reads, writes, scale=1.0, bias=None):
    if bias is None:
        P.op("act", lambda e: e.activation(out=out, in_=in_, func=func, scale=scale), reads=reads, writes=writes)
    else:
        P.op("act", lambda e: e.activation(out=out, in_=in_, func=func, scale=scale, bias=bias), reads=reads, writes=writes)


def CP_(P, eng, out, in_, reads, writes):
    if eng == "act":
        P.op("act", lambda e: e.copy(out=out, in_=in_), reads=reads, writes=writes)
    else:
        P.op(eng, lambda e: e.tensor_copy(out=out, in_=in_), reads=reads, writes=writes)


def MM_(P, out, lhsT, rhs, reads, writes, start=True, stop=True, signal=True):
    P.op("pe", lambda e: e.matmul(out=out, lhsT=lhsT, rhs=rhs, start=start, stop=stop), reads=reads, writes=writes, signal=signal)


def TR_(P, out, in_, ident, reads, writes, signal=True):
    P.op("pe", lambda e: e.transpose(out=out, in_=in_, identity=ident), reads=reads, writes=writes, signal=signal)


def RED_(P, out, in_, reads, writes):
    P.op("dve", lambda e: e.tensor_reduce(out=out, in_=in_, axis=AX.X, op=ALU.add), reads=reads, writes=writes)


def make_cst():
    c = np.zeros((128, 8, 128), np.float32)
    s = np.arange(128)[:, None]
    t = np.arange(128)[None, :]
    same = (s // 64) == (t // 64)
    c[:, 0, :] = np.eye(128)
    c[:, 1, :] = same & (s <= t)
    c[:, 2, :] = same & (s < t)
    c[:, 3, :] = same & (s > t)
    c[:, 4, 0:2] = (s // 64) == np.arange(2)[None, :]
    s64 = s % 64
    t64 = np.arange(64)[None, :]
    c[:, 5, 0:64] = t64 > s64
    c[:, 5, 64:128] = t64 < s64
    c[:, 6, 0:64] = t64 >= s64
    c[:, 6, 64:128] = t64 == s64
    c[:, 7, :] = 8.0 * np.eye(128)
    return c


def make_biasT(rel_bias):
    k = np.arange(128)[:, None]
    q = np.arange(640)[None, :]
    rel = q - k
    idx = np.clip(rel, -63, 128) + 63
    kc = k // 64
    qc = q // 64
    ok = (qc >= kc) & (qc <= kc + 8)
    out = rel_bias[:, idx].astype(np.float32)
    out[:, ~ok] = NEG
    return out


def phase2(P, c, T, scr):
    NT = T // 128
    P.stack = contextlib.ExitStack()
    SB = lambda n, sh, dt=F32: P.sbuf(n, sh, dt)
    RG = lambda n, sh, dt=F32, k=2: Ring([P.sbuf("%s%d" % (n, i), sh, dt) for i in range(k)])
    ident = SB("ident", [128, 128], BF16)
    P.dma("pool", ident.t[:, :], c.cst.t[:, 0, :], c.cst, ident)
    cst = SB("cstf", [128, 8, 128])
    P.dma("sp", cst.t[:, :, :], c.cst.t[:, :, :], c.cst, cst)
    wa2 = SB("wa2", [128, RW], BF16)
    P.dma("pool", wa2.t[0:64, :], c.w2.t[:, :], c.w2, wa2)
    P.dma("pool", wa2.t[64:128, :], c.a2.t[:, :], c.a2, wa2)
    g2b = SB("g2b", [128, RW], BF16)
    P.dma("pool", g2b.t[:, :], c.g2.t[:, :], c.g2, g2b)
    bt = {}
    for nm in ["w0", "a0", "k_k", "k_a", "r_k", "lnx_w", "lnx_b"]:
        bt[nm] = SB("b_" + nm, [128, RW])
        P.dma("sp", bt[nm].t[:, :], bc(getattr(c, nm), RW), getattr(c, nm), bt[nm])
    Linc, Lsl, Lsu, cind = cst.t[:, 1, :], cst.t[:, 2, :], cst.t[:, 3, :], cst.t[:, 4, 0:2]
    b3 = lambda ap: ap.unsqueeze(1).to_broadcast([128, 8, 64])
    MU, ML, MUI, I64 = b3(cst.t[:, 5, 0:64]), b3(cst.t[:, 5, 64:128]), b3(cst.t[:, 6, 0:64]), b3(cst.t[:, 6, 64:128])

    pg = Ring([P.psum("pg%d" % i, [128, 512], F32) for i in range(5)])
    pq = Ring([P.psum("pq%d" % i, [128, 512], F32) for i in range(3)])
    v3 = lambda buf: buf.t.rearrange("p (h v) -> p h v", h=8)
    def bfv(buf, inner):
        return buf.t.bitcast(BF16)[:, 0:8 * inner].rearrange("p (h t) -> p h t", h=8)

    rkvt = RG("rkvt", [128, 1536], BF16)
    lo1 = RG("lo1", [128, 128], BF16); lo2 = RG("lo2", [128, 128], BF16)
    names_f = ["t_w", "sg", "t_a", "a_", "e_pos", "e_neg", "e_prev", "e_rel", "kkr", "sq", "kk", "bb", "t1", "k2", "t2", "yt", "yn", "bv"]
    F = {n: SB(n, [128, RW]) for n in names_f}
    g_ = RG("g_", [128, RW])
    names_b = ["At", "Bt", "Kt", "Rt"]
    Bq = {n: SB(n, [128, RW], BF16) for n in names_b}
    Bh = RG("Bh", [128, RW], BF16); Kh = RG("Kh", [128, RW], BF16)
    ATs = RG("ATs", [64, 8, 128], BF16); RTs = RG("RTs", [64, 8, 128], BF16)
    BTs = SB("BTs", [64, 8, 128], BF16); KTs = SB("KTs", [64, 8, 128], BF16)
    XA = Ring([P.sbuf("XA%d" % i, [128, 8, 64], BF16) for i in range(2)])
    XTA = Ring([P.sbuf("XTA%d" % i, [128, 8, 64], BF16) for i in range(2)])
    MakT = SB("MakT", [128, 8, 64], BF16)
    RBT = RG("RBT", [128, 8, 64], BF16); RKT = SB("RKT", [128, 8, 64], BF16)
    TTb = RG("TTb", [128, 8, 64], BF16)
    W1a = RG("W1a", [128, 8, 64]); Ya = RG("Ya", [128, 8, 64])
    W1 = SB("W1", [128, 8, 64], BF16); U = SB("U", [128, 8, 64], BF16)
    STb = Ring([P.sbuf("STb%d" % i, [64, 8, 64], BF16) for i in range(2)])
    SF = Ring([P.sbuf("SF%d" % i, [64, 8, 64], F32) for i in range(2)])
    tmpS = SB("tmpS", [64, 8, 64]); tmp2 = SB("tmp2", [64, 8, 64])
    pc = RG("pc", [64, 8, 2])
    sm = {n: SB(n, [128, 8]) for n in ["ssq", "nrm", "rn", "s1", "s2", "mean", "msq", "var", "rstd"]}
    bcf = RG("bcf", [128, 8])
    oa = SB("oa", [128, RW], BF16)
    oaT = RG("oaT", [128, 4, 128], BF16)

    st_b = STb.next(); st_f = SF.next()
    P.op("pool", lambda e: e.memset(st_b.t[:, :, :], 0.0), writes=[st_b])
    P.op("pool", lambda e: e.memset(st_f.t[:, :, :], 0.0), writes=[st_f])
    h3 = lambda ap: ap.rearrange("p (h v) -> p h v", h=8)
    bl = lambda ap: ap.unsqueeze(2).to_broadcast([128, 8, 64])

    for i in range(NT):
        tok = slice(i * 128, (i + 1) * 128)
        rk = rkvt.next(); l1 = lo1.next(); l2 = lo2.next()
        P.dma("sp", rk.t[:, :], scr.rkv.t[tok, :], scr.rkv, rk)
        P.dma("sp", l1.t[:, :], scr.lora.t[0:128, tok], scr.lora, l1)
        P.dma("sp", l2.t[:, :], scr.lora.t[128:256, tok], scr.lora, l2)
        r_, k_, v_ = rk.t[:, 0:512], rk.t[:, 512:1024], rk.t[:, 1024:1536]
        p_w = pg.next(); p_a = pg.next(); p_g = pg.next()
        MM_(P, p_w.t[:, :], l1.t[0:64, :], wa2.t[0:64, :], [l1, wa2], [p_w])
        MM_(P, p_a.t[:, :], l1.t[64:128, :], wa2.t[64:128, :], [l1, wa2], [p_a])
        MM_(P, p_g.t[:, :], l2.t[:, :], g2b.t[:, :], [l2, g2b], [p_g])
        f = {n: F[n].t[:, :] for n in names_f}
        TT_(P, "dve", f["t_w"], p_w.t[:, :], bt["w0"].t[:, :], ALU.add, [p_w, bt["w0"]], [F["t_w"]])
        ACT_(P, f["sg"], f["t_w"], AF.Sigmoid, [F["t_w"]], [F["sg"]])
        TT_(P, "dve", f["t_a"], p_a.t[:, :], bt["a0"].t[:, :], ALU.add, [p_a, bt["a0"]], [F["t_a"]])
        ACT_(P, f["a_"], f["t_a"], AF.Sigmoid, [F["t_a"]], [F["a_"]])
        gq = g_.next()
        CP_(P, "act", gq.t[:, :], p_g.t[:, :], [p_g], [gq])
        p1 = pg.next(); p2 = pg.next(); p3 = pg.next(); p4 = pg.next()
        MM_(P, p1.t[:, :], Linc, f["sg"], [cst, F["sg"]], [p1])
        MM_(P, p2.t[:, :], Lsl, f["sg"], [cst, F["sg"]], [p2])
        MM_(P, p3.t[:, :], Lsu, f["sg"], [cst, F["sg"]], [p3])
        for h in range(8):
            MM_(P, p4.t[0:64, 2 * h:2 * h + 2], F["sg"].t[:, h * 64:(h + 1) * 64], cind, [F["sg"], cst], [p4], signal=(h == 7))
        ACT_(P, f["e_pos"], p1.t[:, :], AF.Exp, [p1], [F["e_pos"]], scale=CDEC)
        ACT_(P, f["e_neg"], p1.t[:, :], AF.Exp, [p1], [F["e_neg"]], scale=-CDEC)
        ACT_(P, f["e_prev"], p2.t[:, :], AF.Exp, [p2], [F["e_prev"]], scale=CDEC)
        ACT_(P, f["e_rel"], p3.t[:, :], AF.Exp, [p3], [F["e_rel"]], scale=CDEC)
        pcq = pc.next()
        ACT_(P, pcq.t[:, :, :], p4.t[0:64, 0:16].rearrange("p (h c) -> p h c", h=8), AF.Exp, [p4], [pcq], scale=CDEC)
        if CUT < 2:
            continue
        TT_(P, "dve", f["kkr"], k_, bt["k_k"].t[:, :], ALU.mult, [rk, bt["k_k"]], [F["kkr"]])
        TT_(P, "pool", f["sq"], f["kkr"], f["kkr"], ALU.mult, [F["kkr"]], [F["sq"]])
        RED_(P, sm["ssq"].t[:, :], h3(f["sq"]), [F["sq"]], [sm["ssq"]])
        ACT_(P, sm["nrm"].t[:, :], sm["ssq"].t[:, :], AF.Sqrt, [sm["ssq"]], [sm["nrm"]])
        TS_(P, "dve", sm["nrm"].t[:, :], sm["nrm"].t[:, :], 1e-12, None, ALU.max, None, [sm["nrm"]], [sm["nrm"]])
        P.op("dve", lambda e: e.reciprocal(out=sm["rn"].t[:, :], in_=sm["nrm"].t[:, :]), reads=[sm["nrm"]], writes=[sm["rn"]])
        TT_(P, "dve", h3(f["kk"]), h3(f["kkr"]), bl(sm["rn"].t[:, :]), ALU.mult, [F["kkr"], sm["rn"]], [F["kk"]])
        TT_(P, "dve", f["bb"], f["kk"], f["a_"], ALU.mult, [F["kk"], F["a_"]], [F["bb"]])
        STT_(P, f["t1"], f["a_"], -1.0, bt["k_a"].t[:, :], ALU.add, ALU.mult, [F["a_"], bt["k_a"]], [F["t1"]])
        STT_(P, f["k2"], f["t1"], 1.0, k_, ALU.add, ALU.mult, [F["t1"], rk], [F["k2"]])
        STT_(P, Bq["At"].t[:, :], f["kk"], -1.0, f["e_prev"], ALU.mult, ALU.mult, [F["kk"], F["e_prev"]], [Bq["At"]])
        TT_(P, "dve", Bq["Bt"].t[:, :], f["bb"], f["e_neg"], ALU.mult, [F["bb"], F["e_neg"]], [Bq["Bt"]])
        TT_(P, "dve", Bq["Kt"].t[:, :], f["k2"], f["e_neg"], ALU.mult, [F["k2"], F["e_neg"]], [Bq["Kt"]])
        TT_(P, "dve", Bq["Rt"].t[:, :], r_, f["e_pos"], ALU.mult, [rk, F["e_pos"]], [Bq["Rt"]])
        bh = Bh.next(); kh = Kh.next()
        TT_(P, "pool", bh.t[:, :], f["bb"], f["e_rel"], ALU.mult, [F["bb"], F["e_rel"]], [bh])
        TT_(P, "pool", kh.t[:, :], f["k2"], f["e_rel"], ALU.mult, [F["k2"], F["e_rel"]], [kh])
        TT_(P, "pool", f["t2"], r_, bt["r_k"].t[:, :], ALU.mult, [rk, bt["r_k"]], [F["t2"]])
        TT_(P, "pool", f["t2"], f["t2"], f["k2"], ALU.mult, [F["t2"], F["k2"]], [F["t2"]])
        bq = bcf.next()
        RED_(P, bq.t[:, :], h3(f["t2"]), [F["t2"]], [bq])
        if CUT < 3:
            continue
        ats = ATs.next(); rts = RTs.next()
        for (src, dst, ev) in [("At", ats, "act"), ("Bt", BTs, "dve"), ("Kt", KTs, "act"), ("Rt", rts, "dve")]:
            ps = pg.next()
            pv = bfv(ps, 128)
            for h in range(8):
                TR_(P, pv[0:64, h, :], Bq[src].t[:, h * 64:(h + 1) * 64], ident.t[:, :], [Bq[src], ident], [ps], signal=(h == 7))
            CP_(P, ev, dst.t[:, :, :], pv[0:64, :, :], [ps], [dst])
        if CUT < 4:
            continue
        def prod(lhs, rhs, mask, dst, eng="dve"):
            ps = pg.next()
            for cc in range(2):
                cols = slice(cc * 64, (cc + 1) * 64)
                for h in range(8):
                    MM_(P, v3(ps)[cc * 64:(cc + 1) * 64, h, :], lhs.t[:, h, cols], rhs.t[:, h, cols], [lhs, rhs], [ps],
                        signal=(cc == 1 and h == 7))
            TT_(P, eng, dst.t[:, :, :], v3(ps), mask, ALU.mult, [ps, cst], [dst])
        xt_ = XTA.next(); x_ = XA.next()
        rbt = RBT.next()
        prod(BTs, ats, MU, xt_)
        prod(ats, BTs, ML, x_)
        prod(KTs, ats, MU, MakT)
        prod(BTs, rts, MUI, rbt)
        prod(KTs, rts, MUI, RKT)
        tt = TTb.next()
        TT_(P, "dve", tt.t[:, :, :], xt_.t[:, :, :], I64, ALU.add, [xt_, cst], [tt])
        if CUT < 5:
            continue
        for lev in range(1, 6):
            xn_ = XA.next()
            xtn_ = XTA.next() if lev < 5 else None
            for cc in range(2):
                hs = slice(cc * 64, (cc + 1) * 64)
                px = pg.next()
                for h in range(8):
                    MM_(P, v3(px)[hs, h, :], xt_.t[hs, h, :], x_.t[hs, h, :], [xt_, x_], [px], signal=(h == 7))
                CP_(P, "act", xn_.t[hs, :, :], v3(px)[hs, :, :], [px], [xn_])
                if lev < 5:
                    pxt = pg.next()
                    for h in range(8):
                        MM_(P, v3(pxt)[hs, h, :], x_.t[hs, h, :], xt_.t[hs, h, :], [xt_, x_], [pxt], signal=(h == 7))
                    CP_(P, "act", xtn_.t[hs, :, :], v3(pxt)[hs, :, :], [pxt], [xtn_])
            x_ = xn_
            if lev < 5:
                xt_ = xtn_
            for cc in range(2):
                hs = slice(cc * 64, (cc + 1) * 64)
                pt = pg.next()
                for h in range(8):
                    MM_(P, v3(pt)[hs, h, :], x_.t[hs, h, :], tt.t[hs, h, :], [x_, tt], [pt], signal=(h == 7))
                TT_(P, "dve", tt.t[hs, :, :], v3(pt)[hs, :, :], tt.t[hs, :, :], ALU.add, [pt, tt], [tt])
        if CUT < 6:
            continue
        w1a = W1a.next(); ya = Ya.next()
        for cc in range(2):
            hs = slice(cc * 64, (cc + 1) * 64)
            ps = pg.next()
            for h in range(8):
                MM_(P, v3(ps)[hs, h, :], MakT.t[hs, h, :], rk.t[hs, 1024 + h * 64:1024 + (h + 1) * 64], [MakT, rk], [ps], signal=(h == 7))
            CP_(P, "act", w1a.t[hs, :, :], v3(ps)[hs, :, :], [ps], [w1a])
            ps = pg.next()
            for h in range(8):
                MM_(P, v3(ps)[hs, h, :], RKT.t[hs, h, :], rk.t[hs, 1024 + h * 64:1024 + (h + 1) * 64], [RKT, rk], [ps], signal=(h == 7))
            CP_(P, "act", ya.t[hs, :, :], v3(ps)[hs, :, :], [ps], [ya])
        if CUT < 7:
            continue
        for cc in range(2):
            hs = slice(cc * 64, (cc + 1) * 64)
            cols = slice(cc * 64, (cc + 1) * 64)
            pk = pq.next()
            for h in range(8):
                MM_(P, v3(pk)[0:64, h, :], kh.t[hs, h * 64:(h + 1) * 64], rk.t[hs, 1024 + h * 64:1024 + (h + 1) * 64], [kh, rk], [pk], signal=(h == 7))
            TT_(P, "dve", tmpS.t[:, :, :], st_f.t[:, :, :], pcq.t[:, :, cc:cc + 1].to_broadcast([64, 8, 64]), ALU.mult, [st_f, pcq], [tmpS])
            TT_(P, "dve", tmp2.t[:, :, :], v3(pk)[0:64, :, :], tmpS.t[:, :, :], ALU.add, [pk, tmpS], [tmp2])
            pw = pq.next()
            for h in range(8):
                MM_(P, v3(pw)[hs, h, :], ats.t[:, h, cols], st_b.t[:, h, :], [ats, st_b], [pw], signal=(h == 7))
            TT_(P, "dve", W1.t[hs, :, :], v3(pw)[hs, :, :], w1a.t[hs, :, :], ALU.add, [pw, w1a], [W1])
            prs = pq.next()
            for h in range(8):
                MM_(P, v3(prs)[hs, h, :], rts.t[:, h, cols], st_b.t[:, h, :], [rts, st_b], [prs], signal=(h == 7))
            TT_(P, "dve", h3(f["yt"])[hs, :, :], v3(prs)[hs, :, :], ya.t[hs, :, :], ALU.add, [prs, ya], [F["yt"]])
            pu = pq.next()
            for h in range(8):
                MM_(P, v3(pu)[hs, h, :], tt.t[hs, h, :], W1.t[hs, h, :], [tt, W1], [pu], signal=(h == 7))
            CP_(P, "act", U.t[hs, :, :], v3(pu)[hs, :, :], [pu], [U])
            psn = pq.next()
            for h in range(8):
                MM_(P, v3(psn)[0:64, h, :], bh.t[hs, h * 64:(h + 1) * 64], U.t[hs, h, :], [bh, U], [psn], signal=(h == 7))
            py = pq.next()
            for h in range(8):
                MM_(P, v3(py)[hs, h, :], rbt.t[hs, h, :], U.t[hs, h, :], [rbt, U], [py], signal=(h == 7))
            nb = STb.next(); nf = SF.next()
            TT_(P, "dve", nb.t[:, :, :], v3(psn)[0:64, :, :], tmp2.t[:, :, :], ALU.add, [psn, tmp2], [nb])
            TT_(P, "dve", nf.t[:, :, :], v3(psn)[0:64, :, :], tmp2.t[:, :, :], ALU.add, [psn, tmp2], [nf])
            st_b, st_f = nb, nf
            TT_(P, "dve", h3(f["yt"])[hs, :, :], v3(py)[hs, :, :], h3(f["yt"])[hs, :, :], ALU.add, [py, F["yt"]], [F["yt"]])
        if CUT < 8:
            continue
        yt3 = h3(f["yt"]); yn3 = h3(f["yn"])
        RED_(P, sm["s1"].t[:, :], yt3, [F["yt"]], [sm["s1"]])
        TT_(P, "pool", f["sq"], f["yt"], f["yt"], ALU.mult, [F["yt"]], [F["sq"]])
        RED_(P, sm["s2"].t[:, :], h3(f["sq"]), [F["sq"]], [sm["s2"]])
        TS_(P, "dve", sm["mean"].t[:, :], sm["s1"].t[:, :], 1.0 / 64, None, ALU.mult, None, [sm["s1"]], [sm["mean"]])
        TT_(P, "dve", sm["msq"].t[:, :], sm["mean"].t[:, :], sm["mean"].t[:, :], ALU.mult, [sm["mean"]], [sm["msq"]])
        STT_(P, sm["var"].t[:, :], sm["s2"].t[:, :], 1.0 / 64, sm["msq"].t[:, :], ALU.mult, ALU.subtract, [sm["s2"], sm["msq"]], [sm["var"]])
        ACT_(P, sm["rstd"].t[:, :], sm["var"].t[:, :], AF.Sqrt, [sm["var"]], [sm["rstd"]], bias=64e-5)
        P.op("dve", lambda e: e.reciprocal(out=sm["rstd"].t[:, :], in_=sm["rstd"].t[:, :]), reads=[sm["rstd"]], writes=[sm["rstd"]])
        TT_(P, "dve", yn3, yt3, bl(sm["mean"].t[:, :]), ALU.subtract, [F["yt"], sm["mean"]], [F["yn"]])
        TT_(P, "dve", yn3, yn3, bl(sm["rstd"].t[:, :]), ALU.mult, [F["yn"], sm["rstd"]], [F["yn"]])
        TT_(P, "pool", f["yn"], f["yn"], bt["lnx_w"].t[:, :], ALU.mult, [F["yn"], bt["lnx_w"]], [F["yn"]])
        TT_(P, "pool", f["yn"], f["yn"], bt["lnx_b"].t[:, :], ALU.add, [F["yn"], bt["lnx_b"]], [F["yn"]])
        TT_(P, "dve", h3(f["bv"]), h3(v_), bl(bq.t[:, :]), ALU.mult, [rk, bq], [F["bv"]])
        TT_(P, "pool", f["yn"], f["yn"], f["bv"], ALU.add, [F["yn"], F["bv"]], [F["yn"]])
        TT_(P, "dve", oa.t[:, :], f["yn"], gq.t[:, :], ALU.mult, [F["yn"], gq], [oa])
        ps = pg.next()
        pv = ps.t.bitcast(BF16)[:, 0:512].rearrange("p (c t) -> p c t", c=4)
        for fc in range(4):
            TR_(P, pv[:, fc, :], oa.t[:, fc * 128:(fc + 1) * 128], ident.t[:, :], [oa, ident], [ps], signal=(fc == 3))
        ot_ = oaT.next()
        CP_(P, "act", ot_.t[:, :, :], pv, [ps], [ot_])
        P.dma("sp", scr.oaT.t[:, tok].rearrange("(c p) t -> p c t", p=128), ot_.t[:, :, :], ot_, scr.oaT)
    P.barrier()
    P.emit_phase()
    P.stack.close()


def phase3(P, c, T, scr):
    NT = T // 128
    P.stack = contextlib.ExitStack()
    ident = P.sbuf("ident", [128, 128], BF16)
    P.dma("pool", ident.t[:, :], c.cst.t[:, 0, :], c.cst, ident)
    I8 = P.sbuf("I8", [128, 128], BF16)
    P.dma("pool", I8.t[:, :], c.cst.t[:, 7, :], c.cst, I8)
    kTh = Ring([P.sbuf("kTh%d" % i, [128, T], BF16) for i in range(2)])
    qTh = Ring([P.sbuf("qTh%d" % i, [128, T], BF16) for i in range(2)])
    Vh = Ring([P.sbuf("Vh%d" % i, [128, NT, 65], BF16) for i in range(2)])
    bT = Ring([P.sbuf("bT%d" % i, [128, 640], BF16) for i in range(2)])
    for r in (kTh, qTh):
        for b in r.items:
            P.op("pool", lambda e, b=b: e.memset(b.t[64:128, :], 0.0), writes=[b])
    for b in Vh.items:
        P.op("pool", lambda e, b=b: e.memset(b.t[:, :, 64:65], 1.0), writes=[b])
    PT = [P.sbuf("PT%d" % i, [128, 640], BF16) for i in range(6)]
    ob = P.sbuf("ob", [128, NT, 512], BF16)
    rc = Ring([P.sbuf("rc%d" % i, [128, 1], F32) for i in range(2)])
    pa = Ring([P.psum("pa%d" % i, [128, 512], F32) for i in range(4)])
    pb = Ring([P.psum("pb%d" % i, [128, 512], F32) for i in range(2)])
    pt_ = Ring([P.psum("ptr%d" % i, [128, 512], F32) for i in range(2)])
    for h in range(NH):
        kt = kTh.next(); qt = qTh.next(); vh = Vh.next(); bias = bT.next()
        P.dma("sp", kt.t[0:64, :], scr.kT.t[h * 64:(h + 1) * 64, :], scr.kT, kt)
        P.dma("sp", qt.t[0:64, :], scr.qT.t[h * 64:(h + 1) * 64, :], scr.qT, qt)
        P.dma("sp", vh.t[:, :, 0:64], scr.vA.t[:, h * 64:(h + 1) * 64].rearrange("(m p) d -> p m d", p=128), scr.vA, vh)
        P.dma("pool", bias.t[:, :], c.biasT.t[h, :, :], c.biasT, bias)
        for m in range(NT):
            W = min(640, T - 128 * m)
            pt = PT[m % 6]
            for (c0, c1) in [(0, min(W, 512)), (512, W)]:
                if c1 <= c0:
                    continue
                ps = pa.next()
                n = c1 - c0
                MM_(P, ps.t[:, 0:n], I8.t[:, :], bias.t[:, c0:c1], [I8, bias], [ps], start=True, stop=False, signal=False)
                MM_(P, ps.t[:, 0:n], kt.t[:, m * 128:(m + 1) * 128], qt.t[:, m * 128 + c0:m * 128 + c1], [kt, qt], [ps], start=False, stop=True)
                ACT_(P, pt.t[:, c0:c1], ps.t[:, 0:n], AF.Exp, [ps], [pt], scale=0.125)
            pv = pb.next()
            for cc in range(2):
                cq = 2 * m + cc
                m0 = max(0, (cq - 8) // 2)
                ms = list(range(m0, cq // 2 + 1))
                for mi, mp in enumerate(ms):
                    off = (cq - 2 * mp) * 64
                    MM_(P, pv.t[cc * 64:(cc + 1) * 64, 0:65], PT[mp % 6].t[:, off:off + 64], vh.t[:, mp, :], [PT[mp % 6], vh], [pv],
                        start=(mi == 0), stop=(mi == len(ms) - 1), signal=(mi == len(ms) - 1))
            r_ = rc.next()
            P.op("dve", lambda e, r_=r_, pv=pv: e.reciprocal(out=r_.t[:, :], in_=pv.t[:, 64:65]), reads=[pv], writes=[r_])
            ACT_(P, ob.t[:, m, h * 64:(h + 1) * 64], pv.t[:, 0:64], AF.Copy, [pv, r_], [ob], scale=r_.t[:, :])
    obT = Ring([P.sbuf("obT%d" % i, [128, 4, 128], BF16) for i in range(2)])
    for m in range(NT):
        ps = pt_.next()
        pvw = ps.t.bitcast(BF16)[:, 0:512].rearrange("p (c t) -> p c t", c=4)
        for fc in range(4):
            TR_(P, pvw[:, fc, :], ob.t[:, m, fc * 128:(fc + 1) * 128], ident.t[:, :], [ob, ident], [ps], signal=(fc == 3))
        o_ = obT.next()
        CP_(P, "dve", o_.t[:, :, :], pvw, [ps], [o_])
        P.dma("sp", scr.obT.t[:, m * 128:(m + 1) * 128].rearrange("(c p) t -> p c t", p=128), o_.t[:, :, :], o_, scr.obT)
    P.barrier()
    P.emit_phase()
    P.stack.close()


def phase4a(P, c, T, scr):
    NB = T // 512
    scr.keep = contextlib.ExitStack()
    P.stack = scr.keep
    scr.g2 = P.sbuf("g2c", [128, 8], F32); load_cols(P, "sp", scr.g2, c.ln2_g, D, 8)
    scr.Wup = [P.sbuf("Wup%d" % dc, [128, 2 * DFF], BF16) for dc in range(8)]
    for dc in range(8):
        P.dma("pool", scr.Wup[dc].t[:, :], c.w_ffn_up.t[dc * 128:(dc + 1) * 128, :], c.w_ffn_up, scr.Wup[dc])
        P.op("act", lambda e, dc=dc: e.activation(out=scr.Wup[dc].t[:, :], in_=scr.Wup[dc].t[:, :], func=AF.Copy, scale=scr.g2.t[:, dc:dc + 1]),
             reads=[scr.Wup[dc], scr.g2], writes=[scr.Wup[dc]])
    P.stack = contextlib.ExitStack()
    WA = P.sbuf("WA", [128, 4, D], BF16); WB = P.sbuf("WB", [128, 4, D], BF16); WO = P.sbuf("WO", [128, 8, D], BF16)
    P.dma("pool", WA.t[:, :, :], c.w_br_rwkv.t.rearrange("(c p) n -> p c n", p=128), c.w_br_rwkv, WA)
    P.dma("pool", WB.t[:, :, :], c.w_br_attn.t.rearrange("(c p) n -> p c n", p=128), c.w_br_attn, WB)
    P.dma("pool", WO.t[:, :, :], c.w_o.t.rearrange("(c p) n -> p c n", p=128), c.w_o, WO)
    oa = Ring([P.sbuf("oa%d" % i, [128, 4, 512], BF16) for i in range(2)])
    obb = Ring([P.sbuf("obb%d" % i, [128, 4, 512], BF16) for i in range(2)])
    ga = Ring([P.sbuf("ga%d" % i, [128, 8, 512], BF16) for i in range(2)])
    gbb = Ring([P.sbuf("gbb%d" % i, [128, 8, 512], BF16) for i in range(2)])
    t1 = Ring([P.sbuf("t1_%d" % i, [128, 512], F32) for i in range(2)])
    t2 = Ring([P.sbuf("t2_%d" % i, [128, 512], F32) for i in range(2)])
    mT = Ring([P.sbuf("mT%d" % i, [128, 8, 512], BF16) for i in range(1)])
    xt = Ring([P.sbuf("xt%d" % i, [128, D], F32) for i in range(3)])
    xo = Ring([P.sbuf("xo%d" % i, [128, D], F32) for i in range(2)])
    pm = Ring([P.psum("pm%d" % i, [128, 512], F32) for i in range(6)])
    blk = {}
    xq = {}

    def lb(b):
        tok = slice(b * 512, (b + 1) * 512)
        a_ = oa.next(); b_ = obb.next(); ga_ = ga.next(); gb_ = gbb.next()
        P.dma("sp", a_.t[:, :, :], scr.oaT.t[:, tok].rearrange("(c p) t -> p c t", p=128), scr.oaT, a_)
        P.dma("sp", b_.t[:, :, :], scr.obT.t[:, tok].rearrange("(c p) t -> p c t", p=128), scr.obT, b_)
        P.dma("sp", ga_.t[:, :, :], scr.gates.t[0:1024, tok].rearrange("(c p) t -> p c t", p=128), scr.gates, ga_)
        P.dma("sp", gb_.t[:, :, :], scr.gates.t[1024:2048, tok].rearrange("(c p) t -> p c t", p=128), scr.gates, gb_)
        blk[b] = (a_, b_, ga_, gb_)

    def lx(i):
        x_ = xt.next()
        P.dma("sp", x_.t[:, :], c.x.t[i * 128:(i + 1) * 128, :], c.x, x_)
        xq[i] = x_

    lb(0)
    for i in range(min(2, 4 * NB)):
        lx(i)
    for b in range(NB):
        tok = slice(b * 512, (b + 1) * 512)
        a_, b_, ga_, gb_ = blk.pop(b)
        m_ = mT.next()
        if b + 1 < NB:
            lb(b + 1)
        for cg in range(8):
            pA = pm.next(); pB = pm.next()
            for fc in range(4):
                MM_(P, pA.t[:, :], WA.t[:, fc, cg * 128:(cg + 1) * 128], a_.t[:, fc, :], [WA, a_], [pA], start=(fc == 0), stop=(fc == 3), signal=(fc == 3))
            for fc in range(4):
                MM_(P, pB.t[:, :], WB.t[:, fc, cg * 128:(cg + 1) * 128], b_.t[:, fc, :], [WB, b_], [pB], start=(fc == 0), stop=(fc == 3), signal=(fc == 3))
            u1 = t1.next(); u2 = t2.next()
            TT_(P, "dve", u1.t[:, :], pA.t[:, :], ga_.t[:, cg, :], ALU.mult, [pA, ga_], [u1])
            TT_(P, "dve", u2.t[:, :], pB.t[:, :], gb_.t[:, cg, :], ALU.mult, [pB, gb_], [u2])
            TT_(P, "pool", m_.t[:, cg, :], u1.t[:, :], u2.t[:, :], ALU.add, [u1, u2], [m_])
        for j in range(4):
            i = 4 * b + j
            x_ = xq.pop(i); o_ = xo.next()
            if i + 2 < 4 * NB:
                lx(i + 2)
            for hf in range(2):
                ps = pm.next()
                for cg in range(8):
                    MM_(P, ps.t[:, :], m_.t[:, cg, j * 128:(j + 1) * 128], WO.t[:, cg, hf * 512:(hf + 1) * 512], [m_, WO], [ps],
                        start=(cg == 0), stop=(cg == 7), signal=(cg == 7))
                TT_(P, "dve", o_.t[:, hf * 512:(hf + 1) * 512], ps.t[:, :], x_.t[:, hf * 512:(hf + 1) * 512], ALU.add, [ps, x_], [o_])
            P.dma("sp", scr.x1.t[i * 128:(i + 1) * 128, :], o_.t[:, :], o_, scr.x1)
    P.barrier()
    P.emit_phase()
    P.stack.close()


def norm_scale(P, x_, xn_, s_, r_):
    P.op("act", lambda e: e.activation(out=xn_.t[:, :], in_=x_.t[:, :], func=AF.Square, accum_out=s_.t[:, :]),
         reads=[x_], writes=[xn_, s_])
    P.op("act", lambda e: e.activation(out=r_.t[:, :], in_=s_.t[:, :], func=AF.Sqrt, scale=1.0 / D, bias=1e-6),
         reads=[s_], writes=[r_])
    P.op("dve", lambda e: e.reciprocal(out=r_.t[:, :], in_=r_.t[:, :]), reads=[r_], writes=[r_])
    P.op("act", lambda e: e.activation(out=xn_.t[:, :], in_=x_.t[:, :], func=AF.Copy, scale=r_.t[:, :]),
         reads=[x_, r_], writes=[xn_])


def phase5(P, c, T, scr):
    NB = T // 256
    NG = DFF // 128
    P.stack = contextlib.ExitStack()
    ident = P.sbuf("ident", [128, 128], BF16)
    P.dma("pool", ident.t[:, :], c.cst.t[:, 0, :], c.cst, ident)
    cw = P.sbuf("cw", [128, 3, NG], F32)
    for k in range(3):
        P.dma("sp", cw.t[:, k, :], c.conv_w.t[k, :].rearrange("(c p) -> p c", p=128), c.conv_w, cw, allow_slow_non_contiguous=True)
    cb = P.sbuf("cb", [128, NG], F32); load_cols(P, "sp", cb, c.conv_b, DFF, NG)
    Wup = scr.Wup
    Wdn = P.sbuf("Wdn", [128, NG, D], BF16)
    for gq in range(0, NG, 2):
        P.dma("pool", Wdn.t[:, gq:gq + 2, :], c.w_ffn_down.t[gq * 128:(gq + 2) * 128, :].rearrange("(c p) n -> p c n", p=128), c.w_ffn_down, Wdn)
    carry = P.sbuf("carry", [128, NG, 2], F32)
    P.op("pool", lambda e: e.memset(carry.t[:, :, :], 0.0), writes=[carry])

    X = Ring([P.sbuf("X%d" % i, [128, D], F32) for i in range(6)])
    xn = Ring([P.sbuf("xn%d" % i, [128, D], BF16) for i in range(2)])
    ss = Ring([P.sbuf("ss%d" % i, [128, 1], F32) for i in range(6)])
    rs = Ring([P.sbuf("rs%d" % i, [128, 1], F32) for i in range(6)])
    h2T = [P.sbuf("h2T%d" % i, [128, 8, 256], BF16) for i in range(2)]
    mTt = [P.sbuf("mT%d" % i, [128, NG, 256], BF16) for i in range(2)]
    mT = [[Buf("mT%d_%d" % (i, g), mTt[i].t) for g in range(NG)] for i in range(2)]
    Ab = Ring([P.sbuf("Ab%d" % i, [128, 258], F32) for i in range(3)])
    cv = Ring([P.sbuf("cv%d" % i, [128, 256], F32) for i in range(3)])
    gl = Ring([P.sbuf("gl%d" % i, [128, 256], F32) for i in range(3)])
    pst = Ring([P.psum("pst%d" % i, [128, 8, 128], BF16) for i in range(2)])
    pm = Ring([P.psum("pm%d" % i, [128, 512], F32) for i in range(4)])
    xs = {}
    xns = {}

    def s1l(b):
        xs[b] = []; xns[b] = []
        for j in range(2):
            i = 2 * b + j
            x_ = X.next(); n_ = xn.next()
            xs[b].append(x_); xns[b].append((n_, ss.next(), rs.next()))
            P.dma("sp", x_.t[:, :], scr.x1.t[i * 128:(i + 1) * 128, :], scr.x1, x_)

    def s1n(b, j, k):
        x_ = xs[b][j]; n_, s_, r_ = xns[b][j]
        if k == 0:
            P.op("act", lambda e: e.activation(out=n_.t[:, :], in_=x_.t[:, :], func=AF.Square, accum_out=s_.t[:, :]),
                 reads=[x_], writes=[n_, s_])
        elif k == 1:
            P.op("act", lambda e: e.activation(out=r_.t[:, :], in_=s_.t[:, :], func=AF.Sqrt, scale=1.0 / D, bias=1e-6),
                 reads=[s_], writes=[r_])
            P.op("dve", lambda e: e.reciprocal(out=r_.t[:, :], in_=r_.t[:, :]), reads=[r_], writes=[r_])
        else:
            P.op("act", lambda e: e.activation(out=n_.t[:, :], in_=x_.t[:, :], func=AF.Copy, scale=r_.t[:, :]),
                 reads=[x_, r_], writes=[n_])

    def s1a(b):
        s1l(b)
        for j in range(2):
            for k in range(3):
                s1n(b, j, k)

    def s1b(b):
        h = h2T[b % 2]
        for j in range(2):
            ps = pst.next(); n_ = xns[b][j][0]
            for k in range(8):
                TR_(P, ps.t[:, k, :], n_.t[:, k * 128:(k + 1) * 128], ident.t[:, :], [n_, ident], [ps], signal=(k == 7))
            CP_(P, "dve", h.t[:, :, j * 128:(j + 1) * 128], ps.t[:, :, :], [ps], [h])

    def up(b, g):
        h = h2T[b % 2]
        pA = pm.next(); pG = pm.next()
        for dc in range(8):
            MM_(P, pA.t[:, 0:256], Wup[dc].t[:, g * 128:(g + 1) * 128], h.t[:, dc, :], [Wup[dc], h], [pA], start=(dc == 0), stop=(dc == 7), signal=(dc == 7))
        for dc in range(8):
            MM_(P, pG.t[:, 0:256], Wup[dc].t[:, DFF + g * 128:DFF + (g + 1) * 128], h.t[:, dc, :], [Wup[dc], h], [pG], start=(dc == 0), stop=(dc == 7), signal=(dc == 7))
        A = Ab.next(); cv_ = cv.next(); gl_ = gl.next()
        CP_(P, "pool", A.t[:, 0:2], carry.t[:, g, :], [carry], [A])
        CP_(P, "act", A.t[:, 2:258], pA.t[:, 0:256], [pA], [A])
        ACT_(P, cv_.t[:, :], pA.t[:, 0:256], AF.Identity, [pA, cw, cb], [cv_], scale=cw.t[:, 2, g:g + 1], bias=cb.t[:, g:g + 1])
        STT_(P, cv_.t[:, :], A.t[:, 1:257], cw.t[:, 1, g:g + 1], cv_.t[:, :], ALU.mult, ALU.add, [A, cw, cv_], [cv_])
        STT_(P, cv_.t[:, :], A.t[:, 0:256], cw.t[:, 0, g:g + 1], cv_.t[:, :], ALU.mult, ALU.add, [A, cw, cv_], [cv_])
        CP_(P, "pool", carry.t[:, g, :], A.t[:, 256:258], [A], [carry])
        ACT_(P, gl_.t[:, :], cv_.t[:, :], AF.Gelu, [cv_], [gl_])
        TT_(P, "dve", mTt[b % 2].t[:, g, :], gl_.t[:, :], pG.t[:, 0:256], ALU.mult, [gl_, pG], [mT[b % 2][g]])

    pdn = [P.psum("pdn%d" % i, [128, 512], F32) for i in range(2)]

    def down_steps(b):
        for j in range(2):
            i = 2 * b + j
            x_ = xs[b][j]
            for g in range(NG):
                for hf in range(2):
                    MM_(P, pdn[hf].t[:, :], mTt[b % 2].t[:, g, j * 128:(j + 1) * 128], Wdn.t[:, g, hf * 512:(hf + 1) * 512], [mT[b % 2][g], Wdn], [pdn[hf]],
                        start=(g == 0), stop=(g == NG - 1), signal=(g == NG - 1))
                if g == NG - 1:
                    for hf in range(2):
                        TT_(P, "dve", x_.t[:, hf * 512:(hf + 1) * 512], pdn[hf].t[:, :], x_.t[:, hf * 512:(hf + 1) * 512], ALU.add, [pdn[hf], x_], [x_])
                    P.dma("sp", scr.x2.t[i * 128:(i + 1) * 128, :], x_.t[:, :], x_, scr.x2)
                yield

    def up_block(b, dn):
        for g in range(NG):
            if b + 1 < NB:
                if g == 0:
                    s1l(b + 1)
                if g in (1, 3, 5, 7, 9, 11):
                    q = (g - 1) // 2
                    s1n(b + 1, q // 3, q % 3)
                if g == 14:
                    s1b(b + 1)
            up(b, g)
            if dn is not None:
                for _ in range(2):
                    next(dn, None)

    s1a(0); s1b(0)
    up_block(0, None)
    for b in range(NB):
        dn = down_steps(b)
        if b + 1 < NB:
            up_block(b + 1, dn)
        for _ in dn:
            pass
    P.barrier()
    P.emit_phase()
    P.stack.close()
    scr.keep.close()


def phase6(P, c, T, scr):
    NT = T // 128
    P.stack = contextlib.ExitStack()
    ident = P.sbuf("ident", [128, 128], BF16)
    P.dma("pool", ident.t[:, :], c.cst.t[:, 0, :], c.cst, ident)
    g3 = P.sbuf("g3c", [128, 8], F32); load_cols(P, "sp", g3, c.ln3_g, D, 8)
    Wpg = P.sbuf("Wpg", [128, 8, D], BF16)
    Wple = P.sbuf("Wple", [128, 2, D], BF16)
    P.dma("pool", Wpg.t[:, :, :], c.w_pg.t.rearrange("(c p) n -> p c n", p=128), c.w_pg, Wpg)
    for dc in range(8):
        P.op("act", lambda e, dc=dc: e.activation(out=Wpg.t[:, dc, :], in_=Wpg.t[:, dc, :], func=AF.Copy, scale=g3.t[:, dc:dc + 1]),
             reads=[Wpg, g3], writes=[Wpg])
    P.dma("pool", Wple.t[:, :, :], c.w_ple.t.rearrange("(c p) n -> p c n", p=128), c.w_ple, Wple)
    lnf = P.sbuf("lnf", [128, D], F32)
    P.dma("sp", lnf.t[:, :], bc(c.lnf_g, D), c.lnf_g, lnf)
    bpg = P.sbuf("bpg", [128, D], BF16)
    ones = P.sbuf("ones", [128, 128], BF16)
    P.op("pool", lambda e: e.memset(bpg.t[:, :], 0.0), writes=[bpg])
    P.op("pool", lambda e: e.memset(ones.t[:, :], 0.0), writes=[ones])
    P.op("pool", lambda e: e.memset(ones.t[0:1, :], 1.0), writes=[ones])
    P.dma("pool", bpg.t[0:1, :], c.b_pg.t[0:D].rearrange("(o n) -> o n", o=1), c.b_pg, bpg)
    X = Ring([P.sbuf("X%d" % i, [128, D], F32) for i in range(8)])
    xn = Ring([P.sbuf("xn%d" % i, [128, D], BF16) for i in range(4)])
    ss = Ring([P.sbuf("ss%d" % i, [128, 1], F32) for i in range(10)])
    rs = Ring([P.sbuf("rs%d" % i, [128, 1], F32) for i in range(10)])
    h3T = Ring([P.sbuf("h3T%d" % i, [128, 8, 128], BF16) for i in range(3)])
    pt = Ring([P.sbuf("ptile%d" % i, [128, PLE], F32) for i in range(8)])
    pb_ = Ring([P.sbuf("ptb%d" % i, [128, PLE], BF16) for i in range(4)])
    pT = Ring([P.sbuf("pT%d" % i, [128, 2, 128], BF16) for i in range(3)])
    sgt = Ring([P.sbuf("sgt%d" % i, [128, 512], F32) for i in range(4)])
    tq = Ring([P.sbuf("tq%d" % i, [128, 512], F32) for i in range(4)])
    junk = P.sbuf("junk", [128, D], BF16)
    pst = Ring([P.psum("pst%d" % i, [128, 8, 128], BF16) for i in range(2)])
    pm = Ring([P.psum("pm%d" % i, [128, 512], F32) for i in range(6)])
    st = {}

    ld = {}

    def sl(i):
        x_ = X.next(); p_ = pt.next()
        P.dma("sp", x_.t[:, :], scr.x2.t[i * 128:(i + 1) * 128, :], scr.x2, x_)
        P.dma("sp", p_.t[:, :], c.p.t[i * 128:(i + 1) * 128, :], c.p, p_)
        ld[i] = (x_, p_)

    def sN(i):
        x_, p_ = ld.pop(i)
        n_ = xn.next(); q_ = pb_.next()
        norm_scale(P, x_, n_, ss.next(), rs.next())
        CP_(P, "act", q_.t[:, :], p_.t[:, :], [p_], [q_])
        st[i] = [x_, n_, q_]

    def sT(i):
        x_, n_, q_ = st[i]
        h_ = h3T.next(); t_ = pT.next()
        ps = pst.next()
        for k in range(8):
            TR_(P, ps.t[:, k, :], n_.t[:, k * 128:(k + 1) * 128], ident.t[:, :], [n_, ident], [ps], signal=(k == 7))
        CP_(P, "dve", h_.t[:, :, :], ps.t[:, :, :], [ps], [h_])
        pp = pst.next()
        for pc in range(2):
            TR_(P, pp.t[:, pc, :], q_.t[:, pc * 128:(pc + 1) * 128], ident.t[:, :], [q_, ident], [pp], signal=(pc == 1))
        CP_(P, "dve", t_.t[:, :, :], pp.t[:, 0:2, :], [pp], [t_])
        st[i] = [x_, h_, t_]

    def sM(i):
        x_, h_, t_ = st[i]
        for hf in range(2):
            cs = slice(hf * 512, (hf + 1) * 512)
            pg_ = pm.next()
            MM_(P, pg_.t[:, :], ones.t[:, :], bpg.t[:, cs], [ones, bpg], [pg_], start=True, stop=False, signal=False)
            for dc in range(8):
                MM_(P, pg_.t[:, :], h_.t[:, dc, :], Wpg.t[:, dc, cs], [h_, Wpg], [pg_], start=False, stop=(dc == 7), signal=(dc == 7))
            s_ = sgt.next(); q_ = tq.next()
            ACT_(P, s_.t[:, :], pg_.t[:, :], AF.Sigmoid, [pg_], [s_])
            pe_ = pm.next()
            for pc in range(2):
                MM_(P, pe_.t[:, :], t_.t[:, pc, :], Wple.t[:, pc, cs], [t_, Wple], [pe_], start=(pc == 0), stop=(pc == 1), signal=(pc == 1))
            TT_(P, "dve", q_.t[:, :], pe_.t[:, :], s_.t[:, :], ALU.mult, [pe_, s_], [q_])
            TT_(P, "pool", x_.t[:, cs], x_.t[:, cs], q_.t[:, :], ALU.add, [x_, q_], [x_])

    def sF(i):
        x_ = st.pop(i)[0]
        s_ = ss.next(); r_ = rs.next()
        P.op("act", lambda e: e.activation(out=junk.t[:, :], in_=x_.t[:, :], func=AF.Square, accum_out=s_.t[:, :]),
             reads=[x_], writes=[junk, s_])
        P.op("act", lambda e: e.activation(out=r_.t[:, :], in_=s_.t[:, :], func=AF.Sqrt, scale=1.0 / D, bias=1e-6),
             reads=[s_], writes=[r_])
        P.op("dve", lambda e: e.reciprocal(out=r_.t[:, :], in_=r_.t[:, :]), reads=[r_], writes=[r_])
        STT_(P, x_.t[:, :], x_.t[:, :], r_.t[:, 0:1], lnf.t[:, :], ALU.mult, ALU.mult, [x_, r_, lnf], [x_])
        P.dma("sp", c.out.t[i * 128:(i + 1) * 128, :], x_.t[:, :], x_, c.out)

    for i in range(min(5, NT)):
        sl(i)
    for i in range(-2, NT + 1):
        if 5 <= i + 5 < NT:
            sl(i + 5)
        if 0 <= i + 2 < NT:
            sN(i + 2)
        if 0 <= i + 1 < NT:
            sT(i + 1)
        if 0 <= i < NT:
            sM(i)
        if 0 <= i - 1 < NT:
            sF(i - 1)
    P.wait_all("sp", [c.out])
    P.barrier()
    P.emit_phase()
    P.stack.close()


def build_program(T, dbg=False):
    nc = bass.Bass("TRN2", target_bir_lowering=False)
    P = Prog(nc)
    c = declare_io(P, T)
    scr = make_scratch(P, T, dbg=dbg)
    phase1(P, c, T, scr)
    phase23(P, c, T, scr)
    phase4a(P, c, T, scr)
    phase5(P, c, T, scr)
    phase6(P, c, T, scr)
    P.semstack.close()
    return nc


_W1 = ["ln1_g", "w_in", "mix_mu", "w0", "w2", "a0", "a2", "g2", "k_k", "k_a", "lnx_w", "lnx_b", "gate_b", "w_br_rwkv", "w_br_attn",
       "w_o", "ln2_g", "w_ffn_up", "conv_w", "conv_b", "w_ffn_down", "ln3_g", "w_ple", "w_pg", "b_pg"]


def core_inputs(inputs, b, T):
    d = {"x": inputs["x"][b, :T], "p": inputs["p"][0, b, :T]}
    for k in _W1:
        d[k] = inputs[k][0]
    d["r_k"] = np.reshape(inputs["r_k"][0], (RW,))
    d["lnf_g"] = inputs["lnf_g"]
    d["biasT"] = make_biasT(np.asarray(inputs["rel_bias"][0]))
    d["cst"] = make_cst()
    return {k: np.ascontiguousarray(np.asarray(v), dtype=np.float32) for k, v in d.items()}


def kernel(**inputs):
    B, T = inputs["x"].shape[0], inputs["x"].shape[1]
    nc = build_program(T)
    in_maps = [core_inputs(inputs, b, T) for b in range(B)]
    res = run_bass_kernel_spmd(nc, in_maps, core_ids=list(range(B)))
    return np.stack([np.asarray(r["out"], dtype=np.float32) for r in res.results], axis=0)


def interleave(gens):
    active = list(gens)
    while active:
        for item in list(active):
            g, k = item
            for _ in range(k):
                try:
                    next(g)
                except StopIteration:
                    active.remove(item)
                    break


def take(gen, n):
    for _ in range(n):
        try:
            next(gen)
        except StopIteration:
            return
        yield


def phase23(P, c, T, scr):
    NT = T // 128
    P.stack = contextlib.ExitStack()
    SB = lambda n, sh, dt=F32: P.sbuf(n, sh, dt)
    RG = lambda n, sh, dt=F32, k=2: Ring([P.sbuf("%s%d" % (n, i), sh, dt) for i in range(k)])
    ident = SB("ident", [128, 128], BF16)
    P.dma("pool", ident.t[:, :], c.cst.t[:, 0, :], c.cst, ident)
    I8 = SB("I8", [128, 128], BF16)
    P.dma("pool", I8.t[:, :], c.cst.t[:, 7, :], c.cst, I8)
    cst = SB("cstf", [128, 8, 128])
    P.dma("sp", cst.t[:, :, :], c.cst.t[:, :, :], c.cst, cst)
    wa2 = SB("wa2", [128, RW], BF16)
    P.dma("pool", wa2.t[0:64, :], c.w2.t[:, :], c.w2, wa2)
    P.dma("pool", wa2.t[64:128, :], c.a2.t[:, :], c.a2, wa2)
    g2b = SB("g2b", [128, RW], BF16)
    P.dma("pool", g2b.t[:, :], c.g2.t[:, :], c.g2, g2b)
    bt = {}
    for nm in ["w0", "a0", "k_k", "k_a", "r_k", "lnx_w", "lnx_b"]:
        bt[nm] = SB("b_" + nm, [128, RW])
        P.dma("sp", bt[nm].t[:, :], bc(getattr(c, nm), RW), getattr(c, nm), bt[nm])
    Linc, Lsl, Lsu, cind = cst.t[:, 1, :], cst.t[:, 2, :], cst.t[:, 3, :], cst.t[:, 4, 0:2]
    b3 = lambda ap: ap.unsqueeze(1).to_broadcast([128, 8, 64])
    MU, ML, MUI, I64 = b3(cst.t[:, 5, 0:64]), b3(cst.t[:, 5, 64:128]), b3(cst.t[:, 6, 0:64]), b3(cst.t[:, 6, 64:128])

    pg = Ring([P.psum("pg%d" % i, [128, 512], F32) for i in range(3)])
    pq = Ring([P.psum("pq%d" % i, [128, 512], F32) for i in range(2)])
    pa = Ring([P.psum("pa%d" % i, [128, 512], F32) for i in range(2)])
    pb = Ring([P.psum("pb%d" % i, [128, 512], F32) for i in range(1)])
    v3 = lambda buf: buf.t.rearrange("p (h v) -> p h v", h=8)

    def bfv(buf, inner):
        return buf.t.bitcast(BF16)[:, 0:8 * inner].rearrange("p (h t) -> p h t", h=8)

    rkvt = RG("rkvt", [128, 1536], BF16, 5)
    lo1 = RG("lo1", [128, 128], BF16, 4); lo2 = RG("lo2", [128, 128], BF16, 4)
    names_f = ["t_w", "sg", "t_a", "a_", "e_pos", "e_neg", "e_prev", "e_rel", "kkr", "sq", "kk", "bb", "t1", "k2", "t2"]
    F = {n: SB(n, [128, RW]) for n in names_f}
    Fp = {n: SB("post_" + n, [128, RW]) for n in ["sq", "yn", "bv"]}
    YT = RG("post_yt", [128, RW])
    g_ = RG("g_", [128, RW], F32, 3)
    Bq = {n: SB(n, [128, RW], BF16) for n in ["At", "Bt", "Kt", "Rt"]}
    Bh = RG("Bh", [128, RW], BF16); Kh = RG("Kh", [128, RW], BF16)
    ATs = RG("ATs", [64, 8, 128], BF16); RTs = RG("RTs", [64, 8, 128], BF16)
    BTs = SB("BTs", [64, 8, 128], BF16); KTs = SB("KTs", [64, 8, 128], BF16)
    XA = RG("XA", [128, 8, 64], BF16); XTA = RG("XTA", [128, 8, 64], BF16)
    MakT = SB("MakT", [128, 8, 64], BF16)
    RBT = RG("RBT", [128, 8, 64], BF16); RKT = SB("RKT", [128, 8, 64], BF16)
    TTb = RG("TTb", [128, 8, 64], BF16)
    W1a = RG("W1a", [128, 8, 64]); Ya = RG("Ya", [128, 8, 64])
    W1 = SB("W1", [128, 8, 64], BF16); U = SB("U", [128, 8, 64], BF16)
    STb = RG("STb", [64, 8, 64], BF16); SF = RG("SF", [64, 8, 64], F32)
    tmpS = SB("tmpS", [64, 8, 64]); tmp2 = SB("tmp2", [64, 8, 64])
    pc = RG("pc", [64, 8, 2])
    sm = {n: SB(n, [128, 8]) for n in ["ssq", "nrm", "rn", "s1", "s2", "mean", "msq", "var", "rstd"]}
    bcf = RG("bcf", [128, 8], F32, 3)
    oa = SB("oa", [128, RW], BF16)
    oaT = RG("oaT", [128, 4, 128], BF16)
    h3 = lambda ap: ap.rearrange("p (h v) -> p h v", h=8)
    bl = lambda ap: ap.unsqueeze(2).to_broadcast([128, 8, 64])
    state = {}
    state["b"] = STb.next(); state["f"] = SF.next()
    P.op("pool", lambda e: e.memset(state["b"].t[:, :, :], 0.0), writes=[state["b"]])
    P.op("pool", lambda e: e.memset(state["f"].t[:, :, :], 0.0), writes=[state["f"]])
    tl = {}

    def loads(i):
        tok = slice(i * 128, (i + 1) * 128)
        rk = rkvt.next(); l1 = lo1.next(); l2 = lo2.next()
        P.dma("sp", rk.t[:, :], scr.rkv.t[tok, :], scr.rkv, rk)
        P.dma("sp", l1.t[:, :], scr.lora.t[0:128, tok], scr.lora, l1)
        P.dma("sp", l2.t[:, :], scr.lora.t[128:256, tok], scr.lora, l2)
        tl[i] = dict(rk=rk, l1=l1, l2=l2)

    def genA(i):
        t = tl[i]
        rk, l1, l2 = t["rk"], t["l1"], t["l2"]
        r_, k_ = rk.t[:, 0:512], rk.t[:, 512:1024]
        p_w = pg.next(); p_a = pg.next()
        MM_(P, p_w.t[:, :], l1.t[0:64, :], wa2.t[0:64, :], [l1, wa2], [p_w])
        MM_(P, p_a.t[:, :], l1.t[64:128, :], wa2.t[64:128, :], [l1, wa2], [p_a])
        f = {n: F[n].t[:, :] for n in names_f}
        TT_(P, "dve", f["t_w"], p_w.t[:, :], bt["w0"].t[:, :], ALU.add, [p_w, bt["w0"]], [F["t_w"]])
        ACT_(P, f["sg"], f["t_w"], AF.Sigmoid, [F["t_w"]], [F["sg"]])
        TT_(P, "dve", f["t_a"], p_a.t[:, :], bt["a0"].t[:, :], ALU.add, [p_a, bt["a0"]], [F["t_a"]])
        ACT_(P, f["a_"], f["t_a"], AF.Sigmoid, [F["t_a"]], [F["a_"]])
        yield
        p_g = pg.next()
        MM_(P, p_g.t[:, :], l2.t[:, :], g2b.t[:, :], [l2, g2b], [p_g])
        gq = g_.next(); t["gq"] = gq
        CP_(P, "act", gq.t[:, :], p_g.t[:, :], [p_g], [gq])
        yield
        p1 = pg.next()
        MM_(P, p1.t[:, :], Linc, f["sg"], [cst, F["sg"]], [p1])
        ACT_(P, f["e_pos"], p1.t[:, :], AF.Exp, [p1], [F["e_pos"]], scale=CDEC)
        ACT_(P, f["e_neg"], p1.t[:, :], AF.Exp, [p1], [F["e_neg"]], scale=-CDEC)
        yield
        p2 = pg.next()
        MM_(P, p2.t[:, :], Lsl, f["sg"], [cst, F["sg"]], [p2])
        ACT_(P, f["e_prev"], p2.t[:, :], AF.Exp, [p2], [F["e_prev"]], scale=CDEC)
        yield
        p3 = pg.next()
        MM_(P, p3.t[:, :], Lsu, f["sg"], [cst, F["sg"]], [p3])
        ACT_(P, f["e_rel"], p3.t[:, :], AF.Exp, [p3], [F["e_rel"]], scale=CDEC)
        yield
        p4 = pg.next()
        for h in range(8):
            MM_(P, p4.t[0:64, 2 * h:2 * h + 2], F["sg"].t[:, h * 64:(h + 1) * 64], cind, [F["sg"], cst], [p4], signal=(h == 7))
        pcq = pc.next(); t["pcq"] = pcq
        ACT_(P, pcq.t[:, :, :], p4.t[0:64, 0:16].rearrange("p (h c) -> p h c", h=8), AF.Exp, [p4], [pcq], scale=CDEC)
        yield
        TT_(P, "dve", f["kkr"], k_, bt["k_k"].t[:, :], ALU.mult, [rk, bt["k_k"]], [F["kkr"]])
        TT_(P, "pool", f["sq"], f["kkr"], f["kkr"], ALU.mult, [F["kkr"]], [F["sq"]])
        RED_(P, sm["ssq"].t[:, :], h3(f["sq"]), [F["sq"]], [sm["ssq"]])
        ACT_(P, sm["nrm"].t[:, :], sm["ssq"].t[:, :], AF.Sqrt, [sm["ssq"]], [sm["nrm"]])
        yield
        TS_(P, "dve", sm["nrm"].t[:, :], sm["nrm"].t[:, :], 1e-12, None, ALU.max, None, [sm["nrm"]], [sm["nrm"]])
        P.op("dve", lambda e: e.reciprocal(out=sm["rn"].t[:, :], in_=sm["nrm"].t[:, :]), reads=[sm["nrm"]], writes=[sm["rn"]])
        TT_(P, "dve", h3(f["kk"]), h3(f["kkr"]), bl(sm["rn"].t[:, :]), ALU.mult, [F["kkr"], sm["rn"]], [F["kk"]])
        yield
        TT_(P, "pool", f["bb"], f["kk"], f["a_"], ALU.mult, [F["kk"], F["a_"]], [F["bb"]])
        STT_(P, f["t1"], f["a_"], -1.0, bt["k_a"].t[:, :], ALU.add, ALU.mult, [F["a_"], bt["k_a"]], [F["t1"]])
        STT_(P, f["k2"], f["t1"], 1.0, k_, ALU.add, ALU.mult, [F["t1"], rk], [F["k2"]])
        yield
        STT_(P, Bq["At"].t[:, :], f["kk"], -1.0, f["e_prev"], ALU.mult, ALU.mult, [F["kk"], F["e_prev"]], [Bq["At"]])
        TT_(P, "dve", Bq["Bt"].t[:, :], f["bb"], f["e_neg"], ALU.mult, [F["bb"], F["e_neg"]], [Bq["Bt"]])
        yield
        TT_(P, "dve", Bq["Kt"].t[:, :], f["k2"], f["e_neg"], ALU.mult, [F["k2"], F["e_neg"]], [Bq["Kt"]])
        TT_(P, "pool", Bq["Rt"].t[:, :], r_, f["e_pos"], ALU.mult, [rk, F["e_pos"]], [Bq["Rt"]])
        yield
        bh = Bh.next(); kh = Kh.next(); t["bh"] = bh; t["kh"] = kh
        TT_(P, "pool", bh.t[:, :], f["bb"], f["e_rel"], ALU.mult, [F["bb"], F["e_rel"]], [bh])
        TT_(P, "pool", kh.t[:, :], f["k2"], f["e_rel"], ALU.mult, [F["k2"], F["e_rel"]], [kh])
        yield
        TT_(P, "pool", f["t2"], r_, bt["r_k"].t[:, :], ALU.mult, [rk, bt["r_k"]], [F["t2"]])
        TT_(P, "pool", f["t2"], f["t2"], f["k2"], ALU.mult, [F["t2"], F["k2"]], [F["t2"]])
        bq = bcf.next(); t["bq"] = bq
        RED_(P, bq.t[:, :], h3(f["t2"]), [F["t2"]], [bq])
        yield
        ats = ATs.next(); rts = RTs.next(); t["ats"] = ats; t["rts"] = rts
        for (src, dst, ev) in [("At", ats, "act"), ("Bt", BTs, "dve"), ("Kt", KTs, "act"), ("Rt", rts, "dve")]:
            ps = pg.next()
            pv = bfv(ps, 128)
            for h in range(8):
                TR_(P, pv[0:64, h, :], Bq[src].t[:, h * 64:(h + 1) * 64], ident.t[:, :], [Bq[src], ident], [ps], signal=(h == 7))
            CP_(P, ev, dst.t[:, :, :], pv[0:64, :, :], [ps], [dst])
            yield

        def prod(lhs, rhs, mask, dst, eng="dve"):
            ps = pg.next()
            for cc in range(2):
                cols = slice(cc * 64, (cc + 1) * 64)
                for h in range(8):
                    MM_(P, v3(ps)[cc * 64:(cc + 1) * 64, h, :], lhs.t[:, h, cols], rhs.t[:, h, cols], [lhs, rhs], [ps],
                        signal=(cc == 1 and h == 7))
            TT_(P, eng, dst.t[:, :, :], v3(ps), mask, ALU.mult, [ps, cst], [dst])
        xt_ = XTA.next(); x_ = XA.next()
        rbt = RBT.next(); t["rbt"] = rbt
        prod(BTs, ats, MU, xt_); yield
        prod(ats, BTs, ML, x_); yield
        prod(KTs, ats, MU, MakT); yield
        prod(BTs, rts, MUI, rbt); yield
        prod(KTs, rts, MUI, RKT); yield
        tt = TTb.next(); t["tt"] = tt
        TT_(P, "dve", tt.t[:, :, :], xt_.t[:, :, :], I64, ALU.add, [xt_, cst], [tt])
        for lev in range(1, 6):
            xn_ = XA.next()
            xtn_ = XTA.next() if lev < 5 else None
            for cc in range(2):
                hs = slice(cc * 64, (cc + 1) * 64)
                px = pg.next()
                for h in range(8):
                    MM_(P, v3(px)[hs, h, :], xt_.t[hs, h, :], x_.t[hs, h, :], [xt_, x_], [px], signal=(h == 7))
                CP_(P, "act", xn_.t[hs, :, :], v3(px)[hs, :, :], [px], [xn_])
                yield
                if lev < 5:
                    pxt = pg.next()
                    for h in range(8):
                        MM_(P, v3(pxt)[hs, h, :], x_.t[hs, h, :], xt_.t[hs, h, :], [xt_, x_], [pxt], signal=(h == 7))
                    CP_(P, "act", xtn_.t[hs, :, :], v3(pxt)[hs, :, :], [pxt], [xtn_])
                    yield
            x_ = xn_
            if lev < 5:
                xt_ = xtn_
            for cc in range(2):
                hs = slice(cc * 64, (cc + 1) * 64)
                pt = pg.next()
                for h in range(8):
                    MM_(P, v3(pt)[hs, h, :], x_.t[hs, h, :], tt.t[hs, h, :], [x_, tt], [pt], signal=(h == 7))
                TT_(P, "dve", tt.t[hs, :, :], v3(pt)[hs, :, :], tt.t[hs, :, :], ALU.add, [pt, tt], [tt])
                yield
        w1a = W1a.next(); ya = Ya.next(); t["w1a"] = w1a; t["ya"] = ya
        for cc in range(2):
            hs = slice(cc * 64, (cc + 1) * 64)
            ps = pg.next()
            for h in range(8):
                MM_(P, v3(ps)[hs, h, :], MakT.t[hs, h, :], rk.t[hs, 1024 + h * 64:1024 + (h + 1) * 64], [MakT, rk], [ps], signal=(h == 7))
            CP_(P, "act", w1a.t[hs, :, :], v3(ps)[hs, :, :], [ps], [w1a])
            yield
            ps = pg.next()
            for h in range(8):
                MM_(P, v3(ps)[hs, h, :], RKT.t[hs, h, :], rk.t[hs, 1024 + h * 64:1024 + (h + 1) * 64], [RKT, rk], [ps], signal=(h == 7))
            CP_(P, "act", ya.t[hs, :, :], v3(ps)[hs, :, :], [ps], [ya])
            yield

    def genB(i):
        t = tl[i]
        rk, ats, rts, bh, kh, rbt, tt, w1a, ya, pcq = (t[k] for k in ["rk", "ats", "rts", "bh", "kh", "rbt", "tt", "w1a", "ya", "pcq"])
        yt = YT.next(); t["yt"] = yt
        yt3 = h3(yt.t[:, :])
        for cc in range(2):
            hs = slice(cc * 64, (cc + 1) * 64)
            cols = slice(cc * 64, (cc + 1) * 64)
            st_b, st_f = state["b"], state["f"]
            pw = pq.next()
            for h in range(8):
                MM_(P, v3(pw)[hs, h, :], ats.t[:, h, cols], st_b.t[:, h, :], [ats, st_b], [pw], signal=(h == 7))
            TT_(P, "dve", W1.t[hs, :, :], v3(pw)[hs, :, :], w1a.t[hs, :, :], ALU.add, [pw, w1a], [W1])
            yield
            pk = pq.next()
            for h in range(8):
                MM_(P, v3(pk)[0:64, h, :], kh.t[hs, h * 64:(h + 1) * 64], rk.t[hs, 1024 + h * 64:1024 + (h + 1) * 64], [kh, rk], [pk], signal=(h == 7))
            TT_(P, "dve", tmpS.t[:, :, :], st_f.t[:, :, :], pcq.t[:, :, cc:cc + 1].to_broadcast([64, 8, 64]), ALU.mult, [st_f, pcq], [tmpS])
            TT_(P, "dve", tmp2.t[:, :, :], v3(pk)[0:64, :, :], tmpS.t[:, :, :], ALU.add, [pk, tmpS], [tmp2])
            yield
            pu = pq.next()
            for h in range(8):
                MM_(P, v3(pu)[hs, h, :], tt.t[hs, h, :], W1.t[hs, h, :], [tt, W1], [pu], signal=(h == 7))
            CP_(P, "act", U.t[hs, :, :], v3(pu)[hs, :, :], [pu], [U])
            yield
            prs = pq.next()
            for h in range(8):
                MM_(P, v3(prs)[hs, h, :], rts.t[:, h, cols], st_b.t[:, h, :], [rts, st_b], [prs], signal=(h == 7))
            TT_(P, "dve", yt3[hs, :, :], v3(prs)[hs, :, :], ya.t[hs, :, :], ALU.add, [prs, ya], [yt])
            yield
            psn = pq.next()
            for h in range(8):
                MM_(P, v3(psn)[0:64, h, :], bh.t[hs, h * 64:(h + 1) * 64], U.t[hs, h, :], [bh, U], [psn], signal=(h == 7))
            nb = STb.next(); nf = SF.next()
            TT_(P, "dve", nb.t[:, :, :], v3(psn)[0:64, :, :], tmp2.t[:, :, :], ALU.add, [psn, tmp2], [nb])
            TT_(P, "dve", nf.t[:, :, :], v3(psn)[0:64, :, :], tmp2.t[:, :, :], ALU.add, [psn, tmp2], [nf])
            state["b"], state["f"] = nb, nf
            yield
            py = pq.next()
            for h in range(8):
                MM_(P, v3(py)[hs, h, :], rbt.t[hs, h, :], U.t[hs, h, :], [rbt, U], [py], signal=(h == 7))
            TT_(P, "dve", yt3[hs, :, :], v3(py)[hs, :, :], yt3[hs, :, :], ALU.add, [py, yt], [yt])
            yield

    def genPost(i):
        t = tl[i]
        rk, gq, bq, yt = t["rk"], t["gq"], t["bq"], t["yt"]
        tok = slice(i * 128, (i + 1) * 128)
        v_ = rk.t[:, 1024:1536]
        fp = {n: Fp[n].t[:, :] for n in Fp}
        fp["yt"] = yt.t[:, :]
        yt3 = h3(fp["yt"]); yn3 = h3(fp["yn"])
        RED_(P, sm["s1"].t[:, :], yt3, [yt], [sm["s1"]])
        TT_(P, "pool", fp["sq"], fp["yt"], fp["yt"], ALU.mult, [yt], [Fp["sq"]])
        RED_(P, sm["s2"].t[:, :], h3(fp["sq"]), [Fp["sq"]], [sm["s2"]])
        yield
        TS_(P, "dve", sm["mean"].t[:, :], sm["s1"].t[:, :], 1.0 / 64, None, ALU.mult, None, [sm["s1"]], [sm["mean"]])
        TT_(P, "dve", sm["msq"].t[:, :], sm["mean"].t[:, :], sm["mean"].t[:, :], ALU.mult, [sm["mean"]], [sm["msq"]])
        STT_(P, sm["var"].t[:, :], sm["s2"].t[:, :], 1.0 / 64, sm["msq"].t[:, :], ALU.mult, ALU.subtract, [sm["s2"], sm["msq"]], [sm["var"]])
        ACT_(P, sm["rstd"].t[:, :], sm["var"].t[:, :], AF.Sqrt, [sm["var"]], [sm["rstd"]], bias=64e-5)
        P.op("dve", lambda e: e.reciprocal(out=sm["rstd"].t[:, :], in_=sm["rstd"].t[:, :]), reads=[sm["rstd"]], writes=[sm["rstd"]])
        yield
        TT_(P, "dve", yn3, yt3, bl(sm["mean"].t[:, :]), ALU.subtract, [yt, sm["mean"]], [Fp["yn"]])
        TT_(P, "dve", yn3, yn3, bl(sm["rstd"].t[:, :]), ALU.mult, [Fp["yn"], sm["rstd"]], [Fp["yn"]])
        yield
        TT_(P, "pool", fp["yn"], fp["yn"], bt["lnx_w"].t[:, :], ALU.mult, [Fp["yn"], bt["lnx_w"]], [Fp["yn"]])
        TT_(P, "pool", fp["yn"], fp["yn"], bt["lnx_b"].t[:, :], ALU.add, [Fp["yn"], bt["lnx_b"]], [Fp["yn"]])
        TT_(P, "dve", h3(fp["bv"]), h3(v_), bl(bq.t[:, :]), ALU.mult, [rk, bq], [Fp["bv"]])
        yield
        TT_(P, "pool", fp["yn"], fp["yn"], fp["bv"], ALU.add, [Fp["yn"], Fp["bv"]], [Fp["yn"]])
        TT_(P, "dve", oa.t[:, :], fp["yn"], gq.t[:, :], ALU.mult, [Fp["yn"], gq], [oa])
        yield
        ps = pq.next()
        pv = ps.t.bitcast(BF16)[:, 0:512].rearrange("p (c t) -> p c t", c=4)
        for fc in range(4):
            TR_(P, pv[:, fc, :], oa.t[:, fc * 128:(fc + 1) * 128], ident.t[:, :], [oa, ident], [ps], signal=(fc == 3))
        ot_ = oaT.next()
        CP_(P, "act", ot_.t[:, :, :], pv, [ps], [ot_])
        P.dma("pool", scr.oaT.t[:, tok].rearrange("(c p) t -> p c t", p=128), ot_.t[:, :, :], ot_, scr.oaT)
        yield

    kTh = SB("kTh", [128, T], BF16); qTh = SB("qTh", [128, T], BF16)
    Vh = RG("Vh", [128, NT, 65], BF16)
    bT = RG("bT", [128, 640], BF16)
    for b in (kTh, qTh):
        P.op("pool", lambda e, b=b: e.memset(b.t[64:128, :], 0.0), writes=[b])
    for b in Vh.items:
        P.op("pool", lambda e, b=b: e.memset(b.t[:, :, 64:65], 1.0), writes=[b])
    PT = [SB("PT%d" % i, [128, 640], BF16) for i in range(6)]
    ob = SB("ob", [128, NT, 512], BF16)
    rc = RG("rc", [128, 1])
    obT = RG("obT", [128, 4, 128], BF16)

    def genC():
        for h in range(NH):
            kt = kTh; qt = qTh; vh = Vh.next(); bias = bT.next()
            P.dma("sp", kt.t[0:64, :], scr.kT.t[h * 64:(h + 1) * 64, :], scr.kT, kt)
            P.dma("sp", qt.t[0:64, :], scr.qT.t[h * 64:(h + 1) * 64, :], scr.qT, qt)
            P.dma("sp", vh.t[:, :, 0:64], scr.vA.t[:, h * 64:(h + 1) * 64].rearrange("(m p) d -> p m d", p=128), scr.vA, vh)
            P.dma("pool", bias.t[:, :], c.biasT.t[h, :, :], c.biasT, bias)
            for m in range(NT):
                W = min(640, T - 128 * m)
                pt = PT[m % 6]
                for (c0, c1) in [(0, min(W, 512)), (512, W)]:
                    if c1 <= c0:
                        continue
                    ps = pa.next()
                    n = c1 - c0
                    MM_(P, ps.t[:, 0:n], I8.t[:, :], bias.t[:, c0:c1], [I8, bias], [ps], start=True, stop=False, signal=False)
                    MM_(P, ps.t[:, 0:n], kt.t[:, m * 128:(m + 1) * 128], qt.t[:, m * 128 + c0:m * 128 + c1], [kt, qt], [ps], start=False, stop=True)
                    ACT_(P, pt.t[:, c0:c1], ps.t[:, 0:n], AF.Exp, [ps], [pt], scale=0.125)
                yield
                pv = pb.next()
                for cc in range(2):
                    cq = 2 * m + cc
                    m0 = max(0, (cq - 8) // 2)
                    ms = list(range(m0, cq // 2 + 1))
                    for mi, mp in enumerate(ms):
                        off = (cq - 2 * mp) * 64
                        MM_(P, pv.t[cc * 64:(cc + 1) * 64, 0:65], PT[mp % 6].t[:, off:off + 64], vh.t[:, mp, :], [PT[mp % 6], vh], [pv],
                            start=(mi == 0), stop=(mi == len(ms) - 1), signal=(mi == len(ms) - 1))
                r_ = rc.next()
                P.op("dve", lambda e, r_=r_, pv=pv: e.reciprocal(out=r_.t[:, :], in_=pv.t[:, 64:65]), reads=[pv], writes=[r_])
                ACT_(P, ob.t[:, m, h * 64:(h + 1) * 64], pv.t[:, 0:64], AF.Copy, [pv, r_], [ob], scale=r_.t[:, :])
                yield
        for m in range(NT):
            ps = pa.next()
            pvw = ps.t.bitcast(BF16)[:, 0:512].rearrange("p (c t) -> p c t", c=4)
            for fc in range(4):
                TR_(P, pvw[:, fc, :], ob.t[:, m, fc * 128:(fc + 1) * 128], ident.t[:, :], [ob, ident], [ps], signal=(fc == 3))
            o_ = obT.next()
            CP_(P, "dve", o_.t[:, :, :], pvw, [ps], [o_])
            P.dma("pool", scr.obT.t[:, m * 128:(m + 1) * 128].rearrange("(c p) t -> p c t", p=128), o_.t[:, :, :], o_, scr.obT)
            yield

    gC = genC()
    nC = (NH * NT * 2 + NT + NT - 1) // NT + 1
    for i in range(min(3, NT)):
        loads(i)
    for _ in genA(0):
        pass
    for i in range(NT):
        if i + 3 < NT:
            loads(i + 3)
        gens = [(genB(i), 1)]
        if i >= 1:
            gens.append((genPost(i - 1), 1))
        if i + 1 < NT:
            gens.append((genA(i + 1), 5))
        gens.append((take(gC, nC), 2))
        interleave(gens)
    for _ in genPost(NT - 1):
        pass
    for _ in gC:
        pass
    P.barrier()
    P.emit_phase()
    P.stack.close()
```

```python
import contextlib
import os
CUT = int(os.environ.get("P2CUT", "99"))
import numpy as np
import concourse.bass as bass
import concourse.mybir as mybir
from concourse.bass_utils import run_bass_kernel_spmd

F32 = mybir.dt.float32
BF16 = mybir.dt.bfloat16
AF = mybir.ActivationFunctionType
ALU = mybir.AluOpType
AX = mybir.AxisListType

COMPUTE = ("pe", "act", "dve", "pool")
ENGS = ("pe", "act", "dve", "pool", "sp")


class Buf:
    def __init__(self, name, t=None, is_dram=False):
        self.name = name
        self.t = t
        self.is_dram = is_dram
        self.w = {}
        self.r = {}
        self.dma_key = None
        self.dma_cnt = 0

    def __getitem__(self, k):
        return self.t[k]


class Prog:
    def __init__(self, nc):
        self.nc = nc
        self.stack = contextlib.ExitStack()
        self.semstack = contextlib.ExitStack()
        self.dma_keys = {}
        self.ops = {e: [] for e in ENGS}
        self.cnt = {e: 0 for e in COMPUTE}
        self.pending = {e: False for e in COMPUTE}
        self.waited = {e: {} for e in ENGS}
        self.sems = {}
        self.nbuf = 0

    def sem(self, key):
        if key not in self.sems:
            self.sems[key] = self.semstack.enter_context(self.nc.semaphore("s_" + key))
        return self.sems[key]

    def sbuf(self, name, shape, dtype):
        self.nbuf += 1
        name = "%s_%d" % (name, self.nbuf)
        t = self.stack.enter_context(self.nc.sbuf_tensor(name, list(shape), dtype))
        return Buf(name, t)

    def psum(self, name, shape, dtype):
        self.nbuf += 1
        name = "%s_%d" % (name, self.nbuf)
        t = self.stack.enter_context(self.nc.psum_tensor(name, list(shape), dtype))
        return Buf(name, t)

    def dram(self, name, shape, dtype, kind="Internal"):
        t = self.nc.dram_tensor(name, list(shape), dtype, kind=kind)
        return Buf(name, t, is_dram=True)

    def _collect(self, eng, reads, writes):
        need = {}

        def add(k, v, same_ok):
            if k == eng and not same_ok:
                return
            if need.get(k, 0) < v:
                need[k] = v

        for b in reads:
            for k, v in b.w.items():
                add(k, v, True)
        for b in writes:
            for k, v in b.w.items():
                add(k, v, False)
            for k, v in b.r.items():
                add(k, v, False)
        out = []
        wd = self.waited[eng]
        for k, v in need.items():
            if wd.get(k, 0) < v:
                wd[k] = v
                out.append((k, v))
        return out

    def _record(self, key, val, reads, writes):
        for b in reads:
            if b.r.get(key, 0) < val:
                b.r[key] = val
        for b in writes:
            b.w = {key: val}
            b.r = {}

    def op(self, eng, fn, reads=(), writes=(), signal=True):
        waits = self._collect(eng, reads, writes)
        if signal:
            self.cnt[eng] += 1
            val = self.cnt[eng]
            self.pending[eng] = False
        else:
            val = self.cnt[eng] + 1
            self.pending[eng] = True
        self.ops[eng].append((waits, fn, (eng, 1) if signal else None))
        self._record(eng, val, reads, writes)

    def dma(self, q, out_ap, in_ap, src, dst, sem_on=None, **kw):
        waits = self._collect(q, [src], [dst])
        sb = sem_on if sem_on is not None else (src if dst.is_dram else dst)
        if sb.dma_key is None:
            pool = self.__dict__.setdefault("sem_pool", [])
            sb.dma_sw = (q == "pool")
            if pool and not sb.dma_sw:
                sb.dma_key, sb.dma_cnt = pool.pop()
            else:
                sb.dma_key = "d%d" % len(self.__dict__.setdefault("all_dma_keys", []))
                self.all_dma_keys.append(sb.dma_key)
            if not sb.dma_sw:
                self.__dict__.setdefault("phase_owners", []).append(sb)
        assert sb.dma_sw == (q == "pool"), "buffer %s mixes software and hardware DGE" % sb.name
        key = sb.dma_key
        sb.dma_cnt += 16
        val = sb.dma_cnt
        self.dma_keys[key] = val
        self.ops[q].append((waits, lambda e: e.dma_start(out=out_ap, in_=in_ap, **kw), (key, 16)))
        self._record(key, val, [src], [dst])

    def wait_all(self, eng, bufs):
        waits = self._collect(eng, [], list(bufs))
        if waits:
            self.ops[eng].append((waits, None, None))

    def barrier(self):
        ev = {e: self.cnt[e] for e in COMPUTE if self.cnt[e] > 0}
        ev.update(self.dma_keys)
        for e in ENGS:
            waits = []
            for k, v in ev.items():
                if k == e:
                    continue
                if self.waited[e].get(k, 0) < v:
                    self.waited[e][k] = v
                    waits.append((k, v))
            if waits:
                self.ops[e].append((waits, None, None))

    def emit_phase(self):
        self.phase_idx = getattr(self, "phase_idx", 0) + 1
        with self.nc.named_scope("phase%d" % self.phase_idx):
            self.emit()
        self.ops = {e: [] for e in ENGS}
        for sb in self.__dict__.get("phase_owners", []):
            self.__dict__.setdefault("sem_pool", []).append((sb.dma_key, sb.dma_cnt))
        self.phase_owners = []

    def emit(self):
        nc = self.nc
        for e in COMPUTE:
            assert not self.pending[e], "engine %s ends with an unsignaled op" % e
        handles = {"pe": "tensor", "act": "scalar", "dve": "vector", "pool": "gpsimd", "sp": "sync"}
        for k in list(self.waited["pe"].keys()) + list(COMPUTE):
            self.sem(k)
        for e in ENGS:
            for (waits, fn, inc) in self.ops[e]:
                for k, v in waits:
                    self.sem(k)
                if inc is not None:
                    self.sem(inc[0])
        prog = self

        def replay(name):
            def run(eng):
                for (waits, fn, inc) in prog.ops[name]:
                    for k, v in waits:
                        eng.wait_ge(prog.sems[k], v)
                    if fn is None:
                        continue
                    ins = fn(eng)
                    if inc is not None:
                        ins.then_inc(prog.sems[inc[0]], inc[1])
            return run

        with nc.Block() as block:
            block.tensor(replay("pe"))
            block.scalar(replay("act"))
            block.vector(replay("dve"))
            block.gpsimd(replay("pool"))
            block.sync(replay("sp"))

    def close(self):
        self.stack.close()


D = 1024
NH = 8
HD = 64
RW = 512
RWKV_COLS = 1792
IN_COLS = 5376
DFF = 2816
PLE = 256
CDEC = -0.6065306597126334
NEG = -30000.0


def bc(vec_buf, n):
    return vec_buf.t[0:n].partition_broadcast(128)


class Ctx:
    pass


def declare_io(P, T):
    c = Ctx()
    f = lambda n, s: P.dram(n, s, F32, kind="ExternalInput")
    c.x = f("x", [T, D]); c.p = f("p", [T, PLE])
    c.ln1_g = f("ln1_g", [D]); c.w_in = f("w_in", [D, IN_COLS]); c.mix_mu = f("mix_mu", [RWKV_COLS])
    c.w0 = f("w0", [RW]); c.w2 = f("w2", [64, RW]); c.a0 = f("a0", [RW]); c.a2 = f("a2", [64, RW])
    c.g2 = f("g2", [128, RW]); c.k_k = f("k_k", [RW]); c.k_a = f("k_a", [RW]); c.r_k = f("r_k", [RW])
    c.lnx_w = f("lnx_w", [RW]); c.lnx_b = f("lnx_b", [RW]); c.biasT = f("biasT", [NH, 128, 640])
    c.gate_b = f("gate_b", [2048]); c.w_br_rwkv = f("w_br_rwkv", [RW, D]); c.w_br_attn = f("w_br_attn", [RW, D])
    c.w_o = f("w_o", [D, D]); c.ln2_g = f("ln2_g", [D]); c.w_ffn_up = f("w_ffn_up", [D, 2 * DFF])
    c.conv_w = f("conv_w", [3, DFF]); c.conv_b = f("conv_b", [DFF]); c.w_ffn_down = f("w_ffn_down", [DFF, D])
    c.ln3_g = f("ln3_g", [D]); c.w_ple = f("w_ple", [PLE, D]); c.w_pg = f("w_pg", [D, D]); c.b_pg = f("b_pg", [D])
    c.lnf_g = f("lnf_g", [D])
    c.cst = f("cst", [128, 8, 128])
    c.out = P.dram("out", [T, D], F32, kind="ExternalOutput")
    return c


def rmsnorm_T(P, xt, xn, junk, ss, rs, pst, hT_dst_fn, ident, tag=""):
    P.op("act", lambda e: e.activation(out=junk.t[:, :], in_=xt.t[:, :], func=AF.Square, accum_out=ss.t[:, :]),
         reads=[xt], writes=[junk, ss])
    P.op("act", lambda e: e.activation(out=rs.t[:, :], in_=ss.t[:, :], func=AF.Sqrt, scale=1.0 / D, bias=1e-6),
         reads=[ss], writes=[rs])
    P.op("dve", lambda e: e.reciprocal(out=rs.t[:, :], in_=rs.t[:, :]), reads=[rs], writes=[rs])
    P.op("act", lambda e: e.activation(out=xn.t[:, :], in_=xt.t[:, :], func=AF.Copy, scale=rs.t[:, :]),
         reads=[xt, rs], writes=[xn])
    for c in range(8):
        P.op("pe", lambda e, c=c: e.transpose(out=pst.t[:, c, :], in_=xn.t[:, c * 128:(c + 1) * 128], identity=ident.t[:, :]),
             reads=[xn, ident], writes=[pst], signal=(c == 7))
    hT_dst_fn(pst)


class Ring:
    def __init__(self, items):
        self.items = list(items)
        self.i = 0

    def next(self):
        b = self.items[self.i % len(self.items)]
        self.i += 1
        return b


def load_cols(P, q, dst, vec, n, ncol):
    P.dma(q, dst.t[:, 0:ncol], vec.t.rearrange("(c p) -> p c", p=128), vec, dst, allow_slow_non_contiguous=True)


def phase1(P, c, T, scr):
    NB = T // 512
    P.stack = contextlib.ExitStack()
    ident = P.sbuf("ident", [128, 128], BF16)
    P.dma("pool", ident.t[:, :], c.cst.t[:, 0, :], c.cst, ident)
    g1 = P.sbuf("g1", [128, 8], F32)
    load_cols(P, "sp", g1, c.ln1_g, D, 8)
    gb = P.sbuf("gb", [128, 16], F32)
    load_cols(P, "sp", gb, c.gate_b, 2048, 16)
    mub = P.sbuf("mub", [128, RWKV_COLS], F32)
    omm = P.sbuf("omm", [128, RWKV_COLS], F32)
    P.dma("sp", mub.t[:, :], bc(c.mix_mu, RWKV_COLS), c.mix_mu, mub)
    P.op("dve", lambda e: e.tensor_scalar(out=omm.t[:, :], in0=mub.t[:, :], scalar1=-1.0, scalar2=1.0, op0=ALU.mult, op1=ALU.add),
         reads=[mub], writes=[omm])
    Wr1 = P.sbuf("Wr1", [128, 8, RWKV_COLS], BF16)
    Wr2 = P.sbuf("Wr2", [128, 8, RWKV_COLS], BF16)
    Wo = P.sbuf("Wo", [128, 8, 3584], BF16)
    stg = Ring([P.sbuf("stg%d" % i, [128, RWKV_COLS], F32) for i in range(3)])
    xt = Ring([P.sbuf("xt%d" % i, [128, D], F32) for i in range(2)] + stg.items)
    xn = Ring([P.sbuf("xn%d" % i, [128, D], BF16) for i in range(2)])
    junk = P.sbuf("junk", [128, D], BF16)
    ss = Ring([P.sbuf("ss%d" % i, [128, 1], F32) for i in range(2)])
    rs = Ring([P.sbuf("rs%d" % i, [128, 1], F32) for i in range(2)])
    hT = [P.sbuf("hT%d" % i, [128, 8, 513], BF16) for i in range(2)]
    pst = Ring([P.psum("pst%d" % i, [128, 8, 128], BF16) for i in range(2)])
    pm = Ring([P.psum("pm%d" % i, [128, 512], F32) for i in range(5)])
    of = Ring([P.sbuf("of%d" % i, [128, 512], BF16) for i in range(4)])
    ot = Ring([P.sbuf("ot%d" % i, [128, 1536], BF16) for i in range(2)])
    ov = Ring([P.sbuf("ov%d" % i, [128, 512], BF16) for i in range(2)])

    xq = {}

    def lx(b):
        for j in range(4):
            i = 4 * b + j
            x_ = xt.next()
            P.dma("sp", x_.t[:, 0:D], c.x.t[i * 128:(i + 1) * 128, :], c.x, x_)
            xq[i] = x_

    def nrm(b):
        h = hT[b % 2]
        if b == 0:
            P.op("pool", lambda e, h=h: e.memset(h.t[:, :, 0:1], 0.0), writes=[h])
        else:
            hp = hT[(b - 1) % 2]
            P.op("pool", lambda e, h=h, hp=hp: e.tensor_copy(out=h.t[:, :, 0:1], in_=hp.t[:, :, 512:513]), reads=[hp], writes=[h])
        for j in range(4):
            i = 4 * b + j
            x_ = xq.pop(i)
            xn_ = xn.next(); ss_ = ss.next(); rs_ = rs.next(); ps_ = pst.next()
            P.op("act", lambda e, x_=x_, ss_=ss_: e.activation(out=junk.t[:, :], in_=x_.t[:, 0:D], func=AF.Square, accum_out=ss_.t[:, :]),
                 reads=[x_], writes=[junk, ss_])
            P.op("act", lambda e, ss_=ss_, rs_=rs_: e.activation(out=rs_.t[:, :], in_=ss_.t[:, :], func=AF.Sqrt, scale=1.0 / D, bias=1e-6),
                 reads=[ss_], writes=[rs_])
            P.op("dve", lambda e, rs_=rs_: e.reciprocal(out=rs_.t[:, :], in_=rs_.t[:, :]), reads=[rs_], writes=[rs_])
            P.op("act", lambda e, x_=x_, xn_=xn_, rs_=rs_: e.activation(out=xn_.t[:, :], in_=x_.t[:, 0:D], func=AF.Copy, scale=rs_.t[:, :]),
                 reads=[x_, rs_], writes=[xn_])
            for k in range(8):
                TR_(P, ps_.t[:, k, :], xn_.t[:, k * 128:(k + 1) * 128], ident.t[:, :], [xn_, ident], [ps_], signal=(k == 7))
            CP_(P, "dve", h.t[:, :, 1 + j * 128:1 + (j + 1) * 128], ps_.t[:, :, :], [ps_], [h])

    lx(0)
    nrm(0)
    for pc in range(3):
        for dc in range(8):
            s = stg.next()
            P.dma("sp", s.t[:, :], c.w_in.t[dc * 128:(dc + 1) * 128, pc * 1792:(pc + 1) * 1792], c.w_in, s)
            if pc == 0:
                P.op("dve", lambda e, s=s, dc=dc: e.scalar_tensor_tensor(out=Wr1.t[:, dc, :], in0=s.t[:, :], scalar=g1.t[:, dc:dc + 1],
                                                                   in1=omm.t[:, :], op0=ALU.mult, op1=ALU.mult),
                     reads=[s, g1, omm], writes=[Wr1])
                P.op("dve", lambda e, s=s, dc=dc: e.scalar_tensor_tensor(out=Wr2.t[:, dc, :], in0=s.t[:, :], scalar=g1.t[:, dc:dc + 1],
                                                                    in1=mub.t[:, :], op0=ALU.mult, op1=ALU.mult),
                     reads=[s, g1, mub], writes=[Wr2])
            else:
                P.op("act", lambda e, s=s, dc=dc, pc=pc: e.activation(out=Wo.t[:, dc, (pc - 1) * 1792:pc * 1792], in_=s.t[:, :], func=AF.Copy,
                                                                 scale=g1.t[:, dc:dc + 1]),
                     reads=[s, g1], writes=[Wo])

    for b in range(NB):
        h = hT[b % 2]
        if b + 1 < NB:
            lx(b + 1)
        for j in range(4):
            i = 4 * b + j
            o_ = ot.next()
            for cg in range(3):
                ps = pm.next()
                for dc in range(8):
                    P.op("pe", lambda e, ps=ps, dc=dc, cg=cg, j=j, h=h: e.matmul(out=ps.t[:, :], lhsT=h.t[:, dc, 1 + j * 128:1 + (j + 1) * 128],
                                                                              rhs=Wr1.t[:, dc, cg * 512:(cg + 1) * 512], start=(dc == 0), stop=False),
                         reads=[h, Wr1], writes=[ps], signal=False)
                for dc in range(8):
                    P.op("pe", lambda e, ps=ps, dc=dc, cg=cg, j=j, h=h: e.matmul(out=ps.t[:, :], lhsT=h.t[:, dc, j * 128:(j + 1) * 128],
                                                                              rhs=Wr2.t[:, dc, cg * 512:(cg + 1) * 512], start=False, stop=(dc == 7)),
                         reads=[h, Wr2], writes=[ps], signal=(dc == 7))
                P.op("dve", lambda e, ps=ps, o_=o_, cg=cg: e.tensor_copy(out=o_.t[:, cg * 512:(cg + 1) * 512], in_=ps.t[:, :]), reads=[ps], writes=[o_])
            P.dma("sp", scr.rkv.t[i * 128:(i + 1) * 128, :], o_.t[:, :], o_, scr.rkv)
            ps = pm.next()
            v_ = ov.next()
            for dc in range(8):
                P.op("pe", lambda e, ps=ps, dc=dc, j=j, h=h: e.matmul(out=ps.t[:, :], lhsT=h.t[:, dc, 1 + j * 128:1 + (j + 1) * 128],
                                                                   rhs=Wo.t[:, dc, 1024:1536], start=(dc == 0), stop=(dc == 7)),
                     reads=[h, Wo], writes=[ps], signal=(dc == 7))
            P.op("dve", lambda e, ps=ps, v_=v_: e.tensor_copy(out=v_.t[:, :], in_=ps.t[:, :]), reads=[ps], writes=[v_])
            P.dma("sp", scr.vA.t[i * 128:(i + 1) * 128, :], v_.t[:, :], v_, scr.vA)
        if b + 1 < NB:
            nrm(b + 1)
        tok = slice(b * 512, (b + 1) * 512)
        for g in range(2):
            ps = pm.next()
            c0 = 1536 + g * 128
            for dc in range(8):
                P.op("pe", lambda e, ps=ps, dc=dc, c0=c0, h=h: e.matmul(out=ps.t[:, :], lhsT=Wr1.t[:, dc, c0:c0 + 128], rhs=h.t[:, dc, 1:513],
                                                                     start=(dc == 0), stop=False), reads=[h, Wr1], writes=[ps], signal=False)
            for dc in range(8):
                P.op("pe", lambda e, ps=ps, dc=dc, c0=c0, h=h: e.matmul(out=ps.t[:, :], lhsT=Wr2.t[:, dc, c0:c0 + 128], rhs=h.t[:, dc, 0:512],
                                                                     start=False, stop=(dc == 7)), reads=[h, Wr2], writes=[ps], signal=(dc == 7))
            o_ = of.next()
            if g == 0:
                P.op("act", lambda e, ps=ps, o_=o_: e.activation(out=o_.t[0:64, :], in_=ps.t[0:64, :], func=AF.Tanh), reads=[ps], writes=[o_])
                P.op("act", lambda e, ps=ps, o_=o_: e.copy(out=o_.t[64:128, :], in_=ps.t[64:128, :]), reads=[ps], writes=[o_])
            else:
                P.op("act", lambda e, ps=ps, o_=o_: e.activation(out=o_.t[:, :], in_=ps.t[:, :], func=AF.Sigmoid), reads=[ps], writes=[o_])
            P.dma("sp", scr.lora.t[g * 128:(g + 1) * 128, tok], o_.t[:, :], o_, scr.lora)
        for g in range(8 + 16):
            ps = pm.next()
            c0 = g * 128 if g < 8 else 1536 + (g - 8) * 128
            for dc in range(8):
                P.op("pe", lambda e, ps=ps, dc=dc, c0=c0, h=h: e.matmul(out=ps.t[:, :], lhsT=Wo.t[:, dc, c0:c0 + 128], rhs=h.t[:, dc, 1:513],
                                                                     start=(dc == 0), stop=(dc == 7)), reads=[h, Wo], writes=[ps], signal=(dc == 7))
            o_ = of.next()
            if g < 8:
                P.op("dve", lambda e, ps=ps, o_=o_: e.tensor_copy(out=o_.t[:, :], in_=ps.t[:, :]), reads=[ps], writes=[o_])
                dstb = scr.qT if g < 4 else scr.kT
                P.dma("sp", dstb.t[(g % 4) * 128:(g % 4 + 1) * 128, tok], o_.t[:, :], o_, dstb)
            else:
                gg = g - 8
                P.op("act", lambda e, ps=ps, o_=o_, gg=gg: e.activation(out=o_.t[:, :], in_=ps.t[:, :], func=AF.Sigmoid, bias=gb.t[:, gg:gg + 1]),
                     reads=[ps, gb], writes=[o_])
                P.dma("sp", scr.gates.t[gg * 128:(gg + 1) * 128, tok], o_.t[:, :], o_, scr.gates)
    P.barrier()
    P.emit_phase()
    P.stack.close()


def make_scratch(P, T, dbg=False):
    s = Ctx()
    kind = "ExternalOutput" if dbg else "Internal"
    mk = lambda n, sh, dt=BF16: P.dram(n, sh, dt, kind=kind)
    s.rkv = mk("s_rkv", [T, 1536]); s.vA = mk("s_vA", [T, 512]); s.lora = mk("s_lora", [256, T])
    s.qT = mk("s_qT", [512, T]); s.kT = mk("s_kT", [512, T]); s.gates = mk("s_gates", [2048, T])
    s.oaT = mk("s_oaT", [512, T]); s.obT = mk("s_obT", [512, T]); s.x1 = mk("s_x1", [T, D], F32); s.x2 = mk("s_x2", [T, D], F32)
    return s


def TT_(P, eng, out, in0, in1, op, reads, writes):
    P.op(eng, lambda e: e.tensor_tensor(out=out, in0=in0, in1=in1, op=op), reads=reads, writes=writes)


def STT_(P, out, in0, scalar, in1, op0, op1, reads, writes):
    P.op("dve", lambda e: e.scalar_tensor_tensor(out=out, in0=in0, scalar=scalar, in1=in1, op0=op0, op1=op1), reads=reads, writes=writes)


def TS_(P, eng, out, in0, s1, s2, op0, op1, reads, writes):
    if s2 is None:
        P.op(eng, lambda e: e.tensor_scalar(out=out, in0=in0, scalar1=s1, scalar2=None, op0=op0), reads=reads, writes=writes)
    else:
        P.op(eng, lambda e: e.tensor_scalar(out=out, in0=in0, scalar1=s1, scalar2=s2, op0=op0, op1=op1), reads=reads, writes=writes)


def ACT_(P, out, in_, func, reads, writes, scale=1.0, bias=None):
    if bias is None:
        P.op("act", lambda e: e.activation(out=out, in_=in_, func=func, scale=scale), reads=reads, writes=writes)
    else:
        P.op("act", lambda e: e.activation(out=out, in_=in_, func=func, scale=scale, bias=bias), reads=reads, writes=writes)


def CP_(P, eng, out, in_, reads, writes):
    if eng == "act":
        P.op("act", lambda e: e.copy(out=out, in_=in_), reads=reads, writes=writes)
    else:
        P.op(eng, lambda e: e.tensor_copy(out=out, in_=in_), reads=reads, writes=writes)


def MM_(P, out, lhsT, rhs, reads, writes, start=True, stop=True, signal=True):
    P.op("pe", lambda e: e.matmul(out=out, lhsT=lhsT, rhs=rhs, start=start, stop=stop), reads=reads, writes=writes, signal=signal)


def TR_(P, out, in_, ident, reads, writes, signal=True):
    P.op("pe", lambda e: e.transpose(out=out, in_=in_, identity=ident), reads=reads, writes=writes, signal=signal)


def RED_(P, out, in_, reads, writes):
    P.op("dve", lambda e: e.tensor_reduce(out=out, in_=in_, axis=AX.X, op=ALU.add), reads=reads, writes=writes)


def make_cst():
    c = np.zeros((128, 8, 128), np.float32)
    s = np.arange(128)[:, None]
    t = np.arange(128)[None, :]
    same = (s // 64) == (t // 64)
    c[:, 0, :] = np.eye(128)
    c[:, 1, :] = same & (s <= t)
    c[:, 2, :] = same & (s < t)
    c[:, 3, :] = same & (s > t)
    c[:, 4, 0:2] = (s // 64) == np.arange(2)[None, :]
    s64 = s % 64
    t64 = np.arange(64)[None, :]
    c[:, 5, 0:64] = t64 > s64
    c[:, 5, 64:128] = t64 < s64
    c[:, 6, 0:64] = t64 >= s64
    c[:, 6, 64:128] = t64 == s64
    c[:, 7, :] = 8.0 * np.eye(128)
    return c


def make_biasT(rel_bias):
    k = np.arange(128)[:, None]
    q = np.arange(640)[None, :]
    rel = q - k
    idx = np.clip(rel, -63, 128) + 63
    kc = k // 64
    qc = q // 64
    ok = (qc >= kc) & (qc <= kc + 8)
    out = rel_bias[:, idx].astype(np.float32)
    out[:, ~ok] = NEG
    return out


def phase2(P, c, T, scr):
    NT = T // 128
    P.stack = contextlib.ExitStack()
    SB = lambda n, sh, dt=F32: P.sbuf(n, sh, dt)
    RG = lambda n, sh, dt=F32, k=2: Ring([P.sbuf("%s%d" % (n, i), sh, dt) for i in range(k)])
    ident = SB("ident", [128, 128], BF16)
    P.dma("pool", ident.t[:, :], c.cst.t[:, 0, :], c.cst, ident)
    cst = SB("cstf", [128, 8, 128])
    P.dma("sp", cst.t[:, :, :], c.cst.t[:, :, :], c.cst, cst)
    wa2 = SB("wa2", [128, RW], BF16)
    P.dma("pool", wa2.t[0:64, :], c.w2.t[:, :], c.w2, wa2)
    P.dma("pool", wa2.t[64:128, :], c.a2.t[:, :], c.a2, wa2)
    g2b = SB("g2b", [128, RW], BF16)
    P.dma("pool", g2b.t[:, :], c.g2.t[:, :], c.g2, g2b)
    bt = {}
    for nm in ["w0", "a0", "k_k", "k_a", "r_k", "lnx_w", "lnx_b"]:
        bt[nm] = SB("b_" + nm, [128, RW])
        P.dma("sp", bt[nm].t[:, :], bc(getattr(c, nm), RW), getattr(c, nm), bt[nm])
    Linc, Lsl, Lsu, cind = cst.t[:, 1, :], cst.t[:, 2, :], cst.t[:, 3, :], cst.t[:, 4, 0:2]
    b3 = lambda ap: ap.unsqueeze(1).to_broadcast([128, 8, 64])
    MU, ML, MUI, I64 = b3(cst.t[:, 5, 0:64]), b3(cst.t[:, 5, 64:128]), b3(cst.t[:, 6, 0:64]), b3(cst.t[:, 6, 64:128])

    pg = Ring([P.psum("pg%d" % i, [128, 512], F32) for i in range(5)])
    pq = Ring([P.psum("pq%d" % i, [128, 512], F32) for i in range(3)])
    v3 = lambda buf: buf.t.rearrange("p (h v) -> p h v", h=8)
    def bfv(buf, inner):
        return buf.t.bitcast(BF16)[:, 0:8 * inner].rearrange("p (h t) -> p h t", h=8)

    rkvt = RG("rkvt", [128, 1536], BF16)
    lo1 = RG("lo1", [128, 128], BF16); lo2 = RG("lo2", [128, 128], BF16)
    names_f = ["t_w", "sg", "t_a", "a_", "e_pos", "e_neg", "e_prev", "e_rel", "kkr", "sq", "kk", "bb", "t1", "k2", "t2", "yt", "yn", "bv"]
    F = {n: SB(n, [128, RW]) for n in names_f}
    g_ = RG("g_", [128, RW])
    names_b = ["At", "Bt", "Kt", "Rt"]
    Bq = {n: SB(n, [128, RW], BF16) for n in names_b}
    Bh = RG("Bh", [128, RW], BF16); Kh = RG("Kh", [128, RW], BF16)
    ATs = RG("ATs", [64, 8, 128], BF16); RTs = RG("RTs", [64, 8, 128], BF16)
    BTs = SB("BTs", [64, 8, 128], BF16); KTs = SB("KTs", [64, 8, 128], BF16)
    XA = Ring([P.sbuf("XA%d" % i, [128, 8, 64], BF16) for i in range(2)])
    XTA = Ring([P.sbuf("XTA%d" % i, [128, 8, 64], BF16) for i in range(2)])
    MakT = SB("MakT", [128, 8, 64], BF16)
    RBT = RG("RBT", [128, 8, 64], BF16); RKT = SB("RKT", [128, 8, 64], BF16)
    TTb = RG("TTb", [128, 8, 64], BF16)
    W1a = RG("W1a", [128, 8, 64]); Ya = RG("Ya", [128, 8, 64])
    W1 = SB("W1", [128, 8, 64], BF16); U = SB("U", [128, 8, 64], BF16)
    STb = Ring([P.sbuf("STb%d" % i, [64, 8, 64], BF16) for i in range(2)])
    SF = Ring([P.sbuf("SF%d" % i, [64, 8, 64], F32) for i in range(2)])
    tmpS = SB("tmpS", [64, 8, 64]); tmp2 = SB("tmp2", [64, 8, 64])
    pc = RG("pc", [64, 8, 2])
    sm = {n: SB(n, [128, 8]) for n in ["ssq", "nrm", "rn", "s1", "s2", "mean", "msq", "var", "rstd"]}
    bcf = RG("bcf", [128, 8])
    oa = SB("oa", [128, RW], BF16)
    oaT = RG("oaT", [128, 4, 128], BF16)

    st_b = STb.next(); st_f = SF.next()
    P.op("pool", lambda e: e.memset(st_b.t[:, :, :], 0.0), writes=[st_b])
    P.op("pool", lambda e: e.memset(st_f.t[:, :, :], 0.0), writes=[st_f])
    h3 = lambda ap: ap.rearrange("p (h v) -> p h v", h=8)
    bl = lambda ap: ap.unsqueeze(2).to_broadcast([128, 8, 64])

    for i in range(NT):
        tok = slice(i * 128, (i + 1) * 128)
        rk = rkvt.next(); l1 = lo1.next(); l2 = lo2.next()
        P.dma("sp", rk.t[:, :], scr.rkv.t[tok, :], scr.rkv, rk)
        P.dma("sp", l1.t[:, :], scr.lora.t[0:128, tok], scr.lora, l1)
        P.dma("sp", l2.t[:, :], scr.lora.t[128:256, tok], scr.lora, l2)
        r_, k_, v_ = rk.t[:, 0:512], rk.t[:, 512:1024], rk.t[:, 1024:1536]
        p_w = pg.next(); p_a = pg.next(); p_g = pg.next()
        MM_(P, p_w.t[:, :], l1.t[0:64, :], wa2.t[0:64, :], [l1, wa2], [p_w])
        MM_(P, p_a.t[:, :], l1.t[64:128, :], wa2.t[64:128, :], [l1, wa2], [p_a])
        MM_(P, p_g.t[:, :], l2.t[:, :], g2b.t[:, :], [l2, g2b], [p_g])
        f = {n: F[n].t[:, :] for n in names_f}
        TT_(P, "dve", f["t_w"], p_w.t[:, :], bt["w0"].t[:, :], ALU.add, [p_w, bt["w0"]], [F["t_w"]])
        ACT_(P, f["sg"], f["t_w"], AF.Sigmoid, [F["t_w"]], [F["sg"]])
        TT_(P, "dve", f["t_a"], p_a.t[:, :], bt["a0"].t[:, :], ALU.add, [p_a, bt["a0"]], [F["t_a"]])
        ACT_(P, f["a_"], f["t_a"], AF.Sigmoid, [F["t_a"]], [F["a_"]])
        gq = g_.next()
        CP_(P, "act", gq.t[:, :], p_g.t[:, :], [p_g], [gq])
        p1 = pg.next(); p2 = pg.next(); p3 = pg.next(); p4 = pg.next()
        MM_(P, p1.t[:, :], Linc, f["sg"], [cst, F["sg"]], [p1])
        MM_(P, p2.t[:, :], Lsl, f["sg"], [cst, F["sg"]], [p2])
        MM_(P, p3.t[:, :], Lsu, f["sg"], [cst, F["sg"]], [p3])
        for h in range(8):
            MM_(P, p4.t[0:64, 2 * h:2 * h + 2], F["sg"].t[:, h * 64:(h + 1) * 64], cind, [F["sg"], cst], [p4], signal=(h == 7))
        ACT_(P, f["e_pos"], p1.t[:, :], AF.Exp, [p1], [F["e_pos"]], scale=CDEC)
        ACT_(P, f["e_neg"], p1.t[:, :], AF.Exp, [p1], [F["e_neg"]], scale=-CDEC)
        ACT_(P, f["e_prev"], p2.t[:, :], AF.Exp, [p2], [F["e_prev"]], scale=CDEC)
        ACT_(P, f["e_rel"], p3.t[:, :], AF.Exp, [p3], [F["e_rel"]], scale=CDEC)
        pcq = pc.next()
        ACT_(P, pcq.t[:, :, :], p4.t[0:64, 0:16].rearrange("p (h c) -> p h c", h=8), AF.Exp, [p4], [pcq], scale=CDEC)
        if CUT < 2:
            continue
        TT_(P, "dve", f["kkr"], k_, bt["k_k"].t[:, :], ALU.mult, [rk, bt["k_k"]], [F["kkr"]])
        TT_(P, "pool", f["sq"], f["kkr"], f["kkr"], ALU.mult, [F["kkr"]], [F["sq"]])
        RED_(P, sm["ssq"].t[:, :], h3(f["sq"]), [F["sq"]], [sm["ssq"]])
        ACT_(P, sm["nrm"].t[:, :], sm["ssq"].t[:, :], AF.Sqrt, [sm["ssq"]], [sm["nrm"]])
        TS_(P, "dve", sm["nrm"].t[:, :], sm["nrm"].t[:, :], 1e-12, None, ALU.max, None, [sm["nrm"]], [sm["nrm"]])
        P.op("dve", lambda e: e.reciprocal(out=sm["rn"].t[:, :], in_=sm["nrm"].t[:, :]), reads=[sm["nrm"]], writes=[sm["rn"]])
        TT_(P, "dve", h3(f["kk"]), h3(f["kkr"]), bl(sm["rn"].t[:, :]), ALU.mult, [F["kkr"], sm["rn"]], [F["kk"]])
        TT_(P, "dve", f["bb"], f["kk"], f["a_"], ALU.mult, [F["kk"], F["a_"]], [F["bb"]])
        STT_(P, f["t1"], f["a_"], -1.0, bt["k_a"].t[:, :], ALU.add, ALU.mult, [F["a_"], bt["k_a"]], [F["t1"]])
        STT_(P, f["k2"], f["t1"], 1.0, k_, ALU.add, ALU.mult, [F["t1"], rk], [F["k2"]])
        STT_(P, Bq["At"].t[:, :], f["kk"], -1.0, f["e_prev"], ALU.mult, ALU.mult, [F["kk"], F["e_prev"]], [Bq["At"]])
        TT_(P, "dve", Bq["Bt"].t[:, :], f["bb"], f["e_neg"], ALU.mult, [F["bb"], F["e_neg"]], [Bq["Bt"]])
        TT_(P, "dve", Bq["Kt"].t[:, :], f["k2"], f["e_neg"], ALU.mult, [F["k2"], F["e_neg"]], [Bq["Kt"]])
        TT_(P, "dve", Bq["Rt"].t[:, :], r_, f["e_pos"], ALU.mult, [rk, F["e_pos"]], [Bq["Rt"]])
        bh = Bh.next(); kh = Kh.next()
        TT_(P, "pool", bh.t[:, :], f["bb"], f["e_rel"], ALU.mult, [F["bb"], F["e_rel"]], [bh])
        TT_(P, "pool", kh.t[:, :], f["k2"], f["e_rel"], ALU.mult, [F["k2"], F["e_rel"]], [kh])
        TT_(P, "pool", f["t2"], r_, bt["r_k"].t[:, :], ALU.mult, [rk, bt["r_k"]], [F["t2"]])
        TT_(P, "pool", f["t2"], f["t2"], f["k2"], ALU.mult, [F["t2"], F["k2"]], [F["t2"]])
        bq = bcf.next()
        RED_(P, bq.t[:, :], h3(f["t2"]), [F["t2"]], [bq])
        if CUT < 3:
            continue
        ats = ATs.next(); rts = RTs.next()
        for (src, dst, ev) in [("At", ats, "act"), ("Bt", BTs, "dve"), ("Kt", KTs, "act"), ("Rt", rts, "dve")]:
            ps = pg.next()
            pv = bfv(ps, 128)
            for h in range(8):
                TR_(P, pv[0:64, h, :], Bq[src].t[:, h * 64:(h + 1) * 64], ident.t[:, :], [Bq[src], ident], [ps], signal=(h == 7))
            CP_(P, ev, dst.t[:, :, :], pv[0:64, :, :], [ps], [dst])
        if CUT < 4:
            continue
        def prod(lhs, rhs, mask, dst, eng="dve"):
            ps = pg.next()
            for cc in range(2):
                cols = slice(cc * 64, (cc + 1) * 64)
                for h in range(8):
                    MM_(P, v3(ps)[cc * 64:(cc + 1) * 64, h, :], lhs.t[:, h, cols], rhs.t[:, h, cols], [lhs, rhs], [ps],
                        signal=(cc == 1 and h == 7))
            TT_(P, eng, dst.t[:, :, :], v3(ps), mask, ALU.mult, [ps, cst], [dst])
        xt_ = XTA.next(); x_ = XA.next()
        rbt = RBT.next()
        prod(BTs, ats, MU, xt_)
        prod(ats, BTs, ML, x_)
        prod(KTs, ats, MU, MakT)
        prod(BTs, rts, MUI, rbt)
        prod(KTs, rts, MUI, RKT)
        tt = TTb.next()
        TT_(P, "dve", tt.t[:, :, :], xt_.t[:, :, :], I64, ALU.add, [xt_, cst], [tt])
        if CUT < 5:
            continue
        for lev in range(1, 6):
            xn_ = XA.next()
            xtn_ = XTA.next() if lev < 5 else None
            for cc in range(2):
                hs = slice(cc * 64, (cc + 1) * 64)
                px = pg.next()
                for h in range(8):
                    MM_(P, v3(px)[hs, h, :], xt_.t[hs, h, :], x_.t[hs, h, :], [xt_, x_], [px], signal=(h == 7))
                CP_(P, "act", xn_.t[hs, :, :], v3(px)[hs, :, :], [px], [xn_])
                if lev < 5:
                    pxt = pg.next()
                    for h in range(8):
                        MM_(P, v3(pxt)[hs, h, :], x_.t[hs, h, :], xt_.t[hs, h, :], [xt_, x_], [pxt], signal=(h == 7))
                    CP_(P, "act", xtn_.t[hs, :, :], v3(pxt)[hs, :, :], [pxt], [xtn_])
            x_ = xn_
            if lev < 5:
                xt_ = xtn_
            for cc in range(2):
                hs = slice(cc * 64, (cc + 1) * 64)
                pt = pg.next()
                for h in range(8):
                    MM_(P, v3(pt)[hs, h, :], x_.t[hs, h, :], tt.t[hs, h, :], [x_, tt], [pt], signal=(h == 7))
                TT_(P, "dve", tt.t[hs, :, :], v3(pt)[hs, :, :], tt.t[hs, :, :], ALU.add, [pt, tt], [tt])
        if CUT < 6:
            continue
        w1a = W1a.next(); ya = Ya.next()
        for cc in range(2):
            hs = slice(cc * 64, (cc + 1) * 64)
            ps = pg.next()
            for h in range(8):
                MM_(P, v3(ps)[hs, h, :], MakT.t[hs, h, :], rk.t[hs, 1024 + h * 64:1024 + (h + 1) * 64], [MakT, rk], [ps], signal=(h == 7))
            CP_(P, "act", w1a.t[hs, :, :], v3(ps)[hs, :, :], [ps], [w1a])
            ps = pg.next()
            for h in range(8):
                MM_(P, v3(ps)[hs, h, :], RKT.t[hs, h, :], rk.t[hs, 1024 + h * 64:1024 + (h + 1) * 64], [RKT, rk], [ps], signal=(h == 7))
            CP_(P, "act", ya.t[hs, :, :], v3(ps)[hs, :, :], [ps], [ya])
        if CUT < 7:
            continue
        for cc in range(2):
            hs = slice(cc * 64, (cc + 1) * 64)
            cols = slice(cc * 64, (cc + 1) * 64)
            pk = pq.next()
            for h in range(8):
                MM_(P, v3(pk)[0:64, h, :], kh.t[hs, h * 64:(h + 1) * 64], rk.t[hs, 1024 + h * 64:1024 + (h + 1) * 64], [kh, rk], [pk], signal=(h == 7))
            TT_(P, "dve", tmpS.t[:, :, :], st_f.t[:, :, :], pcq.t[:, :, cc:cc + 1].to_broadcast([64, 8, 64]), ALU.mult, [st_f, pcq], [tmpS])
            TT_(P, "dve", tmp2.t[:, :, :], v3(pk)[0:64, :, :], tmpS.t[:, :, :], ALU.add, [pk, tmpS], [tmp2])
            pw = pq.next()
            for h in range(8):
                MM_(P, v3(pw)[hs, h, :], ats.t[:, h, cols], st_b.t[:, h, :], [ats, st_b], [pw], signal=(h == 7))
            TT_(P, "dve", W1.t[hs, :, :], v3(pw)[hs, :, :], w1a.t[hs, :, :], ALU.add, [pw, w1a], [W1])
            prs = pq.next()
            for h in range(8):
                MM_(P, v3(prs)[hs, h, :], rts.t[:, h, cols], st_b.t[:, h, :], [rts, st_b], [prs], signal=(h == 7))
            TT_(P, "dve", h3(f["yt"])[hs, :, :], v3(prs)[hs, :, :], ya.t[hs, :, :], ALU.add, [prs, ya], [F["yt"]])
            pu = pq.next()
            for h in range(8):
                MM_(P, v3(pu)[hs, h, :], tt.t[hs, h, :], W1.t[hs, h, :], [tt, W1], [pu], signal=(h == 7))
            CP_(P, "act", U.t[hs, :, :], v3(pu)[hs, :, :], [pu], [U])
            psn = pq.next()
            for h in range(8):
                MM_(P, v3(psn)[0:64, h, :], bh.t[hs, h * 64:(h + 1) * 64], U.t[hs, h, :], [bh, U], [psn], signal=(h == 7))
            py = pq.next()
            for h in range(8):
                MM_(P, v3(py)[hs, h, :], rbt.t[hs, h, :], U.t[hs, h, :], [rbt, U], [py], signal=(h == 7))
            nb = STb.next(); nf = SF.next()
            TT_(P, "dve", nb.t[:, :, :], v3(psn)[0:64, :, :], tmp2.t[:, :, :], ALU.add, [psn, tmp2], [nb])
            TT_(P, "dve", nf.t[:, :, :], v3(psn)[0:64, :, :], tmp2.t[:, :, :], ALU.add, [psn, tmp2], [nf])
            st_b, st_f = nb, nf
            TT_(P, "dve", h3(f["yt"])[hs, :, :], v3(py)[hs, :, :], h3(f["yt"])[hs, :, :], ALU.add, [py, F["yt"]], [F["yt"]])
        if CUT < 8:
            continue
        yt3 = h3(f["yt"]); yn3 = h3(f["yn"])
        RED_(P, sm["s1"].t[:, :], yt3, [F["yt"]], [sm["s1"]])
        TT_(P, "pool", f["sq"], f["yt"], f["yt"], ALU.mult, [F["yt"]], [F["sq"]])
        RED_(P, sm["s2"].t[:, :], h3(f["sq"]), [F["sq"]], [sm["s2"]])
        TS_(P, "dve", sm["mean"].t[:, :], sm["s1"].t[:, :], 1.0 / 64, None, ALU.mult, None, [sm["s1"]], [sm["mean"]])
        TT_(P, "dve", sm["msq"].t[:, :], sm["mean"].t[:, :], sm["mean"].t[:, :], ALU.mult, [sm["mean"]], [sm["msq"]])
        STT_(P, sm["var"].t[:, :], sm["s2"].t[:, :], 1.0 / 64, sm["msq"].t[:, :], ALU.mult, ALU.subtract, [sm["s2"], sm["msq"]], [sm["var"]])
        ACT_(P, sm["rstd"].t[:, :], sm["var"].t[:, :], AF.Sqrt, [sm["var"]], [sm["rstd"]], bias=64e-5)
        P.op("dve", lambda e: e.reciprocal(out=sm["rstd"].t[:, :], in_=sm["rstd"].t[:, :]), reads=[sm["rstd"]], writes=[sm["rstd"]])
        TT_(P, "dve", yn3, yt3, bl(sm["mean"].t[:, :]), ALU.subtract, [F["yt"], sm["mean"]], [F["yn"]])
        TT_(P, "dve", yn3, yn3, bl(sm["rstd"].t[:, :]), ALU.mult, [F["yn"], sm["rstd"]], [F["yn"]])
        TT_(P, "pool", f["yn"], f["yn"], bt["lnx_w"].t[:, :], ALU.mult, [F["yn"], bt["lnx_w"]], [F["yn"]])
        TT_(P, "pool", f["yn"], f["yn"], bt["lnx_b"].t[:, :], ALU.add, [F["yn"], bt["lnx_b"]], [F["yn"]])
        TT_(P, "dve", h3(f["bv"]), h3(v_), bl(bq.t[:, :]), ALU.mult, [rk, bq], [F["bv"]])
        TT_(P, "pool", f["yn"], f["yn"], f["bv"], ALU.add, [F["yn"], F["bv"]], [F["yn"]])
        TT_(P, "dve", oa.t[:, :], f["yn"], gq.t[:, :], ALU.mult, [F["yn"], gq], [oa])
        ps = pg.next()
        pv = ps.t.bitcast(BF16)[:, 0:512].rearrange("p (c t) -> p c t", c=4)
        for fc in range(4):
            TR_(P, pv[:, fc, :], oa.t[:, fc * 128:(fc + 1) * 128], ident.t[:, :], [oa, ident], [ps], signal=(fc == 3))
        ot_ = oaT.next()
        CP_(P, "act", ot_.t[:, :, :], pv, [ps], [ot_])
        P.dma("sp", scr.oaT.t[:, tok].rearrange("(c p) t -> p c t", p=128), ot_.t[:, :, :], ot_, scr.oaT)
    P.barrier()
    P.emit_phase()
    P.stack.close()


def phase3(P, c, T, scr):
    NT = T // 128
    P.stack = contextlib.ExitStack()
    ident = P.sbuf("ident", [128, 128], BF16)
    P.dma("pool", ident.t[:, :], c.cst.t[:, 0, :], c.cst, ident)
    I8 = P.sbuf("I8", [128, 128], BF16)
    P.dma("pool", I8.t[:, :], c.cst.t[:, 7, :], c.cst, I8)
    kTh = Ring([P.sbuf("kTh%d" % i, [128, T], BF16) for i in range(2)])
    qTh = Ring([P.sbuf("qTh%d" % i, [128, T], BF16) for i in range(2)])
    Vh = Ring([P.sbuf("Vh%d" % i, [128, NT, 65], BF16) for i in range(2)])
    bT = Ring([P.sbuf("bT%d" % i, [128, 640], BF16) for i in range(2)])
    for r in (kTh, qTh):
        for b in r.items:
            P.op("pool", lambda e, b=b: e.memset(b.t[64:128, :], 0.0), writes=[b])
    for b in Vh.items:
        P.op("pool", lambda e, b=b: e.memset(b.t[:, :, 64:65], 1.0), writes=[b])
    PT = [P.sbuf("PT%d" % i, [128, 640], BF16) for i in range(6)]
    ob = P.sbuf("ob", [128, NT, 512], BF16)
    rc = Ring([P.sbuf("rc%d" % i, [128, 1], F32) for i in range(2)])
    pa = Ring([P.psum("pa%d" % i, [128, 512], F32) for i in range(4)])
    pb = Ring([P.psum("pb%d" % i, [128, 512], F32) for i in range(2)])
    pt_ = Ring([P.psum("ptr%d" % i, [128, 512], F32) for i in range(2)])
    for h in range(NH):
        kt = kTh.next(); qt = qTh.next(); vh = Vh.next(); bias = bT.next()
        P.dma("sp", kt.t[0:64, :], scr.kT.t[h * 64:(h + 1) * 64, :], scr.kT, kt)
        P.dma("sp", qt.t[0:64, :], scr.qT.t[h * 64:(h + 1) * 64, :], scr.qT, qt)
        P.dma("sp", vh.t[:, :, 0:64], scr.vA.t[:, h * 64:(h + 1) * 64].rearrange("(m p) d -> p m d", p=128), scr.vA, vh)
        P.dma("pool", bias.t[:, :], c.biasT.t[h, :, :], c.biasT, bias)
        for m in range(NT):
            W = min(640, T - 128 * m)
            pt = PT[m % 6]
            for (c0, c1) in [(0, min(W, 512)), (512, W)]:
                if c1 <= c0:
                    continue
                ps = pa.next()
                n = c1 - c0
                MM_(P, ps.t[:, 0:n], I8.t[:, :], bias.t[:, c0:c1], [I8, bias], [ps], start=True, stop=False, signal=False)
                MM_(P, ps.t[:, 0:n], kt.t[:, m * 128:(m + 1) * 128], qt.t[:, m * 128 + c0:m * 128 + c1], [kt, qt], [ps], start=False, stop=True)
                ACT_(P, pt.t[:, c0:c1], ps.t[:, 0:n], AF.Exp, [ps], [pt], scale=0.125)
            pv = pb.next()
            for cc in range(2):
                cq = 2 * m + cc
                m0 = max(0, (cq - 8) // 2)
                ms = list(range(m0, cq // 2 + 1))
                for mi, mp in enumerate(ms):
                    off = (cq - 2 * mp) * 64
                    MM_(P, pv.t[cc * 64:(cc + 1) * 64, 0:65], PT[mp % 6].t[:, off:off + 64], vh.t[:, mp, :], [PT[mp % 6], vh], [pv],
                        start=(mi == 0), stop=(mi == len(ms) - 1), signal=(mi == len(ms) - 1))
            r_ = rc.next()
            P.op("dve", lambda e, r_=r_, pv=pv: e.reciprocal(out=r_.t[:, :], in_=pv.t[:, 64:65]), reads=[pv], writes=[r_])
            ACT_(P, ob.t[:, m, h * 64:(h + 1) * 64], pv.t[:, 0:64], AF.Copy, [pv, r_], [ob], scale=r_.t[:, :])
    obT = Ring([P.sbuf("obT%d" % i, [128, 4, 128], BF16) for i in range(2)])
    for m in range(NT):
        ps = pt_.next()
        pvw = ps.t.bitcast(BF16)[:, 0:512].rearrange("p (c t) -> p c t", c=4)
        for fc in range(4):
            TR_(P, pvw[:, fc, :], ob.t[:, m, fc * 128:(fc + 1) * 128], ident.t[:, :], [ob, ident], [ps], signal=(fc == 3))
        o_ = obT.next()
        CP_(P, "dve", o_.t[:, :, :], pvw, [ps], [o_])
        P.dma("sp", scr.obT.t[:, m * 128:(m + 1) * 128].rearrange("(c p) t -> p c t", p=128), o_.t[:, :, :], o_, scr.obT)
    P.barrier()
    P.emit_phase()
    P.stack.close()


def phase4a(P, c, T, scr):
    NB = T // 512
    scr.keep = contextlib.ExitStack()
    P.stack = scr.keep
    scr.g2 = P.sbuf("g2c", [128, 8], F32); load_cols(P, "sp", scr.g2, c.ln2_g, D, 8)
    scr.Wup = [P.sbuf("Wup%d" % dc, [128, 2 * DFF], BF16) for dc in range(8)]
    for dc in range(8):
        P.dma("pool", scr.Wup[dc].t[:, :], c.w_ffn_up.t[dc * 128:(dc + 1) * 128, :], c.w_ffn_up, scr.Wup[dc])
        P.op("act", lambda e, dc=dc: e.activation(out=scr.Wup[dc].t[:, :], in_=scr.Wup[dc].t[:, :], func=AF.Copy, scale=scr.g2.t[:, dc:dc + 1]),
             reads=[scr.Wup[dc], scr.g2], writes=[scr.Wup[dc]])
    P.stack = contextlib.ExitStack()
    WA = P.sbuf("WA", [128, 4, D], BF16); WB = P.sbuf("WB", [128, 4, D], BF16); WO = P.sbuf("WO", [128, 8, D], BF16)
    P.dma("pool", WA.t[:, :, :], c.w_br_rwkv.t.rearrange("(c p) n -> p c n", p=128), c.w_br_rwkv, WA)
    P.dma("pool", WB.t[:, :, :], c.w_br_attn.t.rearrange("(c p) n -> p c n", p=128), c.w_br_attn, WB)
    P.dma("pool", WO.t[:, :, :], c.w_o.t.rearrange("(c p) n -> p c n", p=128), c.w_o, WO)
    oa = Ring([P.sbuf("oa%d" % i, [128, 4, 512], BF16) for i in range(2)])
    obb = Ring([P.sbuf("obb%d" % i, [128, 4, 512], BF16) for i in range(2)])
    ga = Ring([P.sbuf("ga%d" % i, [128, 8, 512], BF16) for i in range(2)])
    gbb = Ring([P.sbuf("gbb%d" % i, [128, 8, 512], BF16) for i in range(2)])
    t1 = Ring([P.sbuf("t1_%d" % i, [128, 512], F32) for i in range(2)])
    t2 = Ring([P.sbuf("t2_%d" % i, [128, 512], F32) for i in range(2)])
    mT = Ring([P.sbuf("mT%d" % i, [128, 8, 512], BF16) for i in range(1)])
    xt = Ring([P.sbuf("xt%d" % i, [128, D], F32) for i in range(3)])
    xo = Ring([P.sbuf("xo%d" % i, [128, D], F32) for i in range(2)])
    pm = Ring([P.psum("pm%d" % i, [128, 512], F32) for i in range(6)])
    blk = {}
    xq = {}

    def lb(b):
        tok = slice(b * 512, (b + 1) * 512)
        a_ = oa.next(); b_ = obb.next(); ga_ = ga.next(); gb_ = gbb.next()
        P.dma("sp", a_.t[:, :, :], scr.oaT.t[:, tok].rearrange("(c p) t -> p c t", p=128), scr.oaT, a_)
        P.dma("sp", b_.t[:, :, :], scr.obT.t[:, tok].rearrange("(c p) t -> p c t", p=128), scr.obT, b_)
        P.dma("sp", ga_.t[:, :, :], scr.gates.t[0:1024, tok].rearrange("(c p) t -> p c t", p=128), scr.gates, ga_)
        P.dma("sp", gb_.t[:, :, :], scr.gates.t[1024:2048, tok].rearrange("(c p) t -> p c t", p=128), scr.gates, gb_)
        blk[b] = (a_, b_, ga_, gb_)

    def lx(i):
        x_ = xt.next()
        P.dma("sp", x_.t[:, :], c.x.t[i * 128:(i + 1) * 128, :], c.x, x_)
        xq[i] = x_

    lb(0)
    for i in range(min(2, 4 * NB)):
        lx(i)
    for b in range(NB):
        tok = slice(b * 512, (b + 1) * 512)
        a_, b_, ga_, gb_ = blk.pop(b)
        m_ = mT.next()
        if b + 1 < NB:
            lb(b + 1)
        for cg in range(8):
            pA = pm.next(); pB = pm.next()
            for fc in range(4):
                MM_(P, pA.t[:, :], WA.t[:, fc, cg * 128:(cg + 1) * 128], a_.t[:, fc, :], [WA, a_], [pA], start=(fc == 0), stop=(fc == 3), signal=(fc == 3))
            for fc in range(4):
                MM_(P, pB.t[:, :], WB.t[:, fc, cg * 128:(cg + 1) * 128], b_.t[:, fc, :], [WB, b_], [pB], start=(fc == 0), stop=(fc == 3), signal=(fc == 3))
            u1 = t1.next(); u2 = t2.next()
            TT_(P, "dve", u1.t[:, :], pA.t[:, :], ga_.t[:, cg, :], ALU.mult, [pA, ga_], [u1])
            TT_(P, "dve", u2.t[:, :], pB.t[:, :], gb_.t[:, cg, :], ALU.mult, [pB, gb_], [u2])
            TT_(P, "pool", m_.t[:, cg, :], u1.t[:, :], u2.t[:, :], ALU.add, [u1, u2], [m_])
        for j in range(4):
            i = 4 * b + j
            x_ = xq.pop(i); o_ = xo.next()
            if i + 2 < 4 * NB:
                lx(i + 2)
            for hf in range(2):
                ps = pm.next()
                for cg in range(8):
                    MM_(P, ps.t[:, :], m_.t[:, cg, j * 128:(j + 1) * 128], WO.t[:, cg, hf * 512:(hf + 1) * 512], [m_, WO], [ps],
                        start=(cg == 0), stop=(cg == 7), signal=(cg == 7))
                TT_(P, "dve", o_.t[:, hf * 512:(hf + 1) * 512], ps.t[:, :], x_.t[:, hf * 512:(hf + 1) * 512], ALU.add, [ps, x_], [o_])
            P.dma("sp", scr.x1.t[i * 128:(i + 1) * 128, :], o_.t[:, :], o_, scr.x1)
    P.barrier()
    P.emit_phase()
    P.stack.close()


def norm_scale(P, x_, xn_, s_, r_):
    P.op("act", lambda e: e.activation(out=xn_.t[:, :], in_=x_.t[:, :], func=AF.Square, accum_out=s_.t[:, :]),
         reads=[x_], writes=[xn_, s_])
    P.op("act", lambda e: e.activation(out=r_.t[:, :], in_=s_.t[:, :], func=AF.Sqrt, scale=1.0 / D, bias=1e-6),
         reads=[s_], writes=[r_])
    P.op("dve", lambda e: e.reciprocal(out=r_.t[:, :], in_=r_.t[:, :]), reads=[r_], writes=[r_])
    P.op("act", lambda e: e.activation(out=xn_.t[:, :], in_=x_.t[:, :], func=AF.Copy, scale=r_.t[:, :]),
         reads=[x_, r_], writes=[xn_])


def phase5(P, c, T, scr):
    NB = T // 256
    NG = DFF // 128
    P.stack = contextlib.ExitStack()
    ident = P.sbuf("ident", [128, 128], BF16)
    P.dma("pool", ident.t[:, :], c.cst.t[:, 0, :], c.cst, ident)
    cw = P.sbuf("cw", [128, 3, NG], F32)
    for k in range(3):
        P.dma("sp", cw.t[:, k, :], c.conv_w.t[k, :].rearrange("(c p) -> p c", p=128), c.conv_w, cw, allow_slow_non_contiguous=True)
    cb = P.sbuf("cb", [128, NG], F32); load_cols(P, "sp", cb, c.conv_b, DFF, NG)
    Wup = scr.Wup
    Wdn = P.sbuf("Wdn", [128, NG, D], BF16)
    for gq in range(0, NG, 2):
        P.dma("pool", Wdn.t[:, gq:gq + 2, :], c.w_ffn_down.t[gq * 128:(gq + 2) * 128, :].rearrange("(c p) n -> p c n", p=128), c.w_ffn_down, Wdn)
    carry = P.sbuf("carry", [128, NG, 2], F32)
    P.op("pool", lambda e: e.memset(carry.t[:, :, :], 0.0), writes=[carry])

    X = Ring([P.sbuf("X%d" % i, [128, D], F32) for i in range(6)])
    xn = Ring([P.sbuf("xn%d" % i, [128, D], BF16) for i in range(2)])
    ss = Ring([P.sbuf("ss%d" % i, [128, 1], F32) for i in range(6)])
    rs = Ring([P.sbuf("rs%d" % i, [128, 1], F32) for i in range(6)])
    h2T = [P.sbuf("h2T%d" % i, [128, 8, 256], BF16) for i in range(2)]
    mTt = [P.sbuf("mT%d" % i, [128, NG, 256], BF16) for i in range(2)]
    mT = [[Buf("mT%d_%d" % (i, g), mTt[i].t) for g in range(NG)] for i in range(2)]
    Ab = Ring([P.sbuf("Ab%d" % i, [128, 258], F32) for i in range(3)])
    cv = Ring([P.sbuf("cv%d" % i, [128, 256], F32) for i in range(3)])
    gl = Ring([P.sbuf("gl%d" % i, [128, 256], F32) for i in range(3)])
    pst = Ring([P.psum("pst%d" % i, [128, 8, 128], BF16) for i in range(2)])
    pm = Ring([P.psum("pm%d" % i, [128, 512], F32) for i in range(4)])
    xs = {}
    xns = {}

    def s1l(b):
        xs[b] = []; xns[b] = []
        for j in range(2):
            i = 2 * b + j
            x_ = X.next(); n_ = xn.next()
            xs[b].append(x_); xns[b].append((n_, ss.next(), rs.next()))
            P.dma("sp", x_.t[:, :], scr.x1.t[i * 128:(i + 1) * 128, :], scr.x1, x_)

    def s1n(b, j, k):
        x_ = xs[b][j]; n_, s_, r_ = xns[b][j]
        if k == 0:
            P.op("act", lambda e: e.activation(out=n_.t[:, :], in_=x_.t[:, :], func=AF.Square, accum_out=s_.t[:, :]),
                 reads=[x_], writes=[n_, s_])
        elif k == 1:
            P.op("act", lambda e: e.activation(out=r_.t[:, :], in_=s_.t[:, :], func=AF.Sqrt, scale=1.0 / D, bias=1e-6),
                 reads=[s_], writes=[r_])
            P.op("dve", lambda e: e.reciprocal(out=r_.t[:, :], in_=r_.t[:, :]), reads=[r_], writes=[r_])
        else:
            P.op("act", lambda e: e.activation(out=n_.t[:, :], in_=x_.t[:, :], func=AF.Copy, scale=r_.t[:, :]),
                 reads=[x_, r_], writes=[n_])

    def s1a(b):
        s1l(b)
        for j in range(2):
            for k in range(3):
                s1n(b, j, k)

    def s1b(b):
        h = h2T[b % 2]
        for j in range(2):
            ps = pst.next(); n_ = xns[b][j][0]
            for k in range(8):
                TR_(P, ps.t[:, k, :], n_.t[:, k * 128:(k + 1) * 128], ident.t[:, :], [n_, ident], [ps], signal=(k == 7))
            CP_(P, "dve", h.t[:, :, j * 128:(j + 1) * 128], ps.t[:, :, :], [ps], [h])

    def up(b, g):
        h = h2T[b % 2]
        pA = pm.next(); pG = pm.next()
        for dc in range(8):
            MM_(P, pA.t[:, 0:256], Wup[dc].t[:, g * 128:(g + 1) * 128], h.t[:, dc, :], [Wup[dc], h], [pA], start=(dc == 0), stop=(dc == 7), signal=(dc == 7))
        for dc in range(8):
            MM_(P, pG.t[:, 0:256], Wup[dc].t[:, DFF + g * 128:DFF + (g + 1) * 128], h.t[:, dc, :], [Wup[dc], h], [pG], start=(dc == 0), stop=(dc == 7), signal=(dc == 7))
        A = Ab.next(); cv_ = cv.next(); gl_ = gl.next()
        CP_(P, "pool", A.t[:, 0:2], carry.t[:, g, :], [carry], [A])
        CP_(P, "act", A.t[:, 2:258], pA.t[:, 0:256], [pA], [A])
        ACT_(P, cv_.t[:, :], pA.t[:, 0:256], AF.Identity, [pA, cw, cb], [cv_], scale=cw.t[:, 2, g:g + 1], bias=cb.t[:, g:g + 1])
        STT_(P, cv_.t[:, :], A.t[:, 1:257], cw.t[:, 1, g:g + 1], cv_.t[:, :], ALU.mult, ALU.add, [A, cw, cv_], [cv_])
        STT_(P, cv_.t[:, :], A.t[:, 0:256], cw.t[:, 0, g:g + 1], cv_.t[:, :], ALU.mult, ALU.add, [A, cw, cv_], [cv_])
        CP_(P, "pool", carry.t[:, g, :], A.t[:, 256:258], [A], [carry])
        ACT_(P, gl_.t[:, :], cv_.t[:, :], AF.Gelu, [cv_], [gl_])
        TT_(P, "dve", mTt[b % 2].t[:, g, :], gl_.t[:, :], pG.t[:, 0:256], ALU.mult, [gl_, pG], [mT[b % 2][g]])

    pdn = [P.psum("pdn%d" % i, [128, 512], F32) for i in range(2)]

    def down_steps(b):
        for j in range(2):
            i = 2 * b + j
            x_ = xs[b][j]
            for g in range(NG):
                for hf in range(2):
                    MM_(P, pdn[hf].t[:, :], mTt[b % 2].t[:, g, j * 128:(j + 1) * 128], Wdn.t[:, g, hf * 512:(hf + 1) * 512], [mT[b % 2][g], Wdn], [pdn[hf]],
                        start=(g == 0), stop=(g == NG - 1), signal=(g == NG - 1))
                if g == NG - 1:
                    for hf in range(2):
                        TT_(P, "dve", x_.t[:, hf * 512:(hf + 1) * 512], pdn[hf].t[:, :], x_.t[:, hf * 512:(hf + 1) * 512], ALU.add, [pdn[hf], x_], [x_])
                    P.dma("sp", scr.x2.t[i * 128:(i + 1) * 128, :], x_.t[:, :], x_, scr.x2)
                yield

    def up_block(b, dn):
        for g in range(NG):
            if b + 1 < NB:
                if g == 0:
                    s1l(b + 1)
                if g in (1, 3, 5, 7, 9, 11):
                    q = (g - 1) // 2
                    s1n(b + 1, q // 3, q % 3)
                if g == 14:
                    s1b(b + 1)
            up(b, g)
            if dn is not None:
                for _ in range(2):
                    next(dn, None)

    s1a(0); s1b(0)
    up_block(0, None)
    for b in range(NB):
        dn = down_steps(b)
        if b + 1 < NB:
            up_block(b + 1, dn)
        for _ in dn:
            pass
    P.barrier()
    P.emit_phase()
    P.stack.close()
    scr.keep.close()


def phase6(P, c, T, scr):
    NT = T // 128
    P.stack = contextlib.ExitStack()
    ident = P.sbuf("ident", [128, 128], BF16)
    P.dma("pool", ident.t[:, :], c.cst.t[:, 0, :], c.cst, ident)
    g3 = P.sbuf("g3c", [128, 8], F32); load_cols(P, "sp", g3, c.ln3_g, D, 8)
    Wpg = P.sbuf("Wpg", [128, 8, D], BF16)
    Wple = P.sbuf("Wple", [128, 2, D], BF16)
    P.dma("pool", Wpg.t[:, :, :], c.w_pg.t.rearrange("(c p) n -> p c n", p=128), c.w_pg, Wpg)
    for dc in range(8):
        P.op("act", lambda e, dc=dc: e.activation(out=Wpg.t[:, dc, :], in_=Wpg.t[:, dc, :], func=AF.Copy, scale=g3.t[:, dc:dc + 1]),
             reads=[Wpg, g3], writes=[Wpg])
    P.dma("pool", Wple.t[:, :, :], c.w_ple.t.rearrange("(c p) n -> p c n", p=128), c.w_ple, Wple)
    lnf = P.sbuf("lnf", [128, D], F32)
    P.dma("sp", lnf.t[:, :], bc(c.lnf_g, D), c.lnf_g, lnf)
    bpg = P.sbuf("bpg", [128, D], BF16)
    ones = P.sbuf("ones", [128, 128], BF16)
    P.op("pool", lambda e: e.memset(bpg.t[:, :], 0.0), writes=[bpg])
    P.op("pool", lambda e: e.memset(ones.t[:, :], 0.0), writes=[ones])
    P.op("pool", lambda e: e.memset(ones.t[0:1, :], 1.0), writes=[ones])
    P.dma("pool", bpg.t[0:1, :], c.b_pg.t[0:D].rearrange("(o n) -> o n", o=1), c.b_pg, bpg)
    X = Ring([P.sbuf("X%d" % i, [128, D], F32) for i in range(8)])
    xn = Ring([P.sbuf("xn%d" % i, [128, D], BF16) for i in range(4)])
    ss = Ring([P.sbuf("ss%d" % i, [128, 1], F32) for i in range(10)])
    rs = Ring([P.sbuf("rs%d" % i, [128, 1], F32) for i in range(10)])
    h3T = Ring([P.sbuf("h3T%d" % i, [128, 8, 128], BF16) for i in range(3)])
    pt = Ring([P.sbuf("ptile%d" % i, [128, PLE], F32) for i in range(8)])
    pb_ = Ring([P.sbuf("ptb%d" % i, [128, PLE], BF16) for i in range(4)])
    pT = Ring([P.sbuf("pT%d" % i, [128, 2, 128], BF16) for i in range(3)])
    sgt = Ring([P.sbuf("sgt%d" % i, [128, 512], F32) for i in range(4)])
    tq = Ring([P.sbuf("tq%d" % i, [128, 512], F32) for i in range(4)])
    junk = P.sbuf("junk", [128, D], BF16)
    pst = Ring([P.psum("pst%d" % i, [128, 8, 128], BF16) for i in range(2)])
    pm = Ring([P.psum("pm%d" % i, [128, 512], F32) for i in range(6)])
    st = {}

    ld = {}

    def sl(i):
        x_ = X.next(); p_ = pt.next()
        P.dma("sp", x_.t[:, :], scr.x2.t[i * 128:(i + 1) * 128, :], scr.x2, x_)
        P.dma("sp", p_.t[:, :], c.p.t[i * 128:(i + 1) * 128, :], c.p, p_)
        ld[i] = (x_, p_)

    def sN(i):
        x_, p_ = ld.pop(i)
        n_ = xn.next(); q_ = pb_.next()
        norm_scale(P, x_, n_, ss.next(), rs.next())
        CP_(P, "act", q_.t[:, :], p_.t[:, :], [p_], [q_])
        st[i] = [x_, n_, q_]

    def sT(i):
        x_, n_, q_ = st[i]
        h_ = h3T.next(); t_ = pT.next()
        ps = pst.next()
        for k in range(8):
            TR_(P, ps.t[:, k, :], n_.t[:, k * 128:(k + 1) * 128], ident.t[:, :], [n_, ident], [ps], signal=(k == 7))
        CP_(P, "dve", h_.t[:, :, :], ps.t[:, :, :], [ps], [h_])
        pp = pst.next()
        for pc in range(2):
            TR_(P, pp.t[:, pc, :], q_.t[:, pc * 128:(pc + 1) * 128], ident.t[:, :], [q_, ident], [pp], signal=(pc == 1))
        CP_(P, "dve", t_.t[:, :, :], pp.t[:, 0:2, :], [pp], [t_])
        st[i] = [x_, h_, t_]

    def sM(i):
        x_, h_, t_ = st[i]
        for hf in range(2):
            cs = slice(hf * 512, (hf + 1) * 512)
            pg_ = pm.next()
            MM_(P, pg_.t[:, :], ones.t[:, :], bpg.t[:, cs], [ones, bpg], [pg_], start=True, stop=False, signal=False)
            for dc in range(8):
                MM_(P, pg_.t[:, :], h_.t[:, dc, :], Wpg.t[:, dc, cs], [h_, Wpg], [pg_], start=False, stop=(dc == 7), signal=(dc == 7))
            s_ = sgt.next(); q_ = tq.next()
            ACT_(P, s_.t[:, :], pg_.t[:, :], AF.Sigmoid, [pg_], [s_])
            pe_ = pm.next()
            for pc in range(2):
                MM_(P, pe_.t[:, :], t_.t[:, pc, :], Wple.t[:, pc, cs], [t_, Wple], [pe_], start=(pc == 0), stop=(pc == 1), signal=(pc == 1))
            TT_(P, "dve", q_.t[:, :], pe_.t[:, :], s_.t[:, :], ALU.mult, [pe_, s_], [q_])
            TT_(P, "pool", x_.t[:, cs], x_.t[:, cs], q_.t[:, :], ALU.add, [x_, q_], [x_])

    def sF(i):
        x_ = st.pop(i)[0]
        s_ = ss.next(); r_ = rs.next()
        P.op("act", lambda e: e.activation(out=junk.t[:, :], in_=x_.t[:, :], func=AF.Square, accum_out=s_.t[:, :]),
             reads=[x_], writes=[junk, s_])
        P.op("act", lambda e: e.activation(out=r_.t[:, :], in_=s_.t[:, :], func=AF.Sqrt, scale=1.0 / D, bias=1e-6),
             reads=[s_], writes=[r_])
        P.op("dve", lambda e: e.reciprocal(out=r_.t[:, :], in_=r_.t[:, :]), reads=[r_], writes=[r_])
        STT_(P, x_.t[:, :], x_.t[:, :], r_.t[:, 0:1], lnf.t[:, :], ALU.mult, ALU.mult, [x_, r_, lnf], [x_])
        P.dma("sp", c.out.t[i * 128:(i + 1) * 128, :], x_.t[:, :], x_, c.out)

    for i in range(min(5, NT)):
        sl(i)
    for i in range(-2, NT + 1):
        if 5 <= i + 5 < NT:
            sl(i + 5)
        if 0 <= i + 2 < NT:
            sN(i + 2)
        if 0 <= i + 1 < NT:
            sT(i + 1)
        if 0 <= i < NT:
            sM(i)
        if 0 <= i - 1 < NT:
            sF(i - 1)
    P.wait_all("sp", [c.out])
    P.barrier()
    P.emit_phase()
    P.stack.close()


def build_program(T, dbg=False):
    nc = bass.Bass("TRN2", target_bir_lowering=False)
    P = Prog(nc)
    c = declare_io(P, T)
    scr = make_scratch(P, T, dbg=dbg)
    phase1(P, c, T, scr)
    phase23(P, c, T, scr)
    phase4a(P, c, T, scr)
    phase5(P, c, T, scr)
    phase6(P, c, T, scr)
    P.semstack.close()
    return nc


_W1 = ["ln1_g", "w_in", "mix_mu", "w0", "w2", "a0", "a2", "g2", "k_k", "k_a", "lnx_w", "lnx_b", "gate_b", "w_br_rwkv", "w_br_attn",
       "w_o", "ln2_g", "w_ffn_up", "conv_w", "conv_b", "w_ffn_down", "ln3_g", "w_ple", "w_pg", "b_pg"]


def core_inputs(inputs, b, T):
    d = {"x": inputs["x"][b, :T], "p": inputs["p"][0, b, :T]}
    for k in _W1:
        d[k] = inputs[k][0]
    d["r_k"] = np.reshape(inputs["r_k"][0], (RW,))
    d["lnf_g"] = inputs["lnf_g"]
    d["biasT"] = make_biasT(np.asarray(inputs["rel_bias"][0]))
    d["cst"] = make_cst()
    return {k: np.ascontiguousarray(np.asarray(v), dtype=np.float32) for k, v in d.items()}


def kernel(**inputs):
    B, T = inputs["x"].shape[0], inputs["x"].shape[1]
    nc = build_program(T)
    in_maps = [core_inputs(inputs, b, T) for b in range(B)]
    res = run_bass_kernel_spmd(nc, in_maps, core_ids=list(range(B)))
    return np.stack([np.asarray(r["out"], dtype=np.float32) for r in res.results], axis=0)


def interleave(gens):
    active = list(gens)
    while active:
        for item in list(active):
            g, k = item
            for _ in range(k):
                try:
                    next(g)
                except StopIteration:
                    active.remove(item)
                    break


def take(gen, n):
    for _ in range(n):
        try:
            next(gen)
        except StopIteration:
            return
        yield


def phase23(P, c, T, scr):
    NT = T // 128
    P.stack = contextlib.ExitStack()
    SB = lambda n, sh, dt=F32: P.sbuf(n, sh, dt)
    RG = lambda n, sh, dt=F32, k=2: Ring([P.sbuf("%s%d" % (n, i), sh, dt) for i in range(k)])
    ident = SB("ident", [128, 128], BF16)
    P.dma("pool", ident.t[:, :], c.cst.t[:, 0, :], c.cst, ident)
    I8 = SB("I8", [128, 128], BF16)
    P.dma("pool", I8.t[:, :], c.cst.t[:, 7, :], c.cst, I8)
    cst = SB("cstf", [128, 8, 128])
    P.dma("sp", cst.t[:, :, :], c.cst.t[:, :, :], c.cst, cst)
    wa2 = SB("wa2", [128, RW], BF16)
    P.dma("pool", wa2.t[0:64, :], c.w2.t[:, :], c.w2, wa2)
    P.dma("pool", wa2.t[64:128, :], c.a2.t[:, :], c.a2, wa2)
    g2b = SB("g2b", [128, RW], BF16)
    P.dma("pool", g2b.t[:, :], c.g2.t[:, :], c.g2, g2b)
    bt = {}
    for nm in ["w0", "a0", "k_k", "k_a", "r_k", "lnx_w", "lnx_b"]:
        bt[nm] = SB("b_" + nm, [128, RW])
        P.dma("sp", bt[nm].t[:, :], bc(getattr(c, nm), RW), getattr(c, nm), bt[nm])
    Linc, Lsl, Lsu, cind = cst.t[:, 1, :], cst.t[:, 2, :], cst.t[:, 3, :], cst.t[:, 4, 0:2]
    b3 = lambda ap: ap.unsqueeze(1).to_broadcast([128, 8, 64])
    MU, ML, MUI, I64 = b3(cst.t[:, 5, 0:64]), b3(cst.t[:, 5, 64:128]), b3(cst.t[:, 6, 0:64]), b3(cst.t[:, 6, 64:128])

    pg = Ring([P.psum("pg%d" % i, [128, 512], F32) for i in range(3)])
    pq = Ring([P.psum("pq%d" % i, [128, 512], F32) for i in range(2)])
    pa = Ring([P.psum("pa%d" % i, [128, 512], F32) for i in range(2)])
    pb = Ring([P.psum("pb%d" % i, [128, 512], F32) for i in range(1)])
    v3 = lambda buf: buf.t.rearrange("p (h v) -> p h v", h=8)

    def bfv(buf, inner):
        return buf.t.bitcast(BF16)[:, 0:8 * inner].rearrange("p (h t) -> p h t", h=8)

    rkvt = RG("rkvt", [128, 1536], BF16, 4)
    lo1 = RG("lo1", [128, 128], BF16, 4); lo2 = RG("lo2", [128, 128], BF16, 4)
    names_f = ["t_w", "sg", "t_a", "a_", "e_pos", "e_neg", "e_prev", "e_rel", "kkr", "sq", "kk", "bb", "t1", "k2", "t2"]
    F = {n: SB(n, [128, RW]) for n in names_f}
    Fp = {n: SB("post_" + n, [128, RW]) for n in ["yt", "sq", "yn", "bv"]}
    g_ = RG("g_", [128, RW])
    Bq = {n: SB(n, [128, RW], BF16) for n in ["At", "Bt", "Kt", "Rt"]}
    Bh = RG("Bh", [128, RW], BF16); Kh = RG("Kh", [128, RW], BF16)
    ATs = RG("ATs", [64, 8, 128], BF16); RTs = RG("RTs", [64, 8, 128], BF16)
    BTs = SB("BTs", [64, 8, 128], BF16); KTs = SB("KTs", [64, 8, 128], BF16)
    XA = RG("XA", [128, 8, 64], BF16); XTA = RG("XTA", [128, 8, 64], BF16)
    MakT = SB("MakT", [128, 8, 64], BF16)
    RBT = RG("RBT", [128, 8, 64], BF16); RKT = SB("RKT", [128, 8, 64], BF16)
    TTb = RG("TTb", [128, 8, 64], BF16)
    W1a = RG("W1a", [128, 8, 64]); Ya = RG("Ya", [128, 8, 64])
    W1 = SB("W1", [128, 8, 64], BF16); U = SB("U", [128, 8, 64], BF16)
    STb = RG("STb", [64, 8, 64], BF16); SF = RG("SF", [64, 8, 64], F32)
    tmpS = SB("tmpS", [64, 8, 64]); tmp2 = SB("tmp2", [64, 8, 64])
    pc = RG("pc", [64, 8, 2])
    sm = {n: SB(n, [128, 8]) for n in ["ssq", "nrm", "rn", "s1", "s2", "mean", "msq", "var", "rstd"]}
    bcf = RG("bcf", [128, 8])
    oa = SB("oa", [128, RW], BF16)
    oaT = RG("oaT", [128, 4, 128], BF16)
    h3 = lambda ap: ap.rearrange("p (h v) -> p h v", h=8)
    bl = lambda ap: ap.unsqueeze(2).to_broadcast([128, 8, 64])
    state = {}
    state["b"] = STb.next(); state["f"] = SF.next()
    P.op("pool", lambda e: e.memset(state["b"].t[:, :, :], 0.0), writes=[state["b"]])
    P.op("pool", lambda e: e.memset(state["f"].t[:, :, :], 0.0), writes=[state["f"]])
    tl = {}

    def loads(i):
        tok = slice(i * 128, (i + 1) * 128)
        rk = rkvt.next(); l1 = lo1.next(); l2 = lo2.next()
        P.dma("sp", rk.t[:, :], scr.rkv.t[tok, :], scr.rkv, rk)
        P.dma("sp", l1.t[:, :], scr.lora.t[0:128, tok], scr.lora, l1)
        P.dma("sp", l2.t[:, :], scr.lora.t[128:256, tok], scr.lora, l2)
        tl[i] = dict(rk=rk, l1=l1, l2=l2)

    def genA(i):
        t = tl[i]
        rk, l1, l2 = t["rk"], t["l1"], t["l2"]
        r_, k_ = rk.t[:, 0:512], rk.t[:, 512:1024]
        p_w = pg.next(); p_a = pg.next()
        MM_(P, p_w.t[:, :], l1.t[0:64, :], wa2.t[0:64, :], [l1, wa2], [p_w])
        MM_(P, p_a.t[:, :], l1.t[64:128, :], wa2.t[64:128, :], [l1, wa2], [p_a])
        f = {n: F[n].t[:, :] for n in names_f}
        TT_(P, "dve", f["t_w"], p_w.t[:, :], bt["w0"].t[:, :], ALU.add, [p_w, bt["w0"]], [F["t_w"]])
        ACT_(P, f["sg"], f["t_w"], AF.Sigmoid, [F["t_w"]], [F["sg"]])
        TT_(P, "dve", f["t_a"], p_a.t[:, :], bt["a0"].t[:, :], ALU.add, [p_a, bt["a0"]], [F["t_a"]])
        ACT_(P, f["a_"], f["t_a"], AF.Sigmoid, [F["t_a"]], [F["a_"]])
        yield
        p_g = pg.next()
        MM_(P, p_g.t[:, :], l2.t[:, :], g2b.t[:, :], [l2, g2b], [p_g])
        gq = g_.next(); t["gq"] = gq
        CP_(P, "act", gq.t[:, :], p_g.t[:, :], [p_g], [gq])
        yield
        p1 = pg.next()
        MM_(P, p1.t[:, :], Linc, f["sg"], [cst, F["sg"]], [p1])
        ACT_(P, f["e_pos"], p1.t[:, :], AF.Exp, [p1], [F["e_pos"]], scale=CDEC)
        ACT_(P, f["e_neg"], p1.t[:, :], AF.Exp, [p1], [F["e_neg"]], scale=-CDEC)
        yield
        p2 = pg.next()
        MM_(P, p2.t[:, :], Lsl, f["sg"], [cst, F["sg"]], [p2])
        ACT_(P, f["e_prev"], p2.t[:, :], AF.Exp, [p2], [F["e_prev"]], scale=CDEC)
        yield
        p3 = pg.next()
        MM_(P, p3.t[:, :], Lsu, f["sg"], [cst, F["sg"]], [p3])
        ACT_(P, f["e_rel"], p3.t[:, :], AF.Exp, [p3], [F["e_rel"]], scale=CDEC)
        yield
        p4 = pg.next()
        for h in range(8):
            MM_(P, p4.t[0:64, 2 * h:2 * h + 2], F["sg"].t[:, h * 64:(h + 1) * 64], cind, [F["sg"], cst], [p4], signal=(h == 7))
        pcq = pc.next(); t["pcq"] = pcq
        ACT_(P, pcq.t[:, :, :], p4.t[0:64, 0:16].rearrange("p (h c) -> p h c", h=8), AF.Exp, [p4], [pcq], scale=CDEC)
        yield
        TT_(P, "dve", f["kkr"], k_, bt["k_k"].t[:, :], ALU.mult, [rk, bt["k_k"]], [F["kkr"]])
        TT_(P, "pool", f["sq"], f["kkr"], f["kkr"], ALU.mult, [F["kkr"]], [F["sq"]])
        RED_(P, sm["ssq"].t[:, :], h3(f["sq"]), [F["sq"]], [sm["ssq"]])
        ACT_(P, sm["nrm"].t[:, :], sm["ssq"].t[:, :], AF.Sqrt, [sm["ssq"]], [sm["nrm"]])
        yield
        TS_(P, "dve", sm["nrm"].t[:, :], sm["nrm"].t[:, :], 1e-12, None, ALU.max, None, [sm["nrm"]], [sm["nrm"]])
        P.op("dve", lambda e: e.reciprocal(out=sm["rn"].t[:, :], in_=sm["nrm"].t[:, :]), reads=[sm["nrm"]], writes=[sm["rn"]])
        TT_(P, "dve", h3(f["kk"]), h3(f["kkr"]), bl(sm["rn"].t[:, :]), ALU.mult, [F["kkr"], sm["rn"]], [F["kk"]])
        yield
        TT_(P, "pool", f["bb"], f["kk"], f["a_"], ALU.mult, [F["kk"], F["a_"]], [F["bb"]])
        STT_(P, f["t1"], f["a_"], -1.0, bt["k_a"].t[:, :], ALU.add, ALU.mult, [F["a_"], bt["k_a"]], [F["t1"]])
        STT_(P, f["k2"], f["t1"], 1.0, k_, ALU.add, ALU.mult, [F["t1"], rk], [F["k2"]])
        yield
        STT_(P, Bq["At"].t[:, :], f["kk"], -1.0, f["e_prev"], ALU.mult, ALU.mult, [F["kk"], F["e_prev"]], [Bq["At"]])
        TT_(P, "dve", Bq["Bt"].t[:, :], f["bb"], f["e_neg"], ALU.mult, [F["bb"], F["e_neg"]], [Bq["Bt"]])
        yield
        TT_(P, "dve", Bq["Kt"].t[:, :], f["k2"], f["e_neg"], ALU.mult, [F["k2"], F["e_neg"]], [Bq["Kt"]])
        TT_(P, "pool", Bq["Rt"].t[:, :], r_, f["e_pos"], ALU.mult, [rk, F["e_pos"]], [Bq["Rt"]])
        yield
        bh = Bh.next(); kh = Kh.next(); t["bh"] = bh; t["kh"] = kh
        TT_(P, "pool", bh.t[:, :], f["bb"], f["e_rel"], ALU.mult, [F["bb"], F["e_rel"]], [bh])
        TT_(P, "pool", kh.t[:, :], f["k2"], f["e_rel"], ALU.mult, [F["k2"], F["e_rel"]], [kh])
        yield
        TT_(P, "pool", f["t2"], r_, bt["r_k"].t[:, :], ALU.mult, [rk, bt["r_k"]], [F["t2"]])
        TT_(P, "pool", f["t2"], f["t2"], f["k2"], ALU.mult, [F["t2"], F["k2"]], [F["t2"]])
        bq = bcf.next(); t["bq"] = bq
        RED_(P, bq.t[:, :], h3(f["t2"]), [F["t2"]], [bq])
        yield
        ats = ATs.next(); rts = RTs.next(); t["ats"] = ats; t["rts"] = rts
        for (src, dst, ev) in [("At", ats, "act"), ("Bt", BTs, "dve"), ("Kt", KTs, "act"), ("Rt", rts, "dve")]:
            ps = pg.next()
            pv = bfv(ps, 128)
            for h in range(8):
                TR_(P, pv[0:64, h, :], Bq[src].t[:, h * 64:(h + 1) * 64], ident.t[:, :], [Bq[src], ident], [ps], signal=(h == 7))
            CP_(P, ev, dst.t[:, :, :], pv[0:64, :, :], [ps], [dst])
            yield

        def prod(lhs, rhs, mask, dst, eng="dve"):
            ps = pg.next()
            for cc in range(2):
                cols = slice(cc * 64, (cc + 1) * 64)
                for h in range(8):
                    MM_(P, v3(ps)[cc * 64:(cc + 1) * 64, h, :], lhs.t[:, h, cols], rhs.t[:, h, cols], [lhs, rhs], [ps],
                        signal=(cc == 1 and h == 7))
            TT_(P, eng, dst.t[:, :, :], v3(ps), mask, ALU.mult, [ps, cst], [dst])
        xt_ = XTA.next(); x_ = XA.next()
        rbt = RBT.next(); t["rbt"] = rbt
        prod(BTs, ats, MU, xt_); yield
        prod(ats, BTs, ML, x_); yield
        prod(KTs, ats, MU, MakT); yield
        prod(BTs, rts, MUI, rbt); yield
        prod(KTs, rts, MUI, RKT); yield
        tt = TTb.next(); t["tt"] = tt
        TT_(P, "dve", tt.t[:, :, :], xt_.t[:, :, :], I64, ALU.add, [xt_, cst], [tt])
        for lev in range(1, 6):
            xn_ = XA.next()
            xtn_ = XTA.next() if lev < 5 else None
            for cc in range(2):
                hs = slice(cc * 64, (cc + 1) * 64)
                px = pg.next()
                for h in range(8):
                    MM_(P, v3(px)[hs, h, :], xt_.t[hs, h, :], x_.t[hs, h, :], [xt_, x_], [px], signal=(h == 7))
                CP_(P, "act", xn_.t[hs, :, :], v3(px)[hs, :, :], [px], [xn_])
                yield
                if lev < 5:
                    pxt = pg.next()
                    for h in range(8):
                        MM_(P, v3(pxt)[hs, h, :], x_.t[hs, h, :], xt_.t[hs, h, :], [xt_, x_], [pxt], signal=(h == 7))
                    CP_(P, "act", xtn_.t[hs, :, :], v3(pxt)[hs, :, :], [pxt], [xtn_])
                    yield
            x_ = xn_
            if lev < 5:
                xt_ = xtn_
            for cc in range(2):
                hs = slice(cc * 64, (cc + 1) * 64)
                pt = pg.next()
                for h in range(8):
                    MM_(P, v3(pt)[hs, h, :], x_.t[hs, h, :], tt.t[hs, h, :], [x_, tt], [pt], signal=(h == 7))
                TT_(P, "dve", tt.t[hs, :, :], v3(pt)[hs, :, :], tt.t[hs, :, :], ALU.add, [pt, tt], [tt])
                yield
        w1a = W1a.next(); ya = Ya.next(); t["w1a"] = w1a; t["ya"] = ya
        for cc in range(2):
            hs = slice(cc * 64, (cc + 1) * 64)
            ps = pg.next()
            for h in range(8):
                MM_(P, v3(ps)[hs, h, :], MakT.t[hs, h, :], rk.t[hs, 1024 + h * 64:1024 + (h + 1) * 64], [MakT, rk], [ps], signal=(h == 7))
            CP_(P, "act", w1a.t[hs, :, :], v3(ps)[hs, :, :], [ps], [w1a])
            yield
            ps = pg.next()
            for h in range(8):
                MM_(P, v3(ps)[hs, h, :], RKT.t[hs, h, :], rk.t[hs, 1024 + h * 64:1024 + (h + 1) * 64], [RKT, rk], [ps], signal=(h == 7))
            CP_(P, "act", ya.t[hs, :, :], v3(ps)[hs, :, :], [ps], [ya])
            yield

    def genB(i):
        t = tl[i]
        rk, ats, rts, bh, kh, rbt, tt, w1a, ya, pcq, gq, bq = (t[k] for k in ["rk", "ats", "rts", "bh", "kh", "rbt", "tt", "w1a", "ya", "pcq", "gq", "bq"])
        tok = slice(i * 128, (i + 1) * 128)
        v_ = rk.t[:, 1024:1536]
        fp = {n: Fp[n].t[:, :] for n in Fp}
        for cc in range(2):
            hs = slice(cc * 64, (cc + 1) * 64)
            cols = slice(cc * 64, (cc + 1) * 64)
            st_b, st_f = state["b"], state["f"]
            pk = pq.next()
            for h in range(8):
                MM_(P, v3(pk)[0:64, h, :], kh.t[hs, h * 64:(h + 1) * 64], rk.t[hs, 1024 + h * 64:1024 + (h + 1) * 64], [kh, rk], [pk], signal=(h == 7))
            TT_(P, "dve", tmpS.t[:, :, :], st_f.t[:, :, :], pcq.t[:, :, cc:cc + 1].to_broadcast([64, 8, 64]), ALU.mult, [st_f, pcq], [tmpS])
            TT_(P, "dve", tmp2.t[:, :, :], v3(pk)[0:64, :, :], tmpS.t[:, :, :], ALU.add, [pk, tmpS], [tmp2])
            yield
            pw = pq.next()
            for h in range(8):
                MM_(P, v3(pw)[hs, h, :], ats.t[:, h, cols], st_b.t[:, h, :], [ats, st_b], [pw], signal=(h == 7))
            TT_(P, "dve", W1.t[hs, :, :], v3(pw)[hs, :, :], w1a.t[hs, :, :], ALU.add, [pw, w1a], [W1])
            yield
            prs = pq.next()
            for h in range(8):
                MM_(P, v3(prs)[hs, h, :], rts.t[:, h, cols], st_b.t[:, h, :], [rts, st_b], [prs], signal=(h == 7))
            TT_(P, "dve", h3(fp["yt"])[hs, :, :], v3(prs)[hs, :, :], ya.t[hs, :, :], ALU.add, [prs, ya], [Fp["yt"]])
            yield
            pu = pq.next()
            for h in range(8):
                MM_(P, v3(pu)[hs, h, :], tt.t[hs, h, :], W1.t[hs, h, :], [tt, W1], [pu], signal=(h == 7))
            CP_(P, "act", U.t[hs, :, :], v3(pu)[hs, :, :], [pu], [U])
            yield
            psn = pq.next()
            for h in range(8):
                MM_(P, v3(psn)[0:64, h, :], bh.t[hs, h * 64:(h + 1) * 64], U.t[hs, h, :], [bh, U], [psn], signal=(h == 7))
            nb = STb.next(); nf = SF.next()
            TT_(P, "dve", nb.t[:, :, :], v3(psn)[0:64, :, :], tmp2.t[:, :, :], ALU.add, [psn, tmp2], [nb])
            TT_(P, "dve", nf.t[:, :, :], v3(psn)[0:64, :, :], tmp2.t[:, :, :], ALU.add, [psn, tmp2], [nf])
            state["b"], state["f"] = nb, nf
            yield
            py = pq.next()
            for h in range(8):
                MM_(P, v3(py)[hs, h, :], rbt.t[hs, h, :], U.t[hs, h, :], [rbt, U], [py], signal=(h == 7))
            TT_(P, "dve", h3(fp["yt"])[hs, :, :], v3(py)[hs, :, :], h3(fp["yt"])[hs, :, :], ALU.add, [py, Fp["yt"]], [Fp["yt"]])
            yield
        yt3 = h3(fp["yt"]); yn3 = h3(fp["yn"])
        RED_(P, sm["s1"].t[:, :], yt3, [Fp["yt"]], [sm["s1"]])
        TT_(P, "pool", fp["sq"], fp["yt"], fp["yt"], ALU.mult, [Fp["yt"]], [Fp["sq"]])
        RED_(P, sm["s2"].t[:, :], h3(fp["sq"]), [Fp["sq"]], [sm["s2"]])
        yield
        TS_(P, "dve", sm["mean"].t[:, :], sm["s1"].t[:, :], 1.0 / 64, None, ALU.mult, None, [sm["s1"]], [sm["mean"]])
        TT_(P, "dve", sm["msq"].t[:, :], sm["mean"].t[:, :], sm["mean"].t[:, :], ALU.mult, [sm["mean"]], [sm["msq"]])
        STT_(P, sm["var"].t[:, :], sm["s2"].t[:, :], 1.0 / 64, sm["msq"].t[:, :], ALU.mult, ALU.subtract, [sm["s2"], sm["msq"]], [sm["var"]])
        ACT_(P, sm["rstd"].t[:, :], sm["var"].t[:, :], AF.Sqrt, [sm["var"]], [sm["rstd"]], bias=64e-5)
        P.op("dve", lambda e: e.reciprocal(out=sm["rstd"].t[:, :], in_=sm["rstd"].t[:, :]), reads=[sm["rstd"]], writes=[sm["rstd"]])
        yield
        TT_(P, "dve", yn3, yt3, bl(sm["mean"].t[:, :]), ALU.subtract, [Fp["yt"], sm["mean"]], [Fp["yn"]])
        TT_(P, "dve", yn3, yn3, bl(sm["rstd"].t[:, :]), ALU.mult, [Fp["yn"], sm["rstd"]], [Fp["yn"]])
        yield
        TT_(P, "pool", fp["yn"], fp["yn"], bt["lnx_w"].t[:, :], ALU.mult, [Fp["yn"], bt["lnx_w"]], [Fp["yn"]])
        TT_(P, "pool", fp["yn"], fp["yn"], bt["lnx_b"].t[:, :], ALU.add, [Fp["yn"], bt["lnx_b"]], [Fp["yn"]])
        TT_(P, "dve", h3(fp["bv"]), h3(v_), bl(bq.t[:, :]), ALU.mult, [rk, bq], [Fp["bv"]])
        yield
        TT_(P, "pool", fp["yn"], fp["yn"], fp["bv"], ALU.add, [Fp["yn"], Fp["bv"]], [Fp["yn"]])
        TT_(P, "dve", oa.t[:, :], fp["yn"], gq.t[:, :], ALU.mult, [Fp["yn"], gq], [oa])
        yield
        ps = pq.next()
        pv = ps.t.bitcast(BF16)[:, 0:512].rearrange("p (c t) -> p c t", c=4)
        for fc in range(4):
            TR_(P, pv[:, fc, :], oa.t[:, fc * 128:(fc + 1) * 128], ident.t[:, :], [oa, ident], [ps], signal=(fc == 3))
        ot_ = oaT.next()
        CP_(P, "act", ot_.t[:, :, :], pv, [ps], [ot_])
        P.dma("pool", scr.oaT.t[:, tok].rearrange("(c p) t -> p c t", p=128), ot_.t[:, :, :], ot_, scr.oaT)
        yield

    kTh = SB("kTh", [128, T], BF16); qTh = SB("qTh", [128, T], BF16)
    Vh = RG("Vh", [128, NT, 65], BF16)
    bT = RG("bT", [128, 640], BF16)
    for b in (kTh, qTh):
        P.op("pool", lambda e, b=b: e.memset(b.t[64:128, :], 0.0), writes=[b])
    for b in Vh.items:
        P.op("pool", lambda e, b=b: e.memset(b.t[:, :, 64:65], 1.0), writes=[b])
    PT = [SB("PT%d" % i, [128, 640], BF16) for i in range(6)]
    ob = SB("ob", [128, NT, 512], BF16)
    rc = RG("rc", [128, 1])
    obT = RG("obT", [128, 4, 128], BF16)

    def genC():
        for h in range(NH):
            kt = kTh; qt = qTh; vh = Vh.next(); bias = bT.next()
            P.dma("sp", kt.t[0:64, :], scr.kT.t[h * 64:(h + 1) * 64, :], scr.kT, kt)
            P.dma("sp", qt.t[0:64, :], scr.qT.t[h * 64:(h + 1) * 64, :], scr.qT, qt)
            P.dma("sp", vh.t[:, :, 0:64], scr.vA.t[:, h * 64:(h + 1) * 64].rearrange("(m p) d -> p m d", p=128), scr.vA, vh)
            P.dma("pool", bias.t[:, :], c.biasT.t[h, :, :], c.biasT, bias)
            for m in range(NT):
                W = min(640, T - 128 * m)
                pt = PT[m % 6]
                for (c0, c1) in [(0, min(W, 512)), (512, W)]:
                    if c1 <= c0:
                        continue
                    ps = pa.next()
                    n = c1 - c0
                    MM_(P, ps.t[:, 0:n], I8.t[:, :], bias.t[:, c0:c1], [I8, bias], [ps], start=True, stop=False, signal=False)
                    MM_(P, ps.t[:, 0:n], kt.t[:, m * 128:(m + 1) * 128], qt.t[:, m * 128 + c0:m * 128 + c1], [kt, qt], [ps], start=False, stop=True)
                    ACT_(P, pt.t[:, c0:c1], ps.t[:, 0:n], AF.Exp, [ps], [pt], scale=0.125)
                yield
                pv = pb.next()
                cq = 2 * m
                ms = list(range(max(0, (cq - 8) // 2), cq // 2 + 1))
                for mi, mp in enumerate(ms):
                    off = (cq - 2 * mp) * 64
                    MM_(P, pv.t[:, 0:65], PT[mp % 6].t[:, off:off + 128], vh.t[:, mp, :], [PT[mp % 6], vh], [pv],
                        start=(mi == 0), stop=(mi == len(ms) - 1), signal=(mi == len(ms) - 1))
                r_ = rc.next()
                P.op("dve", lambda e, r_=r_, pv=pv: e.reciprocal(out=r_.t[:, :], in_=pv.t[:, 64:65]), reads=[pv], writes=[r_])
                ACT_(P, ob.t[:, m, h * 64:(h + 1) * 64], pv.t[:, 0:64], AF.Copy, [pv, r_], [ob], scale=r_.t[:, :])
                yield
        for m in range(NT):
            ps = pa.next()
            pvw = ps.t.bitcast(BF16)[:, 0:512].rearrange("p (c t) -> p c t", c=4)
            for fc in range(4):
                TR_(P, pvw[:, fc, :], ob.t[:, m, fc * 128:(fc + 1) * 128], ident.t[:, :], [ob, ident], [ps], signal=(fc == 3))
            o_ = obT.next()
            CP_(P, "dve", o_.t[:, :, :], pvw, [ps], [o_])
            P.dma("pool", scr.obT.t[:, m * 128:(m + 1) * 128].rearrange("(c p) t -> p c t", p=128), o_.t[:, :, :], o_, scr.obT)
            yield

    gC = genC()
    nC = (NH * NT * 2 + NT + NT - 1) // NT + 1
    loads(0)
    if NT > 1:
        loads(1)
    for _ in genA(0):
        pass
    for i in range(NT):
        if i + 2 < NT:
            loads(i + 2)
        gens = [(genB(i), 1)]
        if i + 1 < NT:
            gens.append((genA(i + 1), 3))
        gens.append((take(gC, nC), 1))
        interleave(gens)
    for _ in gC:
        pass
    P.barrier()
    P.emit_phase()
    P.stack.close()
```

```python
import contextlib
import os
CUT = int(os.environ.get("P2CUT", "99"))
import numpy as np
import concourse.bass as bass
import concourse.mybir as mybir
from concourse.bass_utils import run_bass_kernel_spmd

F32 = mybir.dt.float32
BF16 = mybir.dt.bfloat16
AF = mybir.ActivationFunctionType
ALU = mybir.AluOpType
AX = mybir.AxisListType

COMPUTE = ("pe", "act", "dve", "pool")
ENGS = ("pe", "act", "dve", "pool", "sp")


class Buf:
    def __init__(self, name, t=None, is_dram=False):
        self.name = name
        self.t = t
        self.is_dram = is_dram
        self.w = {}
        self.r = {}
        self.dma_key = None
        self.dma_cnt = 0

    def __getitem__(self, k):
        return self.t[k]


class Prog:
    def __init__(self, nc):
        self.nc = nc
        self.stack = contextlib.ExitStack()
        self.semstack = contextlib.ExitStack()
        self.dma_keys = {}
        self.ops = {e: [] for e in ENGS}
        self.cnt = {e: 0 for e in COMPUTE}
        self.pending = {e: False for e in COMPUTE}
        self.waited = {e: {} for e in ENGS}
        self.sems = {}
        self.nbuf = 0

    def sem(self, key):
        if key not in self.sems:
            self.sems[key] = self.semstack.enter_context(self.nc.semaphore("s_" + key))
        return self.sems[key]

    def sbuf(self, name, shape, dtype):
        self.nbuf += 1
        name = "%s_%d" % (name, self.nbuf)
        t = self.stack.enter_context(self.nc.sbuf_tensor(name, list(shape), dtype))
        return Buf(name, t)

    def psum(self, name, shape, dtype):
        self.nbuf += 1
        name = "%s_%d" % (name, self.nbuf)
        t = self.stack.enter_context(self.nc.psum_tensor(name, list(shape), dtype))
        return Buf(name, t)

    def dram(self, name, shape, dtype, kind="Internal"):
        t = self.nc.dram_tensor(name, list(shape), dtype, kind=kind)
        return Buf(name, t, is_dram=True)

    def _collect(self, eng, reads, writes):
        need = {}

        def add(k, v, same_ok):
            if k == eng and not same_ok:
                return
            if need.get(k, 0) < v:
                need[k] = v

        for b in reads:
            for k, v in b.w.items():
                add(k, v, True)
        for b in writes:
            for k, v in b.w.items():
                add(k, v, False)
            for k, v in b.r.items():
                add(k, v, False)
        out = []
        wd = self.waited[eng]
        for k, v in need.items():
            if wd.get(k, 0) < v:
                wd[k] = v
                out.append((k, v))
        return out

    def _record(self, key, val, reads, writes):
        for b in reads:
            if b.r.get(key, 0) < val:
                b.r[key] = val
        for b in writes:
            b.w = {key: val}
            b.r = {}

    def op(self, eng, fn, reads=(), writes=(), signal=True):
        waits = self._collect(eng, reads, writes)
        if signal:
            self.cnt[eng] += 1
            val = self.cnt[eng]
            self.pending[eng] = False
        else:
            val = self.cnt[eng] + 1
            self.pending[eng] = True
        self.ops[eng].append((waits, fn, (eng, 1) if signal else None))
        self._record(eng, val, reads, writes)

    def dma(self, q, out_ap, in_ap, src, dst, sem_on=None, **kw):
        waits = self._collect(q, [src], [dst])
        sb = sem_on if sem_on is not None else (src if dst.is_dram else dst)
        if sb.dma_key is None:
            pool = self.__dict__.setdefault("sem_pool", [])
            sb.dma_sw = (q == "pool")
            if pool and not sb.dma_sw:
                sb.dma_key, sb.dma_cnt = pool.pop()
            else:
                sb.dma_key = "d%d" % len(self.__dict__.setdefault("all_dma_keys", []))
                self.all_dma_keys.append(sb.dma_key)
            if not sb.dma_sw:
                self.__dict__.setdefault("phase_owners", []).append(sb)
        assert sb.dma_sw == (q == "pool"), "buffer %s mixes software and hardware DGE" % sb.name
        key = sb.dma_key
        sb.dma_cnt += 16
        val = sb.dma_cnt
        self.dma_keys[key] = val
        self.ops[q].append((waits, lambda e: e.dma_start(out=out_ap, in_=in_ap, **kw), (key, 16)))
        self._record(key, val, [src], [dst])

    def wait_all(self, eng, bufs):
        waits = self._collect(eng, [], list(bufs))
        if waits:
            self.ops[eng].append((waits, None, None))

    def barrier(self):
        ev = {e: self.cnt[e] for e in COMPUTE if self.cnt[e] > 0}
        ev.update(self.dma_keys)
        for e in ENGS:
            waits = []
            for k, v in ev.items():
                if k == e:
                    continue
                if self.waited[e].get(k, 0) < v:
                    self.waited[e][k] = v
                    waits.append((k, v))
            if waits:
                self.ops[e].append((waits, None, None))

    def emit_phase(self):
        self.phase_idx = getattr(self, "phase_idx", 0) + 1
        with self.nc.named_scope("phase%d" % self.phase_idx):
            self.emit()
        self.ops = {e: [] for e in ENGS}
        for sb in self.__dict__.get("phase_owners", []):
            self.__dict__.setdefault("sem_pool", []).append((sb.dma_key, sb.dma_cnt))
        self.phase_owners = []

    def emit(self):
        nc = self.nc
        for e in COMPUTE:
            assert not self.pending[e], "engine %s ends with an unsignaled op" % e
        handles = {"pe": "tensor", "act": "scalar", "dve": "vector", "pool": "gpsimd", "sp": "sync"}
        for k in list(self.waited["pe"].keys()) + list(COMPUTE):
            self.sem(k)
        for e in ENGS:
            for (waits, fn, inc) in self.ops[e]:
                for k, v in waits:
                    self.sem(k)
                if inc is not None:
                    self.sem(inc[0])
        prog = self

        def replay(name):
            def run(eng):
                for (waits, fn, inc) in prog.ops[name]:
                    for k, v in waits:
                        eng.wait_ge(prog.sems[k], v)
                    if fn is None:
                        continue
                    ins = fn(eng)
                    if inc is not None:
                        ins.then_inc(prog.sems[inc[0]], inc[1])
            return run

        with nc.Block() as block:
            block.tensor(replay("pe"))
            block.scalar(replay("act"))
            block.vector(replay("dve"))
            block.gpsimd(replay("pool"))
            block.sync(replay("sp"))

    def close(self):
        self.stack.close()


D = 1024
NH = 8
HD = 64
RW = 512
RWKV_COLS = 1792
IN_COLS = 5376
DFF = 2816
PLE = 256
CDEC = -0.6065306597126334
NEG = -30000.0


def bc(vec_buf, n):
    return vec_buf.t[0:n].partition_broadcast(128)


class Ctx:
    pass


def declare_io(P, T):
    c = Ctx()
    f = lambda n, s: P.dram(n, s, F32, kind="ExternalInput")
    c.x = f("x", [T, D]); c.p = f("p", [T, PLE])
    c.ln1_g = f("ln1_g", [D]); c.w_in = f("w_in", [D, IN_COLS]); c.mix_mu = f("mix_mu", [RWKV_COLS])
    c.w0 = f("w0", [RW]); c.w2 = f("w2", [64, RW]); c.a0 = f("a0", [RW]); c.a2 = f("a2", [64, RW])
    c.g2 = f("g2", [128, RW]); c.k_k = f("k_k", [RW]); c.k_a = f("k_a", [RW]); c.r_k = f("r_k", [RW])
    c.lnx_w = f("lnx_w", [RW]); c.lnx_b = f("lnx_b", [RW]); c.biasT = f("biasT", [NH, 128, 640])
    c.gate_b = f("gate_b", [2048]); c.w_br_rwkv = f("w_br_rwkv", [RW, D]); c.w_br_attn = f("w_br_attn", [RW, D])
    c.w_o = f("w_o", [D, D]); c.ln2_g = f("ln2_g", [D]); c.w_ffn_up = f("w_ffn_up", [D, 2 * DFF])
    c.conv_w = f("conv_w", [3, DFF]); c.conv_b = f("conv_b", [DFF]); c.w_ffn_down = f("w_ffn_down", [DFF, D])
    c.ln3_g = f("ln3_g", [D]); c.w_ple = f("w_ple", [PLE, D]); c.w_pg = f("w_pg", [D, D]); c.b_pg = f("b_pg", [D])
    c.lnf_g = f("lnf_g", [D])
    c.cst = f("cst", [128, 8, 128])
    c.out = P.dram("out", [T, D], F32, kind="ExternalOutput")
    return c


def rmsnorm_T(P, xt, xn, junk, ss, rs, pst, hT_dst_fn, ident, tag=""):
    P.op("act", lambda e: e.activation(out=junk.t[:, :], in_=xt.t[:, :], func=AF.Square, accum_out=ss.t[:, :]),
         reads=[xt], writes=[junk, ss])
    P.op("act", lambda e: e.activation(out=rs.t[:, :], in_=ss.t[:, :], func=AF.Sqrt, scale=1.0 / D, bias=1e-6),
         reads=[ss], writes=[rs])
    P.op("dve", lambda e: e.reciprocal(out=rs.t[:, :], in_=rs.t[:, :]), reads=[rs], writes=[rs])
    P.op("act", lambda e: e.activation(out=xn.t[:, :], in_=xt.t[:, :], func=AF.Copy, scale=rs.t[:, :]),
         reads=[xt, rs], writes=[xn])
    for c in range(8):
        P.op("pe", lambda e, c=c: e.transpose(out=pst.t[:, c, :], in_=xn.t[:, c * 128:(c + 1) * 128], identity=ident.t[:, :]),
             reads=[xn, ident], writes=[pst], signal=(c == 7))
    hT_dst_fn(pst)


class Ring:
    def __init__(self, items):
        self.items = list(items)
        self.i = 0

    def next(self):
        b = self.items[self.i % len(self.items)]
        self.i += 1
        return b


def load_cols(P, q, dst, vec, n, ncol):
    P.dma(q, dst.t[:, 0:ncol], vec.t.rearrange("(c p) -> p c", p=128), vec, dst, allow_slow_non_contiguous=True)


def phase1(P, c, T, scr):
    NB = T // 512
    P.stack = contextlib.ExitStack()
    P.cneg = make_cneg(P)
    ident = P.sbuf("ident", [128, 128], BF16)
    P.dma("pool", ident.t[:, :], c.cst.t[:, 0, :], c.cst, ident)
    g1 = P.sbuf("g1", [128, 8], F32)
    load_cols(P, "sp", g1, c.ln1_g, D, 8)
    gb = P.sbuf("gb", [128, 16], F32)
    load_cols(P, "sp", gb, c.gate_b, 2048, 16)
    mub = P.sbuf("mub", [128, RWKV_COLS], F32)
    omm = P.sbuf("omm", [128, RWKV_COLS], F32)
    P.dma("sp", mub.t[:, :], bc(c.mix_mu, RWKV_COLS), c.mix_mu, mub)
    P.op("dve", lambda e: e.tensor_scalar(out=omm.t[:, :], in0=mub.t[:, :], scalar1=-1.0, scalar2=1.0, op0=ALU.mult, op1=ALU.add),
         reads=[mub], writes=[omm])
    Wr1 = P.sbuf("Wr1", [128, 8, RWKV_COLS], BF16)
    Wr2 = P.sbuf("Wr2", [128, 8, RWKV_COLS], BF16)
    Wo = P.sbuf("Wo", [128, 8, 3584], BF16)
    stg = Ring([P.sbuf("stg%d" % i, [128, RWKV_COLS], F32) for i in range(3)])
    xt = Ring([P.sbuf("xt%d" % i, [128, D], F32) for i in range(2)] + stg.items)
    xn = Ring([P.sbuf("xn%d" % i, [128, D], BF16) for i in range(2)])
    junk = P.sbuf("junk", [128, D], BF16)
    ss = Ring([P.sbuf("ss%d" % i, [128, 1], F32) for i in range(2)])
    rs = Ring([P.sbuf("rs%d" % i, [128, 1], F32) for i in range(2)])
    hT = [P.sbuf("hT%d" % i, [128, 8, 513], BF16) for i in range(2)]
    pst = Ring([P.psum("pst%d" % i, [128, 8, 128], BF16) for i in range(2)])
    pm = Ring([P.psum("pm%d" % i, [128, 512], F32) for i in range(5)])
    of = Ring([P.sbuf("of%d" % i, [128, 512], BF16) for i in range(4)])
    ot = Ring([P.sbuf("ot%d" % i, [128, 1536], BF16) for i in range(2)])
    ov = Ring([P.sbuf("ov%d" % i, [128, 512], BF16) for i in range(2)])

    xq = {}

    def lx(b):
        for j in range(4):
            i = 4 * b + j
            x_ = xt.next()
            P.dma("sp", x_.t[:, 0:D], c.x.t[i * 128:(i + 1) * 128, :], c.x, x_)
            xq[i] = x_

    def nrm(b):
        h = hT[b % 2]
        if b == 0:
            P.op("pool", lambda e, h=h: e.memset(h.t[:, :, 0:1], 0.0), writes=[h])
        else:
            hp = hT[(b - 1) % 2]
            P.op("pool", lambda e, h=h, hp=hp: e.tensor_copy(out=h.t[:, :, 0:1], in_=hp.t[:, :, 512:513]), reads=[hp], writes=[h])
        for j in range(4):
            i = 4 * b + j
            x_ = xq.pop(i)
            xn_ = xn.next(); ss_ = ss.next(); rs_ = rs.next(); ps_ = pst.next()
            P.op("act", lambda e, x_=x_, ss_=ss_: e.activation(out=junk.t[:, :], in_=x_.t[:, 0:D], func=AF.Square, accum_out=ss_.t[:, :]),
                 reads=[x_], writes=[junk, ss_])
            RSTD_(P, rs_, rs_.t[:, :], ss_, ss_.t[:, :], 1.0 / D, 1e-6, P.cneg, P.cneg.t[:, 0:1])
            P.op("act", lambda e, x_=x_, xn_=xn_, rs_=rs_: e.activation(out=xn_.t[:, :], in_=x_.t[:, 0:D], func=AF.Copy, scale=rs_.t[:, :]),
                 reads=[x_, rs_], writes=[xn_])
            for k in range(8):
                TR_(P, ps_.t[:, k, :], xn_.t[:, k * 128:(k + 1) * 128], ident.t[:, :], [xn_, ident], [ps_], signal=(k == 7))
            CP_(P, "dve", h.t[:, :, 1 + j * 128:1 + (j + 1) * 128], ps_.t[:, :, :], [ps_], [h])

    lx(0)
    nrm(0)
    for pc in range(3):
        for dc in range(8):
            s = stg.next()
            P.dma("sp", s.t[:, :], c.w_in.t[dc * 128:(dc + 1) * 128, pc * 1792:(pc + 1) * 1792], c.w_in, s)
            if pc == 0:
                P.op("dve", lambda e, s=s, dc=dc: e.scalar_tensor_tensor(out=Wr1.t[:, dc, :], in0=s.t[:, :], scalar=g1.t[:, dc:dc + 1],
                                                                   in1=omm.t[:, :], op0=ALU.mult, op1=ALU.mult),
                     reads=[s, g1, omm], writes=[Wr1])
                P.op("dve", lambda e, s=s, dc=dc: e.scalar_tensor_tensor(out=Wr2.t[:, dc, :], in0=s.t[:, :], scalar=g1.t[:, dc:dc + 1],
                                                                    in1=mub.t[:, :], op0=ALU.mult, op1=ALU.mult),
                     reads=[s, g1, mub], writes=[Wr2])
            else:
                P.op("act", lambda e, s=s, dc=dc, pc=pc: e.activation(out=Wo.t[:, dc, (pc - 1) * 1792:pc * 1792], in_=s.t[:, :], func=AF.Copy,
                                                                 scale=g1.t[:, dc:dc + 1]),
                     reads=[s, g1], writes=[Wo])

    for b in range(NB):
        h = hT[b % 2]
        if b + 1 < NB:
            lx(b + 1)
        for j in range(4):
            i = 4 * b + j
            o_ = ot.next()
            for cg in range(3):
                ps = pm.next()
                for dc in range(8):
                    P.op("pe", lambda e, ps=ps, dc=dc, cg=cg, j=j, h=h: e.matmul(out=ps.t[:, :], lhsT=h.t[:, dc, 1 + j * 128:1 + (j + 1) * 128],
                                                                              rhs=Wr1.t[:, dc, cg * 512:(cg + 1) * 512], start=(dc == 0), stop=False),
                         reads=[h, Wr1], writes=[ps], signal=False)
                for dc in range(8):
                    P.op("pe", lambda e, ps=ps, dc=dc, cg=cg, j=j, h=h: e.matmul(out=ps.t[:, :], lhsT=h.t[:, dc, j * 128:(j + 1) * 128],
                                                                              rhs=Wr2.t[:, dc, cg * 512:(cg + 1) * 512], start=False, stop=(dc == 7)),
                         reads=[h, Wr2], writes=[ps], signal=(dc == 7))
                P.op("dve", lambda e, ps=ps, o_=o_, cg=cg: e.tensor_copy(out=o_.t[:, cg * 512:(cg + 1) * 512], in_=ps.t[:, :]), reads=[ps], writes=[o_])
            P.dma("sp", scr.rkv.t[i * 128:(i + 1) * 128, :], o_.t[:, :], o_, scr.rkv)
            ps = pm.next()
            v_ = ov.next()
            for dc in range(8):
                P.op("pe", lambda e, ps=ps, dc=dc, j=j, h=h: e.matmul(out=ps.t[:, :], lhsT=h.t[:, dc, 1 + j * 128:1 + (j + 1) * 128],
                                                                   rhs=Wo.t[:, dc, 1024:1536], start=(dc == 0), stop=(dc == 7)),
                     reads=[h, Wo], writes=[ps], signal=(dc == 7))
            P.op("dve", lambda e, ps=ps, v_=v_: e.tensor_copy(out=v_.t[:, :], in_=ps.t[:, :]), reads=[ps], writes=[v_])
            P.dma("sp", scr.vA.t[i * 128:(i + 1) * 128, :], v_.t[:, :], v_, scr.vA)
        if b + 1 < NB:
            nrm(b + 1)
        tok = slice(b * 512, (b + 1) * 512)
        for g in range(2):
            ps = pm.next()
            c0 = 1536 + g * 128
            for dc in range(8):
                P.op("pe", lambda e, ps=ps, dc=dc, c0=c0, h=h: e.matmul(out=ps.t[:, :], lhsT=Wr1.t[:, dc, c0:c0 + 128], rhs=h.t[:, dc, 1:513],
                                                                     start=(dc == 0), stop=False), reads=[h, Wr1], writes=[ps], signal=False)
            for dc in range(8):
                P.op("pe", lambda e, ps=ps, dc=dc, c0=c0, h=h: e.matmul(out=ps.t[:, :], lhsT=Wr2.t[:, dc, c0:c0 + 128], rhs=h.t[:, dc, 0:512],
                                                                     start=False, stop=(dc == 7)), reads=[h, Wr2], writes=[ps], signal=(dc == 7))
            o_ = of.next()
            if g == 0:
                P.op("act", lambda e, ps=ps, o_=o_: e.activation(out=o_.t[0:64, :], in_=ps.t[0:64, :], func=AF.Tanh), reads=[ps], writes=[o_])
                P.op("act", lambda e, ps=ps, o_=o_: e.copy(out=o_.t[64:128, :], in_=ps.t[64:128, :]), reads=[ps], writes=[o_])
            else:
                P.op("act", lambda e, ps=ps, o_=o_: e.activation(out=o_.t[:, :], in_=ps.t[:, :], func=AF.Sigmoid), reads=[ps], writes=[o_])
            P.dma("sp", scr.lora.t[g * 128:(g + 1) * 128, tok], o_.t[:, :], o_, scr.lora)
        for g in range(8 + 16):
            ps = pm.next()
            c0 = g * 128 if g < 8 else 1536 + (g - 8) * 128
            for dc in range(8):
                P.op("pe", lambda e, ps=ps, dc=dc, c0=c0, h=h: e.matmul(out=ps.t[:, :], lhsT=Wo.t[:, dc, c0:c0 + 128], rhs=h.t[:, dc, 1:513],
                                                                     start=(dc == 0), stop=(dc == 7)), reads=[h, Wo], writes=[ps], signal=(dc == 7))
            o_ = of.next()
            if g < 8:
                P.op("dve", lambda e, ps=ps, o_=o_: e.tensor_copy(out=o_.t[:, :], in_=ps.t[:, :]), reads=[ps], writes=[o_])
                dstb = scr.qT if g < 4 else scr.kT
                P.dma("sp", dstb.t[(g % 4) * 128:(g % 4 + 1) * 128, tok], o_.t[:, :], o_, dstb)
            else:
                gg = g - 8
                P.op("act", lambda e, ps=ps, o_=o_, gg=gg: e.activation(out=o_.t[:, :], in_=ps.t[:, :], func=AF.Sigmoid, bias=gb.t[:, gg:gg + 1]),
                     reads=[ps, gb], writes=[o_])
                P.dma("sp", scr.gates.t[gg * 128:(gg + 1) * 128, tok], o_.t[:, :], o_, scr.gates)
    P.barrier()
    P.emit_phase()
    P.stack.close()


def make_scratch(P, T, dbg=False):
    s = Ctx()
    kind = "ExternalOutput" if dbg else "Internal"
    mk = lambda n, sh, dt=BF16: P.dram(n, sh, dt, kind=kind)
    s.rkv = mk("s_rkv", [T, 1536]); s.vA = mk("s_vA", [T, 512]); s.lora = mk("s_lora", [256, T])
    s.qT = mk("s_qT", [512, T]); s.kT = mk("s_kT", [512, T]); s.gates = mk("s_gates", [2048, T])
    s.oaT = mk("s_oaT", [512, T]); s.obT = mk("s_obT", [512, T]); s.x1 = mk("s_x1", [T, D], F32); s.x2 = mk("s_x2", [T, D], F32)
    return s


def TT_(P, eng, out, in0, in1, op, reads, writes):
    P.op(eng, lambda e: e.tensor_tensor(out=out, in0=in0, in1=in1, op=op), reads=reads, writes=writes)


def STT_(P, out, in0, scalar, in1, op0, op1, reads, writes):
    P.op("dve", lambda e: e.scalar_tensor_tensor(out=out, in0=in0, scalar=scalar, in1=in1, op0=op0, op1=op1), reads=reads, writes=writes)


def TS_(P, eng, out, in0, s1, s2, op0, op1, reads, writes):
    if s2 is None:
        P.op(eng, lambda e: e.tensor_scalar(out=out, in0=in0, scalar1=s1, scalar2=None, op0=op0), reads=reads, writes=writes)
    else:
        P.op(eng, lambda e: e.tensor_scalar(out=out, in0=in0, scalar1=s1, scalar2=s2, op0=op0, op1=op1), reads=reads, writes=writes)


def ACT_(P, out, in_, func, reads, writes, scale=1.0, bias=None):
    if bias is None:
        P.op("act", lambda e: e.activation(out=out, in_=in_, func=func, scale=scale), reads=reads, writes=writes)
    else:
        P.op("act", lambda e: e.activation(out=out, in_=in_, func=func, scale=scale, bias=bias), reads=reads, writes=writes)


def CP_(P, eng, out, in_, reads, writes):
    if eng == "act":
        P.op("act", lambda e: e.copy(out=out, in_=in_), reads=reads, writes=writes)
    else:
        P.op(eng, lambda e: e.tensor_copy(out=out, in_=in_), reads=reads, writes=writes)


def MM_(P, out, lhsT, rhs, reads, writes, start=True, stop=True, signal=True):
    P.op("pe", lambda e: e.matmul(out=out, lhsT=lhsT, rhs=rhs, start=start, stop=stop), reads=reads, writes=writes, signal=signal)


def TR_(P, out, in_, ident, reads, writes, signal=True):
    P.op("pe", lambda e: e.transpose(out=out, in_=in_, identity=ident), reads=reads, writes=writes, signal=signal)


def RED_(P, out, in_, reads, writes):
    P.op("dve", lambda e: e.tensor_reduce(out=out, in_=in_, axis=AX.X, op=ALU.add), reads=reads, writes=writes)


def RSTD_(P, out_buf, out_ap, in_buf, in_ap, scale, eps, cneg_buf, cneg_ap):
    P.op("dve", lambda e: e.tensor_scalar(out=out_ap, in0=in_ap, scalar1=scale, scalar2=eps, op0=ALU.mult, op1=ALU.add),
         reads=[in_buf], writes=[out_buf])
    P.op("pool", lambda e: e.tensor_tensor(out=out_ap, in0=out_ap, in1=cneg_ap, op=ALU.pow), reads=[out_buf, cneg_buf], writes=[out_buf])


def make_cneg(P):
    cn = P.sbuf("cneg", [128, 8], F32)
    P.op("pool", lambda e: e.memset(cn.t[:, :], -0.5), writes=[cn])
    return cn


def make_cst():
    c = np.zeros((128, 8, 128), np.float32)
    s = np.arange(128)[:, None]
    t = np.arange(128)[None, :]
    same = (s // 64) == (t // 64)
    c[:, 0, :] = np.eye(128)
    c[:, 1, :] = same & (s <= t)
    c[:, 2, :] = same & (s < t)
    c[:, 3, :] = same & (s > t)
    c[:, 4, 0:2] = (s // 64) == np.arange(2)[None, :]
    s64 = s % 64
    t64 = np.arange(64)[None, :]
    c[:, 5, 0:64] = t64 > s64
    c[:, 5, 64:128] = t64 < s64
    c[:, 6, 0:64] = t64 >= s64
    c[:, 6, 64:128] = t64 == s64
    c[:, 7, :] = 8.0 * np.eye(128)
    return c


def make_biasT(rel_bias):
    k = np.arange(128)[:, None]
    q = np.arange(640)[None, :]
    rel = q - k
    idx = np.clip(rel, -63, 128) + 63
    kc = k // 64
    qc = q // 64
    ok = (qc >= kc) & (qc <= kc + 8)
    out = rel_bias[:, idx].astype(np.float32)
    out[:, ~ok] = NEG
    return out


def phase2(P, c, T, scr):
    NT = T // 128
    P.stack = contextlib.ExitStack()
    SB = lambda n, sh, dt=F32: P.sbuf(n, sh, dt)
    RG = lambda n, sh, dt=F32, k=2: Ring([P.sbuf("%s%d" % (n, i), sh, dt) for i in range(k)])
    ident = SB("ident", [128, 128], BF16)
    P.dma("pool", ident.t[:, :], c.cst.t[:, 0, :], c.cst, ident)
    cst = SB("cstf", [128, 8, 128])
    P.dma("sp", cst.t[:, :, :], c.cst.t[:, :, :], c.cst, cst)
    wa2 = SB("wa2", [128, RW], BF16)
    P.dma("pool", wa2.t[0:64, :], c.w2.t[:, :], c.w2, wa2)
    P.dma("pool", wa2.t[64:128, :], c.a2.t[:, :], c.a2, wa2)
    g2b = SB("g2b", [128, RW], BF16)
    P.dma("pool", g2b.t[:, :], c.g2.t[:, :], c.g2, g2b)
    bt = {}
    for nm in ["w0", "a0", "k_k", "k_a", "r_k", "lnx_w", "lnx_b"]:
        bt[nm] = SB("b_" + nm, [128, RW])
        P.dma("sp", bt[nm].t[:, :], bc(getattr(c, nm), RW), getattr(c, nm), bt[nm])
    Linc, Lsl, Lsu, cind = cst.t[:, 1, :], cst.t[:, 2, :], cst.t[:, 3, :], cst.t[:, 4, 0:2]
    b3 = lambda ap: ap.unsqueeze(1).to_broadcast([128, 8, 64])
    MU, ML, MUI, I64 = b3(cst.t[:, 5, 0:64]), b3(cst.t[:, 5, 64:128]), b3(cst.t[:, 6, 0:64]), b3(cst.t[:, 6, 64:128])

    pg = Ring([P.psum("pg%d" % i, [128, 512], F32) for i in range(5)])
    pq = Ring([P.psum("pq%d" % i, [128, 512], F32) for i in range(3)])
    v3 = lambda buf: buf.t.rearrange("p (h v) -> p h v", h=8)
    def bfv(buf, inner):
        return buf.t.bitcast(BF16)[:, 0:8 * inner].rearrange("p (h t) -> p h t", h=8)

    rkvt = RG("rkvt", [128, 1536], BF16)
    lo1 = RG("lo1", [128, 128], BF16); lo2 = RG("lo2", [128, 128], BF16)
    names_f = ["t_w", "sg", "t_a", "a_", "e_pos", "e_neg", "e_prev", "e_rel", "kkr", "sq", "kk", "bb", "t1", "k2", "t2", "yt", "yn", "bv"]
    F = {n: SB(n, [128, RW]) for n in names_f}
    g_ = RG("g_", [128, RW])
    names_b = ["At", "Bt", "Kt", "Rt"]
    Bq = {n: SB(n, [128, RW], BF16) for n in names_b}
    Bh = RG("Bh", [128, RW], BF16); Kh = RG("Kh", [128, RW], BF16)
    ATs = RG("ATs", [64, 8, 128], BF16); RTs = RG("RTs", [64, 8, 128], BF16)
    BTs = SB("BTs", [64, 8, 128], BF16); KTs = SB("KTs", [64, 8, 128], BF16)
    XA = Ring([P.sbuf("XA%d" % i, [128, 8, 64], BF16) for i in range(2)])
    XTA = Ring([P.sbuf("XTA%d" % i, [128, 8, 64], BF16) for i in range(2)])
    MakT = SB("MakT", [128, 8, 64], BF16)
    RBT = RG("RBT", [128, 8, 64], BF16); RKT = SB("RKT", [128, 8, 64], BF16)
    TTb = RG("TTb", [128, 8, 64], BF16)
    W1a = RG("W1a", [128, 8, 64]); Ya = RG("Ya", [128, 8, 64])
    W1 = SB("W1", [128, 8, 64], BF16); U = SB("U", [128, 8, 64], BF16)
    STb = Ring([P.sbuf("STb%d" % i, [64, 8, 64], BF16) for i in range(2)])
    SF = Ring([P.sbuf("SF%d" % i, [64, 8, 64], F32) for i in range(2)])
    tmpS = SB("tmpS", [64, 8, 64]); tmp2 = SB("tmp2", [64, 8, 64])
    pc = RG("pc", [64, 8, 2])
    sm = {n: SB(n, [128, 8]) for n in ["ssq", "nrm", "rn", "s1", "s2", "mean", "msq", "var", "rstd"]}
    bcf = RG("bcf", [128, 8])
    oa = SB("oa", [128, RW], BF16)
    oaT = RG("oaT", [128, 4, 128], BF16)

    st_b = STb.next(); st_f = SF.next()
    P.op("pool", lambda e: e.memset(st_b.t[:, :, :], 0.0), writes=[st_b])
    P.op("pool", lambda e: e.memset(st_f.t[:, :, :], 0.0), writes=[st_f])
    h3 = lambda ap: ap.rearrange("p (h v) -> p h v", h=8)
    bl = lambda ap: ap.unsqueeze(2).to_broadcast([128, 8, 64])

    for i in range(NT):
        tok = slice(i * 128, (i + 1) * 128)
        rk = rkvt.next(); l1 = lo1.next(); l2 = lo2.next()
        P.dma("sp", rk.t[:, :], scr.rkv.t[tok, :], scr.rkv, rk)
        P.dma("sp", l1.t[:, :], scr.lora.t[0:128, tok], scr.lora, l1)
        P.dma("sp", l2.t[:, :], scr.lora.t[128:256, tok], scr.lora, l2)
        r_, k_, v_ = rk.t[:, 0:512], rk.t[:, 512:1024], rk.t[:, 1024:1536]
        p_w = pg.next(); p_a = pg.next(); p_g = pg.next()
        MM_(P, p_w.t[:, :], l1.t[0:64, :], wa2.t[0:64, :], [l1, wa2], [p_w])
        MM_(P, p_a.t[:, :], l1.t[64:128, :], wa2.t[64:128, :], [l1, wa2], [p_a])
        MM_(P, p_g.t[:, :], l2.t[:, :], g2b.t[:, :], [l2, g2b], [p_g])
        f = {n: F[n].t[:, :] for n in names_f}
        TT_(P, "dve", f["t_w"], p_w.t[:, :], bt["w0"].t[:, :], ALU.add, [p_w, bt["w0"]], [F["t_w"]])
        ACT_(P, f["sg"], f["t_w"], AF.Sigmoid, [F["t_w"]], [F["sg"]])
        TT_(P, "dve", f["t_a"], p_a.t[:, :], bt["a0"].t[:, :], ALU.add, [p_a, bt["a0"]], [F["t_a"]])
        ACT_(P, f["a_"], f["t_a"], AF.Sigmoid, [F["t_a"]], [F["a_"]])
        gq = g_.next()
        CP_(P, "act", gq.t[:, :], p_g.t[:, :], [p_g], [gq])
        p1 = pg.next(); p2 = pg.next(); p3 = pg.next(); p4 = pg.next()
        MM_(P, p1.t[:, :], Linc, f["sg"], [cst, F["sg"]], [p1])
        MM_(P, p2.t[:, :], Lsl, f["sg"], [cst, F["sg"]], [p2])
        MM_(P, p3.t[:, :], Lsu, f["sg"], [cst, F["sg"]], [p3])
        for h in range(8):
            MM_(P, p4.t[0:64, 2 * h:2 * h + 2], F["sg"].t[:, h * 64:(h + 1) * 64], cind, [F["sg"], cst], [p4], signal=(h == 7))
        ACT_(P, f["e_pos"], p1.t[:, :], AF.Exp, [p1], [F["e_pos"]], scale=CDEC)
        ACT_(P, f["e_neg"], p1.t[:, :], AF.Exp, [p1], [F["e_neg"]], scale=-CDEC)
        ACT_(P, f["e_prev"], p2.t[:, :], AF.Exp, [p2], [F["e_prev"]], scale=CDEC)
        ACT_(P, f["e_rel"], p3.t[:, :], AF.Exp, [p3], [F["e_rel"]], scale=CDEC)
        pcq = pc.next()
        ACT_(P, pcq.t[:, :, :], p4.t[0:64, 0:16].rearrange("p (h c) -> p h c", h=8), AF.Exp, [p4], [pcq], scale=CDEC)
        if CUT < 2:
            continue
        TT_(P, "dve", f["kkr"], k_, bt["k_k"].t[:, :], ALU.mult, [rk, bt["k_k"]], [F["kkr"]])
        TT_(P, "pool", f["sq"], f["kkr"], f["kkr"], ALU.mult, [F["kkr"]], [F["sq"]])
        RED_(P, sm["ssq"].t[:, :], h3(f["sq"]), [F["sq"]], [sm["ssq"]])
        ACT_(P, sm["nrm"].t[:, :], sm["ssq"].t[:, :], AF.Sqrt, [sm["ssq"]], [sm["nrm"]])
        TS_(P, "dve", sm["nrm"].t[:, :], sm["nrm"].t[:, :], 1e-12, None, ALU.max, None, [sm["nrm"]], [sm["nrm"]])
        P.op("dve", lambda e: e.reciprocal(out=sm["rn"].t[:, :], in_=sm["nrm"].t[:, :]), reads=[sm["nrm"]], writes=[sm["rn"]])
        TT_(P, "dve", h3(f["kk"]), h3(f["kkr"]), bl(sm["rn"].t[:, :]), ALU.mult, [F["kkr"], sm["rn"]], [F["kk"]])
        TT_(P, "dve", f["bb"], f["kk"], f["a_"], ALU.mult, [F["kk"], F["a_"]], [F["bb"]])
        STT_(P, f["t1"], f["a_"], -1.0, bt["k_a"].t[:, :], ALU.add, ALU.mult, [F["a_"], bt["k_a"]], [F["t1"]])
        STT_(P, f["k2"], f["t1"], 1.0, k_, ALU.add, ALU.mult, [F["t1"], rk], [F["k2"]])
        STT_(P, Bq["At"].t[:, :], f["kk"], -1.0, f["e_prev"], ALU.mult, ALU.mult, [F["kk"], F["e_prev"]], [Bq["At"]])
        TT_(P, "dve", Bq["Bt"].t[:, :], f["bb"], f["e_neg"], ALU.mult, [F["bb"], F["e_neg"]], [Bq["Bt"]])
        TT_(P, "dve", Bq["Kt"].t[:, :], f["k2"], f["e_neg"], ALU.mult, [F["k2"], F["e_neg"]], [Bq["Kt"]])
        TT_(P, "dve", Bq["Rt"].t[:, :], r_, f["e_pos"], ALU.mult, [rk, F["e_pos"]], [Bq["Rt"]])
        bh = Bh.next(); kh = Kh.next()
        TT_(P, "pool", bh.t[:, :], f["bb"], f["e_rel"], ALU.mult, [F["bb"], F["e_rel"]], [bh])
        TT_(P, "pool", kh.t[:, :], f["k2"], f["e_rel"], ALU.mult, [F["k2"], F["e_rel"]], [kh])
        TT_(P, "pool", f["t2"], r_, bt["r_k"].t[:, :], ALU.mult, [rk, bt["r_k"]], [F["t2"]])
        TT_(P, "pool", f["t2"], f["t2"], f["k2"], ALU.mult, [F["t2"], F["k2"]], [F["t2"]])
        bq = bcf.next()
        RED_(P, bq.t[:, :], h3(f["t2"]), [F["t2"]], [bq])
        if CUT < 3:
            continue
        ats = ATs.next(); rts = RTs.next()
        for (src, dst, ev) in [("At", ats, "act"), ("Bt", BTs, "dve"), ("Kt", KTs, "act"), ("Rt", rts, "dve")]:
            ps = pg.next()
            pv = bfv(ps, 128)
            for h in range(8):
                TR_(P, pv[0:64, h, :], Bq[src].t[:, h * 64:(h + 1) * 64], ident.t[:, :], [Bq[src], ident], [ps], signal=(h == 7))
            CP_(P, ev, dst.t[:, :, :], pv[0:64, :, :], [ps], [dst])
        if CUT < 4:
            continue
        def prod(lhs, rhs, mask, dst, eng="dve"):
            ps = pg.next()
            for cc in range(2):
                cols = slice(cc * 64, (cc + 1) * 64)
                for h in range(8):
                    MM_(P, v3(ps)[cc * 64:(cc + 1) * 64, h, :], lhs.t[:, h, cols], rhs.t[:, h, cols], [lhs, rhs], [ps],
                        signal=(cc == 1 and h == 7))
            TT_(P, eng, dst.t[:, :, :], v3(ps), mask, ALU.mult, [ps, cst], [dst])
        xt_ = XTA.next(); x_ = XA.next()
        rbt = RBT.next()
        prod(BTs, ats, MU, xt_)
        prod(ats, BTs, ML, x_)
        prod(KTs, ats, MU, MakT)
        prod(BTs, rts, MUI, rbt)
        prod(KTs, rts, MUI, RKT)
        tt = TTb.next()
        TT_(P, "dve", tt.t[:, :, :], xt_.t[:, :, :], I64, ALU.add, [xt_, cst], [tt])
        if CUT < 5:
            continue
        for lev in range(1, 6):
            xn_ = XA.next()
            xtn_ = XTA.next() if lev < 5 else None
            for cc in range(2):
                hs = slice(cc * 64, (cc + 1) * 64)
                px = pg.next()
                for h in range(8):
                    MM_(P, v3(px)[hs, h, :], xt_.t[hs, h, :], x_.t[hs, h, :], [xt_, x_], [px], signal=(h == 7))
                CP_(P, "act", xn_.t[hs, :, :], v3(px)[hs, :, :], [px], [xn_])
                if lev < 5:
                    pxt = pg.next()
                    for h in range(8):
                        MM_(P, v3(pxt)[hs, h, :], x_.t[hs, h, :], xt_.t[hs, h, :], [xt_, x_], [pxt], signal=(h == 7))
                    CP_(P, "act", xtn_.t[hs, :, :], v3(pxt)[hs, :, :], [pxt], [xtn_])
            x_ = xn_
            if lev < 5:
                xt_ = xtn_
            for cc in range(2):
                hs = slice(cc * 64, (cc + 1) * 64)
                pt = pg.next()
                for h in range(8):
                    MM_(P, v3(pt)[hs, h, :], x_.t[hs, h, :], tt.t[hs, h, :], [x_, tt], [pt], signal=(h == 7))
                TT_(P, "dve", tt.t[hs, :, :], v3(pt)[hs, :, :], tt.t[hs, :, :], ALU.add, [pt, tt], [tt])
        if CUT < 6:
            continue
        w1a = W1a.next(); ya = Ya.next()
        for cc in range(2):
            hs = slice(cc * 64, (cc + 1) * 64)
            ps = pg.next()
            for h in range(8):
                MM_(P, v3(ps)[hs, h, :], MakT.t[hs, h, :], rk.t[hs, 1024 + h * 64:1024 + (h + 1) * 64], [MakT, rk], [ps], signal=(h == 7))
            CP_(P, "act", w1a.t[hs, :, :], v3(ps)[hs, :, :], [ps], [w1a])
            ps = pg.next()
            for h in range(8):
                MM_(P, v3(ps)[hs, h, :], RKT.t[hs, h, :], rk.t[hs, 1024 + h * 64:1024 + (h + 1) * 64], [RKT, rk], [ps], signal=(h == 7))
            CP_(P, "act", ya.t[hs, :, :], v3(ps)[hs, :, :], [ps], [ya])
        if CUT < 7:
            continue
        for cc in range(2):
            hs = slice(cc * 64, (cc + 1) * 64)
            cols = slice(cc * 64, (cc + 1) * 64)
            pk = pq.next()
            for h in range(8):
                MM_(P, v3(pk)[0:64, h, :], kh.t[hs, h * 64:(h + 1) * 64], rk.t[hs, 1024 + h * 64:1024 + (h + 1) * 64], [kh, rk], [pk], signal=(h == 7))
            TT_(P, "dve", tmpS.t[:, :, :], st_f.t[:, :, :], pcq.t[:, :, cc:cc + 1].to_broadcast([64, 8, 64]), ALU.mult, [st_f, pcq], [tmpS])
            TT_(P, "dve", tmp2.t[:, :, :], v3(pk)[0:64, :, :], tmpS.t[:, :, :], ALU.add, [pk, tmpS], [tmp2])
            pw = pq.next()
            for h in range(8):
                MM_(P, v3(pw)[hs, h, :], ats.t[:, h, cols], st_b.t[:, h, :], [ats, st_b], [pw], signal=(h == 7))
            TT_(P, "dve", W1.t[hs, :, :], v3(pw)[hs, :, :], w1a.t[hs, :, :], ALU.add, [pw, w1a], [W1])
            prs = pq.next()
            for h in range(8):
                MM_(P, v3(prs)[hs, h, :], rts.t[:, h, cols], st_b.t[:, h, :], [rts, st_b], [prs], signal=(h == 7))
            TT_(P, "dve", h3(f["yt"])[hs, :, :], v3(prs)[hs, :, :], ya.t[hs, :, :], ALU.add, [prs, ya], [F["yt"]])
            pu = pq.next()
            for h in range(8):
                MM_(P, v3(pu)[hs, h, :], tt.t[hs, h, :], W1.t[hs, h, :], [tt, W1], [pu], signal=(h == 7))
            CP_(P, "act", U.t[hs, :, :], v3(pu)[hs, :, :], [pu], [U])
            psn = pq.next()
            for h in range(8):
                MM_(P, v3(psn)[0:64, h, :], bh.t[hs, h * 64:(h + 1) * 64], U.t[hs, h, :], [bh, U], [psn], signal=(h == 7))
            py = pq.next()
            for h in range(8):
                MM_(P, v3(py)[hs, h, :], rbt.t[hs, h, :], U.t[hs, h, :], [rbt, U], [py], signal=(h == 7))
            nb = STb.next(); nf = SF.next()
            TT_(P, "dve", nb.t[:, :, :], v3(psn)[0:64, :, :], tmp2.t[:, :, :], ALU.add, [psn, tmp2], [nb])
            TT_(P, "dve", nf.t[:, :, :], v3(psn)[0:64, :, :], tmp2.t[:, :, :], ALU.add, [psn, tmp2], [nf])
            st_b, st_f = nb, nf
            TT_(P, "dve", h3(f["yt"])[hs, :, :], v3(py)[hs, :, :], h3(f["yt"])[hs, :, :], ALU.add, [py, F["yt"]], [F["yt"]])
        if CUT < 8:
            continue
        yt3 = h3(f["yt"]); yn3 = h3(f["yn"])
        RED_(P, sm["s1"].t[:, :], yt3, [F["yt"]], [sm["s1"]])
        TT_(P, "pool", f["sq"], f["yt"], f["yt"], ALU.mult, [F["yt"]], [F["sq"]])
        RED_(P, sm["s2"].t[:, :], h3(f["sq"]), [F["sq"]], [sm["s2"]])
        TS_(P, "dve", sm["mean"].t[:, :], sm["s1"].t[:, :], 1.0 / 64, None, ALU.mult, None, [sm["s1"]], [sm["mean"]])
        TT_(P, "dve", sm["msq"].t[:, :], sm["mean"].t[:, :], sm["mean"].t[:, :], ALU.mult, [sm["mean"]], [sm["msq"]])
        STT_(P, sm["var"].t[:, :], sm["s2"].t[:, :], 1.0 / 64, sm["msq"].t[:, :], ALU.mult, ALU.subtract, [sm["s2"], sm["msq"]], [sm["var"]])
        ACT_(P, sm["rstd"].t[:, :], sm["var"].t[:, :], AF.Sqrt, [sm["var"]], [sm["rstd"]], bias=64e-5)
        P.op("dve", lambda e: e.reciprocal(out=sm["rstd"].t[:, :], in_=sm["rstd"].t[:, :]), reads=[sm["rstd"]], writes=[sm["rstd"]])
        TT_(P, "dve", yn3, yt3, bl(sm["mean"].t[:, :]), ALU.subtract, [F["yt"], sm["mean"]], [F["yn"]])
        TT_(P, "dve", yn3, yn3, bl(sm["rstd"].t[:, :]), ALU.mult, [F["yn"], sm["rstd"]], [F["yn"]])
        TT_(P, "pool", f["yn"], f["yn"], bt["lnx_w"].t[:, :], ALU.mult, [F["yn"], bt["lnx_w"]], [F["yn"]])
        TT_(P, "pool", f["yn"], f["yn"], bt["lnx_b"].t[:, :], ALU.add, [F["yn"], bt["lnx_b"]], [F["yn"]])
        TT_(P, "dve", h3(f["bv"]), h3(v_), bl(bq.t[:, :]), ALU.mult, [rk, bq], [F["bv"]])
        TT_(P, "pool", f["yn"], f["yn"], f["bv"], ALU.add, [F["yn"], F["bv"]], [F["yn"]])
        TT_(P, "dve", oa.t[:, :], f["yn"], gq.t[:, :], ALU.mult, [F["yn"], gq], [oa])
        ps = pg.next()
        pv = ps.t.bitcast(BF16)[:, 0:512].rearrange("p (c t) -> p c t", c=4)
        for fc in range(4):
            TR_(P, pv[:, fc, :], oa.t[:, fc * 128:(fc + 1) * 128], ident.t[:, :], [oa, ident], [ps], signal=(fc == 3))
        ot_ = oaT.next()
        CP_(P, "act", ot_.t[:, :, :], pv, [ps], [ot_])
        P.dma("sp", scr.oaT.t[:, tok].rearrange("(c p) t -> p c t", p=128), ot_.t[:, :, :], ot_, scr.oaT)
    P.barrier()
    P.emit_phase()
    P.stack.close()


def phase3(P, c, T, scr):
    NT = T // 128
    P.stack = contextlib.ExitStack()
    ident = P.sbuf("ident", [128, 128], BF16)
    P.dma("pool", ident.t[:, :], c.cst.t[:, 0, :], c.cst, ident)
    I8 = P.sbuf("I8", [128, 128], BF16)
    P.dma("pool", I8.t[:, :], c.cst.t[:, 7, :], c.cst, I8)
    kTh = Ring([P.sbuf("kTh%d" % i, [128, T], BF16) for i in range(2)])
    qTh = Ring([P.sbuf("qTh%d" % i, [128, T], BF16) for i in range(2)])
    Vh = Ring([P.sbuf("Vh%d" % i, [128, NT, 65], BF16) for i in range(2)])
    bT = Ring([P.sbuf("bT%d" % i, [128, 640], BF16) for i in range(2)])
    for r in (kTh, qTh):
        for b in r.items:
            P.op("pool", lambda e, b=b: e.memset(b.t[64:128, :], 0.0), writes=[b])
    for b in Vh.items:
        P.op("pool", lambda e, b=b: e.memset(b.t[:, :, 64:65], 1.0), writes=[b])
    PT = [P.sbuf("PT%d" % i, [128, 640], BF16) for i in range(6)]
    ob = P.sbuf("ob", [128, NT, 512], BF16)
    rc = Ring([P.sbuf("rc%d" % i, [128, 1], F32) for i in range(2)])
    pa = Ring([P.psum("pa%d" % i, [128, 512], F32) for i in range(4)])
    pb = Ring([P.psum("pb%d" % i, [128, 512], F32) for i in range(2)])
    pt_ = Ring([P.psum("ptr%d" % i, [128, 512], F32) for i in range(2)])
    for h in range(NH):
        kt = kTh.next(); qt = qTh.next(); vh = Vh.next(); bias = bT.next()
        P.dma("sp", kt.t[0:64, :], scr.kT.t[h * 64:(h + 1) * 64, :], scr.kT, kt)
        P.dma("sp", qt.t[0:64, :], scr.qT.t[h * 64:(h + 1) * 64, :], scr.qT, qt)
        P.dma("sp", vh.t[:, :, 0:64], scr.vA.t[:, h * 64:(h + 1) * 64].rearrange("(m p) d -> p m d", p=128), scr.vA, vh)
        P.dma("pool", bias.t[:, :], c.biasT.t[h, :, :], c.biasT, bias)
        for m in range(NT):
            W = min(640, T - 128 * m)
            pt = PT[m % 6]
            for (c0, c1) in [(0, min(W, 512)), (512, W)]:
                if c1 <= c0:
                    continue
                ps = pa.next()
                n = c1 - c0
                MM_(P, ps.t[:, 0:n], I8.t[:, :], bias.t[:, c0:c1], [I8, bias], [ps], start=True, stop=False, signal=False)
                MM_(P, ps.t[:, 0:n], kt.t[:, m * 128:(m + 1) * 128], qt.t[:, m * 128 + c0:m * 128 + c1], [kt, qt], [ps], start=False, stop=True)
                ACT_(P, pt.t[:, c0:c1], ps.t[:, 0:n], AF.Exp, [ps], [pt], scale=0.125)
            pv = pb.next()
            for cc in range(2):
                cq = 2 * m + cc
                m0 = max(0, (cq - 8) // 2)
                ms = list(range(m0, cq // 2 + 1))
                for mi, mp in enumerate(ms):
                    off = (cq - 2 * mp) * 64
                    MM_(P, pv.t[cc * 64:(cc + 1) * 64, 0:65], PT[mp % 6].t[:, off:off + 64], vh.t[:, mp, :], [PT[mp % 6], vh], [pv],
                        start=(mi == 0), stop=(mi == len(ms) - 1), signal=(mi == len(ms) - 1))
            r_ = rc.next()
            P.op("dve", lambda e, r_=r_, pv=pv: e.reciprocal(out=r_.t[:, :], in_=pv.t[:, 64:65]), reads=[pv], writes=[r_])
            ACT_(P, ob.t[:, m, h * 64:(h + 1) * 64], pv.t[:, 0:64], AF.Copy, [pv, r_], [ob], scale=r_.t[:, :])
    obT = Ring([P.sbuf("obT%d" % i, [128, 4, 128], BF16) for i in range(2)])
    for m in range(NT):
        ps = pt_.next()
        pvw = ps.t.bitcast(BF16)[:, 0:512].rearrange("p (c t) -> p c t", c=4)
        for fc in range(4):
            TR_(P, pvw[:, fc, :], ob.t[:, m, fc * 128:(fc + 1) * 128], ident.t[:, :], [ob, ident], [ps], signal=(fc == 3))
        o_ = obT.next()
        CP_(P, "dve", o_.t[:, :, :], pvw, [ps], [o_])
        P.dma("sp", scr.obT.t[:, m * 128:(m + 1) * 128].rearrange("(c p) t -> p c t", p=128), o_.t[:, :, :], o_, scr.obT)
    P.barrier()
    P.emit_phase()
    P.stack.close()


def phase4a(P, c, T, scr):
    NB = T // 512
    scr.keep = contextlib.ExitStack()
    P.stack = scr.keep
    scr.g2 = P.sbuf("g2c", [128, 8], F32); load_cols(P, "sp", scr.g2, c.ln2_g, D, 8)
    scr.Wup = [P.sbuf("Wup%d" % dc, [128, 2 * DFF], BF16) for dc in range(8)]
    for dc in range(8):
        P.dma("pool", scr.Wup[dc].t[:, :], c.w_ffn_up.t[dc * 128:(dc + 1) * 128, :], c.w_ffn_up, scr.Wup[dc])
        P.op("act", lambda e, dc=dc: e.activation(out=scr.Wup[dc].t[:, :], in_=scr.Wup[dc].t[:, :], func=AF.Copy, scale=scr.g2.t[:, dc:dc + 1]),
             reads=[scr.Wup[dc], scr.g2], writes=[scr.Wup[dc]])
    P.stack = contextlib.ExitStack()
    WA = P.sbuf("WA", [128, 4, D], BF16); WB = P.sbuf("WB", [128, 4, D], BF16); WO = P.sbuf("WO", [128, 8, D], BF16)
    P.dma("pool", WA.t[:, :, :], c.w_br_rwkv.t.rearrange("(c p) n -> p c n", p=128), c.w_br_rwkv, WA)
    P.dma("pool", WB.t[:, :, :], c.w_br_attn.t.rearrange("(c p) n -> p c n", p=128), c.w_br_attn, WB)
    P.dma("pool", WO.t[:, :, :], c.w_o.t.rearrange("(c p) n -> p c n", p=128), c.w_o, WO)
    oa = Ring([P.sbuf("oa%d" % i, [128, 4, 512], BF16) for i in range(2)])
    obb = Ring([P.sbuf("obb%d" % i, [128, 4, 512], BF16) for i in range(2)])
    ga = Ring([P.sbuf("ga%d" % i, [128, 8, 512], BF16) for i in range(2)])
    gbb = Ring([P.sbuf("gbb%d" % i, [128, 8, 512], BF16) for i in range(2)])
    t1 = Ring([P.sbuf("t1_%d" % i, [128, 512], F32) for i in range(2)])
    t2 = Ring([P.sbuf("t2_%d" % i, [128, 512], F32) for i in range(2)])
    mT = Ring([P.sbuf("mT%d" % i, [128, 8, 512], BF16) for i in range(1)])
    xt = Ring([P.sbuf("xt%d" % i, [128, D], F32) for i in range(3)])
    xo = Ring([P.sbuf("xo%d" % i, [128, D], F32) for i in range(2)])
    pm = Ring([P.psum("pm%d" % i, [128, 512], F32) for i in range(6)])
    blk = {}
    xq = {}

    def lb(b):
        tok = slice(b * 512, (b + 1) * 512)
        a_ = oa.next(); b_ = obb.next(); ga_ = ga.next(); gb_ = gbb.next()
        P.dma("sp", a_.t[:, :, :], scr.oaT.t[:, tok].rearrange("(c p) t -> p c t", p=128), scr.oaT, a_)
        P.dma("sp", b_.t[:, :, :], scr.obT.t[:, tok].rearrange("(c p) t -> p c t", p=128), scr.obT, b_)
        P.dma("sp", ga_.t[:, :, :], scr.gates.t[0:1024, tok].rearrange("(c p) t -> p c t", p=128), scr.gates, ga_)
        P.dma("sp", gb_.t[:, :, :], scr.gates.t[1024:2048, tok].rearrange("(c p) t -> p c t", p=128), scr.gates, gb_)
        blk[b] = (a_, b_, ga_, gb_)

    def lx(i):
        x_ = xt.next()
        P.dma("sp", x_.t[:, :], c.x.t[i * 128:(i + 1) * 128, :], c.x, x_)
        xq[i] = x_

    lb(0)
    for i in range(min(2, 4 * NB)):
        lx(i)
    for b in range(NB):
        tok = slice(b * 512, (b + 1) * 512)
        a_, b_, ga_, gb_ = blk.pop(b)
        m_ = mT.next()
        if b + 1 < NB:
            lb(b + 1)
        for cg in range(8):
            pA = pm.next(); pB = pm.next()
            for fc in range(4):
                MM_(P, pA.t[:, :], WA.t[:, fc, cg * 128:(cg + 1) * 128], a_.t[:, fc, :], [WA, a_], [pA], start=(fc == 0), stop=(fc == 3), signal=(fc == 3))
            for fc in range(4):
                MM_(P, pB.t[:, :], WB.t[:, fc, cg * 128:(cg + 1) * 128], b_.t[:, fc, :], [WB, b_], [pB], start=(fc == 0), stop=(fc == 3), signal=(fc == 3))
            u1 = t1.next(); u2 = t2.next()
            TT_(P, "dve", u1.t[:, :], pA.t[:, :], ga_.t[:, cg, :], ALU.mult, [pA, ga_], [u1])
            TT_(P, "dve", u2.t[:, :], pB.t[:, :], gb_.t[:, cg, :], ALU.mult, [pB, gb_], [u2])
            TT_(P, "pool", m_.t[:, cg, :], u1.t[:, :], u2.t[:, :], ALU.add, [u1, u2], [m_])
        for j in range(4):
            i = 4 * b + j
            x_ = xq.pop(i); o_ = xo.next()
            if i + 2 < 4 * NB:
                lx(i + 2)
            for hf in range(2):
                ps = pm.next()
                for cg in range(8):
                    MM_(P, ps.t[:, :], m_.t[:, cg, j * 128:(j + 1) * 128], WO.t[:, cg, hf * 512:(hf + 1) * 512], [m_, WO], [ps],
                        start=(cg == 0), stop=(cg == 7), signal=(cg == 7))
                TT_(P, "dve", o_.t[:, hf * 512:(hf + 1) * 512], ps.t[:, :], x_.t[:, hf * 512:(hf + 1) * 512], ALU.add, [ps, x_], [o_])
            P.dma("sp", scr.x1.t[i * 128:(i + 1) * 128, :], o_.t[:, :], o_, scr.x1)
    P.barrier()
    P.emit_phase()
    P.stack.close()


def norm_scale(P, x_, xn_, s_, r_):
    P.op("act", lambda e: e.activation(out=xn_.t[:, :], in_=x_.t[:, :], func=AF.Square, accum_out=s_.t[:, :]),
         reads=[x_], writes=[xn_, s_])
    RSTD_(P, r_, r_.t[:, :], s_, s_.t[:, :], 1.0 / D, 1e-6, P.cneg, P.cneg.t[:, 0:1])
    P.op("act", lambda e: e.activation(out=xn_.t[:, :], in_=x_.t[:, :], func=AF.Copy, scale=r_.t[:, :]),
         reads=[x_, r_], writes=[xn_])


def phase5(P, c, T, scr):
    NB = T // 256
    NG = DFF // 128
    P.stack = contextlib.ExitStack()
    P.cneg = make_cneg(P)
    ident = P.sbuf("ident", [128, 128], BF16)
    P.dma("pool", ident.t[:, :], c.cst.t[:, 0, :], c.cst, ident)
    cw = P.sbuf("cw", [128, 3, NG], F32)
    for k in range(3):
        P.dma("sp", cw.t[:, k, :], c.conv_w.t[k, :].rearrange("(c p) -> p c", p=128), c.conv_w, cw, allow_slow_non_contiguous=True)
    cb = P.sbuf("cb", [128, NG], F32); load_cols(P, "sp", cb, c.conv_b, DFF, NG)
    Wup = scr.Wup
    Wdn = P.sbuf("Wdn", [128, NG, D], BF16)
    for gq in range(0, NG, 2):
        P.dma("pool", Wdn.t[:, gq:gq + 2, :], c.w_ffn_down.t[gq * 128:(gq + 2) * 128, :].rearrange("(c p) n -> p c n", p=128), c.w_ffn_down, Wdn)
    carry = P.sbuf("carry", [128, NG, 2], F32)
    P.op("pool", lambda e: e.memset(carry.t[:, :, :], 0.0), writes=[carry])

    X = Ring([P.sbuf("X%d" % i, [128, D], F32) for i in range(6)])
    xn = Ring([P.sbuf("xn%d" % i, [128, D], BF16) for i in range(2)])
    ss = Ring([P.sbuf("ss%d" % i, [128, 1], F32) for i in range(6)])
    rs = Ring([P.sbuf("rs%d" % i, [128, 1], F32) for i in range(6)])
    h2T = [P.sbuf("h2T%d" % i, [128, 8, 256], BF16) for i in range(2)]
    mTt = [P.sbuf("mT%d" % i, [128, NG, 256], BF16) for i in range(2)]
    mT = [[Buf("mT%d_%d" % (i, g), mTt[i].t) for g in range(NG)] for i in range(2)]
    Ab = Ring([P.sbuf("Ab%d" % i, [128, 258], F32) for i in range(3)])
    cv = Ring([P.sbuf("cv%d" % i, [128, 256], F32) for i in range(3)])
    gl = Ring([P.sbuf("gl%d" % i, [128, 256], F32) for i in range(3)])
    pst = Ring([P.psum("pst%d" % i, [128, 8, 128], BF16) for i in range(2)])
    pm = Ring([P.psum("pm%d" % i, [128, 512], F32) for i in range(4)])
    xs = {}
    xns = {}

    def s1l(b):
        xs[b] = []; xns[b] = []
        for j in range(2):
            i = 2 * b + j
            x_ = X.next(); n_ = xn.next()
            xs[b].append(x_); xns[b].append((n_, ss.next(), rs.next()))
            P.dma("sp", x_.t[:, :], scr.x1.t[i * 128:(i + 1) * 128, :], scr.x1, x_)

    def s1n(b, j, k):
        x_ = xs[b][j]; n_, s_, r_ = xns[b][j]
        if k == 0:
            P.op("act", lambda e: e.activation(out=n_.t[:, :], in_=x_.t[:, :], func=AF.Square, accum_out=s_.t[:, :]),
                 reads=[x_], writes=[n_, s_])
        elif k == 1:
            RSTD_(P, r_, r_.t[:, :], s_, s_.t[:, :], 1.0 / D, 1e-6, P.cneg, P.cneg.t[:, 0:1])
        else:
            P.op("act", lambda e: e.activation(out=n_.t[:, :], in_=x_.t[:, :], func=AF.Copy, scale=r_.t[:, :]),
                 reads=[x_, r_], writes=[n_])

    def s1a(b):
        s1l(b)
        for j in range(2):
            for k in range(3):
                s1n(b, j, k)

    def s1b(b):
        h = h2T[b % 2]
        for j in range(2):
            ps = pst.next(); n_ = xns[b][j][0]
            for k in range(8):
                TR_(P, ps.t[:, k, :], n_.t[:, k * 128:(k + 1) * 128], ident.t[:, :], [n_, ident], [ps], signal=(k == 7))
            CP_(P, "dve", h.t[:, :, j * 128:(j + 1) * 128], ps.t[:, :, :], [ps], [h])

    def up(b, g):
        h = h2T[b % 2]
        pA = pm.next(); pG = pm.next()
        for dc in range(8):
            MM_(P, pA.t[:, 0:256], Wup[dc].t[:, g * 128:(g + 1) * 128], h.t[:, dc, :], [Wup[dc], h], [pA], start=(dc == 0), stop=(dc == 7), signal=(dc == 7))
        for dc in range(8):
            MM_(P, pG.t[:, 0:256], Wup[dc].t[:, DFF + g * 128:DFF + (g + 1) * 128], h.t[:, dc, :], [Wup[dc], h], [pG], start=(dc == 0), stop=(dc == 7), signal=(dc == 7))
        A = Ab.next(); cv_ = cv.next(); gl_ = gl.next()
        CP_(P, "pool", A.t[:, 0:2], carry.t[:, g, :], [carry], [A])
        CP_(P, "act", A.t[:, 2:258], pA.t[:, 0:256], [pA], [A])
        ACT_(P, cv_.t[:, :], pA.t[:, 0:256], AF.Identity, [pA, cw, cb], [cv_], scale=cw.t[:, 2, g:g + 1], bias=cb.t[:, g:g + 1])
        STT_(P, cv_.t[:, :], A.t[:, 1:257], cw.t[:, 1, g:g + 1], cv_.t[:, :], ALU.mult, ALU.add, [A, cw, cv_], [cv_])
        STT_(P, cv_.t[:, :], A.t[:, 0:256], cw.t[:, 0, g:g + 1], cv_.t[:, :], ALU.mult, ALU.add, [A, cw, cv_], [cv_])
        CP_(P, "pool", carry.t[:, g, :], A.t[:, 256:258], [A], [carry])
        ACT_(P, gl_.t[:, :], cv_.t[:, :], AF.Gelu, [cv_], [gl_])
        TT_(P, "dve", mTt[b % 2].t[:, g, :], gl_.t[:, :], pG.t[:, 0:256], ALU.mult, [gl_, pG], [mT[b % 2][g]])

    pdn = [P.psum("pdn%d" % i, [128, 512], F32) for i in range(2)]

    def down_steps(b):
        for j in range(2):
            i = 2 * b + j
            x_ = xs[b][j]
            for g in range(NG):
                for hf in range(2):
                    MM_(P, pdn[hf].t[:, :], mTt[b % 2].t[:, g, j * 128:(j + 1) * 128], Wdn.t[:, g, hf * 512:(hf + 1) * 512], [mT[b % 2][g], Wdn], [pdn[hf]],
                        start=(g == 0), stop=(g == NG - 1), signal=(g == NG - 1))
                if g == NG - 1:
                    for hf in range(2):
                        TT_(P, "dve", x_.t[:, hf * 512:(hf + 1) * 512], pdn[hf].t[:, :], x_.t[:, hf * 512:(hf + 1) * 512], ALU.add, [pdn[hf], x_], [x_])
                    P.dma("sp", scr.x2.t[i * 128:(i + 1) * 128, :], x_.t[:, :], x_, scr.x2)
                yield

    def up_block(b, dn):
        for g in range(NG):
            if b + 1 < NB:
                if g == 0:
                    s1l(b + 1)
                if g in (1, 3, 5, 7, 9, 11):
                    q = (g - 1) // 2
                    s1n(b + 1, q // 3, q % 3)
                if g == 14:
                    s1b(b + 1)
            up(b, g)
            if dn is not None:
                for _ in range(2):
                    next(dn, None)

    s1a(0); s1b(0)
    up_block(0, None)
    for b in range(NB):
        dn = down_steps(b)
        if b + 1 < NB:
            up_block(b + 1, dn)
        for _ in dn:
            pass
    P.barrier()
    P.emit_phase()
    P.stack.close()
    scr.keep.close()


def phase6(P, c, T, scr):
    NT = T // 128
    P.stack = contextlib.ExitStack()
    P.cneg = make_cneg(P)
    ident = P.sbuf("ident", [128, 128], BF16)
    P.dma("pool", ident.t[:, :], c.cst.t[:, 0, :], c.cst, ident)
    g3 = P.sbuf("g3c", [128, 8], F32); load_cols(P, "sp", g3, c.ln3_g, D, 8)
    Wpg = P.sbuf("Wpg", [128, 8, D], BF16)
    Wple = P.sbuf("Wple", [128, 2, D], BF16)
    P.dma("pool", Wpg.t[:, :, :], c.w_pg.t.rearrange("(c p) n -> p c n", p=128), c.w_pg, Wpg)
    for dc in range(8):
        P.op("act", lambda e, dc=dc: e.activation(out=Wpg.t[:, dc, :], in_=Wpg.t[:, dc, :], func=AF.Copy, scale=g3.t[:, dc:dc + 1]),
             reads=[Wpg, g3], writes=[Wpg])
    P.dma("pool", Wple.t[:, :, :], c.w_ple.t.rearrange("(c p) n -> p c n", p=128), c.w_ple, Wple)
    lnf = P.sbuf("lnf", [128, D], F32)
    P.dma("sp", lnf.t[:, :], bc(c.lnf_g, D), c.lnf_g, lnf)
    bpg = P.sbuf("bpg", [128, D], BF16)
    ones = P.sbuf("ones", [128, 128], BF16)
    P.op("pool", lambda e: e.memset(bpg.t[:, :], 0.0), writes=[bpg])
    P.op("pool", lambda e: e.memset(ones.t[:, :], 0.0), writes=[ones])
    P.op("pool", lambda e: e.memset(ones.t[0:1, :], 1.0), writes=[ones])
    P.dma("pool", bpg.t[0:1, :], c.b_pg.t[0:D].rearrange("(o n) -> o n", o=1), c.b_pg, bpg)
    X = Ring([P.sbuf("X%d" % i, [128, D], F32) for i in range(8)])
    xn = Ring([P.sbuf("xn%d" % i, [128, D], BF16) for i in range(4)])
    ss = Ring([P.sbuf("ss%d" % i, [128, 1], F32) for i in range(10)])
    rs = Ring([P.sbuf("rs%d" % i, [128, 1], F32) for i in range(10)])
    h3T = Ring([P.sbuf("h3T%d" % i, [128, 8, 128], BF16) for i in range(3)])
    pt = Ring([P.sbuf("ptile%d" % i, [128, PLE], F32) for i in range(8)])
    pb_ = Ring([P.sbuf("ptb%d" % i, [128, PLE], BF16) for i in range(4)])
    pT = Ring([P.sbuf("pT%d" % i, [128, 2, 128], BF16) for i in range(3)])
    sgt = Ring([P.sbuf("sgt%d" % i, [128, 512], F32) for i in range(4)])
    tq = Ring([P.sbuf("tq%d" % i, [128, 512], F32) for i in range(4)])
    junk = P.sbuf("junk", [128, D], BF16)
    pst = Ring([P.psum("pst%d" % i, [128, 8, 128], BF16) for i in range(2)])
    pm = Ring([P.psum("pm%d" % i, [128, 512], F32) for i in range(6)])
    st = {}

    ld = {}

    def sl(i):
        x_ = X.next(); p_ = pt.next()
        P.dma("sp", x_.t[:, :], scr.x2.t[i * 128:(i + 1) * 128, :], scr.x2, x_)
        P.dma("sp", p_.t[:, :], c.p.t[i * 128:(i + 1) * 128, :], c.p, p_)
        ld[i] = (x_, p_)

    def sN(i):
        x_, p_ = ld.pop(i)
        n_ = xn.next(); q_ = pb_.next()
        norm_scale(P, x_, n_, ss.next(), rs.next())
        CP_(P, "act", q_.t[:, :], p_.t[:, :], [p_], [q_])
        st[i] = [x_, n_, q_]

    def sT(i):
        x_, n_, q_ = st[i]
        h_ = h3T.next(); t_ = pT.next()
        ps = pst.next()
        for k in range(8):
            TR_(P, ps.t[:, k, :], n_.t[:, k * 128:(k + 1) * 128], ident.t[:, :], [n_, ident], [ps], signal=(k == 7))
        CP_(P, "dve", h_.t[:, :, :], ps.t[:, :, :], [ps], [h_])
        pp = pst.next()
        for pc in range(2):
            TR_(P, pp.t[:, pc, :], q_.t[:, pc * 128:(pc + 1) * 128], ident.t[:, :], [q_, ident], [pp], signal=(pc == 1))
        CP_(P, "dve", t_.t[:, :, :], pp.t[:, 0:2, :], [pp], [t_])
        st[i] = [x_, h_, t_]

    def sM(i):
        x_, h_, t_ = st[i]
        for hf in range(2):
            cs = slice(hf * 512, (hf + 1) * 512)
            pg_ = pm.next()
            MM_(P, pg_.t[:, :], ones.t[:, :], bpg.t[:, cs], [ones, bpg], [pg_], start=True, stop=False, signal=False)
            for dc in range(8):
                MM_(P, pg_.t[:, :], h_.t[:, dc, :], Wpg.t[:, dc, cs], [h_, Wpg], [pg_], start=False, stop=(dc == 7), signal=(dc == 7))
            s_ = sgt.next(); q_ = tq.next()
            ACT_(P, s_.t[:, :], pg_.t[:, :], AF.Sigmoid, [pg_], [s_])
            pe_ = pm.next()
            for pc in range(2):
                MM_(P, pe_.t[:, :], t_.t[:, pc, :], Wple.t[:, pc, cs], [t_, Wple], [pe_], start=(pc == 0), stop=(pc == 1), signal=(pc == 1))
            TT_(P, "dve", q_.t[:, :], pe_.t[:, :], s_.t[:, :], ALU.mult, [pe_, s_], [q_])
            TT_(P, "pool", x_.t[:, cs], x_.t[:, cs], q_.t[:, :], ALU.add, [x_, q_], [x_])

    def sF(i):
        x_ = st.pop(i)[0]
        s_ = ss.next(); r_ = rs.next()
        P.op("act", lambda e: e.activation(out=junk.t[:, :], in_=x_.t[:, :], func=AF.Square, accum_out=s_.t[:, :]),
             reads=[x_], writes=[junk, s_])
        RSTD_(P, r_, r_.t[:, :], s_, s_.t[:, :], 1.0 / D, 1e-6, P.cneg, P.cneg.t[:, 0:1])
        STT_(P, x_.t[:, :], x_.t[:, :], r_.t[:, 0:1], lnf.t[:, :], ALU.mult, ALU.mult, [x_, r_, lnf], [x_])
        P.dma("sp", c.out.t[i * 128:(i + 1) * 128, :], x_.t[:, :], x_, c.out)

    for i in range(min(5, NT)):
        sl(i)
    for i in range(-2, NT + 1):
        if 5 <= i + 5 < NT:
            sl(i + 5)
        if 0 <= i + 2 < NT:
            sN(i + 2)
        if 0 <= i + 1 < NT:
            sT(i + 1)
        if 0 <= i < NT:
            sM(i)
        if 0 <= i - 1 < NT:
            sF(i - 1)
    P.wait_all("sp", [c.out])
    P.barrier()
    P.emit_phase()
    P.stack.close()


def build_program(T, dbg=False):
    nc = bass.Bass("TRN2", target_bir_lowering=False)
    P = Prog(nc)
    c = declare_io(P, T)
    scr = make_scratch(P, T, dbg=dbg)
    phase1(P, c, T, scr)
    phase23(P, c, T, scr)
    phase4a(P, c, T, scr)
    phase5(P, c, T, scr)
    phase6(P, c, T, scr)
    P.semstack.close()
    return nc


_W1 = ["ln1_g", "w_in", "mix_mu", "w0", "w2", "a0", "a2", "g2", "k_k", "k_a", "lnx_w", "lnx_b", "gate_b", "w_br_rwkv", "w_br_attn",
       "w_o", "ln2_g", "w_ffn_up", "conv_w", "conv_b", "w_ffn_down", "ln3_g", "w_ple", "w_pg", "b_pg"]


def core_inputs(inputs, b, T):
    d = {"x": inputs["x"][b, :T], "p": inputs["p"][0, b, :T]}
    for k in _W1:
        d[k] = inputs[k][0]
    d["r_k"] = np.reshape(inputs["r_k"][0], (RW,))
    d["lnf_g"] = inputs["lnf_g"]
    d["biasT"] = make_biasT(np.asarray(inputs["rel_bias"][0]))
    d["cst"] = make_cst()
    return {k: np.ascontiguousarray(np.asarray(v), dtype=np.float32) for k, v in d.items()}


def kernel(**inputs):
    B, T = inputs["x"].shape[0], inputs["x"].shape[1]
    nc = build_program(T)
    in_maps = [core_inputs(inputs, b, T) for b in range(B)]
    res = run_bass_kernel_spmd(nc, in_maps, core_ids=list(range(B)))
    return np.stack([np.asarray(r["out"], dtype=np.float32) for r in res.results], axis=0)


def interleave(gens):
    active = list(gens)
    while active:
        for item in list(active):
            g, k = item
            for _ in range(k):
                try:
                    next(g)
                except StopIteration:
                    active.remove(item)
                    break


def take(gen, n):
    for _ in range(n):
        try:
            next(gen)
        except StopIteration:
            return
        yield


def phase23(P, c, T, scr):
    NT = T // 128
    P.stack = contextlib.ExitStack()
    P.cneg = make_cneg(P)
    SB = lambda n, sh, dt=F32: P.sbuf(n, sh, dt)
    RG = lambda n, sh, dt=F32, k=2: Ring([P.sbuf("%s%d" % (n, i), sh, dt) for i in range(k)])
    ident = SB("ident", [128, 128], BF16)
    P.dma("pool", ident.t[:, :], c.cst.t[:, 0, :], c.cst, ident)
    I8 = SB("I8", [128, 128], BF16)
    P.dma("pool", I8.t[:, :], c.cst.t[:, 7, :], c.cst, I8)
    cst = SB("cstf", [128, 8, 128])
    P.dma("sp", cst.t[:, :, :], c.cst.t[:, :, :], c.cst, cst)
    wa2 = SB("wa2", [128, RW], BF16)
    P.dma("pool", wa2.t[0:64, :], c.w2.t[:, :], c.w2, wa2)
    P.dma("pool", wa2.t[64:128, :], c.a2.t[:, :], c.a2, wa2)
    g2b = SB("g2b", [128, RW], BF16)
    P.dma("pool", g2b.t[:, :], c.g2.t[:, :], c.g2, g2b)
    bt = {}
    for nm in ["w0", "a0", "k_k", "k_a", "r_k", "lnx_w", "lnx_b"]:
        bt[nm] = SB("b_" + nm, [128, RW])
        P.dma("sp", bt[nm].t[:, :], bc(getattr(c, nm), RW), getattr(c, nm), bt[nm])
    Linc, Lsl, Lsu, cind = cst.t[:, 1, :], cst.t[:, 2, :], cst.t[:, 3, :], cst.t[:, 4, 0:2]
    b3 = lambda ap: ap.unsqueeze(1).to_broadcast([128, 8, 64])
    MU, ML, MUI, I64 = b3(cst.t[:, 5, 0:64]), b3(cst.t[:, 5, 64:128]), b3(cst.t[:, 6, 0:64]), b3(cst.t[:, 6, 64:128])

    pg = Ring([P.psum("pg%d" % i, [128, 512], F32) for i in range(3)])
    pq = Ring([P.psum("pq%d" % i, [128, 512], F32) for i in range(2)])
    pa = Ring([P.psum("pa%d" % i, [128, 512], F32) for i in range(2)])
    pb = Ring([P.psum("pb%d" % i, [128, 512], F32) for i in range(1)])
    v3 = lambda buf: buf.t.rearrange("p (h v) -> p h v", h=8)

    def bfv(buf, inner):
        return buf.t.bitcast(BF16)[:, 0:8 * inner].rearrange("p (h t) -> p h t", h=8)

    rkvt = RG("rkvt", [128, 1536], BF16, 4)
    lo1 = RG("lo1", [128, 128], BF16, 4); lo2 = RG("lo2", [128, 128], BF16, 4)
    names_f = ["t_w", "sg", "t_a", "a_", "e_pos", "e_neg", "e_prev", "e_rel", "kkr", "sq", "kk", "bb", "t1", "k2", "t2"]
    F = {n: SB(n, [128, RW]) for n in names_f}
    Fp = {n: SB("post_" + n, [128, RW]) for n in ["yt", "sq", "yn", "bv"]}
    g_ = RG("g_", [128, RW])
    Bq = {n: SB(n, [128, RW], BF16) for n in ["At", "Bt", "Kt", "Rt"]}
    Bh = RG("Bh", [128, RW], BF16); Kh = RG("Kh", [128, RW], BF16)
    ATs = RG("ATs", [64, 8, 128], BF16); RTs = RG("RTs", [64, 8, 128], BF16)
    BTs = SB("BTs", [64, 8, 128], BF16); KTs = SB("KTs", [64, 8, 128], BF16)
    XA = RG("XA", [128, 8, 64], BF16); XTA = RG("XTA", [128, 8, 64], BF16)
    MakT = SB("MakT", [128, 8, 64], BF16)
    RBT = RG("RBT", [128, 8, 64], BF16); RKT = SB("RKT", [128, 8, 64], BF16)
    TTb = RG("TTb", [128, 8, 64], BF16)
    W1a = RG("W1a", [128, 8, 64]); Ya = RG("Ya", [128, 8, 64])
    W1 = SB("W1", [128, 8, 64], BF16); U = SB("U", [128, 8, 64], BF16)
    STb = RG("STb", [64, 8, 64], BF16); SF = RG("SF", [64, 8, 64], F32)
    tmpS = SB("tmpS", [64, 8, 64]); tmp2 = SB("tmp2", [64, 8, 64])
    pc = RG("pc", [64, 8, 2])
    sm = {n: SB(n, [128, 8]) for n in ["ssq", "nrm", "rn", "s1", "s2", "mean", "msq", "var", "rstd"]}
    bcf = RG("bcf", [128, 8])
    oa = SB("oa", [128, RW], BF16)
    oaT = RG("oaT", [128, 4, 128], BF16)
    h3 = lambda ap: ap.rearrange("p (h v) -> p h v", h=8)
    bl = lambda ap: ap.unsqueeze(2).to_broadcast([128, 8, 64])
    state = {}
    state["b"] = STb.next(); state["f"] = SF.next()
    P.op("pool", lambda e: e.memset(state["b"].t[:, :, :], 0.0), writes=[state["b"]])
    P.op("pool", lambda e: e.memset(state["f"].t[:, :, :], 0.0), writes=[state["f"]])
    tl = {}

    def loads(i):
        tok = slice(i * 128, (i + 1) * 128)
        rk = rkvt.next(); l1 = lo1.next(); l2 = lo2.next()
        P.dma("sp", rk.t[:, :], scr.rkv.t[tok, :], scr.rkv, rk)
        P.dma("sp", l1.t[:, :], scr.lora.t[0:128, tok], scr.lora, l1)
        P.dma("sp", l2.t[:, :], scr.lora.t[128:256, tok], scr.lora, l2)
        tl[i] = dict(rk=rk, l1=l1, l2=l2)

    def genA(i):
        t = tl[i]
        rk, l1, l2 = t["rk"], t["l1"], t["l2"]
        r_, k_ = rk.t[:, 0:512], rk.t[:, 512:1024]
        p_w = pg.next(); p_a = pg.next()
        MM_(P, p_w.t[:, :], l1.t[0:64, :], wa2.t[0:64, :], [l1, wa2], [p_w])
        MM_(P, p_a.t[:, :], l1.t[64:128, :], wa2.t[64:128, :], [l1, wa2], [p_a])
        f = {n: F[n].t[:, :] for n in names_f}
        TT_(P, "dve", f["t_w"], p_w.t[:, :], bt["w0"].t[:, :], ALU.add, [p_w, bt["w0"]], [F["t_w"]])
        ACT_(P, f["sg"], f["t_w"], AF.Tanh, [F["t_w"]], [F["sg"]], scale=0.5)
        TS_(P, "pool", f["sg"], f["sg"], 0.5, 0.5, ALU.mult, ALU.add, [F["sg"]], [F["sg"]])
        TT_(P, "dve", f["t_a"], p_a.t[:, :], bt["a0"].t[:, :], ALU.add, [p_a, bt["a0"]], [F["t_a"]])
        ACT_(P, f["a_"], f["t_a"], AF.Tanh, [F["t_a"]], [F["a_"]], scale=0.5)
        TS_(P, "pool", f["a_"], f["a_"], 0.5, 0.5, ALU.mult, ALU.add, [F["a_"]], [F["a_"]])
        yield
        p_g = pg.next()
        MM_(P, p_g.t[:, :], l2.t[:, :], g2b.t[:, :], [l2, g2b], [p_g])
        gq = g_.next(); t["gq"] = gq
        CP_(P, "act", gq.t[:, :], p_g.t[:, :], [p_g], [gq])
        yield
        p1 = pg.next()
        MM_(P, p1.t[:, :], Linc, f["sg"], [cst, F["sg"]], [p1])
        ACT_(P, f["e_pos"], p1.t[:, :], AF.Exp, [p1], [F["e_pos"]], scale=CDEC)
        ACT_(P, f["e_neg"], p1.t[:, :], AF.Exp, [p1], [F["e_neg"]], scale=-CDEC)
        yield
        p2 = pg.next()
        MM_(P, p2.t[:, :], Lsl, f["sg"], [cst, F["sg"]], [p2])
        ACT_(P, f["e_prev"], p2.t[:, :], AF.Exp, [p2], [F["e_prev"]], scale=CDEC)
        yield
        p3 = pg.next()
        MM_(P, p3.t[:, :], Lsu, f["sg"], [cst, F["sg"]], [p3])
        ACT_(P, f["e_rel"], p3.t[:, :], AF.Exp, [p3], [F["e_rel"]], scale=CDEC)
        yield
        p4 = pg.next()
        for h in range(8):
            MM_(P, p4.t[0:64, 2 * h:2 * h + 2], F["sg"].t[:, h * 64:(h + 1) * 64], cind, [F["sg"], cst], [p4], signal=(h == 7))
        pcq = pc.next(); t["pcq"] = pcq
        ACT_(P, pcq.t[:, :, :], p4.t[0:64, 0:16].rearrange("p (h c) -> p h c", h=8), AF.Exp, [p4], [pcq], scale=CDEC)
        yield
        TT_(P, "dve", f["kkr"], k_, bt["k_k"].t[:, :], ALU.mult, [rk, bt["k_k"]], [F["kkr"]])
        TT_(P, "pool", f["sq"], f["kkr"], f["kkr"], ALU.mult, [F["kkr"]], [F["sq"]])
        RED_(P, sm["ssq"].t[:, :], h3(f["sq"]), [F["sq"]], [sm["ssq"]])
        yield
        TS_(P, "dve", sm["rn"].t[:, :], sm["ssq"].t[:, :], 1e-24, None, ALU.max, None, [sm["ssq"]], [sm["rn"]])
        P.op("pool", lambda e: e.tensor_tensor(out=sm["rn"].t[:, :], in0=sm["rn"].t[:, :], in1=P.cneg.t[:, :], op=ALU.pow),
             reads=[sm["rn"], P.cneg], writes=[sm["rn"]])
        TT_(P, "dve", h3(f["kk"]), h3(f["kkr"]), bl(sm["rn"].t[:, :]), ALU.mult, [F["kkr"], sm["rn"]], [F["kk"]])
        yield
        TT_(P, "pool", f["bb"], f["kk"], f["a_"], ALU.mult, [F["kk"], F["a_"]], [F["bb"]])
        STT_(P, f["t1"], f["a_"], -1.0, bt["k_a"].t[:, :], ALU.add, ALU.mult, [F["a_"], bt["k_a"]], [F["t1"]])
        STT_(P, f["k2"], f["t1"], 1.0, k_, ALU.add, ALU.mult, [F["t1"], rk], [F["k2"]])
        yield
        STT_(P, Bq["At"].t[:, :], f["kk"], -1.0, f["e_prev"], ALU.mult, ALU.mult, [F["kk"], F["e_prev"]], [Bq["At"]])
        TT_(P, "dve", Bq["Bt"].t[:, :], f["bb"], f["e_neg"], ALU.mult, [F["bb"], F["e_neg"]], [Bq["Bt"]])
        yield
        TT_(P, "dve", Bq["Kt"].t[:, :], f["k2"], f["e_neg"], ALU.mult, [F["k2"], F["e_neg"]], [Bq["Kt"]])
        TT_(P, "pool", Bq["Rt"].t[:, :], r_, f["e_pos"], ALU.mult, [rk, F["e_pos"]], [Bq["Rt"]])
        yield
        bh = Bh.next(); kh = Kh.next(); t["bh"] = bh; t["kh"] = kh
        TT_(P, "pool", bh.t[:, :], f["bb"], f["e_rel"], ALU.mult, [F["bb"], F["e_rel"]], [bh])
        TT_(P, "pool", kh.t[:, :], f["k2"], f["e_rel"], ALU.mult, [F["k2"], F["e_rel"]], [kh])
        yield
        TT_(P, "pool", f["t2"], r_, bt["r_k"].t[:, :], ALU.mult, [rk, bt["r_k"]], [F["t2"]])
        TT_(P, "pool", f["t2"], f["t2"], f["k2"], ALU.mult, [F["t2"], F["k2"]], [F["t2"]])
        bq = bcf.next(); t["bq"] = bq
        RED_(P, bq.t[:, :], h3(f["t2"]), [F["t2"]], [bq])
        yield
        ats = ATs.next(); rts = RTs.next(); t["ats"] = ats; t["rts"] = rts
        for (src, dst, ev) in [("At", ats, "act"), ("Bt", BTs, "dve"), ("Kt", KTs, "act"), ("Rt", rts, "dve")]:
            ps = pg.next()
            pv = bfv(ps, 128)
            for h in range(8):
                TR_(P, pv[0:64, h, :], Bq[src].t[:, h * 64:(h + 1) * 64], ident.t[:, :], [Bq[src], ident], [ps], signal=(h == 7))
            CP_(P, ev, dst.t[:, :, :], pv[0:64, :, :], [ps], [dst])
            yield

        def prod(lhs, rhs, mask, dst, eng="dve"):
            ps = pg.next()
            for cc in range(2):
                cols = slice(cc * 64, (cc + 1) * 64)
                for h in range(8):
                    MM_(P, v3(ps)[cc * 64:(cc + 1) * 64, h, :], lhs.t[:, h, cols], rhs.t[:, h, cols], [lhs, rhs], [ps],
                        signal=(cc == 1 and h == 7))
            TT_(P, eng, dst.t[:, :, :], v3(ps), mask, ALU.mult, [ps, cst], [dst])
        xt_ = XTA.next(); x_ = XA.next()
        rbt = RBT.next(); t["rbt"] = rbt
        prod(BTs, ats, MU, xt_); yield
        prod(ats, BTs, ML, x_); yield
        prod(KTs, ats, MU, MakT); yield
        prod(BTs, rts, MUI, rbt); yield
        prod(KTs, rts, MUI, RKT); yield
        tt = TTb.next(); t["tt"] = tt
        TT_(P, "dve", tt.t[:, :, :], xt_.t[:, :, :], I64, ALU.add, [xt_, cst], [tt])
        for lev in range(1, 6):
            xn_ = XA.next()
            xtn_ = XTA.next() if lev < 5 else None
            for cc in range(2):
                hs = slice(cc * 64, (cc + 1) * 64)
                px = pg.next()
                for h in range(8):
                    MM_(P, v3(px)[hs, h, :], xt_.t[hs, h, :], x_.t[hs, h, :], [xt_, x_], [px], signal=(h == 7))
                CP_(P, "act", xn_.t[hs, :, :], v3(px)[hs, :, :], [px], [xn_])
                yield
                if lev < 5:
                    pxt = pg.next()
                    for h in range(8):
                        MM_(P, v3(pxt)[hs, h, :], x_.t[hs, h, :], xt_.t[hs, h, :], [xt_, x_], [pxt], signal=(h == 7))
                    CP_(P, "act", xtn_.t[hs, :, :], v3(pxt)[hs, :, :], [pxt], [xtn_])
                    yield
            x_ = xn_
            if lev < 5:
                xt_ = xtn_
            for cc in range(2):
                hs = slice(cc * 64, (cc + 1) * 64)
                pt = pg.next()
                for h in range(8):
                    MM_(P, v3(pt)[hs, h, :], x_.t[hs, h, :], tt.t[hs, h, :], [x_, tt], [pt], signal=(h == 7))
                TT_(P, "dve", tt.t[hs, :, :], v3(pt)[hs, :, :], tt.t[hs, :, :], ALU.add, [pt, tt], [tt])
                yield
        w1a = W1a.next(); ya = Ya.next(); t["w1a"] = w1a; t["ya"] = ya
        for cc in range(2):
            hs = slice(cc * 64, (cc + 1) * 64)
            ps = pg.next()
            for h in range(8):
                MM_(P, v3(ps)[hs, h, :], MakT.t[hs, h, :], rk.t[hs, 1024 + h * 64:1024 + (h + 1) * 64], [MakT, rk], [ps], signal=(h == 7))
            CP_(P, "act", w1a.t[hs, :, :], v3(ps)[hs, :, :], [ps], [w1a])
            yield
            ps = pg.next()
            for h in range(8):
                MM_(P, v3(ps)[hs, h, :], RKT.t[hs, h, :], rk.t[hs, 1024 + h * 64:1024 + (h + 1) * 64], [RKT, rk], [ps], signal=(h == 7))
            CP_(P, "act", ya.t[hs, :, :], v3(ps)[hs, :, :], [ps], [ya])
            yield

    def genB(i):
        t = tl[i]
        rk, ats, rts, bh, kh, rbt, tt, w1a, ya, pcq, gq, bq = (t[k] for k in ["rk", "ats", "rts", "bh", "kh", "rbt", "tt", "w1a", "ya", "pcq", "gq", "bq"])
        tok = slice(i * 128, (i + 1) * 128)
        v_ = rk.t[:, 1024:1536]
        fp = {n: Fp[n].t[:, :] for n in Fp}
        for cc in range(2):
            hs = slice(cc * 64, (cc + 1) * 64)
            cols = slice(cc * 64, (cc + 1) * 64)
            st_b, st_f = state["b"], state["f"]
            pk = pq.next()
            for h in range(8):
                MM_(P, v3(pk)[0:64, h, :], kh.t[hs, h * 64:(h + 1) * 64], rk.t[hs, 1024 + h * 64:1024 + (h + 1) * 64], [kh, rk], [pk], signal=(h == 7))
            TT_(P, "dve", tmpS.t[:, :, :], st_f.t[:, :, :], pcq.t[:, :, cc:cc + 1].to_broadcast([64, 8, 64]), ALU.mult, [st_f, pcq], [tmpS])
            TT_(P, "dve", tmp2.t[:, :, :], v3(pk)[0:64, :, :], tmpS.t[:, :, :], ALU.add, [pk, tmpS], [tmp2])
            yield
            pw = pq.next()
            for h in range(8):
                MM_(P, v3(pw)[hs, h, :], ats.t[:, h, cols], st_b.t[:, h, :], [ats, st_b], [pw], signal=(h == 7))
            TT_(P, "dve", W1.t[hs, :, :], v3(pw)[hs, :, :], w1a.t[hs, :, :], ALU.add, [pw, w1a], [W1])
            yield
            prs = pq.next()
            for h in range(8):
                MM_(P, v3(prs)[hs, h, :], rts.t[:, h, cols], st_b.t[:, h, :], [rts, st_b], [prs], signal=(h == 7))
            TT_(P, "dve", h3(fp["yt"])[hs, :, :], v3(prs)[hs, :, :], ya.t[hs, :, :], ALU.add, [prs, ya], [Fp["yt"]])
            yield
            pu = pq.next()
            for h in range(8):
                MM_(P, v3(pu)[hs, h, :], tt.t[hs, h, :], W1.t[hs, h, :], [tt, W1], [pu], signal=(h == 7))
            CP_(P, "act", U.t[hs, :, :], v3(pu)[hs, :, :], [pu], [U])
            yield
            psn = pq.next()
            for h in range(8):
                MM_(P, v3(psn)[0:64, h, :], bh.t[hs, h * 64:(h + 1) * 64], U.t[hs, h, :], [bh, U], [psn], signal=(h == 7))
            nb = STb.next(); nf = SF.next()
            TT_(P, "dve", nb.t[:, :, :], v3(psn)[0:64, :, :], tmp2.t[:, :, :], ALU.add, [psn, tmp2], [nb])
            TT_(P, "dve", nf.t[:, :, :], v3(psn)[0:64, :, :], tmp2.t[:, :, :], ALU.add, [psn, tmp2], [nf])
            state["b"], state["f"] = nb, nf
            yield
            py = pq.next()
            for h in range(8):
                MM_(P, v3(py)[hs, h, :], rbt.t[hs, h, :], U.t[hs, h, :], [rbt, U], [py], signal=(h == 7))
            TT_(P, "dve", h3(fp["yt"])[hs, :, :], v3(py)[hs, :, :], h3(fp["yt"])[hs, :, :], ALU.add, [py, Fp["yt"]], [Fp["yt"]])
            yield
        yt3 = h3(fp["yt"]); yn3 = h3(fp["yn"])
        RED_(P, sm["s1"].t[:, :], yt3, [Fp["yt"]], [sm["s1"]])
        TT_(P, "pool", fp["sq"], fp["yt"], fp["yt"], ALU.mult, [Fp["yt"]], [Fp["sq"]])
        RED_(P, sm["s2"].t[:, :], h3(fp["sq"]), [Fp["sq"]], [sm["s2"]])
        yield
        TS_(P, "dve", sm["mean"].t[:, :], sm["s1"].t[:, :], 1.0 / 64, None, ALU.mult, None, [sm["s1"]], [sm["mean"]])
        TT_(P, "dve", sm["msq"].t[:, :], sm["mean"].t[:, :], sm["mean"].t[:, :], ALU.mult, [sm["mean"]], [sm["msq"]])
        STT_(P, sm["var"].t[:, :], sm["s2"].t[:, :], 1.0 / 64, sm["msq"].t[:, :], ALU.mult, ALU.subtract, [sm["s2"], sm["msq"]], [sm["var"]])
        RSTD_(P, sm["rstd"], sm["rstd"].t[:, :], sm["var"], sm["var"].t[:, :], 1.0, 64e-5, P.cneg, P.cneg.t[:, :])
        yield
        TT_(P, "dve", yn3, yt3, bl(sm["mean"].t[:, :]), ALU.subtract, [Fp["yt"], sm["mean"]], [Fp["yn"]])
        TT_(P, "dve", yn3, yn3, bl(sm["rstd"].t[:, :]), ALU.mult, [Fp["yn"], sm["rstd"]], [Fp["yn"]])
        yield
        TT_(P, "pool", fp["yn"], fp["yn"], bt["lnx_w"].t[:, :], ALU.mult, [Fp["yn"], bt["lnx_w"]], [Fp["yn"]])
        TT_(P, "pool", fp["yn"], fp["yn"], bt["lnx_b"].t[:, :], ALU.add, [Fp["yn"], bt["lnx_b"]], [Fp["yn"]])
        TT_(P, "dve", h3(fp["bv"]), h3(v_), bl(bq.t[:, :]), ALU.mult, [rk, bq], [Fp["bv"]])
        yield
        TT_(P, "pool", fp["yn"], fp["yn"], fp["bv"], ALU.add, [Fp["yn"], Fp["bv"]], [Fp["yn"]])
        TT_(P, "dve", oa.t[:, :], fp["yn"], gq.t[:, :], ALU.mult, [Fp["yn"], gq], [oa])
        yield
        ps = pq.next()
        pv = ps.t.bitcast(BF16)[:, 0:512].rearrange("p (c t) -> p c t", c=4)
        for fc in range(4):
            TR_(P, pv[:, fc, :], oa.t[:, fc * 128:(fc + 1) * 128], ident.t[:, :], [oa, ident], [ps], signal=(fc == 3))
        ot_ = oaT.next()
        CP_(P, "act", ot_.t[:, :, :], pv, [ps], [ot_])
        P.dma("pool", scr.oaT.t[:, tok].rearrange("(c p) t -> p c t", p=128), ot_.t[:, :, :], ot_, scr.oaT)
        yield

    kTh = SB("kTh", [128, T], BF16); qTh = SB("qTh", [128, T], BF16)
    Vh = RG("Vh", [128, NT, 65], BF16)
    bT = RG("bT", [128, 640], BF16)
    for b in (kTh, qTh):
        P.op("pool", lambda e, b=b: e.memset(b.t[64:128, :], 0.0), writes=[b])
    for b in Vh.items:
        P.op("pool", lambda e, b=b: e.memset(b.t[:, :, 64:65], 1.0), writes=[b])
    PT = [SB("PT%d" % i, [128, 640], BF16) for i in range(6)]
    ob = SB("ob", [128, NT, 512], BF16)
    rc = RG("rc", [128, 1])
    obT = RG("obT", [128, 4, 128], BF16)

    def genC():
        for h in range(NH):
            kt = kTh; qt = qTh; vh = Vh.next(); bias = bT.next()
            P.dma("sp", kt.t[0:64, :], scr.kT.t[h * 64:(h + 1) * 64, :], scr.kT, kt)
            P.dma("sp", qt.t[0:64, :], scr.qT.t[h * 64:(h + 1) * 64, :], scr.qT, qt)
            P.dma("sp", vh.t[:, :, 0:64], scr.vA.t[:, h * 64:(h + 1) * 64].rearrange("(m p) d -> p m d", p=128), scr.vA, vh)
            P.dma("pool", bias.t[:, :], c.biasT.t[h, :, :], c.biasT, bias)
            for m in range(NT):
                W = min(640, T - 128 * m)
                pt = PT[m % 6]
                for (c0, c1) in [(0, min(W, 512)), (512, W)]:
                    if c1 <= c0:
                        continue
                    ps = pa.next()
                    n = c1 - c0
                    MM_(P, ps.t[:, 0:n], I8.t[:, :], bias.t[:, c0:c1], [I8, bias], [ps], start=True, stop=False, signal=False)
                    MM_(P, ps.t[:, 0:n], kt.t[:, m * 128:(m + 1) * 128], qt.t[:, m * 128 + c0:m * 128 + c1], [kt, qt], [ps], start=False, stop=True)
                    ACT_(P, pt.t[:, c0:c1], ps.t[:, 0:n], AF.Exp, [ps], [pt], scale=0.125)
                yield
                pv = pb.next()
                cq = 2 * m
                ms = list(range(max(0, (cq - 8) // 2), cq // 2 + 1))
                for mi, mp in enumerate(ms):
                    off = (cq - 2 * mp) * 64
                    MM_(P, pv.t[:, 0:65], PT[mp % 6].t[:, off:off + 128], vh.t[:, mp, :], [PT[mp % 6], vh], [pv],
                        start=(mi == 0), stop=(mi == len(ms) - 1), signal=(mi == len(ms) - 1))
                r_ = rc.next()
                P.op("dve", lambda e, r_=r_, pv=pv: e.reciprocal(out=r_.t[:, :], in_=pv.t[:, 64:65]), reads=[pv], writes=[r_])
                ACT_(P, ob.t[:, m, h * 64:(h + 1) * 64], pv.t[:, 0:64], AF.Copy, [pv, r_], [ob], scale=r_.t[:, :])
                yield
        for m in range(NT):
            ps = pa.next()
            pvw = ps.t.bitcast(BF16)[:, 0:512].rearrange("p (c t) -> p c t", c=4)
            for fc in range(4):
                TR_(P, pvw[:, fc, :], ob.t[:, m, fc * 128:(fc + 1) * 128], ident.t[:, :], [ob, ident], [ps], signal=(fc == 3))
            o_ = obT.next()
            CP_(P, "dve", o_.t[:, :, :], pvw, [ps], [o_])
            P.dma("pool", scr.obT.t[:, m * 128:(m + 1) * 128].rearrange("(c p) t -> p c t", p=128), o_.t[:, :, :], o_, scr.obT)
            yield

    gC = genC()
    nC = (NH * NT * 2 + NT + NT - 1) // NT + 1
    loads(0)
    if NT > 1:
        loads(1)
    for _ in genA(0):
        pass
    for i in range(NT):
        if i + 2 < NT:
            loads(i + 2)
        gens = [(genB(i), 1)]
        if i + 1 < NT:
            gens.append((genA(i + 1), 3))
        gens.append((take(gC, nC), 1))
        interleave(gens)
    for _ in gC:
        pass
    P.barrier()
    P.emit_phase()
    P.stack.close()
```

```python
import contextlib
import os
CUT = int(os.environ.get("P2CUT", "99"))
import numpy as np
import concourse.bass as bass
import concourse.mybir as mybir
from concourse.bass_utils import run_bass_kernel_spmd

F32 = mybir.dt.float32
BF16 = mybir.dt.bfloat16
AF = mybir.ActivationFunctionType
ALU = mybir.AluOpType
AX = mybir.AxisListType

COMPUTE = ("pe", "act", "dve", "pool")
ENGS = ("pe", "act", "dve", "pool", "sp")


class Buf:
    def __init__(self, name, t=None, is_dram=False):
        self.name = name
        self.t = t
        self.is_dram = is_dram
        self.w = {}
        self.r = {}
        self.dma_key = None
        self.dma_cnt = 0

    def __getitem__(self, k):
        return self.t[k]


class Prog:
    def __init__(self, nc):
        self.nc = nc
        self.stack = contextlib.ExitStack()
        self.semstack = contextlib.ExitStack()
        self.dma_keys = {}
        self.ops = {e: [] for e in ENGS}
        self.cnt = {e: 0 for e in COMPUTE}
        self.pending = {e: False for e in COMPUTE}
        self.waited = {e: {} for e in ENGS}
        self.sems = {}
        self.nbuf = 0

    def sem(self, key):
        if key not in self.sems:
            self.sems[key] = self.semstack.enter_context(self.nc.semaphore("s_" + key))
        return self.sems[key]

    def sbuf(self, name, shape, dtype):
        self.nbuf += 1
        name = "%s_%d" % (name, self.nbuf)
        t = self.stack.enter_context(self.nc.sbuf_tensor(name, list(shape), dtype))
        return Buf(name, t)

    def psum(self, name, shape, dtype):
        self.nbuf += 1
        name = "%s_%d" % (name, self.nbuf)
        t = self.stack.enter_context(self.nc.psum_tensor(name, list(shape), dtype))
        return Buf(name, t)

    def dram(self, name, shape, dtype, kind="Internal"):
        t = self.nc.dram_tensor(name, list(shape), dtype, kind=kind)
        return Buf(name, t, is_dram=True)

    def _collect(self, eng, reads, writes):
        need = {}

        def add(k, v, same_ok):
            if k == eng and not same_ok and eng == "pe":
                return
            if need.get(k, 0) < v:
                need[k] = v

        for b in reads:
            for k, v in b.w.items():
                add(k, v, True)
        for b in writes:
            for k, v in b.w.items():
                add(k, v, False)
            for k, v in b.r.items():
                add(k, v, False)
        out = []
        wd = self.waited[eng]
        for k, v in need.items():
            if wd.get(k, 0) < v:
                wd[k] = v
                out.append((k, v))
        return out

    def _record(self, key, val, reads, writes):
        for b in reads:
            if b.r.get(key, 0) < val:
                b.r[key] = val
        for b in writes:
            b.w = {key: val}
            b.r = {}

    def op(self, eng, fn, reads=(), writes=(), signal=True):
        waits = self._collect(eng, reads, writes)
        if signal:
            self.cnt[eng] += 1
            val = self.cnt[eng]
            self.pending[eng] = False
        else:
            val = self.cnt[eng] + 1
            self.pending[eng] = True
        self.ops[eng].append((waits, fn, (eng, 1) if signal else None))
        self._record(eng, val, reads, writes)

    def dma(self, q, out_ap, in_ap, src, dst, sem_on=None, **kw):
        waits = self._collect(q, [src], [dst])
        sb = sem_on if sem_on is not None else (src if dst.is_dram else dst)
        if sb.dma_key is None:
            pool = self.__dict__.setdefault("sem_pool", [])
            sb.dma_sw = (q == "pool")
            if pool and not sb.dma_sw:
                sb.dma_key, sb.dma_cnt = pool.pop()
            else:
                sb.dma_key = "d%d" % len(self.__dict__.setdefault("all_dma_keys", []))
                self.all_dma_keys.append(sb.dma_key)
            if not sb.dma_sw:
                self.__dict__.setdefault("phase_owners", []).append(sb)
        assert sb.dma_sw == (q == "pool"), "buffer %s mixes software and hardware DGE" % sb.name
        key = sb.dma_key
        sb.dma_cnt += 16
        val = sb.dma_cnt
        self.dma_keys[key] = val
        self.ops[q].append((waits, lambda e: e.dma_start(out=out_ap, in_=in_ap, **kw), (key, 16)))
        self._record(key, val, [src], [dst])

    def wait_all(self, eng, bufs):
        waits = self._collect(eng, [], list(bufs))
        if waits:
            self.ops[eng].append((waits, None, None))

    def barrier(self):
        ev = {e: self.cnt[e] for e in COMPUTE if self.cnt[e] > 0}
        ev.update(self.dma_keys)
        for e in ENGS:
            waits = []
            for k, v in ev.items():
                if k == e:
                    continue
                if self.waited[e].get(k, 0) < v:
                    self.waited[e][k] = v
                    waits.append((k, v))
            if waits:
                self.ops[e].append((waits, None, None))

    def emit_phase(self):
        self.phase_idx = getattr(self, "phase_idx", 0) + 1
        with self.nc.named_scope("phase%d" % self.phase_idx):
            self.emit()
        self.ops = {e: [] for e in ENGS}
        for sb in self.__dict__.get("phase_owners", []):
            self.__dict__.setdefault("sem_pool", []).append((sb.dma_key, sb.dma_cnt))
        self.phase_owners = []

    def emit(self):
        nc = self.nc
        for e in COMPUTE:
            assert not self.pending[e], "engine %s ends with an unsignaled op" % e
        handles = {"pe": "tensor", "act": "scalar", "dve": "vector", "pool": "gpsimd", "sp": "sync"}
        for k in list(self.waited["pe"].keys()) + list(COMPUTE):
            self.sem(k)
        for e in ENGS:
            for (waits, fn, inc) in self.ops[e]:
                for k, v in waits:
                    self.sem(k)
                if inc is not None:
                    self.sem(inc[0])
        prog = self

        def replay(name):
            def run(eng):
                for (waits, fn, inc) in prog.ops[name]:
                    for k, v in waits:
                        eng.wait_ge(prog.sems[k], v)
                    if fn is None:
                        continue
                    ins = fn(eng)
                    if inc is not None:
                        ins.then_inc(prog.sems[inc[0]], inc[1])
            return run

        with nc.Block() as block:
            block.tensor(replay("pe"))
            block.scalar(replay("act"))
            block.vector(replay("dve"))
            block.gpsimd(replay("pool"))
            block.sync(replay("sp"))

    def close(self):
        self.stack.close()


D = 1024
NH = 8
HD = 64
RW = 512
RWKV_COLS = 1792
IN_COLS = 5376
DFF = 2816
PLE = 256
CDEC = -0.6065306597126334
NEG = -30000.0


def bc(vec_buf, n):
    return vec_buf.t[0:n].partition_broadcast(128)


class Ctx:
    pass


def declare_io(P, T):
    c = Ctx()
    f = lambda n, s: P.dram(n, s, F32, kind="ExternalInput")
    c.x = f("x", [T, D]); c.p = f("p", [T, PLE])
    c.ln1_g = f("ln1_g", [D]); c.w_in = f("w_in", [D, IN_COLS]); c.mix_mu = f("mix_mu", [RWKV_COLS])
    c.w0 = f("w0", [RW]); c.w2 = f("w2", [64, RW]); c.a0 = f("a0", [RW]); c.a2 = f("a2", [64, RW])
    c.g2 = f("g2", [128, RW]); c.k_k = f("k_k", [RW]); c.k_a = f("k_a", [RW]); c.r_k = f("r_k", [RW])
    c.lnx_w = f("lnx_w", [RW]); c.lnx_b = f("lnx_b", [RW]); c.biasT = f("biasT", [NH, 128, 640])
    c.gate_b = f("gate_b", [2048]); c.w_br_rwkv = f("w_br_rwkv", [RW, D]); c.w_br_attn = f("w_br_attn", [RW, D])
    c.w_o = f("w_o", [D, D]); c.ln2_g = f("ln2_g", [D]); c.w_ffn_up = f("w_ffn_up", [D, 2 * DFF])
    c.conv_w = f("conv_w", [3, DFF]); c.conv_b = f("conv_b", [DFF]); c.w_ffn_down = f("w_ffn_down", [DFF, D])
    c.ln3_g = f("ln3_g", [D]); c.w_ple = f("w_ple", [PLE, D]); c.w_pg = f("w_pg", [D, D]); c.b_pg = f("b_pg", [D])
    c.lnf_g = f("lnf_g", [D])
    c.cst = f("cst", [128, 8, 128])
    c.out = P.dram("out", [T, D], F32, kind="ExternalOutput")
    return c


def rmsnorm_T(P, xt, xn, junk, ss, rs, pst, hT_dst_fn, ident, tag=""):
    P.op("act", lambda e: e.activation(out=junk.t[:, :], in_=xt.t[:, :], func=AF.Square, accum_out=ss.t[:, :]),
         reads=[xt], writes=[junk, ss])
    P.op("act", lambda e: e.activation(out=rs.t[:, :], in_=ss.t[:, :], func=AF.Sqrt, scale=1.0 / D, bias=1e-6),
         reads=[ss], writes=[rs])
    P.op("dve", lambda e: e.reciprocal(out=rs.t[:, :], in_=rs.t[:, :]), reads=[rs], writes=[rs])
    P.op("act", lambda e: e.activation(out=xn.t[:, :], in_=xt.t[:, :], func=AF.Copy, scale=rs.t[:, :]),
         reads=[xt, rs], writes=[xn])
    for c in range(8):
        P.op("pe", lambda e, c=c: e.transpose(out=pst.t[:, c, :], in_=xn.t[:, c * 128:(c + 1) * 128], identity=ident.t[:, :]),
             reads=[xn, ident], writes=[pst], signal=(c == 7))
    hT_dst_fn(pst)


class Ring:
    def __init__(self, items):
        self.items = list(items)
        self.i = 0

    def next(self):
        b = self.items[self.i % len(self.items)]
        self.i += 1
        return b


def load_cols(P, q, dst, vec, n, ncol):
    P.dma(q, dst.t[:, 0:ncol], vec.t.rearrange("(c p) -> p c", p=128), vec, dst, allow_slow_non_contiguous=True)


def phase1(P, c, T, scr):
    NB = T // 512
    P.stack = contextlib.ExitStack()
    P.cneg = make_cneg(P)
    ident = P.sbuf("ident", [128, 128], BF16)
    P.dma("pool", ident.t[:, :], c.cst.t[:, 0, :], c.cst, ident)
    g1 = P.sbuf("g1", [128, 8], F32)
    load_cols(P, "sp", g1, c.ln1_g, D, 8)
    gb = P.sbuf("gb", [128, 16], F32)
    load_cols(P, "sp", gb, c.gate_b, 2048, 16)
    mub = P.sbuf("mub", [128, RWKV_COLS], F32)
    omm = P.sbuf("omm", [128, RWKV_COLS], F32)
    P.dma("sp", mub.t[:, :], bc(c.mix_mu, RWKV_COLS), c.mix_mu, mub)
    P.op("dve", lambda e: e.tensor_scalar(out=omm.t[:, :], in0=mub.t[:, :], scalar1=-1.0, scalar2=1.0, op0=ALU.mult, op1=ALU.add),
         reads=[mub], writes=[omm])
    Wr1 = P.sbuf("Wr1", [128, 8, RWKV_COLS], BF16)
    Wr2 = P.sbuf("Wr2", [128, 8, RWKV_COLS], BF16)
    Wo = P.sbuf("Wo", [128, 8, 3584], BF16)
    stg = Ring([P.sbuf("stg%d" % i, [128, RWKV_COLS], F32) for i in range(3)])
    xt = Ring([P.sbuf("xt%d" % i, [128, D], F32) for i in range(2)] + stg.items)
    xn = Ring([P.sbuf("xn%d" % i, [128, D], BF16) for i in range(2)])
    junk = P.sbuf("junk", [128, D], BF16)
    ss = Ring([P.sbuf("ss%d" % i, [128, 1], F32) for i in range(2)])
    rs = Ring([P.sbuf("rs%d" % i, [128, 1], F32) for i in range(2)])
    hT = [P.sbuf("hT%d" % i, [128, 8, 513], BF16) for i in range(2)]
    pst = Ring([P.psum("pst%d" % i, [128, 8, 128], BF16) for i in range(2)])
    pm = Ring([P.psum("pm%d" % i, [128, 512], F32) for i in range(5)])
    of = Ring([P.sbuf("of%d" % i, [128, 512], BF16) for i in range(4)])
    ot = Ring([P.sbuf("ot%d" % i, [128, 1536], BF16) for i in range(2)])
    ov = Ring([P.sbuf("ov%d" % i, [128, 512], BF16) for i in range(2)])

    xq = {}

    def lx(b):
        for j in range(4):
            i = 4 * b + j
            x_ = xt.next()
            P.dma("sp", x_.t[:, 0:D], c.x.t[i * 128:(i + 1) * 128, :], c.x, x_)
            xq[i] = x_

    def nrm(b):
        h = hT[b % 2]
        if b == 0:
            P.op("pool", lambda e, h=h: e.memset(h.t[:, :, 0:1], 0.0), writes=[h])
        else:
            hp = hT[(b - 1) % 2]
            P.op("pool", lambda e, h=h, hp=hp: e.tensor_copy(out=h.t[:, :, 0:1], in_=hp.t[:, :, 512:513]), reads=[hp], writes=[h])
        for j in range(4):
            i = 4 * b + j
            x_ = xq.pop(i)
            xn_ = xn.next(); ss_ = ss.next(); rs_ = rs.next(); ps_ = pst.next()
            P.op("act", lambda e, x_=x_, ss_=ss_: e.activation(out=junk.t[:, :], in_=x_.t[:, 0:D], func=AF.Square, accum_out=ss_.t[:, :]),
                 reads=[x_], writes=[junk, ss_])
            RSTD_(P, rs_, rs_.t[:, :], ss_, ss_.t[:, :], 1.0 / D, 1e-6, P.cneg, P.cneg.t[:, 0:1])
            P.op("act", lambda e, x_=x_, xn_=xn_, rs_=rs_: e.activation(out=xn_.t[:, :], in_=x_.t[:, 0:D], func=AF.Copy, scale=rs_.t[:, :]),
                 reads=[x_, rs_], writes=[xn_])
            for k in range(8):
                TR_(P, ps_.t[:, k, :], xn_.t[:, k * 128:(k + 1) * 128], ident.t[:, :], [xn_, ident], [ps_], signal=(k == 7))
            CP_(P, "dve", h.t[:, :, 1 + j * 128:1 + (j + 1) * 128], ps_.t[:, :, :], [ps_], [h])

    lx(0)
    nrm(0)
    for pc in range(3):
        for dc in range(8):
            s = stg.next()
            P.dma("sp", s.t[:, :], c.w_in.t[dc * 128:(dc + 1) * 128, pc * 1792:(pc + 1) * 1792], c.w_in, s)
            if pc == 0:
                P.op("dve", lambda e, s=s, dc=dc: e.scalar_tensor_tensor(out=Wr1.t[:, dc, :], in0=s.t[:, :], scalar=g1.t[:, dc:dc + 1],
                                                                   in1=omm.t[:, :], op0=ALU.mult, op1=ALU.mult),
                     reads=[s, g1, omm], writes=[Wr1])
                P.op("dve", lambda e, s=s, dc=dc: e.scalar_tensor_tensor(out=Wr2.t[:, dc, :], in0=s.t[:, :], scalar=g1.t[:, dc:dc + 1],
                                                                    in1=mub.t[:, :], op0=ALU.mult, op1=ALU.mult),
                     reads=[s, g1, mub], writes=[Wr2])
            else:
                P.op("act", lambda e, s=s, dc=dc, pc=pc: e.activation(out=Wo.t[:, dc, (pc - 1) * 1792:pc * 1792], in_=s.t[:, :], func=AF.Copy,
                                                                 scale=g1.t[:, dc:dc + 1]),
                     reads=[s, g1], writes=[Wo])

    for b in range(NB):
        h = hT[b % 2]
        if b + 1 < NB:
            lx(b + 1)
        for j in range(4):
            i = 4 * b + j
            o_ = ot.next()
            for cg in range(3):
                ps = pm.next()
                for dc in range(8):
                    P.op("pe", lambda e, ps=ps, dc=dc, cg=cg, j=j, h=h: e.matmul(out=ps.t[:, :], lhsT=h.t[:, dc, 1 + j * 128:1 + (j + 1) * 128],
                                                                              rhs=Wr1.t[:, dc, cg * 512:(cg + 1) * 512], start=(dc == 0), stop=False),
                         reads=[h, Wr1], writes=[ps], signal=False)
                for dc in range(8):
                    P.op("pe", lambda e, ps=ps, dc=dc, cg=cg, j=j, h=h: e.matmul(out=ps.t[:, :], lhsT=h.t[:, dc, j * 128:(j + 1) * 128],
                                                                              rhs=Wr2.t[:, dc, cg * 512:(cg + 1) * 512], start=False, stop=(dc == 7)),
                         reads=[h, Wr2], writes=[ps], signal=(dc == 7))
                P.op("dve", lambda e, ps=ps, o_=o_, cg=cg: e.tensor_copy(out=o_.t[:, cg * 512:(cg + 1) * 512], in_=ps.t[:, :]), reads=[ps], writes=[o_])
            P.dma("sp", scr.rkv.t[i * 128:(i + 1) * 128, :], o_.t[:, :], o_, scr.rkv)
            ps = pm.next()
            v_ = ov.next()
            for dc in range(8):
                P.op("pe", lambda e, ps=ps, dc=dc, j=j, h=h: e.matmul(out=ps.t[:, :], lhsT=h.t[:, dc, 1 + j * 128:1 + (j + 1) * 128],
                                                                   rhs=Wo.t[:, dc, 1024:1536], start=(dc == 0), stop=(dc == 7)),
                     reads=[h, Wo], writes=[ps], signal=(dc == 7))
            P.op("dve", lambda e, ps=ps, v_=v_: e.tensor_copy(out=v_.t[:, :], in_=ps.t[:, :]), reads=[ps], writes=[v_])
            P.dma("sp", scr.vA.t[i * 128:(i + 1) * 128, :], v_.t[:, :], v_, scr.vA)
        if b + 1 < NB:
            nrm(b + 1)
        tok = slice(b * 512, (b + 1) * 512)
        for g in range(2):
            ps = pm.next()
            c0 = 1536 + g * 128
            for dc in range(8):
                P.op("pe", lambda e, ps=ps, dc=dc, c0=c0, h=h: e.matmul(out=ps.t[:, :], lhsT=Wr1.t[:, dc, c0:c0 + 128], rhs=h.t[:, dc, 1:513],
                                                                     start=(dc == 0), stop=False), reads=[h, Wr1], writes=[ps], signal=False)
            for dc in range(8):
                P.op("pe", lambda e, ps=ps, dc=dc, c0=c0, h=h: e.matmul(out=ps.t[:, :], lhsT=Wr2.t[:, dc, c0:c0 + 128], rhs=h.t[:, dc, 0:512],
                                                                     start=False, stop=(dc == 7)), reads=[h, Wr2], writes=[ps], signal=(dc == 7))
            o_ = of.next()
            if g == 0:
                P.op("act", lambda e, ps=ps, o_=o_: e.activation(out=o_.t[0:64, :], in_=ps.t[0:64, :], func=AF.Tanh), reads=[ps], writes=[o_])
                P.op("act", lambda e, ps=ps, o_=o_: e.copy(out=o_.t[64:128, :], in_=ps.t[64:128, :]), reads=[ps], writes=[o_])
            else:
                P.op("act", lambda e, ps=ps, o_=o_: e.activation(out=o_.t[:, :], in_=ps.t[:, :], func=AF.Sigmoid), reads=[ps], writes=[o_])
            P.dma("sp", scr.lora.t[g * 128:(g + 1) * 128, tok], o_.t[:, :], o_, scr.lora)
        for g in range(8 + 16):
            ps = pm.next()
            c0 = g * 128 if g < 8 else 1536 + (g - 8) * 128
            for dc in range(8):
                P.op("pe", lambda e, ps=ps, dc=dc, c0=c0, h=h: e.matmul(out=ps.t[:, :], lhsT=Wo.t[:, dc, c0:c0 + 128], rhs=h.t[:, dc, 1:513],
                                                                     start=(dc == 0), stop=(dc == 7)), reads=[h, Wo], writes=[ps], signal=(dc == 7))
            o_ = of.next()
            if g < 8:
                P.op("dve", lambda e, ps=ps, o_=o_: e.tensor_copy(out=o_.t[:, :], in_=ps.t[:, :]), reads=[ps], writes=[o_])
                dstb = scr.qT if g < 4 else scr.kT
                P.dma("sp", dstb.t[(g % 4) * 128:(g % 4 + 1) * 128, tok], o_.t[:, :], o_, dstb)
            else:
                gg = g - 8
                P.op("act", lambda e, ps=ps, o_=o_, gg=gg: e.activation(out=o_.t[:, :], in_=ps.t[:, :], func=AF.Sigmoid, bias=gb.t[:, gg:gg + 1]),
                     reads=[ps, gb], writes=[o_])
                P.dma("sp", scr.gates.t[gg * 128:(gg + 1) * 128, tok], o_.t[:, :], o_, scr.gates)
    P.barrier()
    P.emit_phase()
    P.stack.close()


def make_scratch(P, T, dbg=False):
    s = Ctx()
    kind = "ExternalOutput" if dbg else "Internal"
    mk = lambda n, sh, dt=BF16: P.dram(n, sh, dt, kind=kind)
    s.rkv = mk("s_rkv", [T, 1536]); s.vA = mk("s_vA", [T, 512]); s.lora = mk("s_lora", [256, T])
    s.qT = mk("s_qT", [512, T]); s.kT = mk("s_kT", [512, T]); s.gates = mk("s_gates", [2048, T])
    s.oaT = mk("s_oaT", [512, T]); s.obT = mk("s_obT", [512, T]); s.x1 = mk("s_x1", [T, D], F32); s.x2 = mk("s_x2", [T, D], F32)
    return s


def TT_(P, eng, out, in0, in1, op, reads, writes):
    P.op(eng, lambda e: e.tensor_tensor(out=out, in0=in0, in1=in1, op=op), reads=reads, writes=writes)


def STT_(P, out, in0, scalar, in1, op0, op1, reads, writes):
    P.op("dve", lambda e: e.scalar_tensor_tensor(out=out, in0=in0, scalar=scalar, in1=in1, op0=op0, op1=op1), reads=reads, writes=writes)


def TS_(P, eng, out, in0, s1, s2, op0, op1, reads, writes):
    if s2 is None:
        P.op(eng, lambda e: e.tensor_scalar(out=out, in0=in0, scalar1=s1, scalar2=None, op0=op0), reads=reads, writes=writes)
    else:
        P.op(eng, lambda e: e.tensor_scalar(out=out, in0=in0, scalar1=s1, scalar2=s2, op0=op0, op1=op1), reads=reads, writes=writes)


def ACT_(P, out, in_, func, reads, writes, scale=1.0, bias=None):
    if bias is None:
        P.op("act", lambda e: e.activation(out=out, in_=in_, func=func, scale=scale), reads=reads, writes=writes)
    else:
        P.op("act", lambda e: e.activation(out=out, in_=in_, func=func, scale=scale, bias=bias), reads=reads, writes=writes)


def CP_(P, eng, out, in_, reads, writes):
    if eng == "act":
        P.op("act", lambda e: e.copy(out=out, in_=in_), reads=reads, writes=writes)
    else:
        P.op(eng, lambda e: e.tensor_copy(out=out, in_=in_), reads=reads, writes=writes)


def MM_(P, out, lhsT, rhs, reads, writes, start=True, stop=True, signal=True):
    P.op("pe", lambda e: e.matmul(out=out, lhsT=lhsT, rhs=rhs, start=start, stop=stop), reads=reads, writes=writes, signal=signal)


def TR_(P, out, in_, ident, reads, writes, signal=True):
    P.op("pe", lambda e: e.transpose(out=out, in_=in_, identity=ident), reads=reads, writes=writes, signal=signal)


def RED_(P, out, in_, reads, writes):
    P.op("dve", lambda e: e.tensor_reduce(out=out, in_=in_, axis=AX.X, op=ALU.add), reads=reads, writes=writes)


def RSTD_(P, out_buf, out_ap, in_buf, in_ap, scale, eps, cneg_buf, cneg_ap):
    P.op("dve", lambda e: e.tensor_scalar(out=out_ap, in0=in_ap, scalar1=scale, scalar2=eps, op0=ALU.mult, op1=ALU.add),
         reads=[in_buf], writes=[out_buf])
    P.op("pool", lambda e: e.tensor_tensor(out=out_ap, in0=out_ap, in1=cneg_ap, op=ALU.pow), reads=[out_buf, cneg_buf], writes=[out_buf])


def make_cneg(P):
    cn = P.sbuf("cneg", [128, 8], F32)
    P.op("pool", lambda e: e.memset(cn.t[:, :], -0.5), writes=[cn])
    return cn


def make_cst():
    c = np.zeros((128, 8, 128), np.float32)
    s = np.arange(128)[:, None]
    t = np.arange(128)[None, :]
    same = (s // 64) == (t // 64)
    c[:, 0, :] = np.eye(128)
    c[:, 1, :] = same & (s <= t)
    c[:, 2, :] = same & (s < t)
    c[:, 3, :] = same & (s > t)
    c[:, 4, 0:2] = (s // 64) == np.arange(2)[None, :]
    s64 = s % 64
    t64 = np.arange(64)[None, :]
    c[:, 5, 0:64] = t64 > s64
    c[:, 5, 64:128] = t64 < s64
    c[:, 6, 0:64] = t64 >= s64
    c[:, 6, 64:128] = t64 == s64
    c[:, 7, :] = 8.0 * np.eye(128)
    return c


def make_biasT(rel_bias):
    k = np.arange(128)[:, None]
    q = np.arange(640)[None, :]
    rel = q - k
    idx = np.clip(rel, -63, 128) + 63
    kc = k // 64
    qc = q // 64
    ok = (qc >= kc) & (qc <= kc + 8)
    out = rel_bias[:, idx].astype(np.float32)
    out[:, ~ok] = NEG
    return out


def phase2(P, c, T, scr):
    NT = T // 128
    P.stack = contextlib.ExitStack()
    SB = lambda n, sh, dt=F32: P.sbuf(n, sh, dt)
    RG = lambda n, sh, dt=F32, k=2: Ring([P.sbuf("%s%d" % (n, i), sh, dt) for i in range(k)])
    ident = SB("ident", [128, 128], BF16)
    P.dma("pool", ident.t[:, :], c.cst.t[:, 0, :], c.cst, ident)
    cst = SB("cstf", [128, 8, 128])
    P.dma("sp", cst.t[:, :, :], c.cst.t[:, :, :], c.cst, cst)
    wa2 = SB("wa2", [128, RW], BF16)
    P.dma("pool", wa2.t[0:64, :], c.w2.t[:, :], c.w2, wa2)
    P.dma("pool", wa2.t[64:128, :], c.a2.t[:, :], c.a2, wa2)
    g2b = SB("g2b", [128, RW], BF16)
    P.dma("pool", g2b.t[:, :], c.g2.t[:, :], c.g2, g2b)
    bt = {}
    for nm in ["w0", "a0", "k_k", "k_a", "r_k", "lnx_w", "lnx_b"]:
        bt[nm] = SB("b_" + nm, [128, RW])
        P.dma("sp", bt[nm].t[:, :], bc(getattr(c, nm), RW), getattr(c, nm), bt[nm])
    Linc, Lsl, Lsu, cind = cst.t[:, 1, :], cst.t[:, 2, :], cst.t[:, 3, :], cst.t[:, 4, 0:2]
    b3 = lambda ap: ap.unsqueeze(1).to_broadcast([128, 8, 64])
    MU, ML, MUI, I64 = b3(cst.t[:, 5, 0:64]), b3(cst.t[:, 5, 64:128]), b3(cst.t[:, 6, 0:64]), b3(cst.t[:, 6, 64:128])

    pg = Ring([P.psum("pg%d" % i, [128, 512], F32) for i in range(5)])
    pq = Ring([P.psum("pq%d" % i, [128, 512], F32) for i in range(3)])
    v3 = lambda buf: buf.t.rearrange("p (h v) -> p h v", h=8)
    def bfv(buf, inner):
        return buf.t.bitcast(BF16)[:, 0:8 * inner].rearrange("p (h t) -> p h t", h=8)

    rkvt = RG("rkvt", [128, 1536], BF16)
    lo1 = RG("lo1", [128, 128], BF16); lo2 = RG("lo2", [128, 128], BF16)
    names_f = ["t_w", "sg", "t_a", "a_", "e_pos", "e_neg", "e_prev", "e_rel", "kkr", "sq", "kk", "bb", "t1", "k2", "t2", "yt", "yn", "bv"]
    F = {n: SB(n, [128, RW]) for n in names_f}
    g_ = RG("g_", [128, RW])
    names_b = ["At", "Bt", "Kt", "Rt"]
    Bq = {n: SB(n, [128, RW], BF16) for n in names_b}
    Bh = RG("Bh", [128, RW], BF16); Kh = RG("Kh", [128, RW], BF16)
    ATs = RG("ATs", [64, 8, 128], BF16); RTs = RG("RTs", [64, 8, 128], BF16)
    BTs = SB("BTs", [64, 8, 128], BF16); KTs = SB("KTs", [64, 8, 128], BF16)
    XA = Ring([P.sbuf("XA%d" % i, [128, 8, 64], BF16) for i in range(2)])
    XTA = Ring([P.sbuf("XTA%d" % i, [128, 8, 64], BF16) for i in range(2)])
    MakT = SB("MakT", [128, 8, 64], BF16)
    RBT = RG("RBT", [128, 8, 64], BF16); RKT = SB("RKT", [128, 8, 64], BF16)
    TTb = RG("TTb", [128, 8, 64], BF16)
    W1a = RG("W1a", [128, 8, 64]); Ya = RG("Ya", [128, 8, 64])
    W1 = SB("W1", [128, 8, 64], BF16); U = SB("U", [128, 8, 64], BF16)
    STb = Ring([P.sbuf("STb%d" % i, [64, 8, 64], BF16) for i in range(2)])
    SF = Ring([P.sbuf("SF%d" % i, [64, 8, 64], F32) for i in range(2)])
    tmpS = SB("tmpS", [64, 8, 64]); tmp2 = SB("tmp2", [64, 8, 64])
    pc = RG("pc", [64, 8, 2])
    sm = {n: SB(n, [128, 8]) for n in ["ssq", "nrm", "rn", "s1", "s2", "mean", "msq", "var", "rstd"]}
    bcf = RG("bcf", [128, 8])
    oa = SB("oa", [128, RW], BF16)
    oaT = RG("oaT", [128, 4, 128], BF16)

    st_b = STb.next(); st_f = SF.next()
    P.op("pool", lambda e: e.memset(st_b.t[:, :, :], 0.0), writes=[st_b])
    P.op("pool", lambda e: e.memset(st_f.t[:, :, :], 0.0), writes=[st_f])
    h3 = lambda ap: ap.rearrange("p (h v) -> p h v", h=8)
    bl = lambda ap: ap.unsqueeze(2).to_broadcast([128, 8, 64])

    for i in range(NT):
        tok = slice(i * 128, (i + 1) * 128)
        rk = rkvt.next(); l1 = lo1.next(); l2 = lo2.next()
        P.dma("sp", rk.t[:, :], scr.rkv.t[tok, :], scr.rkv, rk)
        P.dma("sp", l1.t[:, :], scr.lora.t[0:128, tok], scr.lora, l1)
        P.dma("sp", l2.t[:, :], scr.lora.t[128:256, tok], scr.lora, l2)
        r_, k_, v_ = rk.t[:, 0:512], rk.t[:, 512:1024], rk.t[:, 1024:1536]
        p_w = pg.next(); p_a = pg.next(); p_g = pg.next()
        MM_(P, p_w.t[:, :], l1.t[0:64, :], wa2.t[0:64, :], [l1, wa2], [p_w])
        MM_(P, p_a.t[:, :], l1.t[64:128, :], wa2.t[64:128, :], [l1, wa2], [p_a])
        MM_(P, p_g.t[:, :], l2.t[:, :], g2b.t[:, :], [l2, g2b], [p_g])
        f = {n: F[n].t[:, :] for n in names_f}
        TT_(P, "dve", f["t_w"], p_w.t[:, :], bt["w0"].t[:, :], ALU.add, [p_w, bt["w0"]], [F["t_w"]])
        ACT_(P, f["sg"], f["t_w"], AF.Sigmoid, [F["t_w"]], [F["sg"]])
        TT_(P, "dve", f["t_a"], p_a.t[:, :], bt["a0"].t[:, :], ALU.add, [p_a, bt["a0"]], [F["t_a"]])
        ACT_(P, f["a_"], f["t_a"], AF.Sigmoid, [F["t_a"]], [F["a_"]])
        gq = g_.next()
        CP_(P, "act", gq.t[:, :], p_g.t[:, :], [p_g], [gq])
        p1 = pg.next(); p2 = pg.next(); p3 = pg.next(); p4 = pg.next()
        MM_(P, p1.t[:, :], Linc, f["sg"], [cst, F["sg"]], [p1])
        MM_(P, p2.t[:, :], Lsl, f["sg"], [cst, F["sg"]], [p2])
        MM_(P, p3.t[:, :], Lsu, f["sg"], [cst, F["sg"]], [p3])
        for h in range(8):
            MM_(P, p4.t[0:64, 2 * h:2 * h + 2], F["sg"].t[:, h * 64:(h + 1) * 64], cind, [F["sg"], cst], [p4], signal=(h == 7))
        ACT_(P, f["e_pos"], p1.t[:, :], AF.Exp, [p1], [F["e_pos"]], scale=CDEC)
        ACT_(P, f["e_neg"], p1.t[:, :], AF.Exp, [p1], [F["e_neg"]], scale=-CDEC)
        ACT_(P, f["e_prev"], p2.t[:, :], AF.Exp, [p2], [F["e_prev"]], scale=CDEC)
        ACT_(P, f["e_rel"], p3.t[:, :], AF.Exp, [p3], [F["e_rel"]], scale=CDEC)
        pcq = pc.next()
        ACT_(P, pcq.t[:, :, :], p4.t[0:64, 0:16].rearrange("p (h c) -> p h c", h=8), AF.Exp, [p4], [pcq], scale=CDEC)
        if CUT < 2:
            continue
        TT_(P, "dve", f["kkr"], k_, bt["k_k"].t[:, :], ALU.mult, [rk, bt["k_k"]], [F["kkr"]])
        TT_(P, "pool", f["sq"], f["kkr"], f["kkr"], ALU.mult, [F["kkr"]], [F["sq"]])
        RED_(P, sm["ssq"].t[:, :], h3(f["sq"]), [F["sq"]], [sm["ssq"]])
        ACT_(P, sm["nrm"].t[:, :], sm["ssq"].t[:, :], AF.Sqrt, [sm["ssq"]], [sm["nrm"]])
        TS_(P, "dve", sm["nrm"].t[:, :], sm["nrm"].t[:, :], 1e-12, None, ALU.max, None, [sm["nrm"]], [sm["nrm"]])
        P.op("dve", lambda e: e.reciprocal(out=sm["rn"].t[:, :], in_=sm["nrm"].t[:, :]), reads=[sm["nrm"]], writes=[sm["rn"]])
        TT_(P, "dve", h3(f["kk"]), h3(f["kkr"]), bl(sm["rn"].t[:, :]), ALU.mult, [F["kkr"], sm["rn"]], [F["kk"]])
        TT_(P, "dve", f["bb"], f["kk"], f["a_"], ALU.mult, [F["kk"], F["a_"]], [F["bb"]])
        STT_(P, f["t1"], f["a_"], -1.0, bt["k_a"].t[:, :], ALU.add, ALU.mult, [F["a_"], bt["k_a"]], [F["t1"]])
        STT_(P, f["k2"], f["t1"], 1.0, k_, ALU.add, ALU.mult, [F["t1"], rk], [F["k2"]])
        STT_(P, Bq["At"].t[:, :], f["kk"], -1.0, f["e_prev"], ALU.mult, ALU.mult, [F["kk"], F["e_prev"]], [Bq["At"]])
        TT_(P, "dve", Bq["Bt"].t[:, :], f["bb"], f["e_neg"], ALU.mult, [F["bb"], F["e_neg"]], [Bq["Bt"]])
        TT_(P, "dve", Bq["Kt"].t[:, :], f["k2"], f["e_neg"], ALU.mult, [F["k2"], F["e_neg"]], [Bq["Kt"]])
        TT_(P, "dve", Bq["Rt"].t[:, :], r_, f["e_pos"], ALU.mult, [rk, F["e_pos"]], [Bq["Rt"]])
        bh = Bh.next(); kh = Kh.next()
        TT_(P, "pool", bh.t[:, :], f["bb"], f["e_rel"], ALU.mult, [F["bb"], F["e_rel"]], [bh])
        TT_(P, "pool", kh.t[:, :], f["k2"], f["e_rel"], ALU.mult, [F["k2"], F["e_rel"]], [kh])
        TT_(P, "pool", f["t2"], r_, bt["r_k"].t[:, :], ALU.mult, [rk, bt["r_k"]], [F["t2"]])
        TT_(P, "pool", f["t2"], f["t2"], f["k2"], ALU.mult, [F["t2"], F["k2"]], [F["t2"]])
        bq = bcf.next()
        RED_(P, bq.t[:, :], h3(f["t2"]), [F["t2"]], [bq])
        if CUT < 3:
            continue
        ats = ATs.next(); rts = RTs.next()
        for (src, dst, ev) in [("At", ats, "act"), ("Bt", BTs, "dve"), ("Kt", KTs, "act"), ("Rt", rts, "dve")]:
            ps = pg.next()
            pv = bfv(ps, 128)
            for h in range(8):
                TR_(P, pv[0:64, h, :], Bq[src].t[:, h * 64:(h + 1) * 64], ident.t[:, :], [Bq[src], ident], [ps], signal=(h == 7))
            CP_(P, ev, dst.t[:, :, :], pv[0:64, :, :], [ps], [dst])
        if CUT < 4:
            continue
        def prod(lhs, rhs, mask, dst, eng="dve"):
            ps = pg.next()
            for cc in range(2):
                cols = slice(cc * 64, (cc + 1) * 64)
                for h in range(8):
                    MM_(P, v3(ps)[cc * 64:(cc + 1) * 64, h, :], lhs.t[:, h, cols], rhs.t[:, h, cols], [lhs, rhs], [ps],
                        signal=(cc == 1 and h == 7))
            TT_(P, eng, dst.t[:, :, :], v3(ps), mask, ALU.mult, [ps, cst], [dst])
        xt_ = XTA.next(); x_ = XA.next()
        rbt = RBT.next()
        prod(BTs, ats, MU, xt_)
        prod(ats, BTs, ML, x_)
        prod(KTs, ats, MU, MakT)
        prod(BTs, rts, MUI, rbt)
        prod(KTs, rts, MUI, RKT)
        tt = TTb.next()
        TT_(P, "dve", tt.t[:, :, :], xt_.t[:, :, :], I64, ALU.add, [xt_, cst], [tt])
        if CUT < 5:
            continue
        for lev in range(1, 6):
            xn_ = XA.next()
            xtn_ = XTA.next() if lev < 5 else None
            for cc in range(2):
                hs = slice(cc * 64, (cc + 1) * 64)
                px = pg.next()
                for h in range(8):
                    MM_(P, v3(px)[hs, h, :], xt_.t[hs, h, :], x_.t[hs, h, :], [xt_, x_], [px], signal=(h == 7))
                CP_(P, "act", xn_.t[hs, :, :], v3(px)[hs, :, :], [px], [xn_])
                if lev < 5:
                    pxt = pg.next()
                    for h in range(8):
                        MM_(P, v3(pxt)[hs, h, :], x_.t[hs, h, :], xt_.t[hs, h, :], [xt_, x_], [pxt], signal=(h == 7))
                    CP_(P, "act", xtn_.t[hs, :, :], v3(pxt)[hs, :, :], [pxt], [xtn_])
            x_ = xn_
            if lev < 5:
                xt_ = xtn_
            for cc in range(2):
                hs = slice(cc * 64, (cc + 1) * 64)
                pt = pg.next()
                for h in range(8):
                    MM_(P, v3(pt)[hs, h, :], x_.t[hs, h, :], tt.t[hs, h, :], [x_, tt], [pt], signal=(h == 7))
                TT_(P, "dve", tt.t[hs, :, :], v3(pt)[hs, :, :], tt.t[hs, :, :], ALU.add, [pt, tt], [tt])
        if CUT < 6:
            continue
        w1a = W1a.next(); ya = Ya.next()
        for cc in range(2):
            hs = slice(cc * 64, (cc + 1) * 64)
            ps = pg.next()
            for h in range(8):
                MM_(P, v3(ps)[hs, h, :], MakT.t[hs, h, :], rk.t[hs, 1024 + h * 64:1024 + (h + 1) * 64], [MakT, rk], [ps], signal=(h == 7))
            CP_(P, "act", w1a.t[hs, :, :], v3(ps)[hs, :, :], [ps], [w1a])
            ps = pg.next()
            for h in range(8):
                MM_(P, v3(ps)[hs, h, :], RKT.t[hs, h, :], rk.t[hs, 1024 + h * 64:1024 + (h + 1) * 64], [RKT, rk], [ps], signal=(h == 7))
            CP_(P, "act", ya.t[hs, :, :], v3(ps)[hs, :, :], [ps], [ya])
        if CUT < 7:
            continue
        for cc in range(2):
            hs = slice(cc * 64, (cc + 1) * 64)
            cols = slice(cc * 64, (cc + 1) * 64)
            pk = pq.next()
            for h in range(8):
                MM_(P, v3(pk)[0:64, h, :], kh.t[hs, h * 64:(h + 1) * 64], rk.t[hs, 1024 + h * 64:1024 + (h + 1) * 64], [kh, rk], [pk], signal=(h == 7))
            TT_(P, "dve", tmpS.t[:, :, :], st_f.t[:, :, :], pcq.t[:, :, cc:cc + 1].to_broadcast([64, 8, 64]), ALU.mult, [st_f, pcq], [tmpS])
            TT_(P, "dve", tmp2.t[:, :, :], v3(pk)[0:64, :, :], tmpS.t[:, :, :], ALU.add, [pk, tmpS], [tmp2])
            pw = pq.next()
            for h in range(8):
                MM_(P, v3(pw)[hs, h, :], ats.t[:, h, cols], st_b.t[:, h, :], [ats, st_b], [pw], signal=(h == 7))
            TT_(P, "dve", W1.t[hs, :, :], v3(pw)[hs, :, :], w1a.t[hs, :, :], ALU.add, [pw, w1a], [W1])
            prs = pq.next()
            for h in range(8):
                MM_(P, v3(prs)[hs, h, :], rts.t[:, h, cols], st_b.t[:, h, :], [rts, st_b], [prs], signal=(h == 7))
            TT_(P, "dve", h3(f["yt"])[hs, :, :], v3(prs)[hs, :, :], ya.t[hs, :, :], ALU.add, [prs, ya], [F["yt"]])
            pu = pq.next()
            for h in range(8):
                MM_(P, v3(pu)[hs, h, :], tt.t[hs, h, :], W1.t[hs, h, :], [tt, W1], [pu], signal=(h == 7))
            CP_(P, "act", U.t[hs, :, :], v3(pu)[hs, :, :], [pu], [U])
            psn = pq.next()
            for h in range(8):
                MM_(P, v3(psn)[0:64, h, :], bh.t[hs, h * 64:(h + 1) * 64], U.t[hs, h, :], [bh, U], [psn], signal=(h == 7))
            py = pq.next()
            for h in range(8):
                MM_(P, v3(py)[hs, h, :], rbt.t[hs, h, :], U.t[hs, h, :], [rbt, U], [py], signal=(h == 7))
            nb = STb.next(); nf = SF.next()
            TT_(P, "dve", nb.t[:, :, :], v3(psn)[0:64, :, :], tmp2.t[:, :, :], ALU.add, [psn, tmp2], [nb])
            TT_(P, "dve", nf.t[:, :, :], v3(psn)[0:64, :, :], tmp2.t[:, :, :], ALU.add, [psn, tmp2], [nf])
            st_b, st_f = nb, nf
            TT_(P, "dve", h3(f["yt"])[hs, :, :], v3(py)[hs, :, :], h3(f["yt"])[hs, :, :], ALU.add, [py, F["yt"]], [F["yt"]])
        if CUT < 8:
            continue
        yt3 = h3(f["yt"]); yn3 = h3(f["yn"])
        RED_(P, sm["s1"].t[:, :], yt3, [F["yt"]], [sm["s1"]])
        TT_(P, "pool", f["sq"], f["yt"], f["yt"], ALU.mult, [F["yt"]], [F["sq"]])
        RED_(P, sm["s2"].t[:, :], h3(f["sq"]), [F["sq"]], [sm["s2"]])
        TS_(P, "dve", sm["mean"].t[:, :], sm["s1"].t[:, :], 1.0 / 64, None, ALU.mult, None, [sm["s1"]], [sm["mean"]])
        TT_(P, "dve", sm["msq"].t[:, :], sm["mean"].t[:, :], sm["mean"].t[:, :], ALU.mult, [sm["mean"]], [sm["msq"]])
        STT_(P, sm["var"].t[:, :], sm["s2"].t[:, :], 1.0 / 64, sm["msq"].t[:, :], ALU.mult, ALU.subtract, [sm["s2"], sm["msq"]], [sm["var"]])
        ACT_(P, sm["rstd"].t[:, :], sm["var"].t[:, :], AF.Sqrt, [sm["var"]], [sm["rstd"]], bias=64e-5)
        P.op("dve", lambda e: e.reciprocal(out=sm["rstd"].t[:, :], in_=sm["rstd"].t[:, :]), reads=[sm["rstd"]], writes=[sm["rstd"]])
        TT_(P, "dve", yn3, yt3, bl(sm["mean"].t[:, :]), ALU.subtract, [F["yt"], sm["mean"]], [F["yn"]])
        TT_(P, "dve", yn3, yn3, bl(sm["rstd"].t[:, :]), ALU.mult, [F["yn"], sm["rstd"]], [F["yn"]])
        TT_(P, "pool", f["yn"], f["yn"], bt["lnx_w"].t[:, :], ALU.mult, [F["yn"], bt["lnx_w"]], [F["yn"]])
        TT_(P, "pool", f["yn"], f["yn"], bt["lnx_b"].t[:, :], ALU.add, [F["yn"], bt["lnx_b"]], [F["yn"]])
        TT_(P, "dve", h3(f["bv"]), h3(v_), bl(bq.t[:, :]), ALU.mult, [rk, bq], [F["bv"]])
        TT_(P, "pool", f["yn"], f["yn"], f["bv"], ALU.add, [F["yn"], F["bv"]], [F["yn"]])
        TT_(P, "dve", oa.t[:, :], f["yn"], gq.t[:, :], ALU.mult, [F["yn"], gq], [oa])
        ps = pg.next()
        pv = ps.t.bitcast(BF16)[:, 0:512].rearrange("p (c t) -> p c t", c=4)
        for fc in range(4):
            TR_(P, pv[:, fc, :], oa.t[:, fc * 128:(fc + 1) * 128], ident.t[:, :], [oa, ident], [ps], signal=(fc == 3))
        ot_ = oaT.next()
        CP_(P, "act", ot_.t[:, :, :], pv, [ps], [ot_])
        P.dma("sp", scr.oaT.t[:, tok].rearrange("(c p) t -> p c t", p=128), ot_.t[:, :, :], ot_, scr.oaT)
    P.barrier()
    P.emit_phase()
    P.stack.close()


def phase3(P, c, T, scr):
    NT = T // 128
    P.stack = contextlib.ExitStack()
    ident = P.sbuf("ident", [128, 128], BF16)
    P.dma("pool", ident.t[:, :], c.cst.t[:, 0, :], c.cst, ident)
    I8 = P.sbuf("I8", [128, 128], BF16)
    P.dma("pool", I8.t[:, :], c.cst.t[:, 7, :], c.cst, I8)
    kTh = Ring([P.sbuf("kTh%d" % i, [128, T], BF16) for i in range(2)])
    qTh = Ring([P.sbuf("qTh%d" % i, [128, T], BF16) for i in range(2)])
    Vh = Ring([P.sbuf("Vh%d" % i, [128, NT, 65], BF16) for i in range(2)])
    bT = Ring([P.sbuf("bT%d" % i, [128, 640], BF16) for i in range(2)])
    for r in (kTh, qTh):
        for b in r.items:
            P.op("pool", lambda e, b=b: e.memset(b.t[64:128, :], 0.0), writes=[b])
    for b in Vh.items:
        P.op("pool", lambda e, b=b: e.memset(b.t[:, :, 64:65], 1.0), writes=[b])
    PT = [P.sbuf("PT%d" % i, [128, 640], BF16) for i in range(6)]
    ob = P.sbuf("ob", [128, NT, 512], BF16)
    rc = Ring([P.sbuf("rc%d" % i, [128, 1], F32) for i in range(2)])
    pa = Ring([P.psum("pa%d" % i, [128, 512], F32) for i in range(4)])
    pb = Ring([P.psum("pb%d" % i, [128, 512], F32) for i in range(2)])
    pt_ = Ring([P.psum("ptr%d" % i, [128, 512], F32) for i in range(2)])
    for h in range(NH):
        kt = kTh.next(); qt = qTh.next(); vh = Vh.next(); bias = bT.next()
        P.dma("sp", kt.t[0:64, :], scr.kT.t[h * 64:(h + 1) * 64, :], scr.kT, kt)
        P.dma("sp", qt.t[0:64, :], scr.qT.t[h * 64:(h + 1) * 64, :], scr.qT, qt)
        P.dma("sp", vh.t[:, :, 0:64], scr.vA.t[:, h * 64:(h + 1) * 64].rearrange("(m p) d -> p m d", p=128), scr.vA, vh)
        P.dma("pool", bias.t[:, :], c.biasT.t[h, :, :], c.biasT, bias)
        for m in range(NT):
            W = min(640, T - 128 * m)
            pt = PT[m % 6]
            for (c0, c1) in [(0, min(W, 512)), (512, W)]:
                if c1 <= c0:
                    continue
                ps = pa.next()
                n = c1 - c0
                MM_(P, ps.t[:, 0:n], I8.t[:, :], bias.t[:, c0:c1], [I8, bias], [ps], start=True, stop=False, signal=False)
                MM_(P, ps.t[:, 0:n], kt.t[:, m * 128:(m + 1) * 128], qt.t[:, m * 128 + c0:m * 128 + c1], [kt, qt], [ps], start=False, stop=True)
                ACT_(P, pt.t[:, c0:c1], ps.t[:, 0:n], AF.Exp, [ps], [pt], scale=0.125)
            pv = pb.next()
            for cc in range(2):
                cq = 2 * m + cc
                m0 = max(0, (cq - 8) // 2)
                ms = list(range(m0, cq // 2 + 1))
                for mi, mp in enumerate(ms):
                    off = (cq - 2 * mp) * 64
                    MM_(P, pv.t[cc * 64:(cc + 1) * 64, 0:65], PT[mp % 6].t[:, off:off + 64], vh.t[:, mp, :], [PT[mp % 6], vh], [pv],
                        start=(mi == 0), stop=(mi == len(ms) - 1), signal=(mi == len(ms) - 1))
            r_ = rc.next()
            P.op("dve", lambda e, r_=r_, pv=pv: e.reciprocal(out=r_.t[:, :], in_=pv.t[:, 64:65]), reads=[pv], writes=[r_])
            ACT_(P, ob.t[:, m, h * 64:(h + 1) * 64], pv.t[:, 0:64], AF.Copy, [pv, r_], [ob], scale=r_.t[:, :])
    obT = Ring([P.sbuf("obT%d" % i, [128, 4, 128], BF16) for i in range(2)])
    for m in range(NT):
        ps = pt_.next()
        pvw = ps.t.bitcast(BF16)[:, 0:512].rearrange("p (c t) -> p c t", c=4)
        for fc in range(4):
            TR_(P, pvw[:, fc, :], ob.t[:, m, fc * 128:(fc + 1) * 128], ident.t[:, :], [ob, ident], [ps], signal=(fc == 3))
        o_ = obT.next()
        CP_(P, "dve", o_.t[:, :, :], pvw, [ps], [o_])
        P.dma("sp", scr.obT.t[:, m * 128:(m + 1) * 128].rearrange("(c p) t -> p c t", p=128), o_.t[:, :, :], o_, scr.obT)
    P.barrier()
    P.emit_phase()
    P.stack.close()


def phase4a(P, c, T, scr):
    NB = T // 512
    scr.keep = contextlib.ExitStack()
    P.stack = scr.keep
    scr.g2 = P.sbuf("g2c", [128, 8], F32); load_cols(P, "sp", scr.g2, c.ln2_g, D, 8)
    scr.Wup = [P.sbuf("Wup%d" % dc, [128, 2 * DFF], BF16) for dc in range(8)]
    for dc in range(8):
        P.dma("pool", scr.Wup[dc].t[:, :], c.w_ffn_up.t[dc * 128:(dc + 1) * 128, :], c.w_ffn_up, scr.Wup[dc])
        P.op("act", lambda e, dc=dc: e.activation(out=scr.Wup[dc].t[:, :], in_=scr.Wup[dc].t[:, :], func=AF.Copy, scale=scr.g2.t[:, dc:dc + 1]),
             reads=[scr.Wup[dc], scr.g2], writes=[scr.Wup[dc]])
    P.stack = contextlib.ExitStack()
    WA = P.sbuf("WA", [128, 4, D], BF16); WB = P.sbuf("WB", [128, 4, D], BF16); WO = P.sbuf("WO", [128, 8, D], BF16)
    P.dma("pool", WA.t[:, :, :], c.w_br_rwkv.t.rearrange("(c p) n -> p c n", p=128), c.w_br_rwkv, WA)
    P.dma("pool", WB.t[:, :, :], c.w_br_attn.t.rearrange("(c p) n -> p c n", p=128), c.w_br_attn, WB)
    P.dma("pool", WO.t[:, :, :], c.w_o.t.rearrange("(c p) n -> p c n", p=128), c.w_o, WO)
    oa = Ring([P.sbuf("oa%d" % i, [128, 4, 512], BF16) for i in range(2)])
    obb = Ring([P.sbuf("obb%d" % i, [128, 4, 512], BF16) for i in range(2)])
    ga = Ring([P.sbuf("ga%d" % i, [128, 8, 512], BF16) for i in range(2)])
    gbb = Ring([P.sbuf("gbb%d" % i, [128, 8, 512], BF16) for i in range(2)])
    t1 = Ring([P.sbuf("t1_%d" % i, [128, 512], F32) for i in range(2)])
    t2 = Ring([P.sbuf("t2_%d" % i, [128, 512], F32) for i in range(2)])
    mT = Ring([P.sbuf("mT%d" % i, [128, 8, 512], BF16) for i in range(1)])
    xt = Ring([P.sbuf("xt%d" % i, [128, D], F32) for i in range(3)])
    xo = Ring([P.sbuf("xo%d" % i, [128, D], F32) for i in range(2)])
    pm = Ring([P.psum("pm%d" % i, [128, 512], F32) for i in range(6)])
    blk = {}
    xq = {}

    def lb(b):
        tok = slice(b * 512, (b + 1) * 512)
        a_ = oa.next(); b_ = obb.next(); ga_ = ga.next(); gb_ = gbb.next()
        P.dma("sp", a_.t[:, :, :], scr.oaT.t[:, tok].rearrange("(c p) t -> p c t", p=128), scr.oaT, a_)
        P.dma("sp", b_.t[:, :, :], scr.obT.t[:, tok].rearrange("(c p) t -> p c t", p=128), scr.obT, b_)
        P.dma("sp", ga_.t[:, :, :], scr.gates.t[0:1024, tok].rearrange("(c p) t -> p c t", p=128), scr.gates, ga_)
        P.dma("sp", gb_.t[:, :, :], scr.gates.t[1024:2048, tok].rearrange("(c p) t -> p c t", p=128), scr.gates, gb_)
        blk[b] = (a_, b_, ga_, gb_)

    def lx(i):
        x_ = xt.next()
        P.dma("sp", x_.t[:, :], c.x.t[i * 128:(i + 1) * 128, :], c.x, x_)
        xq[i] = x_

    lb(0)
    for i in range(min(2, 4 * NB)):
        lx(i)
    for b in range(NB):
        tok = slice(b * 512, (b + 1) * 512)
        a_, b_, ga_, gb_ = blk.pop(b)
        m_ = mT.next()
        if b + 1 < NB:
            lb(b + 1)
        for cg in range(8):
            pA = pm.next(); pB = pm.next()
            for fc in range(4):
                MM_(P, pA.t[:, :], WA.t[:, fc, cg * 128:(cg + 1) * 128], a_.t[:, fc, :], [WA, a_], [pA], start=(fc == 0), stop=(fc == 3), signal=(fc == 3))
            for fc in range(4):
                MM_(P, pB.t[:, :], WB.t[:, fc, cg * 128:(cg + 1) * 128], b_.t[:, fc, :], [WB, b_], [pB], start=(fc == 0), stop=(fc == 3), signal=(fc == 3))
            u1 = t1.next(); u2 = t2.next()
            TT_(P, "dve", u1.t[:, :], pA.t[:, :], ga_.t[:, cg, :], ALU.mult, [pA, ga_], [u1])
            TT_(P, "dve", u2.t[:, :], pB.t[:, :], gb_.t[:, cg, :], ALU.mult, [pB, gb_], [u2])
            TT_(P, "pool", m_.t[:, cg, :], u1.t[:, :], u2.t[:, :], ALU.add, [u1, u2], [m_])
        for j in range(4):
            i = 4 * b + j
            x_ = xq.pop(i); o_ = xo.next()
            if i + 2 < 4 * NB:
                lx(i + 2)
            for hf in range(2):
                ps = pm.next()
                for cg in range(8):
                    MM_(P, ps.t[:, :], m_.t[:, cg, j * 128:(j + 1) * 128], WO.t[:, cg, hf * 512:(hf + 1) * 512], [m_, WO], [ps],
                        start=(cg == 0), stop=(cg == 7), signal=(cg == 7))
                TT_(P, "dve", o_.t[:, hf * 512:(hf + 1) * 512], ps.t[:, :], x_.t[:, hf * 512:(hf + 1) * 512], ALU.add, [ps, x_], [o_])
            P.dma("sp", scr.x1.t[i * 128:(i + 1) * 128, :], o_.t[:, :], o_, scr.x1)
    P.barrier()
    P.emit_phase()
    P.stack.close()


def norm_scale(P, x_, xn_, s_, r_):
    P.op("act", lambda e: e.activation(out=xn_.t[:, :], in_=x_.t[:, :], func=AF.Square, accum_out=s_.t[:, :]),
         reads=[x_], writes=[xn_, s_])
    RSTD_(P, r_, r_.t[:, :], s_, s_.t[:, :], 1.0 / D, 1e-6, P.cneg, P.cneg.t[:, 0:1])
    P.op("act", lambda e: e.activation(out=xn_.t[:, :], in_=x_.t[:, :], func=AF.Copy, scale=r_.t[:, :]),
         reads=[x_, r_], writes=[xn_])


def phase5(P, c, T, scr):
    NB = T // 256
    NG = DFF // 128
    P.stack = contextlib.ExitStack()
    P.cneg = make_cneg(P)
    ident = P.sbuf("ident", [128, 128], BF16)
    P.dma("pool", ident.t[:, :], c.cst.t[:, 0, :], c.cst, ident)
    cw = P.sbuf("cw", [128, 3, NG], F32)
    for k in range(3):
        P.dma("sp", cw.t[:, k, :], c.conv_w.t[k, :].rearrange("(c p) -> p c", p=128), c.conv_w, cw, allow_slow_non_contiguous=True)
    cb = P.sbuf("cb", [128, NG], F32); load_cols(P, "sp", cb, c.conv_b, DFF, NG)
    Wup = scr.Wup
    Wdn = P.sbuf("Wdn", [128, NG, D], BF16)
    for gq in range(0, NG, 2):
        P.dma("pool", Wdn.t[:, gq:gq + 2, :], c.w_ffn_down.t[gq * 128:(gq + 2) * 128, :].rearrange("(c p) n -> p c n", p=128), c.w_ffn_down, Wdn)
    carry = P.sbuf("carry", [128, NG, 2], F32)
    P.op("pool", lambda e: e.memset(carry.t[:, :, :], 0.0), writes=[carry])

    X = Ring([P.sbuf("X%d" % i, [128, D], F32) for i in range(6)])
    xn = Ring([P.sbuf("xn%d" % i, [128, D], BF16) for i in range(2)])
    ss = Ring([P.sbuf("ss%d" % i, [128, 1], F32) for i in range(6)])
    rs = Ring([P.sbuf("rs%d" % i, [128, 1], F32) for i in range(6)])
    h2T = [P.sbuf("h2T%d" % i, [128, 8, 256], BF16) for i in range(2)]
    mTt = [P.sbuf("mT%d" % i, [128, NG, 256], BF16) for i in range(2)]
    mT = [[Buf("mT%d_%d" % (i, g), mTt[i].t) for g in range(NG)] for i in range(2)]
    Ab = Ring([P.sbuf("Ab%d" % i, [128, 258], F32) for i in range(3)])
    cv = Ring([P.sbuf("cv%d" % i, [128, 256], F32) for i in range(3)])
    gl = Ring([P.sbuf("gl%d" % i, [128, 256], F32) for i in range(3)])
    pst = Ring([P.psum("pst%d" % i, [128, 8, 128], BF16) for i in range(2)])
    pm = Ring([P.psum("pm%d" % i, [128, 512], F32) for i in range(4)])
    xs = {}
    xns = {}

    def s1l(b):
        xs[b] = []; xns[b] = []
        for j in range(2):
            i = 2 * b + j
            x_ = X.next(); n_ = xn.next()
            xs[b].append(x_); xns[b].append((n_, ss.next(), rs.next()))
            P.dma("sp", x_.t[:, :], scr.x1.t[i * 128:(i + 1) * 128, :], scr.x1, x_)

    def s1n(b, j, k):
        x_ = xs[b][j]; n_, s_, r_ = xns[b][j]
        if k == 0:
            P.op("act", lambda e: e.activation(out=n_.t[:, :], in_=x_.t[:, :], func=AF.Square, accum_out=s_.t[:, :]),
                 reads=[x_], writes=[n_, s_])
        elif k == 1:
            RSTD_(P, r_, r_.t[:, :], s_, s_.t[:, :], 1.0 / D, 1e-6, P.cneg, P.cneg.t[:, 0:1])
        else:
            P.op("act", lambda e: e.activation(out=n_.t[:, :], in_=x_.t[:, :], func=AF.Copy, scale=r_.t[:, :]),
                 reads=[x_, r_], writes=[n_])

    def s1a(b):
        s1l(b)
        for j in range(2):
            for k in range(3):
                s1n(b, j, k)

    def s1b(b):
        h = h2T[b % 2]
        for j in range(2):
            ps = pst.next(); n_ = xns[b][j][0]
            for k in range(8):
                TR_(P, ps.t[:, k, :], n_.t[:, k * 128:(k + 1) * 128], ident.t[:, :], [n_, ident], [ps], signal=(k == 7))
            CP_(P, "dve", h.t[:, :, j * 128:(j + 1) * 128], ps.t[:, :, :], [ps], [h])

    def up(b, g):
        h = h2T[b % 2]
        pA = pm.next(); pG = pm.next()
        for dc in range(8):
            MM_(P, pA.t[:, 0:256], Wup[dc].t[:, g * 128:(g + 1) * 128], h.t[:, dc, :], [Wup[dc], h], [pA], start=(dc == 0), stop=(dc == 7), signal=(dc == 7))
        for dc in range(8):
            MM_(P, pG.t[:, 0:256], Wup[dc].t[:, DFF + g * 128:DFF + (g + 1) * 128], h.t[:, dc, :], [Wup[dc], h], [pG], start=(dc == 0), stop=(dc == 7), signal=(dc == 7))
        A = Ab.next(); cv_ = cv.next(); gl_ = gl.next()
        CP_(P, "pool", A.t[:, 0:2], carry.t[:, g, :], [carry], [A])
        CP_(P, "act", A.t[:, 2:258], pA.t[:, 0:256], [pA], [A])
        ACT_(P, cv_.t[:, :], pA.t[:, 0:256], AF.Identity, [pA, cw, cb], [cv_], scale=cw.t[:, 2, g:g + 1], bias=cb.t[:, g:g + 1])
        STT_(P, cv_.t[:, :], A.t[:, 1:257], cw.t[:, 1, g:g + 1], cv_.t[:, :], ALU.mult, ALU.add, [A, cw, cv_], [cv_])
        STT_(P, cv_.t[:, :], A.t[:, 0:256], cw.t[:, 0, g:g + 1], cv_.t[:, :], ALU.mult, ALU.add, [A, cw, cv_], [cv_])
        CP_(P, "pool", carry.t[:, g, :], A.t[:, 256:258], [A], [carry])
        ACT_(P, gl_.t[:, :], cv_.t[:, :], AF.Gelu, [cv_], [gl_])
        TT_(P, "dve", mTt[b % 2].t[:, g, :], gl_.t[:, :], pG.t[:, 0:256], ALU.mult, [gl_, pG], [mT[b % 2][g]])

    pdn = [P.psum("pdn%d" % i, [128, 512], F32) for i in range(2)]

    def down_steps(b):
        for j in range(2):
            i = 2 * b + j
            x_ = xs[b][j]
            for g in range(NG):
                for hf in range(2):
                    MM_(P, pdn[hf].t[:, :], mTt[b % 2].t[:, g, j * 128:(j + 1) * 128], Wdn.t[:, g, hf * 512:(hf + 1) * 512], [mT[b % 2][g], Wdn], [pdn[hf]],
                        start=(g == 0), stop=(g == NG - 1), signal=(g == NG - 1))
                if g == NG - 1:
                    for hf in range(2):
                        TT_(P, "dve", x_.t[:, hf * 512:(hf + 1) * 512], pdn[hf].t[:, :], x_.t[:, hf * 512:(hf + 1) * 512], ALU.add, [pdn[hf], x_], [x_])
                    P.dma("sp", scr.x2.t[i * 128:(i + 1) * 128, :], x_.t[:, :], x_, scr.x2)
                yield

    def up_block(b, dn):
        for g in range(NG):
            if b + 1 < NB:
                if g == 0:
                    s1l(b + 1)
                if g in (1, 3, 5, 7, 9, 11):
                    q = (g - 1) // 2
                    s1n(b + 1, q // 3, q % 3)
                if g == 14:
                    s1b(b + 1)
            up(b, g)
            if dn is not None:
                for _ in range(2):
                    next(dn, None)

    s1a(0); s1b(0)
    up_block(0, None)
    for b in range(NB):
        dn = down_steps(b)
        if b + 1 < NB:
            up_block(b + 1, dn)
        for _ in dn:
            pass
    P.barrier()
    P.emit_phase()
    P.stack.close()
    scr.keep.close()


def phase6(P, c, T, scr):
    NT = T // 128
    P.stack = contextlib.ExitStack()
    P.cneg = make_cneg(P)
    ident = P.sbuf("ident", [128, 128], BF16)
    P.dma("pool", ident.t[:, :], c.cst.t[:, 0, :], c.cst, ident)
    g3 = P.sbuf("g3c", [128, 8], F32); load_cols(P, "sp", g3, c.ln3_g, D, 8)
    Wpg = P.sbuf("Wpg", [128, 8, D], BF16)
    Wple = P.sbuf("Wple", [128, 2, D], BF16)
    P.dma("pool", Wpg.t[:, :, :], c.w_pg.t.rearrange("(c p) n -> p c n", p=128), c.w_pg, Wpg)
    for dc in range(8):
        P.op("act", lambda e, dc=dc: e.activation(out=Wpg.t[:, dc, :], in_=Wpg.t[:, dc, :], func=AF.Copy, scale=g3.t[:, dc:dc + 1]),
             reads=[Wpg, g3], writes=[Wpg])
    P.dma("pool", Wple.t[:, :, :], c.w_ple.t.rearrange("(c p) n -> p c n", p=128), c.w_ple, Wple)
    lnf = P.sbuf("lnf", [128, D], F32)
    P.dma("sp", lnf.t[:, :], bc(c.lnf_g, D), c.lnf_g, lnf)
    bpg = P.sbuf("bpg", [128, D], BF16)
    ones = P.sbuf("ones", [128, 128], BF16)
    P.op("pool", lambda e: e.memset(bpg.t[:, :], 0.0), writes=[bpg])
    P.op("pool", lambda e: e.memset(ones.t[:, :], 0.0), writes=[ones])
    P.op("pool", lambda e: e.memset(ones.t[0:1, :], 1.0), writes=[ones])
    P.dma("pool", bpg.t[0:1, :], c.b_pg.t[0:D].rearrange("(o n) -> o n", o=1), c.b_pg, bpg)
    X = Ring([P.sbuf("X%d" % i, [128, D], F32) for i in range(8)])
    xn = Ring([P.sbuf("xn%d" % i, [128, D], BF16) for i in range(4)])
    ss = Ring([P.sbuf("ss%d" % i, [128, 1], F32) for i in range(10)])
    rs = Ring([P.sbuf("rs%d" % i, [128, 1], F32) for i in range(10)])
    h3T = Ring([P.sbuf("h3T%d" % i, [128, 8, 128], BF16) for i in range(3)])
    pt = Ring([P.sbuf("ptile%d" % i, [128, PLE], F32) for i in range(8)])
    pb_ = Ring([P.sbuf("ptb%d" % i, [128, PLE], BF16) for i in range(4)])
    pT = Ring([P.sbuf("pT%d" % i, [128, 2, 128], BF16) for i in range(3)])
    sgt = Ring([P.sbuf("sgt%d" % i, [128, 512], F32) for i in range(4)])
    tq = Ring([P.sbuf("tq%d" % i, [128, 512], F32) for i in range(4)])
    junk = P.sbuf("junk", [128, D], BF16)
    pst = Ring([P.psum("pst%d" % i, [128, 8, 128], BF16) for i in range(2)])
    pm = Ring([P.psum("pm%d" % i, [128, 512], F32) for i in range(6)])
    st = {}

    ld = {}

    def sl(i):
        x_ = X.next(); p_ = pt.next()
        P.dma("sp", x_.t[:, :], scr.x2.t[i * 128:(i + 1) * 128, :], scr.x2, x_)
        P.dma("sp", p_.t[:, :], c.p.t[i * 128:(i + 1) * 128, :], c.p, p_)
        ld[i] = (x_, p_)

    def sN(i):
        x_, p_ = ld.pop(i)
        n_ = xn.next(); q_ = pb_.next()
        norm_scale(P, x_, n_, ss.next(), rs.next())
        CP_(P, "act", q_.t[:, :], p_.t[:, :], [p_], [q_])
        st[i] = [x_, n_, q_]

    def sT(i):
        x_, n_, q_ = st[i]
        h_ = h3T.next(); t_ = pT.next()
        ps = pst.next()
        for k in range(8):
            TR_(P, ps.t[:, k, :], n_.t[:, k * 128:(k + 1) * 128], ident.t[:, :], [n_, ident], [ps], signal=(k == 7))
        CP_(P, "dve", h_.t[:, :, :], ps.t[:, :, :], [ps], [h_])
        pp = pst.next()
        for pc in range(2):
            TR_(P, pp.t[:, pc, :], q_.t[:, pc * 128:(pc + 1) * 128], ident.t[:, :], [q_, ident], [pp], signal=(pc == 1))
        CP_(P, "dve", t_.t[:, :, :], pp.t[:, 0:2, :], [pp], [t_])
        st[i] = [x_, h_, t_]

    def sM(i):
        x_, h_, t_ = st[i]
        for hf in range(2):
            cs = slice(hf * 512, (hf + 1) * 512)
            pg_ = pm.next()
            MM_(P, pg_.t[:, :], ones.t[:, :], bpg.t[:, cs], [ones, bpg], [pg_], start=True, stop=False, signal=False)
            for dc in range(8):
                MM_(P, pg_.t[:, :], h_.t[:, dc, :], Wpg.t[:, dc, cs], [h_, Wpg], [pg_], start=False, stop=(dc == 7), signal=(dc == 7))
            s_ = sgt.next(); q_ = tq.next()
            ACT_(P, s_.t[:, :], pg_.t[:, :], AF.Sigmoid, [pg_], [s_])
            pe_ = pm.next()
            for pc in range(2):
                MM_(P, pe_.t[:, :], t_.t[:, pc, :], Wple.t[:, pc, cs], [t_, Wple], [pe_], start=(pc == 0), stop=(pc == 1), signal=(pc == 1))
            TT_(P, "dve", q_.t[:, :], pe_.t[:, :], s_.t[:, :], ALU.mult, [pe_, s_], [q_])
            TT_(P, "pool", x_.t[:, cs], x_.t[:, cs], q_.t[:, :], ALU.add, [x_, q_], [x_])

    def sF(i):
        x_ = st.pop(i)[0]
        s_ = ss.next(); r_ = rs.next()
        P.op("act", lambda e: e.activation(out=junk.t[:, :], in_=x_.t[:, :], func=AF.Square, accum_out=s_.t[:, :]),
             reads=[x_], writes=[junk, s_])
        RSTD_(P, r_, r_.t[:, :], s_, s_.t[:, :], 1.0 / D, 1e-6, P.cneg, P.cneg.t[:, 0:1])
        STT_(P, x_.t[:, :], x_.t[:, :], r_.t[:, 0:1], lnf.t[:, :], ALU.mult, ALU.mult, [x_, r_, lnf], [x_])
        P.dma("sp", c.out.t[i * 128:(i + 1) * 128, :], x_.t[:, :], x_, c.out)

    for i in range(min(5, NT)):
        sl(i)
    for i in range(-2, NT + 1):
        if 5 <= i + 5 < NT:
            sl(i + 5)
        if 0 <= i + 2 < NT:
            sN(i + 2)
        if 0 <= i + 1 < NT:
            sT(i + 1)
        if 0 <= i < NT:
            sM(i)
        if 0 <= i - 1 < NT:
            sF(i - 1)
    P.wait_all("sp", [c.out])
    P.barrier()
    P.emit_phase()
    P.stack.close()


def build_program(T, dbg=False):
    nc = bass.Bass("TRN2", target_bir_lowering=False)
    P = Prog(nc)
    c = declare_io(P, T)
    scr = make_scratch(P, T, dbg=dbg)
    phase1(P, c, T, scr)
    phase23(P, c, T, scr)
    phase4a(P, c, T, scr)
    phase5(P, c, T, scr)
    phase6(P, c, T, scr)
    P.semstack.close()
    return nc


_W1 = ["ln1_g", "w_in", "mix_mu", "w0", "w2", "a0", "a2", "g2", "k_k", "k_a", "lnx_w", "lnx_b", "gate_b", "w_br_rwkv", "w_br_attn",
       "w_o", "ln2_g", "w_ffn_up", "conv_w", "conv_b", "w_ffn_down", "ln3_g", "w_ple", "w_pg", "b_pg"]


def core_inputs(inputs, b, T):
    d = {"x": inputs["x"][b, :T], "p": inputs["p"][0, b, :T]}
    for k in _W1:
        d[k] = inputs[k][0]
    d["r_k"] = np.reshape(inputs["r_k"][0], (RW,))
    d["lnf_g"] = inputs["lnf_g"]
    d["biasT"] = make_biasT(np.asarray(inputs["rel_bias"][0]))
    d["cst"] = make_cst()
    return {k: np.ascontiguousarray(np.asarray(v), dtype=np.float32) for k, v in d.items()}


def kernel(**inputs):
    B, T = inputs["x"].shape[0], inputs["x"].shape[1]
    nc = build_program(T)
    in_maps = [core_inputs(inputs, b, T) for b in range(B)]
    res = run_bass_kernel_spmd(nc, in_maps, core_ids=list(range(B)))
    return np.stack([np.asarray(r["out"], dtype=np.float32) for r in res.results], axis=0)


def interleave(gens):
    active = list(gens)
    while active:
        for item in list(active):
            g, k = item
            for _ in range(k):
                try:
                    next(g)
                except StopIteration:
                    active.remove(item)
                    break


def take(gen, n):
    for _ in range(n):
        try:
            next(gen)
        except StopIteration:
            return
        yield


def phase23(P, c, T, scr):
    NT = T // 128
    P.stack = contextlib.ExitStack()
    P.cneg = make_cneg(P)
    SB = lambda n, sh, dt=F32: P.sbuf(n, sh, dt)
    RG = lambda n, sh, dt=F32, k=2: Ring([P.sbuf("%s%d" % (n, i), sh, dt) for i in range(k)])
    ident = SB("ident", [128, 128], BF16)
    P.dma("pool", ident.t[:, :], c.cst.t[:, 0, :], c.cst, ident)
    I8 = SB("I8", [128, 128], BF16)
    P.dma("pool", I8.t[:, :], c.cst.t[:, 7, :], c.cst, I8)
    cst = SB("cstf", [128, 8, 128])
    P.dma("sp", cst.t[:, :, :], c.cst.t[:, :, :], c.cst, cst)
    wa2 = SB("wa2", [128, RW], BF16)
    P.dma("pool", wa2.t[0:64, :], c.w2.t[:, :], c.w2, wa2)
    P.dma("pool", wa2.t[64:128, :], c.a2.t[:, :], c.a2, wa2)
    g2b = SB("g2b", [128, RW], BF16)
    P.dma("pool", g2b.t[:, :], c.g2.t[:, :], c.g2, g2b)
    bt = {}
    for nm in ["w0", "a0", "k_k", "k_a", "r_k", "lnx_w", "lnx_b"]:
        bt[nm] = SB("b_" + nm, [128, RW])
        P.dma("sp", bt[nm].t[:, :], bc(getattr(c, nm), RW), getattr(c, nm), bt[nm])
    Linc, Lsl, Lsu, cind = cst.t[:, 1, :], cst.t[:, 2, :], cst.t[:, 3, :], cst.t[:, 4, 0:2]
    b3 = lambda ap: ap.unsqueeze(1).to_broadcast([128, 8, 64])
    MU, ML, MUI, I64 = b3(cst.t[:, 5, 0:64]), b3(cst.t[:, 5, 64:128]), b3(cst.t[:, 6, 0:64]), b3(cst.t[:, 6, 64:128])

    pg = Ring([P.psum("pg%d" % i, [128, 512], F32) for i in range(3)])
    pq = Ring([P.psum("pq%d" % i, [128, 512], F32) for i in range(2)])
    pa = Ring([P.psum("pa%d" % i, [128, 512], F32) for i in range(2)])
    pb = Ring([P.psum("pb%d" % i, [128, 512], F32) for i in range(1)])
    v3 = lambda buf: buf.t.rearrange("p (h v) -> p h v", h=8)

    def bfv(buf, inner):
        return buf.t.bitcast(BF16)[:, 0:8 * inner].rearrange("p (h t) -> p h t", h=8)

    rkvt = RG("rkvt", [128, 1536], BF16, 4)
    lo1 = RG("lo1", [128, 128], BF16, 4); lo2 = RG("lo2", [128, 128], BF16, 4)
    names_f = ["t_w", "sg", "t_a", "a_", "e_pos", "e_neg", "e_prev", "e_rel", "kkr", "sq", "kk", "bb", "t1", "k2", "t2"]
    F = {n: SB(n, [128, RW]) for n in names_f}
    Fp = {n: SB("post_" + n, [128, RW]) for n in ["yt", "sq", "yn", "bv"]}
    g_ = RG("g_", [128, RW])
    Bq = {n: SB(n, [128, RW], BF16) for n in ["At", "Bt", "Kt", "Rt"]}
    Bh = RG("Bh", [128, RW], BF16); Kh = RG("Kh", [128, RW], BF16)
    ATs = RG("ATs", [64, 8, 128], BF16); RTs = RG("RTs", [64, 8, 128], BF16)
    BTs = SB("BTs", [64, 8, 128], BF16); KTs = SB("KTs", [64, 8, 128], BF16)
    XA = RG("XA", [128, 8, 64], BF16); XTA = RG("XTA", [128, 8, 64], BF16)
    MakT = SB("MakT", [128, 8, 64], BF16)
    RBT = RG("RBT", [128, 8, 64], BF16); RKT = SB("RKT", [128, 8, 64], BF16)
    TTb = RG("TTb", [128, 8, 64], BF16)
    W1a = RG("W1a", [128, 8, 64]); Ya = RG("Ya", [128, 8, 64])
    W1 = SB("W1", [128, 8, 64], BF16); U = SB("U", [128, 8, 64], BF16)
    STb = RG("STb", [64, 8, 64], BF16); SF = RG("SF", [64, 8, 64], F32)
    tmpS = SB("tmpS", [64, 8, 64]); tmp2 = SB("tmp2", [64, 8, 64])
    pc = RG("pc", [64, 8, 2])
    sm = {n: SB(n, [128, 8]) for n in ["ssq", "nrm", "rn", "s1", "s2", "mean", "msq", "var", "rstd"]}
    bcf = RG("bcf", [128, 8])
    oa = SB("oa", [128, RW], BF16)
    oaT = RG("oaT", [128, 4, 128], BF16)
    h3 = lambda ap: ap.rearrange("p (h v) -> p h v", h=8)
    bl = lambda ap: ap.unsqueeze(2).to_broadcast([128, 8, 64])
    state = {}
    state["b"] = STb.next(); state["f"] = SF.next()
    P.op("pool", lambda e: e.memset(state["b"].t[:, :, :], 0.0), writes=[state["b"]])
    P.op("pool", lambda e: e.memset(state["f"].t[:, :, :], 0.0), writes=[state["f"]])
    tl = {}

    def loads(i):
        tok = slice(i * 128, (i + 1) * 128)
        rk = rkvt.next(); l1 = lo1.next(); l2 = lo2.next()
        P.dma("sp", rk.t[:, :], scr.rkv.t[tok, :], scr.rkv, rk)
        P.dma("sp", l1.t[:, :], scr.lora.t[0:128, tok], scr.lora, l1)
        P.dma("sp", l2.t[:, :], scr.lora.t[128:256, tok], scr.lora, l2)
        tl[i] = dict(rk=rk, l1=l1, l2=l2)

    def genA(i):
        t = tl[i]
        rk, l1, l2 = t["rk"], t["l1"], t["l2"]
        r_, k_ = rk.t[:, 0:512], rk.t[:, 512:1024]
        p_w = pg.next(); p_a = pg.next()
        MM_(P, p_w.t[:, :], l1.t[0:64, :], wa2.t[0:64, :], [l1, wa2], [p_w])
        MM_(P, p_a.t[:, :], l1.t[64:128, :], wa2.t[64:128, :], [l1, wa2], [p_a])
        f = {n: F[n].t[:, :] for n in names_f}
        TT_(P, "dve", f["t_w"], p_w.t[:, :], bt["w0"].t[:, :], ALU.add, [p_w, bt["w0"]], [F["t_w"]])
        ACT_(P, f["sg"], f["t_w"], AF.Tanh, [F["t_w"]], [F["sg"]], scale=0.5)
        TS_(P, "pool", f["sg"], f["sg"], 0.5, 0.5, ALU.mult, ALU.add, [F["sg"]], [F["sg"]])
        TT_(P, "dve", f["t_a"], p_a.t[:, :], bt["a0"].t[:, :], ALU.add, [p_a, bt["a0"]], [F["t_a"]])
        ACT_(P, f["a_"], f["t_a"], AF.Tanh, [F["t_a"]], [F["a_"]], scale=0.5)
        TS_(P, "pool", f["a_"], f["a_"], 0.5, 0.5, ALU.mult, ALU.add, [F["a_"]], [F["a_"]])
        yield
        p_g = pg.next()
        MM_(P, p_g.t[:, :], l2.t[:, :], g2b.t[:, :], [l2, g2b], [p_g])
        gq = g_.next(); t["gq"] = gq
        CP_(P, "act", gq.t[:, :], p_g.t[:, :], [p_g], [gq])
        yield
        p1 = pg.next()
        MM_(P, p1.t[:, :], Linc, f["sg"], [cst, F["sg"]], [p1])
        ACT_(P, f["e_pos"], p1.t[:, :], AF.Exp, [p1], [F["e_pos"]], scale=CDEC)
        ACT_(P, f["e_neg"], p1.t[:, :], AF.Exp, [p1], [F["e_neg"]], scale=-CDEC)
        yield
        p2 = pg.next()
        MM_(P, p2.t[:, :], Lsl, f["sg"], [cst, F["sg"]], [p2])
        ACT_(P, f["e_prev"], p2.t[:, :], AF.Exp, [p2], [F["e_prev"]], scale=CDEC)
        yield
        p3 = pg.next()
        MM_(P, p3.t[:, :], Lsu, f["sg"], [cst, F["sg"]], [p3])
        ACT_(P, f["e_rel"], p3.t[:, :], AF.Exp, [p3], [F["e_rel"]], scale=CDEC)
        yield
        p4 = pg.next()
        for h in range(8):
            MM_(P, p4.t[0:64, 2 * h:2 * h + 2], F["sg"].t[:, h * 64:(h + 1) * 64], cind, [F["sg"], cst], [p4], signal=(h == 7))
        pcq = pc.next(); t["pcq"] = pcq
        ACT_(P, pcq.t[:, :, :], p4.t[0:64, 0:16].rearrange("p (h c) -> p h c", h=8), AF.Exp, [p4], [pcq], scale=CDEC)
        yield
        TT_(P, "dve", f["kkr"], k_, bt["k_k"].t[:, :], ALU.mult, [rk, bt["k_k"]], [F["kkr"]])
        TT_(P, "pool", f["sq"], f["kkr"], f["kkr"], ALU.mult, [F["kkr"]], [F["sq"]])
        RED_(P, sm["ssq"].t[:, :], h3(f["sq"]), [F["sq"]], [sm["ssq"]])
        yield
        TS_(P, "dve", sm["rn"].t[:, :], sm["ssq"].t[:, :], 1e-24, None, ALU.max, None, [sm["ssq"]], [sm["rn"]])
        P.op("pool", lambda e: e.tensor_tensor(out=sm["rn"].t[:, :], in0=sm["rn"].t[:, :], in1=P.cneg.t[:, :], op=ALU.pow),
             reads=[sm["rn"], P.cneg], writes=[sm["rn"]])
        TT_(P, "dve", h3(f["kk"]), h3(f["kkr"]), bl(sm["rn"].t[:, :]), ALU.mult, [F["kkr"], sm["rn"]], [F["kk"]])
        yield
        TT_(P, "pool", f["bb"], f["kk"], f["a_"], ALU.mult, [F["kk"], F["a_"]], [F["bb"]])
        STT_(P, f["t1"], f["a_"], -1.0, bt["k_a"].t[:, :], ALU.add, ALU.mult, [F["a_"], bt["k_a"]], [F["t1"]])
        STT_(P, f["k2"], f["t1"], 1.0, k_, ALU.add, ALU.mult, [F["t1"], rk], [F["k2"]])
        yield
        STT_(P, Bq["At"].t[:, :], f["kk"], -1.0, f["e_prev"], ALU.mult, ALU.mult, [F["kk"], F["e_prev"]], [Bq["At"]])
        TT_(P, "dve", Bq["Bt"].t[:, :], f["bb"], f["e_neg"], ALU.mult, [F["bb"], F["e_neg"]], [Bq["Bt"]])
        yield
        TT_(P, "dve", Bq["Kt"].t[:, :], f["k2"], f["e_neg"], ALU.mult, [F["k2"], F["e_neg"]], [Bq["Kt"]])
        TT_(P, "pool", Bq["Rt"].t[:, :], r_, f["e_pos"], ALU.mult, [rk, F["e_pos"]], [Bq["Rt"]])
        yield
        bh = Bh.next(); kh = Kh.next(); t["bh"] = bh; t["kh"] = kh
        TT_(P, "pool", bh.t[:, :], f["bb"], f["e_rel"], ALU.mult, [F["bb"], F["e_rel"]], [bh])
        TT_(P, "pool", kh.t[:, :], f["k2"], f["e_rel"], ALU.mult, [F["k2"], F["e_rel"]], [kh])
        yield
        TT_(P, "pool", f["t2"], r_, bt["r_k"].t[:, :], ALU.mult, [rk, bt["r_k"]], [F["t2"]])
        TT_(P, "pool", f["t2"], f["t2"], f["k2"], ALU.mult, [F["t2"], F["k2"]], [F["t2"]])
        bq = bcf.next(); t["bq"] = bq
        RED_(P, bq.t[:, :], h3(f["t2"]), [F["t2"]], [bq])
        yield
        ats = ATs.next(); rts = RTs.next(); t["ats"] = ats; t["rts"] = rts
        for (src, dst, ev) in [("At", ats, "act"), ("Bt", BTs, "dve"), ("Kt", KTs, "act"), ("Rt", rts, "dve")]:
            ps = pg.next()
            pv = bfv(ps, 128)
            for h in range(8):
                TR_(P, pv[0:64, h, :], Bq[src].t[:, h * 64:(h + 1) * 64], ident.t[:, :], [Bq[src], ident], [ps], signal=(h == 7))
            CP_(P, ev, dst.t[:, :, :], pv[0:64, :, :], [ps], [dst])
            yield

        def prod(lhs, rhs, mask, dst, eng="dve"):
            ps = pg.next()
            for cc in range(2):
                cols = slice(cc * 64, (cc + 1) * 64)
                for h in range(8):
                    MM_(P, v3(ps)[cc * 64:(cc + 1) * 64, h, :], lhs.t[:, h, cols], rhs.t[:, h, cols], [lhs, rhs], [ps],
                        signal=(cc == 1 and h == 7))
            TT_(P, eng, dst.t[:, :, :], v3(ps), mask, ALU.mult, [ps, cst], [dst])
        xt_ = XTA.next(); x_ = XA.next()
        rbt = RBT.next(); t["rbt"] = rbt
        prod(BTs, ats, MU, xt_); yield
        prod(ats, BTs, ML, x_); yield
        prod(KTs, ats, MU, MakT); yield
        prod(BTs, rts, MUI, rbt); yield
        prod(KTs, rts, MUI, RKT); yield
        tt = TTb.next(); t["tt"] = tt
        TT_(P, "dve", tt.t[:, :, :], xt_.t[:, :, :], I64, ALU.add, [xt_, cst], [tt])
        for lev in range(1, 6):
            xn_ = XA.next()
            xtn_ = XTA.next() if lev < 5 else None
            for cc in range(2):
                hs = slice(cc * 64, (cc + 1) * 64)
                px = pg.next()
                for h in range(8):
                    MM_(P, v3(px)[hs, h, :], xt_.t[hs, h, :], x_.t[hs, h, :], [xt_, x_], [px], signal=(h == 7))
                CP_(P, "act", xn_.t[hs, :, :], v3(px)[hs, :, :], [px], [xn_])
                yield
                if lev < 5:
                    pxt = pg.next()
                    for h in range(8):
                        MM_(P, v3(pxt)[hs, h, :], x_.t[hs, h, :], xt_.t[hs, h, :], [xt_, x_], [pxt], signal=(h == 7))
                    CP_(P, "act", xtn_.t[hs, :, :], v3(pxt)[hs, :, :], [pxt], [xtn_])
                    yield
            x_ = xn_
            if lev < 5:
                xt_ = xtn_
            for cc in range(2):
                hs = slice(cc * 64, (cc + 1) * 64)
                pt = pg.next()
                for h in range(8):
                    MM_(P, v3(pt)[hs, h, :], x_.t[hs, h, :], tt.t[hs, h, :], [x_, tt], [pt], signal=(h == 7))
                TT_(P, "dve", tt.t[hs, :, :], v3(pt)[hs, :, :], tt.t[hs, :, :], ALU.add, [pt, tt], [tt])
                yield
        w1a = W1a.next(); ya = Ya.next(); t["w1a"] = w1a; t["ya"] = ya
        for cc in range(2):
            hs = slice(cc * 64, (cc + 1) * 64)
            ps = pg.next()
            for h in range(8):
                MM_(P, v3(ps)[hs, h, :], MakT.t[hs, h, :], rk.t[hs, 1024 + h * 64:1024 + (h + 1) * 64], [MakT, rk], [ps], signal=(h == 7))
            CP_(P, "act", w1a.t[hs, :, :], v3(ps)[hs, :, :], [ps], [w1a])
            yield
            ps = pg.next()
            for h in range(8):
                MM_(P, v3(ps)[hs, h, :], RKT.t[hs, h, :], rk.t[hs, 1024 + h * 64:1024 + (h + 1) * 64], [RKT, rk], [ps], signal=(h == 7))
            CP_(P, "act", ya.t[hs, :, :], v3(ps)[hs, :, :], [ps], [ya])
            yield

    def genB(i):
        t = tl[i]
        rk, ats, rts, bh, kh, rbt, tt, w1a, ya, pcq, gq, bq = (t[k] for k in ["rk", "ats", "rts", "bh", "kh", "rbt", "tt", "w1a", "ya", "pcq", "gq", "bq"])
        tok = slice(i * 128, (i + 1) * 128)
        v_ = rk.t[:, 1024:1536]
        fp = {n: Fp[n].t[:, :] for n in Fp}
        for cc in range(2):
            hs = slice(cc * 64, (cc + 1) * 64)
            cols = slice(cc * 64, (cc + 1) * 64)
            st_b, st_f = state["b"], state["f"]
            pk = pq.next()
            for h in range(8):
                MM_(P, v3(pk)[0:64, h, :], kh.t[hs, h * 64:(h + 1) * 64], rk.t[hs, 1024 + h * 64:1024 + (h + 1) * 64], [kh, rk], [pk], signal=(h == 7))
            TT_(P, "dve", tmpS.t[:, :, :], st_f.t[:, :, :], pcq.t[:, :, cc:cc + 1].to_broadcast([64, 8, 64]), ALU.mult, [st_f, pcq], [tmpS])
            TT_(P, "dve", tmp2.t[:, :, :], v3(pk)[0:64, :, :], tmpS.t[:, :, :], ALU.add, [pk, tmpS], [tmp2])
            yield
            pw = pq.next()
            for h in range(8):
                MM_(P, v3(pw)[hs, h, :], ats.t[:, h, cols], st_b.t[:, h, :], [ats, st_b], [pw], signal=(h == 7))
            TT_(P, "dve", W1.t[hs, :, :], v3(pw)[hs, :, :], w1a.t[hs, :, :], ALU.add, [pw, w1a], [W1])
            yield
            prs = pq.next()
            for h in range(8):
                MM_(P, v3(prs)[hs, h, :], rts.t[:, h, cols], st_b.t[:, h, :], [rts, st_b], [prs], signal=(h == 7))
            TT_(P, "dve", h3(fp["yt"])[hs, :, :], v3(prs)[hs, :, :], ya.t[hs, :, :], ALU.add, [prs, ya], [Fp["yt"]])
            yield
            pu = pq.next()
            for h in range(8):
                MM_(P, v3(pu)[hs, h, :], tt.t[hs, h, :], W1.t[hs, h, :], [tt, W1], [pu], signal=(h == 7))
            CP_(P, "act", U.t[hs, :, :], v3(pu)[hs, :, :], [pu], [U])
            yield
            psn = pq.next()
            for h in range(8):
                MM_(P, v3(psn)[0:64, h, :], bh.t[hs, h * 64:(h + 1) * 64], U.t[hs, h, :], [bh, U], [psn], signal=(h == 7))
            nb = STb.next(); nf = SF.next()
            TT_(P, "dve", nb.t[:, :, :], v3(psn)[0:64, :, :], tmp2.t[:, :, :], ALU.add, [psn, tmp2], [nb])
            TT_(P, "dve", nf.t[:, :, :], v3(psn)[0:64, :, :], tmp2.t[:, :, :], ALU.add, [psn, tmp2], [nf])
            state["b"], state["f"] = nb, nf
            yield
            py = pq.next()
            for h in range(8):
                MM_(P, v3(py)[hs, h, :], rbt.t[hs, h, :], U.t[hs, h, :], [rbt, U], [py], signal=(h == 7))
            TT_(P, "dve", h3(fp["yt"])[hs, :, :], v3(py)[hs, :, :], h3(fp["yt"])[hs, :, :], ALU.add, [py, Fp["yt"]], [Fp["yt"]])
            yield
        yt3 = h3(fp["yt"]); yn3 = h3(fp["yn"])
        RED_(P, sm["s1"].t[:, :], yt3, [Fp["yt"]], [sm["s1"]])
        TT_(P, "pool", fp["sq"], fp["yt"], fp["yt"], ALU.mult, [Fp["yt"]], [Fp["sq"]])
        RED_(P, sm["s2"].t[:, :], h3(fp["sq"]), [Fp["sq"]], [sm["s2"]])
        yield
        TS_(P, "dve", sm["mean"].t[:, :], sm["s1"].t[:, :], 1.0 / 64, None, ALU.mult, None, [sm["s1"]], [sm["mean"]])
        TT_(P, "dve", sm["msq"].t[:, :], sm["mean"].t[:, :], sm["mean"].t[:, :], ALU.mult, [sm["mean"]], [sm["msq"]])
        STT_(P, sm["var"].t[:, :], sm["s2"].t[:, :], 1.0 / 64, sm["msq"].t[:, :], ALU.mult, ALU.subtract, [sm["s2"], sm["msq"]], [sm["var"]])
        RSTD_(P, sm["rstd"], sm["rstd"].t[:, :], sm["var"], sm["var"].t[:, :], 1.0, 64e-5, P.cneg, P.cneg.t[:, :])
        yield
        TT_(P, "dve", yn3, yt3, bl(sm["mean"].t[:, :]), ALU.subtract, [Fp["yt"], sm["mean"]], [Fp["yn"]])
        TT_(P, "dve", yn3, yn3, bl(sm["rstd"].t[:, :]), ALU.mult, [Fp["yn"], sm["rstd"]], [Fp["yn"]])
        yield
        TT_(P, "pool", fp["yn"], fp["yn"], bt["lnx_w"].t[:, :], ALU.mult, [Fp["yn"], bt["lnx_w"]], [Fp["yn"]])
        TT_(P, "pool", fp["yn"], fp["yn"], bt["lnx_b"].t[:, :], ALU.add, [Fp["yn"], bt["lnx_b"]], [Fp["yn"]])
        TT_(P, "dve", h3(fp["bv"]), h3(v_), bl(bq.t[:, :]), ALU.mult, [rk, bq], [Fp["bv"]])
        yield
        TT_(P, "pool", fp["yn"], fp["yn"], fp["bv"], ALU.add, [Fp["yn"], Fp["bv"]], [Fp["yn"]])
        TT_(P, "dve", oa.t[:, :], fp["yn"], gq.t[:, :], ALU.mult, [Fp["yn"], gq], [oa])
        yield
        ps = pq.next()
        pv = ps.t.bitcast(BF16)[:, 0:512].rearrange("p (c t) -> p c t", c=4)
        for fc in range(4):
            TR_(P, pv[:, fc, :], oa.t[:, fc * 128:(fc + 1) * 128], ident.t[:, :], [oa, ident], [ps], signal=(fc == 3))
        ot_ = oaT.next()
        CP_(P, "act", ot_.t[:, :, :], pv, [ps], [ot_])
        P.dma("pool", scr.oaT.t[:, tok].rearrange("(c p) t -> p c t", p=128), ot_.t[:, :, :], ot_, scr.oaT)
        yield

    kTh = SB("kTh", [128, T], BF16); qTh = SB("qTh", [128, T], BF16)
    Vh = RG("Vh", [128, NT, 65], BF16)
    bT = RG("bT", [128, 640], BF16)
    for b in (kTh, qTh):
        P.op("pool", lambda e, b=b: e.memset(b.t[64:128, :], 0.0), writes=[b])
    for b in Vh.items:
        P.op("pool", lambda e, b=b: e.memset(b.t[:, :, 64:65], 1.0), writes=[b])
    PT = [SB("PT%d" % i, [128, 640], BF16) for i in range(6)]
    ob = SB("ob", [128, NT, 512], BF16)
    rc = RG("rc", [128, 1])
    obT = RG("obT", [128, 4, 128], BF16)

    def genC():
        for h in range(NH):
            kt = kTh; qt = qTh; vh = Vh.next(); bias = bT.next()
            P.dma("sp", kt.t[0:64, :], scr.kT.t[h * 64:(h + 1) * 64, :], scr.kT, kt)
            P.dma("sp", qt.t[0:64, :], scr.qT.t[h * 64:(h + 1) * 64, :], scr.qT, qt)
            P.dma("sp", vh.t[:, :, 0:64], scr.vA.t[:, h * 64:(h + 1) * 64].rearrange("(m p) d -> p m d", p=128), scr.vA, vh)
            P.dma("pool", bias.t[:, :], c.biasT.t[h, :, :], c.biasT, bias)
            for m in range(NT):
                W = min(640, T - 128 * m)
                pt = PT[m % 6]
                for (c0, c1) in [(0, min(W, 512)), (512, W)]:
                    if c1 <= c0:
                        continue
                    ps = pa.next()
                    n = c1 - c0
                    MM_(P, ps.t[:, 0:n], I8.t[:, :], bias.t[:, c0:c1], [I8, bias], [ps], start=True, stop=False, signal=False)
                    MM_(P, ps.t[:, 0:n], kt.t[:, m * 128:(m + 1) * 128], qt.t[:, m * 128 + c0:m * 128 + c1], [kt, qt], [ps], start=False, stop=True)
                    ACT_(P, pt.t[:, c0:c1], ps.t[:, 0:n], AF.Exp, [ps], [pt], scale=0.125)
                yield
                pv = pb.next()
                cq = 2 * m
                ms = list(range(max(0, (cq - 8) // 2), cq // 2 + 1))
                for mi, mp in enumerate(ms):
                    off = (cq - 2 * mp) * 64
                    MM_(P, pv.t[:, 0:65], PT[mp % 6].t[:, off:off + 128], vh.t[:, mp, :], [PT[mp % 6], vh], [pv],
                        start=(mi == 0), stop=(mi == len(ms) - 1), signal=(mi == len(ms) - 1))
                r_ = rc.next()
                P.op("dve", lambda e, r_=r_, pv=pv: e.reciprocal(out=r_.t[:, :], in_=pv.t[:, 64:65]), reads=[pv], writes=[r_])
                ACT_(P, ob.t[:, m, h * 64:(h + 1) * 64], pv.t[:, 0:64], AF.Copy, [pv, r_], [ob], scale=r_.t[:, :])
                yield
        for m in range(NT):
            ps = pa.next()
            pvw = ps.t.bitcast(BF16)[:, 0:512].rearrange("p (c t) -> p c t", c=4)
            for fc in range(4):
                TR_(P, pvw[:, fc, :], ob.t[:, m, fc * 128:(fc + 1) * 128], ident.t[:, :], [ob, ident], [ps], signal=(fc == 3))
            o_ = obT.next()
            CP_(P, "dve", o_.t[:, :, :], pvw, [ps], [o_])
            P.dma("pool", scr.obT.t[:, m * 128:(m + 1) * 128].rearrange("(c p) t -> p c t", p=128), o_.t[:, :, :], o_, scr.obT)
            yield

    gC = genC()
    nC = (NH * NT * 2 + NT + NT - 1) // NT + 1
    loads(0)
    if NT > 1:
        loads(1)
    for _ in genA(0):
        pass
    for i in range(NT):
        if i + 2 < NT:
            loads(i + 2)
        gens = [(genB(i), 1)]
        if i + 1 < NT:
            gens.append((genA(i + 1), 3))
        gens.append((take(gC, nC), 1))
        interleave(gens)
    for _ in gC:
        pass
    P.barrier()
    P.emit_phase()
    P.stack.close()
```
